# Optimizing a Trainium2 kernel written in Bass

```python
import math
import jax
import jax.numpy as jnp
from jax import lax
import numpy as np

D_MODEL = 1024
BATCH = 8
SEQ = 2048
DEPTH = 2

HEAD_DIM = 64
DSA_HEADS = 16
DSA_KV_DIM = 64
IDX_HEADS = 8
IDX_DIM = 32
DSA_TOPK = 256
NSA_HEADS = 16
NSA_KV_HEADS = 2
NSA_GROUP = NSA_HEADS // NSA_KV_HEADS
CMP_BLOCK = 32
CMP_STRIDE = 16
SEL_BLOCK = 64
SEL_TOPN = 4
WINDOW = 512
SSD_INNER = 2 * D_MODEL
SSD_HEAD_DIM = 64
SSD_HEADS = SSD_INNER // SSD_HEAD_DIM
SSD_GROUPS = 4
SSD_HPG = SSD_HEADS // SSD_GROUPS
SSD_STATE = 128
SSD_CONV_DIM = SSD_INNER + 2 * SSD_GROUPS * SSD_STATE
CONV_WIDTH = 4
SSD_CHUNK = 128
D_FF = 4 * D_MODEL
Q_BLOCK = 128
EPS = 1e-6
NEG = -1e30
IN_SPLITS = (DSA_HEADS * HEAD_DIM, DSA_KV_DIM, DSA_KV_DIM, IDX_HEADS * IDX_DIM, IDX_DIM, IDX_HEADS,
             NSA_HEADS * HEAD_DIM, 6 * NSA_KV_HEADS * HEAD_DIM, 3 * NSA_HEADS,
             SSD_INNER, SSD_CONV_DIM, SSD_HEADS, 3 * D_MODEL)
D_IN = sum(IN_SPLITS)

kernel_name = 'hybrid_dsa_nsa_ssd_gated_block'


def rms_norm(x, g):
    xf = x.astype(jnp.float32)
    y = xf * lax.rsqrt(jnp.mean(xf * xf, axis=-1, keepdims=True) + EPS)
    return (y * g).astype(x.dtype)


def _to_blocks(a):
    b, s = a.shape[:2]
    return jnp.moveaxis(a.reshape((b, s // Q_BLOCK, Q_BLOCK) + a.shape[2:]), 1, 0)


def _from_blocks(a):
    nb, b = a.shape[:2]
    return jnp.moveaxis(a, 0, 1).reshape((b, nb * a.shape[2]) + a.shape[3:])


def dsa_mixer(q, k, v, iq, ik, iw):
    seq = q.shape[1]
    n_sel = min(DSA_TOPK, seq // 4)
    key_pos = jnp.arange(seq)
    scale = HEAD_DIM ** -0.5
    gather = jax.vmap(lambda a, i: a[i])

    def block(args):
        qb, iqb, iwb, start = args
        t = start + jnp.arange(Q_BLOCK)
        idx = jnp.einsum('bqhd,bsd->bqhs', iqb, ik)
        idx = jnp.einsum('bqh,bqhs->bqs', iwb, jax.nn.relu(idx)).astype(jnp.float32)
        idx = jnp.where((key_pos[None, :] <= t[:, None])[None], idx, -jnp.inf)
        _, sel = lax.top_k(idx, n_sel)
        k_sel = gather(k, sel)
        v_sel = gather(v, sel)
        s = jnp.einsum('bqhd,bqkd->bqhk', qb, k_sel).astype(jnp.float32) * scale
        s = jnp.where((sel <= t[None, :, None])[:, :, None, :], s, NEG)
        p = jax.nn.softmax(s, axis=-1).astype(v.dtype)
        return jnp.einsum('bqhk,bqkd->bqhd', p, v_sel)

    starts = jnp.arange(seq // Q_BLOCK) * Q_BLOCK
    out = lax.map(block, (_to_blocks(q), _to_blocks(iq), _to_blocks(iw), starts))
    return _from_blocks(out)


def nsa_mixer(q, kc, vc, ks, vs, kw, vw, gates, k_norm_g, cmp_pos, cmp_w):
    bsz, seq = q.shape[:2]
    scale = HEAD_DIM ** -0.5
    n_cmp = (seq - CMP_BLOCK) // CMP_STRIDE + 1
    win_idx = jnp.arange(n_cmp)[:, None] * CMP_STRIDE + jnp.arange(CMP_BLOCK)[None, :]

    def compress(a, pos, w):
        blocks = a[:, win_idx] + pos[None, None, :, None, :]
        return jnp.einsum('bnlgd,lde->bnge', blocks, w)

    k_cmp = rms_norm(compress(kc, cmp_pos[0], cmp_w[0]), k_norm_g[0])
    v_cmp = compress(vc, cmp_pos[1], cmp_w[1])
    cmp_start = jnp.arange(n_cmp) * CMP_STRIDE
    cmp_end = cmp_start + CMP_BLOCK - 1
    n_blk = seq // SEL_BLOCK
    n_sel = min(SEL_TOPN, n_blk)
    blk_start = jnp.arange(n_blk) * SEL_BLOCK
    cover = ((cmp_start[:, None] < blk_start[None, :] + SEL_BLOCK)
             & (cmp_start[:, None] + CMP_BLOCK > blk_start[None, :])).astype(jnp.float32)
    ks = rms_norm(ks, k_norm_g[1])
    ks_blocks = ks.reshape(bsz, n_blk, SEL_BLOCK, NSA_KV_HEADS, HEAD_DIM).transpose(0, 3, 1, 2, 4)
    vs_blocks = vs.reshape(bsz, n_blk, SEL_BLOCK, NSA_KV_HEADS, HEAD_DIM).transpose(0, 3, 1, 2, 4)
    gather_blocks = jax.vmap(jax.vmap(lambda blk, i: blk[i]))
    kw = rms_norm(kw, k_norm_g[2])
    pad = ((0, 0), (WINDOW, 0), (0, 0), (0, 0))
    kw_pad = jnp.pad(kw, pad)
    vw_pad = jnp.pad(vw, pad)
    n_slc_keys = n_sel * SEL_BLOCK

    def block(args):
        qb, gb, start = args
        t = start + jnp.arange(Q_BLOCK)
        s = jnp.einsum('bqghd,bngd->bqghn', qb, k_cmp).astype(jnp.float32) * scale
        vis = (cmp_end[None, :] <= t[:, None])[None, :, None, None, :]
        s = jnp.where(vis, s, NEG)
        p_cmp = jax.nn.softmax(s, axis=-1) * vis
        o_cmp = jnp.einsum('bqghn,bngd->bqghd', p_cmp.astype(v_cmp.dtype), v_cmp)
        imp = jnp.einsum('bqghn,nj->bqgj', p_cmp, cover)
        cur = t // SEL_BLOCK
        j = jnp.arange(n_blk)
        forced = ((j[None, :] == cur[:, None]) | (j[None, :] == 0))[None, :, None, :]
        future = (j[None, :] > cur[:, None])[None, :, None, :]
        imp = jnp.where(forced, jnp.inf, jnp.where(future, -jnp.inf, imp))
        _, sel = lax.top_k(imp, n_sel)
        sel_t = sel.transpose(0, 2, 1, 3)
        bq = qb.shape[0]
        k_sel = gather_blocks(ks_blocks, sel_t).reshape(bq, NSA_KV_HEADS, Q_BLOCK, n_slc_keys, HEAD_DIM)
        v_sel = gather_blocks(vs_blocks, sel_t).reshape(bq, NSA_KV_HEADS, Q_BLOCK, n_slc_keys, HEAD_DIM)
        pos = (sel_t[..., None] * SEL_BLOCK + jnp.arange(SEL_BLOCK)).reshape(bq, NSA_KV_HEADS, Q_BLOCK, n_slc_keys)
        ok = (pos <= t[None, None, :, None]).transpose(0, 2, 1, 3)[:, :, :, None, :]
        s = jnp.einsum('bqghd,bgqkd->bqghk', qb, k_sel).astype(jnp.float32) * scale
        s = jnp.where(ok, s, NEG)
        p = jax.nn.softmax(s, axis=-1).astype(v_sel.dtype)
        o_slc = jnp.einsum('bqghk,bgqkd->bqghd', p, v_sel)
        kwb = lax.dynamic_slice_in_dim(kw_pad, start, WINDOW + Q_BLOCK, axis=1)
        vwb = lax.dynamic_slice_in_dim(vw_pad, start, WINDOW + Q_BLOCK, axis=1)
        pos_w = start - WINDOW + jnp.arange(WINDOW + Q_BLOCK)
        ok_w = ((pos_w[None, :] <= t[:, None]) & (pos_w[None, :] > t[:, None] - WINDOW)
                & (pos_w[None, :] >= 0))[None, :, None, None, :]
        s = jnp.einsum('bqghd,bkgd->bqghk', qb, kwb).astype(jnp.float32) * scale
        s = jnp.where(ok_w, s, NEG)
        p = jax.nn.softmax(s, axis=-1).astype(vwb.dtype)
        o_win = jnp.einsum('bqghk,bkgd->bqghd', p, vwb)
        return gb[..., 0:1] * o_cmp + gb[..., 1:2] * o_slc + gb[..., 2:3] * o_win

    starts = jnp.arange(seq // Q_BLOCK) * Q_BLOCK
    out = lax.map(block, (_to_blocks(q), _to_blocks(gates), starts))
    return _from_blocks(out)


def causal_conv(x, w, b):
    out = lax.conv_general_dilated(x, w[:, None, :], window_strides=(1,), padding=[(CONV_WIDTH - 1, 0)],
                                   dimension_numbers=('NWC', 'WIO', 'NWC'), feature_group_count=x.shape[-1])
    return out + b


def ssd_mixer(z, xbc, dt, conv_w, conv_b, dt_bias, a_log, d_skip, norm_g):
    bsz, seq = z.shape[:2]
    nc = seq // SSD_CHUNK
    xbc = jax.nn.silu(causal_conv(xbc, conv_w, conv_b))
    gn = SSD_GROUPS * SSD_STATE
    xs = xbc[..., :SSD_INNER].reshape(bsz, nc, SSD_CHUNK, SSD_GROUPS, SSD_HPG, SSD_HEAD_DIM)
    bm = xbc[..., SSD_INNER:SSD_INNER + gn].reshape(bsz, nc, SSD_CHUNK, SSD_GROUPS, SSD_STATE)
    cm = xbc[..., SSD_INNER + gn:].reshape(bsz, nc, SSD_CHUNK, SSD_GROUPS, SSD_STATE)
    dtc = jax.nn.softplus(dt.astype(jnp.float32) + dt_bias).reshape(bsz, nc, SSD_CHUNK, SSD_GROUPS, SSD_HPG)
    a = -jnp.exp(a_log.astype(jnp.float32)).reshape(SSD_GROUPS, SSD_HPG)
    a_cum = jnp.cumsum(dtc * a, axis=2)
    seg = a_cum[:, :, :, None] - a_cum[:, :, None, :]
    tril = jnp.tril(jnp.ones((SSD_CHUNK, SSD_CHUNK), dtype=bool))[None, None, :, :, None, None]
    lmat = jnp.exp(jnp.where(tril, seg, -jnp.inf))
    cb = jnp.einsum('bcign,bcjgn->bcijg', cm, bm)
    y_diag = jnp.einsum('bcijg,bcijgh,bcjgh,bcjghp->bcighp', cb, lmat, dtc, xs)
    decay_states = jnp.exp(a_cum[:, :, -1:] - a_cum)
    states = jnp.einsum('bcjgn,bcjgh,bcjghp->bcghpn', bm, decay_states * dtc, xs)
    chunk_decay = jnp.exp(a_cum[:, :, -1])

    def step(h, inp):
        s_c, d_c = inp
        return h * d_c[..., None, None] + s_c, h

    h0 = jnp.zeros((bsz, SSD_GROUPS, SSD_HPG, SSD_HEAD_DIM, SSD_STATE), states.dtype)
    _, h_in = lax.scan(step, h0, (jnp.moveaxis(states, 1, 0), jnp.moveaxis(chunk_decay, 1, 0)))
    h_in = jnp.moveaxis(h_in, 0, 1)
    y_off = jnp.einsum('bcign,bcghpn,bcigh->bcighp', cm, h_in, jnp.exp(a_cum))
    y = y_diag + y_off + xs * d_skip.reshape(SSD_GROUPS, SSD_HPG)[:, :, None]
    y = y.reshape(bsz, seq, SSD_INNER)
    y = rms_norm(y * jax.nn.silu(z.astype(jnp.float32)), norm_g)
    return y.astype(z.dtype)


def setup_inputs(seed: int = 0) -> dict:
    key = jax.random.key(seed)
    ks = jax.random.split(key, 24)
    f32 = jnp.float32

    def nrm(k, shape, scale):
        return jax.random.normal(k, shape, f32) * scale

    def gain(k, shape):
        return 1.0 + 0.02 * jax.random.normal(k, shape, f32)

    dt0 = jnp.exp(jax.random.uniform(ks[12], (DEPTH, SSD_HEADS), f32, math.log(1e-3), math.log(1e-1)))
    return {
        'x': nrm(ks[0], (BATCH, SEQ, D_MODEL), 1.0),
        'norm1_g': gain(ks[1], (DEPTH, D_MODEL)),
        'w_in': nrm(ks[2], (DEPTH, D_MODEL, D_IN), D_MODEL ** -0.5),
        'dsa_q_norm': gain(ks[3], (DEPTH, HEAD_DIM)),
        'dsa_k_norm': gain(ks[4], (DEPTH, DSA_KV_DIM)),
        'nsa_q_norm': gain(ks[5], (DEPTH, HEAD_DIM)),
        'nsa_k_norm': gain(ks[6], (DEPTH, 3, HEAD_DIM)),
        'nsa_cmp_pos': nrm(ks[7], (DEPTH, 2, CMP_BLOCK, HEAD_DIM), 0.02),
        'nsa_cmp_w': nrm(ks[8], (DEPTH, 2, CMP_BLOCK, HEAD_DIM, HEAD_DIM), (CMP_BLOCK * HEAD_DIM) ** -0.5),
        'ssd_conv_w': nrm(ks[9], (DEPTH, CONV_WIDTH, SSD_CONV_DIM), CONV_WIDTH ** -0.5),
        'ssd_conv_b': nrm(ks[10], (DEPTH, SSD_CONV_DIM), 0.02),
        'ssd_dt_bias': dt0 + jnp.log(-jnp.expm1(-dt0)),
        'ssd_a_log': jnp.log(jax.random.uniform(ks[11], (DEPTH, SSD_HEADS), f32, 1.0, 16.0)),
        'ssd_d': gain(ks[13], (DEPTH, SSD_HEADS)),
        'ssd_norm_g': gain(ks[14], (DEPTH, SSD_INNER)),
        'w_br_dsa': nrm(ks[15], (DEPTH, DSA_HEADS * HEAD_DIM, D_MODEL), (DSA_HEADS * HEAD_DIM) ** -0.5),
        'w_br_nsa': nrm(ks[16], (DEPTH, NSA_HEADS * HEAD_DIM, D_MODEL), (NSA_HEADS * HEAD_DIM) ** -0.5),
        'w_br_ssd': nrm(ks[17], (DEPTH, SSD_INNER, D_MODEL), SSD_INNER ** -0.5),
        'w_out': nrm(ks[18], (DEPTH, D_MODEL, D_MODEL), D_MODEL ** -0.5),
        'norm2_g': gain(ks[19], (DEPTH, D_MODEL)),
        'w_ff1': nrm(ks[20], (DEPTH, D_MODEL, D_FF), D_MODEL ** -0.5),
        'w_ff2': nrm(ks[21], (DEPTH, D_FF, D_MODEL), D_FF ** -0.5),
    }


def reference(x, norm1_g, w_in, dsa_q_norm, dsa_k_norm, nsa_q_norm, nsa_k_norm, nsa_cmp_pos, nsa_cmp_w,
              ssd_conv_w, ssd_conv_b, ssd_dt_bias, ssd_a_log, ssd_d, ssd_norm_g,
              w_br_dsa, w_br_nsa, w_br_ssd, w_out, norm2_g, w_ff1, w_ff2):
    bsz, seq = x.shape[:2]
    offsets = [int(o) for o in np.cumsum(IN_SPLITS)[:-1]]
    for l in range(DEPTH):
        h = rms_norm(x, norm1_g[l])
        u = h @ w_in[l]
        (dq, dk, dv, iq, ik, iw, nq, nkv, ng, sz, sxbc, sdt, mg) = jnp.split(u, offsets, axis=-1)
        y_a = dsa_mixer(rms_norm(dq.reshape(bsz, seq, DSA_HEADS, HEAD_DIM), dsa_q_norm[l]),
                        rms_norm(dk, dsa_k_norm[l]), dv,
                        iq.reshape(bsz, seq, IDX_HEADS, IDX_DIM), ik, iw)
        y_a = y_a.reshape(bsz, seq, DSA_HEADS * HEAD_DIM)
        nkv = nkv.reshape(bsz, seq, 6, NSA_KV_HEADS, HEAD_DIM)
        y_b = nsa_mixer(rms_norm(nq.reshape(bsz, seq, NSA_KV_HEADS, NSA_GROUP, HEAD_DIM), nsa_q_norm[l]),
                        nkv[:, :, 0], nkv[:, :, 1], nkv[:, :, 2], nkv[:, :, 3], nkv[:, :, 4], nkv[:, :, 5],
                        jax.nn.sigmoid(ng.reshape(bsz, seq, NSA_KV_HEADS, NSA_GROUP, 3)),
                        nsa_k_norm[l], nsa_cmp_pos[l], nsa_cmp_w[l])
        y_b = y_b.reshape(bsz, seq, NSA_HEADS * HEAD_DIM)
        y_c = ssd_mixer(sz, sxbc, sdt, ssd_conv_w[l], ssd_conv_b[l], ssd_dt_bias[l], ssd_a_log[l],
                        ssd_d[l], ssd_norm_g[l])
        gates = jax.nn.sigmoid(mg.reshape(bsz, seq, 3, D_MODEL))
        mix = (gates[:, :, 0] * (y_a @ w_br_dsa[l]) + gates[:, :, 1] * (y_b @ w_br_nsa[l])
               + gates[:, :, 2] * (y_c @ w_br_ssd[l]))
        x = x + mix @ w_out[l]
        h = rms_norm(x, norm2_g[l])
        x = x + jnp.square(jax.nn.relu(h @ w_ff1[l])) @ w_ff2[l]
    return x
```

```python
import contextlib
import os
import numpy as np
import concourse.bass as bass
import concourse.mybir as mybir
from concourse.bass_utils import run_bass_kernel_spmd

F32 = mybir.dt.float32
BF16 = mybir.dt.bfloat16
AF = mybir.ActivationFunctionType
ALU = mybir.AluOpType
AX = mybir.AxisListType

EPOCH = 240
NPOOL = 90

D = 1024
S = 2048
NT = 16
DEPTH = 2
EPS = 1e-6
OFF = dict(dq=0, dk=1024, dv=1088, iq=1152, ik=1408, iw=1440, nq=1448, nkv=2472, ng=3240,
           sz=3288, sxbc=5336, sdt=8408, mg=8440)
D_IN = 11512
NEGM = -30000.0


def _region(ap):
    t = ap.tensor
    name = t.name
    dims = [(int(s), int(c)) for s, c in ap.ap]
    off = int(ap.offset)
    if "DRam" in type(t).__name__:
        lo = hi = off
        for s, c in dims:
            if s >= 0:
                hi += s * (c - 1)
            else:
                lo += s * (c - 1)
        return (name, 0, 1, lo, hi + 1)
    if "PSum" in type(t).__name__:
        return (name, 0, 128, 0, 1 << 30)
    ps, pc = dims[0]
    if ps == 0:
        p0, f0 = 0, off
    else:
        p0 = off // ps
        f0 = off - p0 * ps
    lo = hi = f0
    for s, c in dims[1:]:
        if s >= 0:
            hi += s * (c - 1)
        else:
            lo += s * (c - 1)
    return (name, p0, p0 + pc, lo, hi + 1)


def _overlap(a, b):
    return a[1] < b[2] and b[1] < a[2] and a[3] < b[4] and b[3] < a[4]


def _contains(a, b):
    return a[1] <= b[1] and b[2] <= a[2] and a[3] <= b[3] and b[4] <= a[4]


class Prog:
    ENGS = ("pe", "act", "dve", "pool", "sp")

    def __init__(self, nc):
        self.nc = nc
        self.stack = contextlib.ExitStack()
        self.eng = {"pe": nc.tensor, "act": nc.scalar, "dve": nc.vector, "pool": nc.gpsimd, "sp": nc.sync}
        self.pool = [self.stack.enter_context(nc.semaphore(f"sp{i}")) for i in range(NPOOL)]
        self.bsem = [self.stack.enter_context(nc.semaphore(f"bar{i}")) for i in range(4)]
        self.nbar = 0
        self.uid = 0
        self.n_wait = 0
        self.n_ins = 0
        self.max_used = 0
        self._reset()

    def _reset(self):
        self.free = list(range(NPOOL))
        self.esem = {e: None for e in self.ENGS}
        self.ecnt = {e: 0 for e in self.ENGS}
        self.pend = {e: False for e in self.ENGS}
        self.seen = {e: {} for e in self.ENGS}
        self.dset = []
        self.dall = {}
        self.dnext = 0
        self.trk = {}

    def _alloc(self):
        assert self.free, "semaphore pool exhausted; add a barrier"
        i = self.free.pop()
        self.max_used = max(self.max_used, NPOOL - len(self.free))
        return i

    def sems_left(self):
        return len(self.free)

    def sb(self, name, shape, dtype, stack=None):
        self.uid += 1
        return (stack or self.stack).enter_context(
            self.nc.sbuf_tensor(f"{name}_{self.uid}", list(shape), dtype))

    def ps(self, name, shape, dtype, stack=None):
        self.uid += 1
        return (stack or self.stack).enter_context(
            self.nc.psum_tensor(f"{name}_{self.uid}", list(shape), dtype))

    def dram(self, name, shape, dtype):
        self.uid += 1
        return self.nc.dram_tensor(f"{name}_{self.uid}", list(shape), dtype).ap()

    @contextlib.contextmanager
    def scope(self):
        st = contextlib.ExitStack()
        try:
            yield st
        finally:
            self.barrier()
            st.close()

    def _deps(self, reads, writes):
        deps = []
        for ap in reads:
            r = _region(ap)
            t = self.trk.get(r[0])
            if t is None:
                continue
            for (wr, ev) in t["w"]:
                if _overlap(wr, r):
                    deps.append(ev)
        for ap in writes:
            r = _region(ap)
            t = self.trk.get(r[0])
            if t is None:
                continue
            for (wr, ev) in t["w"]:
                if _overlap(wr, r):
                    deps.append(ev)
            for (sk, rr), val in t["r"].items():
                if _overlap(rr, r):
                    deps.append((sk, val))
        return deps

    def _record(self, reads, writes, ev):
        for ap in reads:
            r = _region(ap)
            t = self.trk.setdefault(r[0], {"w": [], "r": {}})
            k = (ev[0], r)
            if t["r"].get(k, -1) < ev[1]:
                t["r"][k] = ev[1]
        for ap in writes:
            r = _region(ap)
            t = self.trk.setdefault(r[0], {"w": [], "r": {}})
            t["w"] = [(wr, e) for (wr, e) in t["w"] if not _contains(r, wr)]
            t["w"].append((r, ev))
            t["r"] = {k: v for k, v in t["r"].items() if not _contains(r, k[1])}

    def _wait(self, e, deps):
        eng = self.eng[e]
        need = {}
        for sk, val in deps:
            if self.seen[e].get(sk, 0) >= val:
                continue
            if need.get(sk, 0) < val:
                need[sk] = val
        for sk, val in need.items():
            assert 0 < val <= 255
            eng.wait_ge(self.pool[sk[1]], val)
            self.seen[e][sk] = val
            self.n_wait += 1

    def _next_ev(self, e, commit):
        if self.esem[e] is None or self.ecnt[e] >= EPOCH:
            self.esem[e] = self._alloc()
            self.ecnt[e] = 0
        ev = ((e, self.esem[e]), self.ecnt[e] + 1)
        if commit:
            self.ecnt[e] += 1
        return ev

    def op(self, e, fn, reads, writes, inc=True):
        deps = self._deps(reads, writes)
        if e == "pe":
            deps = [d for d in deps if d[0][0] != "pe"]
        self._wait(e, deps)
        ins = fn(self.eng[e])
        self.n_ins += 1
        ev = self._next_ev(e, inc)
        if inc:
            ins.then_inc(self.pool[ev[0][1]], 1)
            self.pend[e] = False
        else:
            self.pend[e] = True
        self._record(reads, writes, ev)
        return ins

    def dma(self, out, in_, q="sp", **kw):
        deps = self._deps([in_], [out])
        if q == "pool":
            ent = [self._alloc(), 0]
        else:
            if len(self.dset) < 8:
                self.dset.append([self._alloc(), 0])
            k = self.dnext % len(self.dset)
            self.dnext += 1
            if self.dset[k][1] >= 15:
                self.dset[k] = [self._alloc(), 0]
            ent = self.dset[k]
        if ent[1] > 0:
            deps.append((("d", ent[0]), 16 * ent[1]))
        self._wait(q, deps)
        ins = self.eng[q].dma_start(out=out, in_=in_, **kw)
        ins.then_inc(self.pool[ent[0]], 16)
        ent[1] += 1
        self.dall[ent[0]] = ent[1]
        self.n_ins += 1
        self._record([in_], [out], (("d", ent[0]), 16 * ent[1]))
        return ins

    def barrier(self):
        evs = []
        for e in self.ENGS:
            assert not self.pend[e], f"pending non-inc op on {e} at barrier"
            if self.esem[e] is not None and self.ecnt[e] > 0:
                evs.append(((e, self.esem[e]), self.ecnt[e]))
        for k, c in self.dall.items():
            evs.append((("d", k), 16 * c))
        for e in self.ENGS:
            self._wait(e, evs)
        b0 = self.bsem[2 * (self.nbar % 2)]
        b1 = self.bsem[2 * (self.nbar % 2) + 1]
        p0 = self.bsem[2 * ((self.nbar + 1) % 2)]
        p1 = self.bsem[2 * ((self.nbar + 1) % 2) + 1]
        for e in self.ENGS:
            if e != "sp":
                self.eng[e].sem_inc(b0, 1)
        sp = self.eng["sp"]
        sp.wait_ge(b0, 4)
        used = [i for i in range(NPOOL) if i not in set(self.free)]
        for i in used:
            sp.sem_clear(self.pool[i])
        sp.sem_clear(p0)
        sp.sem_clear(p1)
        sp.sem_inc(b1, 1)
        for e in self.ENGS:
            if e != "sp":
                self.eng[e].wait_ge(b1, 1)
        self.nbar += 1
        self._reset()

    def maybe_barrier(self, min_free=35):
        if len(self.free) < min_free:
            self.barrier()

    def finish(self):
        self.barrier()
        self.stack.close()

    def mm(self, out, lhsT, rhs, start=True, stop=True, inc=None, sgc=False):
        if inc is None:
            inc = stop
        return self.op("pe", lambda e: e.matmul(out, lhsT=lhsT, rhs=rhs, start=start, stop=stop,
                                                skip_group_check=sgc),
                       [lhsT, rhs], [out], inc=inc)

    def tr(self, out, in_, ident, inc=True):
        return self.op("pe", lambda e: e.transpose(out=out, in_=in_, identity=ident), [in_, ident], [out], inc=inc)

    def act(self, out, in_, func, bias=None, scale=None, accum=None):
        kw = {}
        reads = [in_]
        writes = [out]
        if bias is not None:
            kw["bias"] = bias
            if not isinstance(bias, (int, float)):
                reads.append(bias)
        if scale is not None:
            kw["scale"] = scale
            if not isinstance(scale, (int, float)):
                reads.append(scale)
        if accum is not None:
            kw["accum_out"] = accum
            writes.append(accum)
        return self.op("act", lambda e: e.activation(out=out, in_=in_, func=func, **kw), reads, writes)

    def cp(self, eng, out, in_):
        if eng == "act":
            return self.op("act", lambda e: e.copy(out=out, in_=in_), [in_], [out])
        return self.op(eng, lambda e: e.tensor_copy(out=out, in_=in_), [in_], [out])

    def tt(self, eng, out, in0, in1, op):
        return self.op(eng, lambda e: e.tensor_tensor(out=out, in0=in0, in1=in1, op=op), [in0, in1], [out])

    def ts(self, eng, out, in0, s1, s2=None, op0=ALU.mult, op1=None, accum=None):
        reads = [in0] + [s for s in (s1, s2) if s is not None and not isinstance(s, (int, float))]
        writes = [out] + ([accum] if accum is not None else [])
        kw = {}
        if op1 is not None:
            kw["op1"] = op1
        if accum is not None:
            kw["accum_out"] = accum
        return self.op(eng, lambda e: e.tensor_scalar(out=out, in0=in0, scalar1=s1, scalar2=s2, op0=op0, **kw),
                       reads, writes)

    def stt(self, eng, out, in0, scalar, in1, op0, op1):
        reads = [in0, in1] + ([scalar] if not isinstance(scalar, (int, float)) else [])
        return self.op(eng, lambda e: e.scalar_tensor_tensor(out=out, in0=in0, scalar=scalar, in1=in1,
                                                             op0=op0, op1=op1), reads, [out])

    def amul(self, out, in_, val):
        return self.op("act", lambda e: e.mul(out=out, in_=in_, mul=val), [in_], [out])

    def memset(self, eng, ap, val):
        return self.op(eng, lambda e: e.memset(ap, val), [], [ap])

    def recip(self, out, in_):
        return self.op("dve", lambda e: e.reciprocal(out=out, in_=in_), [in_], [out])


def _consts():
    c = {}
    p = np.arange(128)
    c["c_ident"] = np.eye(128, dtype=np.float32)
    c["c_ones"] = np.ones((128, 128), np.float32)
    c["c_blk64"] = (p[:, None] // 64 == p[None, :] // 64).astype(np.float32)
    c["c_tri"] = (p[:, None] <= p[None, :]).astype(np.float32)
    c["c_cneg"] = np.where(p[:, None] > p[None, :], NEGM, 0.0).astype(np.float32)
    c["c_wneg"] = np.where(p[:, None] <= p[None, :], NEGM, 0.0).astype(np.float32)
    c["c_cnegtm"] = np.where(p[None, :] > p[:, None], -3e30, 0.0).astype(np.float32)
    s = np.arange(S)
    c["c_E"] = (s[None, :] // 64 == np.arange(32)[:, None]).astype(np.float32)
    n = np.arange(128)
    vis = (16 * n[:, None] + 31 <= s[None, :])
    c["c_visneg"] = np.where(vis, 0.0, NEGM).astype(np.float32)
    j = np.arange(32)
    cover = ((16 * n[:, None] < 64 * j[None, :] + 64) & (16 * n[:, None] + 32 > 64 * j[None, :]))
    ce = np.zeros((128, 33), np.float32)
    ce[:, 0] = 1.0
    ce[:, 1:] = cover
    ce[127] = 0.0
    c["c_cover"] = ce
    t = (np.arange(NT)[None, :, None] * 128 + p[:, None, None])
    cur = t // 64
    jj = j[None, None, :]
    forced = (jj == cur) | (jj == 0)
    future = jj > cur
    c["c_selkeep"] = (~(forced | future)).astype(np.float32)
    c["c_selbias"] = np.where(forced, 1e9, np.where(future, -1e9, 0.0)).astype(np.float32)
    return c


class Builder:
    def __init__(self, debug=None, nlayers=DEPTH):
        self.debug = debug
        self.nlayers = nlayers
        import os
        self.lim = int(os.environ.get("KLIM", "16"))
        nc = bass.Bass("TRN2", target_bir_lowering=False)
        self.nc = nc
        self.P = Prog(nc)
        self.inp = {}
        self.out = {}

    def din(self, name, shape, dt=F32):
        self.inp[name] = self.nc.dram_tensor(name, list(shape), dt, kind="ExternalInput").ap()
        return self.inp[name]

    def dout(self, name, shape, dt=F32):
        self.out[name] = self.nc.dram_tensor(name, list(shape), dt, kind="ExternalOutput").ap()
        return self.out[name]

    def load_consts(self):
        P = self.P
        C = {}
        shapes = {k: v.shape for k, v in _consts().items()}
        for k, shp in shapes.items():
            self.din(k, shp)
        self.wstage = [P.sb("wstage", [128, 4096], F32) for _ in range(2)]
        self.wsi = 0

        def ld(name, key, shape, dt, rows=None):
            t = P.sb(name, shape, dt)
            src = self.inp[key]
            if dt == F32:
                P.dma(t[:], src)
            else:
                n = int(np.prod(shape[1:]))
                flat = "p a b -> p (a b)" if len(shape) == 3 else None
                for a0 in range(0, n, 4096):
                    b0 = min(n, a0 + 4096)
                    stg = self.wstage[self.wsi % 2]
                    self.wsi += 1
                    P.dma(stg[0:shape[0], 0:b0 - a0], src[:, a0:b0])
                    P.cp("pool", t[:, a0:b0], stg[0:shape[0], 0:b0 - a0])
            return t
        C["ident_f"] = ld("ident_f", "c_ident", [128, 128], F32)
        C["ident"] = ld("ident", "c_ident", [128, 128], BF16)
        C["ones_f"] = ld("ones_f", "c_ones", [128, 128], F32)
        C["blk64_f"] = ld("blk64_f", "c_blk64", [128, 128], F32)
        C["tri_f"] = ld("tri_f", "c_tri", [128, 128], F32)
        C["cneg"] = ld("cneg", "c_cneg", [128, 128], BF16)
        C["zero"] = P.sb("zero", [128, 128], BF16)
        P.memset("pool", C["zero"][:], 0.0)
        C["cneg_f"] = ld("cneg_f", "c_cneg", [128, 128], F32)
        C["wneg"] = ld("wneg", "c_wneg", [128, 128], BF16)
        C["cnegtm"] = ld("cnegtm", "c_cnegtm", [128, 128], F32)
        C["E"] = ld("E", "c_E", [32, S], BF16)
        C["visneg"] = ld("visneg", "c_visneg", [128, S], BF16)
        C["cover"] = ld("cover", "c_cover", [128, 33], BF16)
        C["selkeep"] = ld("selkeep", "c_selkeep", [128, NT, 32], F32)
        C["selbias"] = ld("selbias", "c_selbias", [128, NT, 32], F32)
        self.C = C

    def wload(self, dst, wdram, l, c0, n, rows=None):
        w2 = wdram[l] if rows is None else wdram[l][rows[0]:rows[1]]
        src = w2.rearrange("(kc p) n -> p kc n", p=128)[:, :, c0:c0 + n]
        nk = src.shape[1]
        step = max(1, 4096 // nk)
        for a in range(0, n, step):
            b = min(n, a + step)
            stg = self.wstage[self.wsi % 2]
            self.wsi += 1
            v = stg[:, 0:nk * (b - a)].rearrange("p (k n) -> p k n", k=nk)
            self.P.dma(v, src[:, :, a:b], q="sp")
            self.P.cp("pool", dst[:, :, a:b], v)

    def proj_fm(self, ps, Wt, c0, M, hT, t0, N):
        for kc in range(8):
            self.P.mm(ps[0:M, 0:N], lhsT=Wt[:, kc, c0:c0 + M], rhs=hT[:, kc, t0:t0 + N],
                      start=(kc == 0), stop=(kc == 7))

    def proj_tm(self, ps, Wt, c0, n, hT, tile):
        for kc in range(8):
            self.P.mm(ps[:, 0:n], lhsT=hT[:, kc, tile * 128:(tile + 1) * 128], rhs=Wt[:, kc, c0:c0 + n],
                      start=(kc == 0), stop=(kc == 7))

    def norm_fm(self, ps, ps2, M, N, gcol, out, sq, rs):
        P, C = self.P, self.C
        P.act(sq[0:M, 0:N], ps[0:M, 0:N], AF.Square)
        P.mm(ps2[0:M, 0:N], lhsT=C["blk64_f"][0:M, 0:M], rhs=sq[0:M, 0:N])
        P.act(rs[0:M, 0:N], ps2[0:M, 0:N], AF.Sqrt, bias=EPS, scale=1.0 / 64)
        P.recip(rs[0:M, 0:N], rs[0:M, 0:N])
        P.stt("dve", out, ps[0:M, 0:N], gcol, rs[0:M, 0:N], ALU.mult, ALU.mult)

    def gain_col(self, name, vec64, stack):
        t = self.P.sb(name, [128, 1], F32, stack)
        src = vec64.rearrange("(p o) -> p o", o=1)
        self.P.dma(t[0:64, :], src)
        self.P.dma(t[64:128, :], src)
        return t

    def phase_norm(self, xd, gvec, hT, store_T=None):
        P, C = self.P, self.C
        with P.scope() as st:
            gb = P.sb("gb", [128, D], F32, st)
            P.dma(gb[:], gvec.partition_broadcast(128))
            xb = [P.sb("xb", [128, D], F32, st) for _ in range(2)]
            sq = P.sb("sq", [128, D], F32, st)
            hb = [P.sb("hb", [128, D], BF16, st) for _ in range(2)]
            ss = P.sb("ss", [128, 2], F32, st)
            pt = [P.ps("pt", [128, 8, 128], BF16, st) for _ in range(2)]
            for tt in range(NT):
                x_t = xb[tt % 2]
                P.dma(x_t[:], xd[tt * 128:(tt + 1) * 128, :])
                s1 = ss[:, tt % 2:tt % 2 + 1]
                P.act(sq[:], x_t[:], AF.Square, accum=s1)
                P.act(s1, s1, AF.Sqrt, bias=EPS, scale=1.0 / D)
                P.recip(s1, s1)
                h_t = hb[tt % 2]
                P.stt("dve", h_t[:], x_t[:], s1, gb[:], ALU.mult, ALU.mult)
                p_t = pt[tt % 2]
                for kc in range(8):
                    P.tr(p_t[:, kc, :], h_t[:, kc * 128:(kc + 1) * 128], C["ident"][:], inc=(kc == 7))
                P.cp("act" if tt % 2 else "dve", hT[:, :, tt * 128:(tt + 1) * 128], p_t[:])

    def phase_dsa(self, l, hT, ya_d):
        P, C = self.P, self.C
        I = self.inp
        w_in = I["w_in"]
        with P.scope() as st:
            Wq = P.sb("Wq", [128, 8, 1024], BF16, st)
            self.wload(Wq, w_in, l, OFF["dq"], 1024)
            Wk = P.sb("Wk", [128, 8, 128], BF16, st)
            self.wload(Wk[:, :, 0:64], w_in, l, OFF["dk"], 64)
            self.wload(Wk[:, :, 64:128], w_in, l, OFF["dk"], 64)
            Wv = P.sb("Wv", [128, 8, 64], BF16, st)
            self.wload(Wv, w_in, l, OFF["dv"], 64)
            Wiq = P.sb("Wiq", [128, 8, 256], BF16, st)
            self.wload(Wiq, w_in, l, OFF["iq"], 256)
            Wik = P.sb("Wik", [128, 8, 128], BF16, st)
            for r in range(4):
                self.wload(Wik[:, :, r * 32:(r + 1) * 32], w_in, l, OFF["ik"], 32)
            Wiw = P.sb("Wiw", [128, 8, 8], BF16, st)
            self.wload(Wiw, w_in, l, OFF["iw"], 8)
            gq = self.gain_col("gq", I["dsa_q_norm"][l], st)
            gk = self.gain_col("gk", I["dsa_k_norm"][l], st)

            kT = P.sb("kT", [128, S], BF16, st)
            v1 = P.sb("v1", [128, NT, 65], BF16, st)
            ikT4 = P.sb("ikT4", [128, S], BF16, st)
            ikbd = P.sb("ikbd", [128, NT, 4, 128], BF16, st)
            qT = P.sb("qT", [128, 8, 512], BF16, st)
            iqT = P.sb("iqT", [128, 2, 512], BF16, st)
            iw = P.sb("iw", [128, 8], F32, st)
            acc = P.sb("acc", [128, S], F32, st)
            work = P.sb("work", [128, S], F32, st)
            mx = P.sb("mx", [128, 8], F32, st)
            negm = P.sb("negm", [128, S], BF16, st)
            negmT = P.sb("negmT", [128, NT, 128], BF16, st)
            rb = [P.sb("rb", [128, 512], F32, st) for _ in range(2)]
            pTb = [P.sb("pTb", [128, 512], BF16, st) for _ in range(2)]
            ytm = P.sb("ytm", [128, 1024], BF16, st)
            yT = P.sb("yT", [128, 8, 512], BF16, st)
            sq = P.sb("sq", [128, 512], F32, st)
            rs = P.sb("rs", [128, 512], F32, st)
            rc = P.sb("rc", [128, 4], F32, st)
            bank = [P.ps("bk", [128, 512], F32, st) for _ in range(6)]
            tb = [P.ps("tb", [128, 8, 128], BF16, st) for _ in range(2)]
            Sb, Ob, Ib = bank[0:2], bank[2:4], bank[4:6]

            P.memset("pool", v1[:, :, 64:65], 1.0)
            P.memset("pool", ikbd[:], 0.0)
            for Q in range(4):
                t0 = Q * 512
                self.proj_fm(Ib[0], Wk, 0, 128, hT, t0, 512)
                self.norm_fm(Ib[0], Ib[1], 128, 512, gk[:, 0:1], kT[:, t0:t0 + 512], sq, rs)
                self.proj_fm(Ib[0], Wik, 0, 128, hT, t0, 512)
                P.cp("act", ikT4[:, t0:t0 + 512], Ib[0][:, :])
            for hh in range(4):
                P.dma(ikbd[32 * hh:32 * hh + 32, :, hh, :],
                      ikT4[32 * hh:32 * hh + 32, :].rearrange("p (k s) -> p k s", s=128))
            for tt in range(NT):
                ps = Ib[tt % 2]
                self.proj_tm(ps, Wv, 0, 64, hT, tt)
                P.cp("act", v1[:, tt, 0:64], ps[:, 0:64])

            if self.debug == "dsa_k":
                o = self.dout("dbg", [128, S], BF16)
                P.dma(o, kT[:])
                o2 = self.dout("dbg2", [128, NT * 4 * 128], BF16)
                P.dma(o2, ikbd[:].rearrange("p a b c -> p (a b c)"))
                o3 = self.dout("dbg3", [128, NT * 65], BF16)
                P.dma(o3, v1[:].rearrange("p a b -> p (a b)"))
                return
            si = 0
            ii = 0
            for Q in range(4):
                t0 = Q * 512
                for c in range(8):
                    self.proj_fm(Ib[0], Wq, c * 128, 128, hT, t0, 512)
                    self.norm_fm(Ib[0], Ib[1], 128, 512, gq[:, 0:1], qT[:, c, :], sq, rs)
                for c in range(2):
                    self.proj_fm(Ib[c], Wiq, c * 128, 128, hT, t0, 512)
                    P.cp("act", iqT[:, c, :], Ib[c][:, :])
                for tl in range(4):
                    qt = Q * 4 + tl
                    if qt >= self.lim:
                        continue
                    nk = qt + 1
                    tsl = slice(tl * 128, (tl + 1) * 128)
                    P.maybe_barrier()
                    self.proj_tm(Ib[0], Wiw, 0, 8, hT, qt)
                    P.cp("act", iw[:], Ib[0][:, 0:8])
                    for kt in range(nk):
                        a_kt = acc[:, kt * 128:(kt + 1) * 128]
                        for c in range(2):
                            ps = Ib[ii % 2]
                            r = rb[ii % 2]
                            ii += 1
                            P.mm(ps[:, :], lhsT=iqT[:, c, tsl], rhs=ikbd[:, kt].rearrange("p a b -> p (a b)"))
                            P.act(r[:], ps[:, :], AF.Relu)
                            for hh in range(4):
                                h = 4 * c + hh
                                if h == 0:
                                    P.ts("dve", a_kt, r[:, 0:128], iw[:, 0:1], None, op0=ALU.mult)
                                else:
                                    P.stt("dve", a_kt, r[:, hh * 128:(hh + 1) * 128], iw[:, h:h + 1], a_kt,
                                          ALU.mult, ALU.add)
                    d_sl = slice(qt * 128, (qt + 1) * 128)
                    if self.debug == "dsa_acc" and qt == 2:
                        o = self.dout("dbg", [128, 384])
                        P.dma(o, acc[:, 0:384])
                        o2 = self.dout("dbg2", [128, 8])
                        P.dma(o2, iw[:])
                        return
                    if qt >= 2 and os.environ.get('KTOPK', '1') == '1':
                        P.tt("dve", acc[:, d_sl], acc[:, d_sl], C["cnegtm"][:], ALU.add)
                        cur = acc
                        for r_ in range(int(os.environ.get("KROUNDS", "32"))):
                            P.op("dve", lambda e: e.max(out=mx[:], in_=cur[:, 0:nk * 128]),
                                 [cur[:, 0:nk * 128]], [mx[:]])
                            P.op("dve", lambda e: e.match_replace(out=work[:, 0:nk * 128], in_to_replace=mx[:],
                                                                  in_values=cur[:, 0:nk * 128], imm_value=-1e30),
                                 [mx[:], cur[:, 0:nk * 128]], [work[:, 0:nk * 128]])
                            cur = work
                        if self.debug == "dsa_topk":
                            o = self.dout("dbg", [128, 384])
                            P.dma(o, work[:, 0:384])
                            return
                        P.ts("dve", negm[:, 0:nk * 128], work[:, 0:nk * 128], -1e29, 1.0, op0=ALU.is_le,
                             op1=ALU.subtract)
                    else:
                        P.memset("pool", negm[:, 0:nk * 128], 0.0)
                    if self.debug == "dsa_negm" and qt == 2:
                        o = self.dout("dbg", [128, 384], BF16)
                        P.dma(o, negm[:, 0:384])
                        return
                    for k0 in range(0, nk, 8):
                        k1 = min(nk, k0 + 8)
                        t_ = tb[0]
                        for kt in range(k0, k1):
                            P.tr(t_[:, kt - k0, :], negm[:, kt * 128:(kt + 1) * 128], C["ident"][:],
                                 inc=(kt == k1 - 1))
                        P.amul(negmT[:, k0:k1, :], t_[:, 0:k1 - k0, :], -NEGM)
                    P.tt("pool", negmT[:, qt, :], negmT[:, qt, :], C["cneg"][:], ALU.add)
                    if self.debug == "dsa_negmT" and qt == 2:
                        o = self.dout("dbg", [128, 384], BF16)
                        P.dma(o, negmT[:, 0:3, :].rearrange("p a b -> p (a b)"))
                        return
                    for hg in range(4):
                        O = Ob[hg % 2]
                        O3 = O[:, 0:260].rearrange("p (h e) -> p h e", e=65)
                        for kt in range(nk):
                            Sp = Sb[si % 2]
                            pT = pTb[si % 2]
                            si += 1
                            for hh in range(4):
                                h = 4 * hg + hh
                                c = h // 2
                                lo = 64 * (h % 2)
                                reg = Sp[:, hh * 128:(hh + 1) * 128]
                                P.mm(reg, lhsT=C["ident"][:], rhs=negmT[:, kt, :], start=True, stop=False, inc=False)
                                P.mm(reg, lhsT=kT[lo:lo + 64, kt * 128:(kt + 1) * 128], rhs=qT[lo:lo + 64, c, tsl],
                                     start=False, stop=True, inc=(hh == 3))
                            P.act(pT[:], Sp[:, :], AF.Exp, scale=0.125)
                            for hh in range(4):
                                P.mm(O3[:, hh, :], lhsT=pT[:, hh * 128:(hh + 1) * 128], rhs=v1[:, kt, :],
                                     start=(kt == 0 and hh == 0), stop=(kt == nk - 1), inc=(hh == 3), sgc=True)
                        P.recip(rc[:], O3[:, :, 64])
                        P.tt("dve", ytm[:, hg * 256:(hg + 1) * 256].rearrange("p (h e) -> p h e", e=64),
                             O3[:, :, 0:64], rc[:].unsqueeze(2).to_broadcast([128, 4, 64]), ALU.mult)
                    t_ = tb[1]
                    for c in range(8):
                        P.tr(t_[:, c, :], ytm[:, c * 128:(c + 1) * 128], C["ident"][:], inc=(c == 7))
                    P.cp("act", yT[:, :, tsl], t_[:])
                P.dma(ya_d[:, :, t0:t0 + 512], yT[:], q="sp")

    def phase_nsa(self, l, hT, yb_d):
        P, C = self.P, self.C
        I = self.inp
        w_in = I["w_in"]
        KV = OFF["nkv"]
        with P.scope() as st:
            Wq = P.sb("Wnq", [128, 8, 1024], BF16, st)
            self.wload(Wq, w_in, l, OFF["nq"], 1024)
            Wkv = P.sb("Wkv", [128, 8, 768], BF16, st)
            self.wload(Wkv, w_in, l, KV, 768)
            Wks = P.sb("Wks", [128, 8, 256], BF16, st)
            Wkw = P.sb("Wkw", [128, 8, 256], BF16, st)
            for g in range(2):
                for r in range(2):
                    self.wload(Wks[:, :, g * 128 + r * 64:g * 128 + r * 64 + 64], w_in, l, KV + 256 + g * 64, 64)
                    self.wload(Wkw[:, :, g * 128 + r * 64:g * 128 + r * 64 + 64], w_in, l, KV + 512 + g * 64, 64)
            Wng = P.sb("Wng", [128, 8, 48], BF16, st)
            self.wload(Wng, w_in, l, OFF["ng"], 48)
            gq = self.gain_col("gnq", I["nsa_q_norm"][l], st)
            gks = self.gain_col("gks", I["nsa_k_norm"][l, 1], st)
            gkw = self.gain_col("gkw", I["nsa_k_norm"][l, 2], st)
            gkc = P.sb("gkc", [128, 64], F32, st)
            P.dma(gkc[:], I["nsa_k_norm"][l, 0].partition_broadcast(128))
            cmpw = []
            posB = []
            for kv in range(2):
                cw = P.sb("cmpw", [128, 32, 64], BF16, st)
                stg = self.wstage[self.wsi % 2]
                self.wsi += 1
                sv = stg[:, 0:2048].rearrange("p (l e) -> p l e", e=64)
                src = I["nsa_cmp_w"][l, kv].rearrange("l d e -> d l e")
                P.dma(sv[0:64], src)
                P.dma(sv[64:128], src)
                P.cp("pool", cw[:], sv)
                cmpw.append(cw)
                pt_ = P.sb("posT", [128, 32], F32, st)
                psrc = I["nsa_cmp_pos"][l, kv].rearrange("l d -> d l")
                P.dma(pt_[0:64, :], psrc, allow_slow_non_contiguous=True)
                P.dma(pt_[64:128, :], psrc, allow_slow_non_contiguous=True)
                pb = P.sb("posB", [128, 32, 127], BF16, st)
                P.cp("pool", pb[:], pt_[:].unsqueeze(2).to_broadcast([128, 32, 127]))
                posB.append(pb)

            kcT = P.sb("kcT", [128, S], BF16, st)
            vcT = P.sb("vcT", [128, S], BF16, st)
            ksT = [P.sb("ksT", [128, S], BF16, st) for _ in range(2)]
            kwT = [P.sb("kwT", [128, S], BF16, st) for _ in range(2)]
            vs1 = P.sb("vs1", [128, NT, 2, 65], BF16, st)
            vw1 = P.sb("vw1", [128, NT, 2, 65], BF16, st)
            kd = P.sb("kd", [128, 128], BF16, st)
            kcmpT = [P.sb("kcmpT", [128, 128], BF16, st) for _ in range(2)]
            vext = [P.sb("vext", [128, 97], BF16, st) for _ in range(2)]
            nqT = P.sb("nqT", [128, 8, 512], BF16, st)
            gs = P.sb("gs", [128, 48], F32, st)
            yacc = P.sb("yacc", [128, 16, 64], F32, st)
            ytmp = P.sb("ytmp", [128, 4, 64], F32, st)
            impn = P.sb("impn", [128, 16, 32], F32, st)
            imp = P.sb("imp", [128, 32], F32, st)
            mx = P.sb("mx8", [128, 8], F32, st)
            negblk = P.sb("negblk", [128, 32], BF16, st)
            negblkT = [P.sb("negblkT", [32, 128], BF16, st) for _ in range(2)]
            pTb = [P.sb("pTb", [128, 512], BF16, st) for _ in range(2)]
            ytm = P.sb("ytm", [128, 1024], BF16, st)
            yT = P.sb("yT", [128, 8, 512], BF16, st)
            sq = P.sb("sq", [128, 512], F32, st)
            rs = P.sb("rs", [128, 512], F32, st)
            rc = P.sb("rc", [128, 4], F32, st)
            cf = P.sb("cf", [128, 4], F32, st)
            ss = P.sb("ss1", [128, 1], F32, st)
            bank = [P.ps("bk", [128, 512], F32, st) for _ in range(6)]
            tb = [P.ps("tb", [128, 8, 128], BF16, st) for _ in range(2)]
            Sb, Ob, Mb = bank[0:2], bank[2:4], bank[4:6]

            P.memset("pool", vs1[:, :, :, 64:65], 1.0)
            P.memset("pool", vw1[:, :, :, 64:65], 1.0)
            P.memset("pool", kd[:], 0.0)
            for g in range(2):
                P.memset("pool", vext[g][:], 0.0)
            for Q in range(4):
                t0 = Q * 512
                self.proj_fm(Mb[0], Wkv, 0, 128, hT, t0, 512)
                P.cp("act", kcT[:, t0:t0 + 512], Mb[0][:, :])
                self.proj_fm(Mb[1], Wkv, 128, 128, hT, t0, 512)
                P.cp("act", vcT[:, t0:t0 + 512], Mb[1][:, :])
                for g in range(2):
                    self.proj_fm(Mb[0], Wks, g * 128, 128, hT, t0, 512)
                    self.norm_fm(Mb[0], Mb[1], 128, 512, gks[:, 0:1], ksT[g][:, t0:t0 + 512], sq, rs)
                    self.proj_fm(Mb[0], Wkw, g * 128, 128, hT, t0, 512)
                    self.norm_fm(Mb[0], Mb[1], 128, 512, gkw[:, 0:1], kwT[g][:, t0:t0 + 512], sq, rs)
            for tt in range(NT):
                ps = Mb[tt % 2]
                self.proj_tm(ps, Wkv, 384, 128, hT, tt)
                P.cp("act", vs1[:, tt, :, 0:64], ps[:, 0:128].rearrange("p (g e) -> p g e", e=64))
                self.proj_tm(ps, Wkv, 640, 128, hT, tt)
                P.cp("act", vw1[:, tt, :, 0:64], ps[:, 0:128].rearrange("p (g e) -> p g e", e=64))
            P.maybe_barrier()
            def strided(t, lo, off):
                return bass.AP(t[:].tensor, lo * S + off, [[S, 64], [16, 127]])
            for g in range(2):
                lo = g * 64
                for kv, src in ((0, kcT), (1, vcT)):
                    ps = Mb[kv]
                    for li in range(32):
                        P.mm(ps[0:127, 0:64], lhsT=strided(src, lo, li), rhs=cmpw[kv][lo:lo + 64, li, :],
                             start=(li == 0), stop=False, inc=False)
                    for li in range(32):
                        P.mm(ps[0:127, 0:64], lhsT=posB[kv][lo:lo + 64, li, :], rhs=cmpw[kv][lo:lo + 64, li, :],
                             start=False, stop=(li == 31), inc=(li == 31))
                P.act(sq[0:127, 0:64], Mb[0][0:127, 0:64], AF.Square, accum=ss[0:127, :])
                P.act(ss[0:127, :], ss[0:127, :], AF.Sqrt, bias=EPS, scale=1.0 / 64)
                P.recip(ss[0:127, :], ss[0:127, :])
                P.stt("dve", kd[0:127, 0:64], Mb[0][0:127, 0:64], ss[0:127, 0:1], gkc[0:127, :], ALU.mult, ALU.mult)
                P.cp("pool", kd[0:127, 64:128], kd[0:127, 0:64])
                P.tr(tb[0][:, 0, :], kd[:], C["ident"][:])
                P.cp("act", kcmpT[g][:], tb[0][:, 0, :])
                P.cp("act", vext[g][0:127, 0:64], Mb[1][0:127, 0:64])
                P.cp("pool", vext[g][0:127, 64:97], C["cover"][0:127, :])
            if self.debug == "nsa_k":
                o = self.dout("dbg", [128, 128], BF16)
                P.dma(o, kcmpT[1][:])
                o2 = self.dout("dbg2", [128, 97], BF16)
                P.dma(o2, vext[1][:])
                return
            P.maybe_barrier()

            si = 0
            oi = 0
            for Q in range(4):
                t0 = Q * 512
                for c in range(8):
                    self.proj_fm(Mb[0], Wq, c * 128, 128, hT, t0, 512)
                    self.norm_fm(Mb[0], Mb[1], 128, 512, gq[:, 0:1], nqT[:, c, :], sq, rs)
                for tl in range(4):
                    qt = Q * 4 + tl
                    if qt >= self.lim:
                        continue
                    P.maybe_barrier()
                    tsl = slice(tl * 128, (tl + 1) * 128)
                    self.proj_tm(Mb[0], Wng, 0, 48, hT, qt)
                    P.act(gs[:], Mb[0][:, 0:48], AF.Sigmoid)
                    gs3 = gs[:].rearrange("p (h b) -> p h b", b=3)

                    def scores(Sp, sub, kT_g, kt_cols, masks):
                        for hh in range(4):
                            h = 4 * sub + hh
                            c = h // 2
                            lo = 64 * (h % 2)
                            reg = Sp[:, hh * 128:(hh + 1) * 128]
                            first = True
                            for (ml, mr) in masks:
                                P.mm(reg, lhsT=ml, rhs=mr, start=first, stop=False, inc=False)
                                first = False
                            P.mm(reg, lhsT=kT_g[lo:lo + 64, kt_cols], rhs=nqT[lo:lo + 64, c, tsl],
                                 start=first, stop=True, inc=(hh == 3))

                    def finalize(O3, sub, br, first):
                        hs = slice(4 * sub, 4 * sub + 4)
                        P.ts("dve", rc[:], O3[:, :, 64], 1e-30, None, op0=ALU.max)
                        P.recip(rc[:], rc[:])
                        if br == 0:
                            P.tt("dve", impn[:, hs, :], O3[:, :, 65:97], rc[:].unsqueeze(2).to_broadcast([128, 4, 32]),
                                 ALU.mult)
                        P.tt("dve", cf[:], rc[:], gs3[:, hs, br], ALU.mult)
                        if first:
                            P.tt("dve", yacc[:, hs, :], O3[:, :, 0:64], cf[:].unsqueeze(2).to_broadcast([128, 4, 64]),
                                 ALU.mult)
                        else:
                            P.tt("dve", ytmp[:], O3[:, :, 0:64], cf[:].unsqueeze(2).to_broadcast([128, 4, 64]),
                                 ALU.mult)
                            P.tt("pool", yacc[:, hs, :], yacc[:, hs, :], ytmp[:], ALU.add)

                    for sub in range(4):
                        g = sub // 2
                        Sp = Sb[si % 2]
                        pT = pTb[si % 2]
                        si += 1
                        scores(Sp, sub, kcmpT[g], slice(0, 128), [(C["ident"][:], C["visneg"][:, qt * 128:(qt + 1) * 128])])
                        P.act(pT[:], Sp[:, :], AF.Exp, scale=0.125)
                        O = Ob[oi % 2]
                        oi += 1
                        O3 = O[:, 0:388].rearrange("p (h e) -> p h e", e=97)
                        for hh in range(4):
                            P.mm(O3[:, hh, :], lhsT=pT[:, hh * 128:(hh + 1) * 128], rhs=vext[g][:, :],
                                 start=(hh == 0), stop=True, inc=(hh == 3), sgc=True)
                        finalize(O3, sub, 0, True)
                    for g in range(2):
                        P.op("dve", lambda e: e.tensor_reduce(out=imp[:], in_=impn[:, 8 * g:8 * g + 8, :].rearrange("p h j -> p j h"),
                                                              axis=AX.X, op=ALU.add),
                             [impn[:, 8 * g:8 * g + 8, :]], [imp[:]])
                        P.tt("dve", imp[:], imp[:], C["selkeep"][:, qt, :], ALU.mult)
                        P.tt("dve", imp[:], imp[:], C["selbias"][:, qt, :], ALU.add)
                        P.op("dve", lambda e: e.max(out=mx[:], in_=imp[:]), [imp[:]], [mx[:]])
                        P.ts("dve", negblk[:], imp[:], mx[:, 3:4], 1.0, op0=ALU.is_ge, op1=ALU.subtract)
                        P.tr(tb[0][0:32, 0, :], negblk[:], C["ident"][:])
                        P.amul(negblkT[g][:], tb[0][0:32, 0, :], -NEGM)
                    for br in (1, 2):
                        if str(br) not in os.environ.get("NSABR", "12"):
                            continue
                        kts = list(range(0, qt + 1)) if br == 1 else list(range(max(0, qt - 4), qt + 1))
                        kT_l = ksT if br == 1 else kwT
                        v_l = vs1 if br == 1 else vw1
                        for sub in range(4):
                            g = sub // 2
                            O = Ob[oi % 2]
                            oi += 1
                            O3 = O[:, 0:260].rearrange("p (h e) -> p h e", e=65)
                            for ki, kt in enumerate(kts):
                                Sp = Sb[si % 2]
                                pT = pTb[si % 2]
                                si += 1
                                masks = []
                                if br == 1:
                                    masks.append((C["E"][:, kt * 128:(kt + 1) * 128], negblkT[g][:]))
                                if kt == qt:
                                    masks.append((C["ident"][:], C["cneg"][:]))
                                if br == 2 and kt == qt - 4:
                                    masks.append((C["ident"][:], C["wneg"][:]))
                                if not masks:
                                    masks.append((C["ident"][:], C["zero"][:]))
                                scores(Sp, sub, kT_l[g], slice(kt * 128, (kt + 1) * 128), masks)
                                P.act(pT[:], Sp[:, :], AF.Exp, scale=0.125)
                                for hh in range(4):
                                    P.mm(O3[:, hh, :], lhsT=pT[:, hh * 128:(hh + 1) * 128], rhs=v_l[:, kt, g, :],
                                         start=(ki == 0 and hh == 0), stop=(ki == len(kts) - 1), inc=(hh == 3), sgc=True)
                            finalize(O3, sub, br, False)
                    P.cp("act", ytm[:], yacc[:].rearrange("p h e -> p (h e)"))
                    t_ = tb[1]
                    for c in range(8):
                        P.tr(t_[:, c, :], ytm[:, c * 128:(c + 1) * 128], C["ident"][:], inc=(c == 7))
                    P.cp("act", yT[:, :, tsl], t_[:])
                P.dma(yb_d[:, :, t0:t0 + 512], yT[:], q="sp")

    def phase_ssd(self, l, hT, yc_d, zs_d, xact_d):
        P, C = self.P, self.C
        I = self.inp
        w_in = I["w_in"]
        with P.scope() as st:
            Wz = P.sb("Wz", [128, 8, 2048], BF16, st)
            self.wload(Wz, w_in, l, OFF["sz"], 2048)
            zt = [P.sb("zt", [128, 2048], F32, st) for _ in range(2)]
            bank = [P.ps("bk", [128, 512], F32, st) for _ in range(4)]
            bi = 0
            for tt in range(NT):
                z_t = zt[tt % 2]
                for nb in range(4):
                    ps = bank[bi % 4]
                    bi += 1
                    self.proj_tm(ps, Wz, nb * 512, 512, hT, tt)
                    P.act(z_t[:, nb * 512:(nb + 1) * 512], ps[:, :], AF.Silu)
                P.dma(zs_d[tt * 128:(tt + 1) * 128, :], z_t[:])
        with P.scope() as st:
            cw = P.sb("cw", [128, 24, 4], F32, st)
            cb = P.sb("cb", [128, 24], F32, st)
            for cc in range(24):
                P.dma(cw[:, cc, :], I["ssd_conv_w"][l][:, cc * 128:(cc + 1) * 128].rearrange("k c -> c k"),
                      allow_slow_non_contiguous=True)
                P.dma(cb[:, cc:cc + 1], I["ssd_conv_b"][l][cc * 128:(cc + 1) * 128].rearrange("(c o) -> c o", o=1))
            Wx = [P.sb("Wx", [128, 8, 128], BF16, st) for _ in range(2)]
            raw = [P.sb("raw", [128, 3 + S], F32, st) for _ in range(2)]
            accb = [P.sb("accb", [128, S], F32, st) for _ in range(2)]
            xa = [P.sb("xa", [128, S], BF16, st) for _ in range(2)]
            ctmp = P.sb("ctmp", [128, S], F32, st)
            bank = [P.ps("bk", [128, 512], F32, st) for _ in range(4)]
            for r_ in raw:
                P.memset("pool", r_[:, 0:3], 0.0)
            bi = 0
            for cc in range(24):
                W_ = Wx[cc % 2]
                self.wload(W_, w_in, l, OFF["sxbc"] + cc * 128, 128)
                rw = raw[cc % 2]
                for Q in range(4):
                    ps = bank[bi % 4]
                    bi += 1
                    self.proj_fm(ps, W_, 0, 128, hT, Q * 512, 512)
                    P.cp("act", rw[:, 3 + Q * 512:3 + (Q + 1) * 512], ps[:, :])
                eng = "dve" if cc % 2 == 0 else "pool"
                ac = accb[cc % 2]
                P.ts(eng, ac[:], rw[:, 3:3 + S], cw[:, cc, 3:4], None, op0=ALU.mult)
                for k in (2, 1, 0):
                    if eng == "dve":
                        P.stt(eng, ac[:], rw[:, k:k + S], cw[:, cc, k:k + 1], ac[:], ALU.mult, ALU.add)
                    else:
                        P.ts(eng, ctmp[:], rw[:, k:k + S], cw[:, cc, k:k + 1], None, op0=ALU.mult)
                        P.tt(eng, ac[:], ac[:], ctmp[:], ALU.add)
                P.act(xa[cc % 2][:], ac[:], AF.Silu, bias=cb[:, cc:cc + 1])
                P.dma(xact_d[:, cc, :], xa[cc % 2][:])
                P.maybe_barrier()
        with P.scope() as st:
            Wdt = P.sb("Wdt", [128, 8, 32], BF16, st)
            self.wload(Wdt, w_in, l, OFF["sdt"], 32)
            dtb = P.sb("dtb", [128, 32], F32, st)
            P.dma(dtb[:], I["ssd_dt_bias"][l].partition_broadcast(128))
            a_b = P.sb("a_b", [128, 32], F32, st)
            P.dma(a_b[:], I["ssd_a_log"][l].partition_broadcast(128))
            P.act(a_b[:], a_b[:], AF.Exp)
            P.ts("dve", a_b[:], a_b[:], -1.0, None, op0=ALU.mult)
            D_b = P.sb("D_b", [128, 32], F32, st)
            P.dma(D_b[:], I["ssd_d"][l].partition_broadcast(128))
            ng_b = P.sb("ng_b", [128, 2048], F32, st)
            P.dma(ng_b[:], I["ssd_norm_g"][l].partition_broadcast(128))
            xab = [P.sb("xab", [128, 24, 128], BF16, st) for _ in range(2)]
            zsb = [P.sb("zsb", [128, 2048], F32, st) for _ in range(2)]
            xs_tm = P.sb("xs_tm", [128, 2048], BF16, st)
            bm_tm = P.sb("bm_tm", [128, 512], BF16, st)
            xs_w = P.sb("xs_w", [128, 2048], BF16, st)
            rhs1 = P.sb("rhs1", [128, 8, 128], F32, st)
            rhs2 = P.sb("rhs2", [128, 8, 128], F32, st)
            Lg = P.sb("Lg", [128, 8, 128], BF16, st)
            MTg = P.sb("MTg", [128, 8, 128], BF16, st)
            cbT = P.sb("cbT", [128, 4, 128], BF16, st)
            H = [P.sb("H", [128, 512], F32, st) for _ in range(4)]
            Hbf = [P.sb("Hbf", [128, 512], BF16, st) for _ in range(4)]
            y = P.sb("y", [128, 2048], F32, st)
            tmp = P.sb("tmp", [128, 512], F32, st)
            ynb = P.sb("ynb", [128, 2048], BF16, st)
            ycT = P.sb("ycT", [128, 16, 128], BF16, st)
            sm = {n: P.sb(n, [128, 32], F32, st) for n in
                  ("dtr", "dtc", "lndt", "da", "acum", "alast", "nb", "ea", "wj", "dec")}
            ss = P.sb("ss2", [128, 1], F32, st)
            R = [P.ps("R", [128, 512], F32, st) for _ in range(2)]
            Yp = P.ps("Yp", [128, 512], F32, st)
            Yo = P.ps("Yo", [128, 512], F32, st)
            STp = P.ps("STp", [128, 512], F32, st)
            M0 = P.ps("M0", [128, 512], F32, st)
            M1 = P.ps("M1", [128, 512], F32, st)
            tb = P.ps("tb", [128, 8, 128], BF16, st)
            for g in range(4):
                P.memset("pool", H[g][:], 0.0)
                P.memset("pool", Hbf[g][:], 0.0)
            for c in range(NT):
                if c >= self.lim:
                    continue
                P.maybe_barrier()
                csl = slice(c * 128, (c + 1) * 128)
                xa_ = xab[c % 2]
                zs_ = zsb[c % 2]
                P.dma(xa_[:], xact_d[:, :, csl])
                P.dma(zs_[:], zs_d[csl, :])
                for k0 in (0, 8, 16):
                    n = 8 if k0 < 16 else 4
                    for j in range(n):
                        P.tr(tb[:, j, :], xa_[:, k0 + j, :], C["ident"][:], inc=(j == n - 1))
                    if k0 < 16:
                        P.cp("act", xs_tm[:, k0 * 128:(k0 + 8) * 128], tb[:].rearrange("p a b -> p (a b)"))
                    else:
                        P.cp("act", bm_tm[:], tb[:, 0:4, :].rearrange("p a b -> p (a b)"))
                self.proj_tm(M1, Wdt, 0, 32, hT, c)
                P.tt("dve", sm["dtr"][:], M1[:, 0:32], dtb[:], ALU.add)
                P.act(sm["dtc"][:], sm["dtr"][:], AF.Exp)
                P.act(sm["dtc"][:], sm["dtc"][:], AF.Ln, bias=1.0)
                P.act(sm["lndt"][:], sm["dtc"][:], AF.Ln)
                P.tt("dve", sm["da"][:], sm["dtc"][:], a_b[:], ALU.mult)
                P.mm(M1[:, 64:96], lhsT=C["tri_f"][:], rhs=sm["da"][:])
                P.cp("dve", sm["acum"][:], M1[:, 64:96])
                P.mm(M1[:, 128:160], lhsT=C["ones_f"][:], rhs=sm["da"][:])
                P.cp("dve", sm["alast"][:], M1[:, 128:160])
                P.tt("dve", sm["nb"][:], sm["lndt"][:], sm["acum"][:], ALU.subtract)
                P.act(sm["ea"][:], sm["acum"][:], AF.Exp)
                P.tt("dve", sm["wj"][:], sm["alast"][:], sm["acum"][:], ALU.subtract)
                P.act(sm["wj"][:], sm["wj"][:], AF.Exp)
                P.tt("dve", sm["wj"][:], sm["wj"][:], sm["dtc"][:], ALU.mult)
                P.act(sm["dec"][:], sm["alast"][:], AF.Exp)
                for g in range(4):
                    P.mm(M0[:, g * 128:(g + 1) * 128], lhsT=xa_[:, 16 + g, :], rhs=xa_[:, 20 + g, :],
                         start=(g == 0), stop=True, inc=(g == 3), sgc=True)
                P.cp("act", cbT[:].rearrange("p a b -> p (a b)"), M0[:, :])
                P.tt("pool", xs_w[:].rearrange("p (h e) -> p h e", e=64), xs_tm[:].rearrange("p (h e) -> p h e", e=64),
                     sm["wj"][:].unsqueeze(2).to_broadcast([128, 32, 64]), ALU.mult)
                for g in range(4):
                    hs = slice(8 * g, 8 * g + 8)
                    gsl = slice(g * 512, (g + 1) * 512)
                    P.tt("dve", rhs1[:], C["tri_f"][:].unsqueeze(1).to_broadcast([128, 8, 128]),
                         sm["da"][:, hs].unsqueeze(2).to_broadcast([128, 8, 128]), ALU.mult)
                    P.tt("pool", rhs2[:], C["cneg_f"][:].unsqueeze(1).to_broadcast([128, 8, 128]),
                         sm["nb"][:, hs].unsqueeze(2).to_broadcast([128, 8, 128]), ALU.add)
                    for hf in range(2):
                        P.mm(R[hf][:, :], lhsT=C["ones_f"][:], rhs=rhs1[:, 4 * hf:4 * hf + 4, :].rearrange("p a b -> p (a b)"),
                             start=True, stop=False, inc=False)
                        P.mm(R[hf][:, :], lhsT=C["ident_f"][:], rhs=rhs2[:, 4 * hf:4 * hf + 4, :].rearrange("p a b -> p (a b)"),
                             start=False, stop=True)
                        P.act(Lg[:, 4 * hf:4 * hf + 4, :].rearrange("p a b -> p (a b)"), R[hf][:, :], AF.Exp)
                    P.tt("dve" if g % 2 else "pool", MTg[:], Lg[:], cbT[:, g, :].unsqueeze(1).to_broadcast([128, 8, 128]), ALU.mult)
                    for hh in range(8):
                        P.mm(Yp[:, hh * 64:(hh + 1) * 64], lhsT=MTg[:, hh, :], rhs=xs_tm[:, (8 * g + hh) * 64:(8 * g + hh + 1) * 64],
                             start=(hh == 0), stop=True, inc=(hh == 7), sgc=True)
                    P.mm(Yo[:, :], lhsT=xa_[:, 20 + g, :], rhs=Hbf[g][:])
                    yv = y[:, gsl].rearrange("p (h e) -> p h e", e=64)
                    P.tt("dve", yv, Yo[:, :].rearrange("p (h e) -> p h e", e=64),
                         sm["ea"][:, hs].unsqueeze(2).to_broadcast([128, 8, 64]), ALU.mult)
                    P.tt("dve", y[:, gsl], Yp[:, :], y[:, gsl], ALU.add)
                    P.tt("pool", tmp[:].rearrange("p (h e) -> p h e", e=64), xs_tm[:, gsl].rearrange("p (h e) -> p h e", e=64),
                         D_b[:, hs].unsqueeze(2).to_broadcast([128, 8, 64]), ALU.mult)
                    P.tt("pool", y[:, gsl], y[:, gsl], tmp[:], ALU.add)
                    P.mm(STp[:, :], lhsT=bm_tm[:, g * 128:(g + 1) * 128], rhs=xs_w[:, gsl])
                    Hv = H[g][:].rearrange("p (h e) -> p h e", e=64)
                    P.tt("pool", Hv, Hv, sm["dec"][:, hs].unsqueeze(2).to_broadcast([128, 8, 64]), ALU.mult)
                    P.tt("dve", H[g][:], STp[:, :], H[g][:], ALU.add)
                    P.cp("act", Hbf[g][:], H[g][:])
                P.tt("dve", y[:], y[:], zs_[:], ALU.mult)
                P.act(zs_[:], y[:], AF.Square, accum=ss[:])
                P.act(ss[:], ss[:], AF.Sqrt, bias=EPS, scale=1.0 / 2048)
                P.recip(ss[:], ss[:])
                P.stt("dve", ynb[:], y[:], ss[:, 0:1], ng_b[:], ALU.mult, ALU.mult)
                for k0 in (0, 8):
                    for j in range(8):
                        P.tr(tb[:, j, :], ynb[:, (k0 + j) * 128:(k0 + j + 1) * 128], C["ident"][:], inc=(j == 7))
                    P.cp("act", ycT[:, k0:k0 + 8, :], tb[:])
                P.dma(yc_d[:, :, csl], ycT[:])

    def phase_merge(self, l, hT, ya_d, yb_d, yc_d, mix_d):
        P, C = self.P, self.C
        I = self.inp
        with P.scope() as st:
            yh = [P.sb("yha", [128, 8, 1024], BF16, st), P.sb("yhb", [128, 8, 1024], BF16, st),
                  P.sb("yhc", [128, 16, 1024], BF16, st)]
            Wy = [[P.sb("Wya", [128, 8, 128], BF16, st), P.sb("Wyb", [128, 8, 128], BF16, st),
                   P.sb("Wyc", [128, 16, 128], BF16, st)] for _ in range(2)]
            Wg = [[P.sb("Wg", [128, 8, 128], BF16, st) for _ in range(3)] for _ in range(2)]
            sg = [P.sb("sg", [128, 512], F32, st) for _ in range(2)]
            macc = P.sb("macc", [128, 512], F32, st)
            mt = P.sb("mt", [128, 512], F32, st)
            mixh = P.sb("mixh", [128, 8, 1024], BF16, st)
            bank = [P.ps("bk", [128, 512], F32, st) for _ in range(6)]
            wsrc = [I["w_br_dsa"], I["w_br_nsa"], I["w_br_ssd"]]
            ysrc = [ya_d, yb_d, yc_d]
            bi = 0
            gi = 0
            it = 0
            for half in range(2):
                hsl = slice(half * 1024, (half + 1) * 1024)
                for br in range(3):
                    P.dma(yh[br][:], ysrc[br][:, :, hsl])
                for oc in range(8):
                    Wy_ = Wy[it % 2]
                    Wg_ = Wg[it % 2]
                    it += 1
                    for br in range(3):
                        self.wload(Wy_[br], wsrc[br], l, oc * 128, 128)
                        self.wload(Wg_[br], I["w_in"], l, OFF["mg"] + br * 1024 + oc * 128, 128)
                    for pc in range(2):
                        psl = slice(pc * 512, (pc + 1) * 512)
                        tok = half * 1024 + pc * 512
                        for br in range(3):
                            psy = bank[bi % 6]
                            bi += 1
                            psg = bank[bi % 6]
                            bi += 1
                            nk = 16 if br == 2 else 8
                            for kc in range(nk):
                                P.mm(psy[:, :], lhsT=Wy_[br][:, kc, :], rhs=yh[br][:, kc, psl],
                                     start=(kc == 0), stop=(kc == nk - 1))
                            self.proj_fm(psg, Wg_[br], 0, 128, hT, tok, 512)
                            s_ = sg[gi % 2]
                            gi += 1
                            P.act(s_[:], psg[:, :], AF.Sigmoid)
                            if br == 0:
                                P.tt("dve", macc[:], psy[:, :], s_[:], ALU.mult)
                            else:
                                P.tt("dve", mt[:], psy[:, :], s_[:], ALU.mult)
                                if br == 1:
                                    P.tt("pool", macc[:], macc[:], mt[:], ALU.add)
                                else:
                                    P.tt("pool", mixh[:, oc, psl], macc[:], mt[:], ALU.add)
                    P.maybe_barrier()
                P.dma(mix_d[:, :, hsl], mixh[:])

    def phase_out_norm(self, l, x_d, mix_d, x1_d, hT):
        P, C = self.P, self.C
        I = self.inp
        with P.scope() as st:
            Wo = P.sb("Wo", [128, 8, 1024], BF16, st)
            self.wload(Wo, I["w_out"], l, 0, 1024)
            gb = P.sb("gb2", [128, D], F32, st)
            P.dma(gb[:], I["norm2_g"][l].partition_broadcast(128))
            mq = [P.sb("mq", [128, 8, 128], BF16, st) for _ in range(2)]
            xb = [P.sb("xb", [128, D], F32, st) for _ in range(2)]
            x1b = [P.sb("x1b", [128, D], F32, st) for _ in range(2)]
            sq = P.sb("sq", [128, D], F32, st)
            hb = [P.sb("hb", [128, D], BF16, st) for _ in range(2)]
            ss = P.sb("ss", [128, 2], F32, st)
            bank = [P.ps("bk", [128, 512], F32, st) for _ in range(4)]
            pt = [P.ps("pt", [128, 8, 128], BF16, st) for _ in range(2)]
            for tt in range(NT):
                tsl = slice(tt * 128, (tt + 1) * 128)
                m_ = mq[tt % 2]
                P.dma(m_[:], mix_d[:, :, tsl])
                x_t = xb[tt % 2]
                P.dma(x_t[:], x_d[tsl, :])
                x1 = x1b[tt % 2]
                for hf in range(2):
                    ps = bank[(2 * tt + hf) % 4]
                    for oc in range(8):
                        P.mm(ps[:, :], lhsT=m_[:, oc, :], rhs=Wo[:, oc, hf * 512:(hf + 1) * 512],
                             start=(oc == 0), stop=(oc == 7))
                    P.tt("dve", x1[:, hf * 512:(hf + 1) * 512], ps[:, :], x_t[:, hf * 512:(hf + 1) * 512], ALU.add)
                P.dma(x1_d[tsl, :], x1[:])
                s1 = ss[:, tt % 2:tt % 2 + 1]
                P.act(sq[:], x1[:], AF.Square, accum=s1)
                P.act(s1, s1, AF.Sqrt, bias=EPS, scale=1.0 / D)
                P.recip(s1, s1)
                h_t = hb[tt % 2]
                P.stt("dve", h_t[:], x1[:], s1, gb[:], ALU.mult, ALU.mult)
                p_t = pt[tt % 2]
                for kc in range(8):
                    P.tr(p_t[:, kc, :], h_t[:, kc * 128:(kc + 1) * 128], C["ident"][:], inc=(kc == 7))
                P.cp("act", hT[:, :, tsl], p_t[:])

    def phase_ffn(self, l, hT, x1_d, xo_d):
        P, C = self.P, self.C
        I = self.inp
        with P.scope() as st:
            W1 = P.sb("W1", [128, 8, 4096], BF16, st)
            self.wload(W1, I["w_ff1"], l, 0, 4096)
            aT = P.sb("aT", [128, 32, 512], BF16, st)
            W2b = [P.sb("W2b", [128, 8, 512], BF16, st) for _ in range(2)]
            rb = [P.sb("rbf", [128, 512], F32, st) for _ in range(2)]
            xt = [P.sb("xt", [128, 512], F32, st) for _ in range(2)]
            xo = [P.sb("xo", [128, 512], F32, st) for _ in range(2)]
            fb = [P.ps("fb", [128, 512], F32, st) for _ in range(2)]
            ob = [P.ps("ob", [128, 512], F32, st) for _ in range(4)]
            wi = 0
            xi = 0
            for Q in range(4):
                for fc in range(32):
                    ps = fb[fc % 2]
                    self.proj_fm(ps, W1, fc * 128, 128, hT, Q * 512, 512)
                    r = rb[fc % 2]
                    P.act(r[:], ps[:, :], AF.Relu)
                    P.tt("pool" if fc % 2 else "dve", aT[:, fc, :], r[:], r[:], ALU.mult)
                for hf in range(2):
                    for blk in range(4):
                        W2_ = W2b[wi % 2]
                        wi += 1
                        self.wload(W2_, I["w_ff2"], l, hf * 512, 512, rows=(blk * 1024, (blk + 1) * 1024))
                        for tl in range(4):
                            for k in range(8):
                                P.mm(ob[tl][:, :], lhsT=aT[:, blk * 8 + k, tl * 128:(tl + 1) * 128], rhs=W2_[:, k, :],
                                     start=(blk == 0 and k == 0), stop=(blk == 3 and k == 7), inc=(k == 7))
                    for tl in range(4):
                        rows = slice((Q * 4 + tl) * 128, (Q * 4 + tl + 1) * 128)
                        cols = slice(hf * 512, (hf + 1) * 512)
                        x_ = xt[xi % 2]
                        o_ = xo[xi % 2]
                        xi += 1
                        P.dma(x_[:], x1_d[rows, cols])
                        P.tt("dve", o_[:], ob[tl][:, :], x_[:], ALU.add)
                        P.dma(xo_d[rows, cols], o_[:])
                P.maybe_barrier()

    def build(self):
        P = self.P
        I = self.inp
        self.din("x", [S, D])
        for name, shp in (("norm1_g", [DEPTH, D]), ("w_in", [DEPTH, D, D_IN]), ("dsa_q_norm", [DEPTH, 64]),
                          ("dsa_k_norm", [DEPTH, 64]), ("nsa_q_norm", [DEPTH, 64]), ("nsa_k_norm", [DEPTH, 3, 64]),
                          ("nsa_cmp_pos", [DEPTH, 2, 32, 64]), ("nsa_cmp_w", [DEPTH, 2, 32, 64, 64]),
                          ("ssd_conv_w", [DEPTH, 4, 3072]), ("ssd_conv_b", [DEPTH, 3072]),
                          ("ssd_dt_bias", [DEPTH, 32]), ("ssd_a_log", [DEPTH, 32]), ("ssd_d", [DEPTH, 32]),
                          ("ssd_norm_g", [DEPTH, 2048]), ("w_br_dsa", [DEPTH, D, D]), ("w_br_nsa", [DEPTH, D, D]),
                          ("w_br_ssd", [DEPTH, 2 * D, D]), ("w_out", [DEPTH, D, D]), ("norm2_g", [DEPTH, D]),
                          ("w_ff1", [DEPTH, D, 4 * D]), ("w_ff2", [DEPTH, 4 * D, D])):
            self.din(name, shp)
        self.load_consts()
        out = self.dout("out", [S, D])
        hT = P.sb("hT", [128, 8, S], BF16)
        ya_d = P.dram("ya", [128, 8, S], BF16)
        yb_d = P.dram("yb", [128, 8, S], BF16)
        yc_d = P.dram("yc", [128, 16, S], BF16)
        zs_d = P.dram("zs", [S, 2048], F32)
        xact_d = P.dram("xact", [128, 24, S], BF16)
        mix_d = P.dram("mix", [128, 8, S], BF16)
        x1_d = P.dram("x1", [S, D], F32)
        xm_d = P.dram("xm", [S, D], F32)
        dbg = self.debug
        x_cur = I["x"]
        for l in range(self.nlayers):
            x_nxt = out if l == self.nlayers - 1 else xm_d
            self.phase_norm(x_cur, I["norm1_g"][l], hT)
            if dbg == "hT":
                return self._dump(hT[:], [128, 8, S], BF16)
            if dbg in ("nsa", "nsa_k"):
                self.phase_nsa(l, hT, yb_d)
                return self._dump(yb_d, [128, 8, S], BF16)
            if dbg == "ssd":
                self.phase_ssd(l, hT, yc_d, zs_d, xact_d)
                return self._dump(yc_d, [128, 16, S], BF16)
            self.phase_dsa(l, hT, ya_d)
            if dbg is not None and dbg.startswith("dsa"):
                return self._dump(ya_d, [128, 8, S], BF16)
            self.phase_nsa(l, hT, yb_d)
            self.phase_ssd(l, hT, yc_d, zs_d, xact_d)
            self.phase_merge(l, hT, ya_d, yb_d, yc_d, mix_d)
            if dbg == "mix":
                return self._dump(mix_d, [128, 8, S], BF16)
            self.phase_out_norm(l, x_cur, mix_d, x1_d, hT)
            if dbg == "x1":
                return self._dump(x1_d, [S, D], F32)
            self.phase_ffn(l, hT, x1_d, x_nxt)
            x_cur = x_nxt
        P.finish()

    def _dump(self, src, shape, dt):
        if "dbg" not in self.out:
            o = self.dout("dbg", shape, dt)
            self.P.dma(o, src)
        self.P.finish()


_CACHE = {}


def kernel(**inputs):
    n = 8
    B = Builder()
    B.build()
    consts = _consts()
    shared = {}
    for name in B.inp:
        if name in consts:
            shared[name] = consts[name]
        elif name != "x":
            shared[name] = np.ascontiguousarray(np.asarray(inputs[name], dtype=np.float32))
    x = np.asarray(inputs["x"], dtype=np.float32)
    in_maps = []
    for b in range(n):
        m = dict(shared)
        m["x"] = np.ascontiguousarray(x[b])
        in_maps.append(m)
    res = run_bass_kernel_spmd(B.nc, in_maps, core_ids=list(range(n)))
    return np.stack([np.asarray(res.results[b]["out"], dtype=np.float32) for b in range(n)], axis=0)
```

```python
import contextlib
import os
import numpy as np
import concourse.bass as bass
import concourse.mybir as mybir
from concourse.bass_utils import run_bass_kernel_spmd

F32 = mybir.dt.float32
BF16 = mybir.dt.bfloat16
AF = mybir.ActivationFunctionType
ALU = mybir.AluOpType
AX = mybir.AxisListType

EPOCH = 240
NPOOL = 90

D = 1024
S = 2048
NT = 16
DEPTH = 2
EPS = 1e-6
OFF = dict(dq=0, dk=1024, dv=1088, iq=1152, ik=1408, iw=1440, nq=1448, nkv=2472, ng=3240,
           sz=3288, sxbc=5336, sdt=8408, mg=8440)
D_IN = 11512
NEGM = -30000.0


def _region(ap):
    t = ap.tensor
    name = t.name
    dims = [(int(s), int(c)) for s, c in ap.ap]
    off = int(ap.offset)
    if "DRam" in type(t).__name__:
        lo = hi = off
        for s, c in dims:
            if s >= 0:
                hi += s * (c - 1)
            else:
                lo += s * (c - 1)
        return (name, 0, 1, lo, hi + 1)
    if "PSum" in type(t).__name__:
        return (name, 0, 128, 0, 1 << 30)
    ps, pc = dims[0]
    if ps == 0:
        p0, f0 = 0, off
    else:
        p0 = off // ps
        f0 = off - p0 * ps
    lo = hi = f0
    for s, c in dims[1:]:
        if s >= 0:
            hi += s * (c - 1)
        else:
            lo += s * (c - 1)
    return (name, p0, p0 + pc, lo, hi + 1)


def _overlap(a, b):
    return a[1] < b[2] and b[1] < a[2] and a[3] < b[4] and b[3] < a[4]


def _contains(a, b):
    return a[1] <= b[1] and b[2] <= a[2] and a[3] <= b[3] and b[4] <= a[4]


class Prog:
    ENGS = ("pe", "act", "dve", "pool", "sp")

    def __init__(self, nc):
        self.nc = nc
        self.stack = contextlib.ExitStack()
        self.eng = {"pe": nc.tensor, "act": nc.scalar, "dve": nc.vector, "pool": nc.gpsimd, "sp": nc.sync}
        self.pool = [self.stack.enter_context(nc.semaphore(f"sp{i}")) for i in range(NPOOL)]
        self.bsem = [self.stack.enter_context(nc.semaphore(f"bar{i}")) for i in range(4)]
        self.nbar = 0
        self.uid = 0
        self.n_wait = 0
        self.n_ins = 0
        self.max_used = 0
        self._reset()

    def _reset(self):
        self.free = list(range(NPOOL))
        self.esem = {e: None for e in self.ENGS}
        self.ecnt = {e: 0 for e in self.ENGS}
        self.pend = {e: False for e in self.ENGS}
        self.seen = {e: {} for e in self.ENGS}
        self.dset = []
        self.dall = {}
        self.dnext = 0
        self.trk = {}

    def _alloc(self):
        assert self.free, "semaphore pool exhausted; add a barrier"
        i = self.free.pop()
        self.max_used = max(self.max_used, NPOOL - len(self.free))
        return i

    def sems_left(self):
        return len(self.free)

    def sb(self, name, shape, dtype, stack=None):
        self.uid += 1
        return (stack or self.stack).enter_context(
            self.nc.sbuf_tensor(f"{name}_{self.uid}", list(shape), dtype))

    def ps(self, name, shape, dtype, stack=None):
        self.uid += 1
        return (stack or self.stack).enter_context(
            self.nc.psum_tensor(f"{name}_{self.uid}", list(shape), dtype))

    def dram(self, name, shape, dtype):
        self.uid += 1
        return self.nc.dram_tensor(f"{name}_{self.uid}", list(shape), dtype).ap()

    @contextlib.contextmanager
    def scope(self):
        st = contextlib.ExitStack()
        try:
            yield st
        finally:
            self.barrier()
            st.close()

    def _deps(self, reads, writes):
        deps = []
        for ap in reads:
            r = _region(ap)
            t = self.trk.get(r[0])
            if t is None:
                continue
            for (wr, ev) in t["w"]:
                if _overlap(wr, r):
                    deps.append(ev)
        for ap in writes:
            r = _region(ap)
            t = self.trk.get(r[0])
            if t is None:
                continue
            for (wr, ev) in t["w"]:
                if _overlap(wr, r):
                    deps.append(ev)
            for (sk, rr), val in t["r"].items():
                if _overlap(rr, r):
                    deps.append((sk, val))
        return deps

    def _record(self, reads, writes, ev):
        for ap in reads:
            r = _region(ap)
            t = self.trk.setdefault(r[0], {"w": [], "r": {}})
            k = (ev[0], r)
            if t["r"].get(k, -1) < ev[1]:
                t["r"][k] = ev[1]
        for ap in writes:
            r = _region(ap)
            t = self.trk.setdefault(r[0], {"w": [], "r": {}})
            t["w"] = [(wr, e) for (wr, e) in t["w"] if not _contains(r, wr)]
            t["w"].append((r, ev))
            t["r"] = {k: v for k, v in t["r"].items() if not _contains(r, k[1])}

    def _wait(self, e, deps):
        eng = self.eng[e]
        need = {}
        for sk, val in deps:
            if self.seen[e].get(sk, 0) >= val:
                continue
            if need.get(sk, 0) < val:
                need[sk] = val
        for sk, val in need.items():
            assert 0 < val <= 255
            eng.wait_ge(self.pool[sk[1]], val)
            self.seen[e][sk] = val
            self.n_wait += 1

    def _next_ev(self, e, commit):
        if self.esem[e] is None or self.ecnt[e] >= EPOCH:
            self.esem[e] = self._alloc()
            self.ecnt[e] = 0
        ev = ((e, self.esem[e]), self.ecnt[e] + 1)
        if commit:
            self.ecnt[e] += 1
        return ev

    def op(self, e, fn, reads, writes, inc=True):
        deps = self._deps(reads, writes)
        if e == "pe":
            deps = [d for d in deps if d[0][0] != "pe"]
        self._wait(e, deps)
        ins = fn(self.eng[e])
        self.n_ins += 1
        ev = self._next_ev(e, inc)
        if inc:
            ins.then_inc(self.pool[ev[0][1]], 1)
            self.pend[e] = False
        else:
            self.pend[e] = True
        self._record(reads, writes, ev)
        return ins

    def dma(self, out, in_, q="sp", **kw):
        deps = self._deps([in_], [out])
        if q == "pool":
            ent = [self._alloc(), 0]
        else:
            if len(self.dset) < 8:
                self.dset.append([self._alloc(), 0])
            k = self.dnext % len(self.dset)
            self.dnext += 1
            if self.dset[k][1] >= 15:
                self.dset[k] = [self._alloc(), 0]
            ent = self.dset[k]
        if ent[1] > 0:
            deps.append((("d", ent[0]), 16 * ent[1]))
        self._wait(q, deps)
        ins = self.eng[q].dma_start(out=out, in_=in_, **kw)
        ins.then_inc(self.pool[ent[0]], 16)
        ent[1] += 1
        self.dall[ent[0]] = ent[1]
        self.n_ins += 1
        self._record([in_], [out], (("d", ent[0]), 16 * ent[1]))
        return ins

    def barrier(self):
        evs = []
        for e in self.ENGS:
            assert not self.pend[e], f"pending non-inc op on {e} at barrier"
            if self.esem[e] is not None and self.ecnt[e] > 0:
                evs.append(((e, self.esem[e]), self.ecnt[e]))
        for k, c in self.dall.items():
            evs.append((("d", k), 16 * c))
        for e in self.ENGS:
            self._wait(e, evs)
        b0 = self.bsem[2 * (self.nbar % 2)]
        b1 = self.bsem[2 * (self.nbar % 2) + 1]
        p0 = self.bsem[2 * ((self.nbar + 1) % 2)]
        p1 = self.bsem[2 * ((self.nbar + 1) % 2) + 1]
        for e in self.ENGS:
            if e != "sp":
                self.eng[e].sem_inc(b0, 1)
        sp = self.eng["sp"]
        sp.wait_ge(b0, 4)
        used = [i for i in range(NPOOL) if i not in set(self.free)]
        for i in used:
            sp.sem_clear(self.pool[i])
        sp.sem_clear(p0)
        sp.sem_clear(p1)
        sp.sem_inc(b1, 1)
        for e in self.ENGS:
            if e != "sp":
                self.eng[e].wait_ge(b1, 1)
        self.nbar += 1
        self._reset()

    def maybe_barrier(self, min_free=35):
        if len(self.free) < min_free:
            self.barrier()

    def finish(self):
        self.barrier()
        self.stack.close()

    def mm(self, out, lhsT, rhs, start=True, stop=True, inc=None, sgc=False):
        if inc is None:
            inc = stop
        return self.op("pe", lambda e: e.matmul(out, lhsT=lhsT, rhs=rhs, start=start, stop=stop,
                                                skip_group_check=sgc),
                       [lhsT, rhs], [out], inc=inc)

    def tr(self, out, in_, ident, inc=True):
        return self.op("pe", lambda e: e.transpose(out=out, in_=in_, identity=ident), [in_, ident], [out], inc=inc)

    def act(self, out, in_, func, bias=None, scale=None, accum=None):
        kw = {}
        reads = [in_]
        writes = [out]
        if bias is not None:
            kw["bias"] = bias
            if not isinstance(bias, (int, float)):
                reads.append(bias)
        if scale is not None:
            kw["scale"] = scale
            if not isinstance(scale, (int, float)):
                reads.append(scale)
        if accum is not None:
            kw["accum_out"] = accum
            writes.append(accum)
        return self.op("act", lambda e: e.activation(out=out, in_=in_, func=func, **kw), reads, writes)

    def cp(self, eng, out, in_):
        if eng == "act":
            return self.op("act", lambda e: e.copy(out=out, in_=in_), [in_], [out])
        return self.op(eng, lambda e: e.tensor_copy(out=out, in_=in_), [in_], [out])

    def tt(self, eng, out, in0, in1, op):
        return self.op(eng, lambda e: e.tensor_tensor(out=out, in0=in0, in1=in1, op=op), [in0, in1], [out])

    def ts(self, eng, out, in0, s1, s2=None, op0=ALU.mult, op1=None, accum=None):
        reads = [in0] + [s for s in (s1, s2) if s is not None and not isinstance(s, (int, float))]
        writes = [out] + ([accum] if accum is not None else [])
        kw = {}
        if op1 is not None:
            kw["op1"] = op1
        if accum is not None:
            kw["accum_out"] = accum
        return self.op(eng, lambda e: e.tensor_scalar(out=out, in0=in0, scalar1=s1, scalar2=s2, op0=op0, **kw),
                       reads, writes)

    def stt(self, eng, out, in0, scalar, in1, op0, op1):
        reads = [in0, in1] + ([scalar] if not isinstance(scalar, (int, float)) else [])
        return self.op(eng, lambda e: e.scalar_tensor_tensor(out=out, in0=in0, scalar=scalar, in1=in1,
                                                             op0=op0, op1=op1), reads, [out])

    def amul(self, out, in_, val):
        return self.op("act", lambda e: e.mul(out=out, in_=in_, mul=val), [in_], [out])

    def memset(self, eng, ap, val):
        return self.op(eng, lambda e: e.memset(ap, val), [], [ap])

    def recip(self, out, in_):
        return self.op("dve", lambda e: e.reciprocal(out=out, in_=in_), [in_], [out])


def _consts():
    c = {}
    p = np.arange(128)
    c["c_ident"] = np.eye(128, dtype=np.float32)
    c["c_ones"] = np.ones((128, 128), np.float32)
    c["c_blk64"] = (p[:, None] // 64 == p[None, :] // 64).astype(np.float32)
    c["c_tri"] = (p[:, None] <= p[None, :]).astype(np.float32)
    c["c_cneg"] = np.where(p[:, None] > p[None, :], NEGM, 0.0).astype(np.float32)
    c["c_wneg"] = np.where(p[:, None] <= p[None, :], NEGM, 0.0).astype(np.float32)
    c["c_cnegtm"] = np.where(p[None, :] > p[:, None], -3e30, 0.0).astype(np.float32)
    s = np.arange(S)
    c["c_E"] = (s[None, :] // 64 == np.arange(32)[:, None]).astype(np.float32)
    n = np.arange(128)
    vis = (16 * n[:, None] + 31 <= s[None, :])
    c["c_visneg"] = np.where(vis, 0.0, NEGM).astype(np.float32)
    j = np.arange(32)
    cover = ((16 * n[:, None] < 64 * j[None, :] + 64) & (16 * n[:, None] + 32 > 64 * j[None, :]))
    ce = np.zeros((128, 33), np.float32)
    ce[:, 0] = 1.0
    ce[:, 1:] = cover
    ce[127] = 0.0
    c["c_cover"] = ce
    t = (np.arange(NT)[None, :, None] * 128 + p[:, None, None])
    cur = t // 64
    jj = j[None, None, :]
    forced = (jj == cur) | (jj == 0)
    future = jj > cur
    c["c_selkeep"] = (~(forced | future)).astype(np.float32)
    c["c_selbias"] = np.where(forced, 1e9, np.where(future, -1e9, 0.0)).astype(np.float32)
    return c


class Builder:
    def __init__(self, debug=None, nlayers=DEPTH):
        self.debug = debug
        self.nlayers = nlayers
        import os
        self.lim = int(os.environ.get("KLIM", "16"))
        nc = bass.Bass("TRN2", target_bir_lowering=False)
        self.nc = nc
        self.P = Prog(nc)
        self.inp = {}
        self.out = {}

    def din(self, name, shape, dt=F32):
        self.inp[name] = self.nc.dram_tensor(name, list(shape), dt, kind="ExternalInput").ap()
        return self.inp[name]

    def dout(self, name, shape, dt=F32):
        self.out[name] = self.nc.dram_tensor(name, list(shape), dt, kind="ExternalOutput").ap()
        return self.out[name]

    def load_consts(self):
        P = self.P
        C = {}
        shapes = {k: v.shape for k, v in _consts().items()}
        for k, shp in shapes.items():
            self.din(k, shp)
        self.wstage = [P.sb("wstage", [128, 4096], F32) for _ in range(2)]
        self.wsi = 0

        def ld(name, key, shape, dt, rows=None):
            t = P.sb(name, shape, dt)
            src = self.inp[key]
            if dt == F32:
                P.dma(t[:], src)
            else:
                n = int(np.prod(shape[1:]))
                flat = "p a b -> p (a b)" if len(shape) == 3 else None
                for a0 in range(0, n, 4096):
                    b0 = min(n, a0 + 4096)
                    stg = self.wstage[self.wsi % 2]
                    self.wsi += 1
                    P.dma(stg[0:shape[0], 0:b0 - a0], src[:, a0:b0])
                    P.cp("pool", t[:, a0:b0], stg[0:shape[0], 0:b0 - a0])
            return t
        C["ident_f"] = ld("ident_f", "c_ident", [128, 128], F32)
        C["ident"] = ld("ident", "c_ident", [128, 128], BF16)
        C["ones_f"] = ld("ones_f", "c_ones", [128, 128], F32)
        C["blk64_f"] = ld("blk64_f", "c_blk64", [128, 128], F32)
        C["tri_f"] = ld("tri_f", "c_tri", [128, 128], F32)
        C["cneg"] = ld("cneg", "c_cneg", [128, 128], BF16)
        C["zero"] = P.sb("zero", [128, 128], BF16)
        P.memset("pool", C["zero"][:], 0.0)
        C["cneg_f"] = ld("cneg_f", "c_cneg", [128, 128], F32)
        C["wneg"] = ld("wneg", "c_wneg", [128, 128], BF16)
        C["cnegtm"] = ld("cnegtm", "c_cnegtm", [128, 128], F32)
        C["E"] = ld("E", "c_E", [32, S], BF16)
        C["visneg"] = ld("visneg", "c_visneg", [128, S], BF16)
        C["cover"] = ld("cover", "c_cover", [128, 33], BF16)
        C["selkeep"] = ld("selkeep", "c_selkeep", [128, NT, 32], F32)
        C["selbias"] = ld("selbias", "c_selbias", [128, NT, 32], F32)
        self.C = C

    def wload(self, dst, wdram, l, c0, n, rows=None):
        w2 = wdram[l] if rows is None else wdram[l][rows[0]:rows[1]]
        src = w2.rearrange("(kc p) n -> p kc n", p=128)[:, :, c0:c0 + n]
        nk = src.shape[1]
        step = max(1, 4096 // nk)
        for a in range(0, n, step):
            b = min(n, a + step)
            stg = self.wstage[self.wsi % 2]
            self.wsi += 1
            v = stg[:, 0:nk * (b - a)].rearrange("p (k n) -> p k n", k=nk)
            self.P.dma(v, src[:, :, a:b], q="sp")
            self.P.cp("pool", dst[:, :, a:b], v)

    def proj_fm(self, ps, Wt, c0, M, hT, t0, N):
        for kc in range(8):
            self.P.mm(ps[0:M, 0:N], lhsT=Wt[:, kc, c0:c0 + M], rhs=hT[:, kc, t0:t0 + N],
                      start=(kc == 0), stop=(kc == 7))

    def proj_tm(self, ps, Wt, c0, n, hT, tile):
        for kc in range(8):
            self.P.mm(ps[:, 0:n], lhsT=hT[:, kc, tile * 128:(tile + 1) * 128], rhs=Wt[:, kc, c0:c0 + n],
                      start=(kc == 0), stop=(kc == 7))

    def norm_fm(self, ps, ps2, M, N, gcol, out, sq, rs):
        P, C = self.P, self.C
        P.act(sq[0:M, 0:N], ps[0:M, 0:N], AF.Square)
        P.mm(ps2[0:M, 0:N], lhsT=C["blk64_f"][0:M, 0:M], rhs=sq[0:M, 0:N])
        P.act(rs[0:M, 0:N], ps2[0:M, 0:N], AF.Sqrt, bias=EPS, scale=1.0 / 64)
        P.recip(rs[0:M, 0:N], rs[0:M, 0:N])
        P.stt("dve", out, ps[0:M, 0:N], gcol, rs[0:M, 0:N], ALU.mult, ALU.mult)

    def gain_col(self, name, vec64, stack):
        t = self.P.sb(name, [128, 1], F32, stack)
        src = vec64.rearrange("(p o) -> p o", o=1)
        self.P.dma(t[0:64, :], src)
        self.P.dma(t[64:128, :], src)
        return t

    def phase_norm(self, xd, gvec, hT, store_T=None):
        P, C = self.P, self.C
        with P.scope() as st:
            gb = P.sb("gb", [128, D], F32, st)
            P.dma(gb[:], gvec.partition_broadcast(128))
            xb = [P.sb("xb", [128, D], F32, st) for _ in range(2)]
            sq = P.sb("sq", [128, D], F32, st)
            hb = [P.sb("hb", [128, D], BF16, st) for _ in range(2)]
            ss = P.sb("ss", [128, 2], F32, st)
            pt = [P.ps("pt", [128, 8, 128], BF16, st) for _ in range(2)]
            for tt in range(NT):
                x_t = xb[tt % 2]
                P.dma(x_t[:], xd[tt * 128:(tt + 1) * 128, :])
                s1 = ss[:, tt % 2:tt % 2 + 1]
                P.act(sq[:], x_t[:], AF.Square, accum=s1)
                P.act(s1, s1, AF.Sqrt, bias=EPS, scale=1.0 / D)
                P.recip(s1, s1)
                h_t = hb[tt % 2]
                P.stt("dve", h_t[:], x_t[:], s1, gb[:], ALU.mult, ALU.mult)
                p_t = pt[tt % 2]
                for kc in range(8):
                    P.tr(p_t[:, kc, :], h_t[:, kc * 128:(kc + 1) * 128], C["ident"][:], inc=(kc == 7))
                P.cp("act" if tt % 2 else "dve", hT[:, :, tt * 128:(tt + 1) * 128], p_t[:])

    def phase_dsa(self, l, hT, ya_d):
        P, C = self.P, self.C
        I = self.inp
        w_in = I["w_in"]
        with P.scope() as st:
            Wq = P.sb("Wq", [128, 8, 1024], BF16, st)
            self.wload(Wq, w_in, l, OFF["dq"], 1024)
            Wk = P.sb("Wk", [128, 8, 128], BF16, st)
            self.wload(Wk[:, :, 0:64], w_in, l, OFF["dk"], 64)
            self.wload(Wk[:, :, 64:128], w_in, l, OFF["dk"], 64)
            Wv = P.sb("Wv", [128, 8, 64], BF16, st)
            self.wload(Wv, w_in, l, OFF["dv"], 64)
            Wiq = P.sb("Wiq", [128, 8, 256], BF16, st)
            self.wload(Wiq, w_in, l, OFF["iq"], 256)
            Wik = P.sb("Wik", [128, 8, 128], BF16, st)
            for r in range(4):
                self.wload(Wik[:, :, r * 32:(r + 1) * 32], w_in, l, OFF["ik"], 32)
            Wiw = P.sb("Wiw", [128, 8, 8], BF16, st)
            self.wload(Wiw, w_in, l, OFF["iw"], 8)
            gq = self.gain_col("gq", I["dsa_q_norm"][l], st)
            gk = self.gain_col("gk", I["dsa_k_norm"][l], st)

            kT = P.sb("kT", [128, S], BF16, st)
            v1 = P.sb("v1", [128, NT, 65], BF16, st)
            ikT4 = P.sb("ikT4", [128, S], BF16, st)
            ikbd = P.sb("ikbd", [128, NT, 4, 128], BF16, st)
            qz = P.sb("qz", [128, 16, 512], BF16, st)
            iqT = P.sb("iqT", [128, 2, S], BF16, st)
            iw = P.sb("iw", [128, NT, 8], F32, st)
            acc = P.sb("acc", [128, S], F32, st)
            work = P.sb("work", [128, S], F32, st)
            mx = P.sb("mx", [128, 8], F32, st)
            negm = P.sb("negm", [128, S], BF16, st)
            negmT = [P.sb("negmT", [128, NT, 128], BF16, st) for _ in range(2)]
            rb = [P.sb("rb", [128, 512], F32, st) for _ in range(2)]
            pTb = [P.sb("pTb", [128, 512], BF16, st) for _ in range(2)]
            ytm = P.sb("ytm", [128, 1024], BF16, st)
            yT = P.sb("yT", [128, 8, 512], BF16, st)
            sq = P.sb("sq", [128, 512], F32, st)
            rs = P.sb("rs", [128, 512], F32, st)
            rc = P.sb("rc", [128, 4], F32, st)
            bank = [P.ps("bk", [128, 512], F32, st) for _ in range(6)]
            tb = [P.ps("tb", [128, 8, 128], BF16, st) for _ in range(2)]
            Sb, Ob, Ib = bank[0:2], bank[2:4], bank[4:6]

            P.memset("pool", v1[:, :, 64:65], 1.0)
            P.memset("pool", ikbd[:], 0.0)
            P.memset("pool", qz[:], 0.0)
            for Q in range(4):
                t0 = Q * 512
                self.proj_fm(Ib[0], Wk, 0, 128, hT, t0, 512)
                self.norm_fm(Ib[0], Ib[1], 128, 512, gk[:, 0:1], kT[:, t0:t0 + 512], sq, rs)
                self.proj_fm(Ib[0], Wik, 0, 128, hT, t0, 512)
                P.cp("act", ikT4[:, t0:t0 + 512], Ib[0][:, :])
                for c in range(2):
                    self.proj_fm(Ib[c], Wiq, c * 128, 128, hT, t0, 512)
                    P.cp("act", iqT[:, c, t0:t0 + 512], Ib[c][:, :])
            for hh in range(4):
                P.dma(ikbd[32 * hh:32 * hh + 32, :, hh, :],
                      ikT4[32 * hh:32 * hh + 32, :].rearrange("p (k s) -> p k s", s=128))
            for tt in range(NT):
                ps = Ib[tt % 2]
                self.proj_tm(ps, Wv, 0, 64, hT, tt)
                P.cp("act", v1[:, tt, 0:64], ps[:, 0:64])
                self.proj_tm(ps, Wiw, 64, 8, hT, tt) if False else None
            for tt in range(NT):
                ps = Ib[tt % 2]
                self.proj_tm(ps, Wiw, 0, 8, hT, tt)
                P.cp("act", iw[:, tt, :], ps[:, 0:8])

            cnt = {"si": 0, "ii": 0}

            def stage_q(Q):
                t0 = Q * 512
                for c in range(8):
                    self.proj_fm(Ib[0], Wq, c * 128, 128, hT, t0, 512)
                    P.act(sq[:, :], Ib[0][:, :], AF.Square)
                    P.mm(Ib[1][:, :], lhsT=C["blk64_f"][:], rhs=sq[:, :])
                    P.act(rs[:, :], Ib[1][:, :], AF.Sqrt, bias=EPS, scale=1.0 / 64)
                    P.recip(rs[:, :], rs[:, :])
                    for hf in range(2):
                        psl = slice(64 * hf, 64 * hf + 64)
                        P.stt("dve", qz[psl, 2 * c + hf, :], Ib[0][psl, :], gq[psl, 0:1], rs[psl, :], ALU.mult, ALU.mult)

            def stage_a(qt):
                nk = qt + 1
                tsl = slice(qt * 128, (qt + 1) * 128)
                if qt < 2:
                    P.memset("pool", negm[:, 0:nk * 128], 0.0)
                    return
                for kt in range(nk):
                    a_kt = acc[:, kt * 128:(kt + 1) * 128]
                    for c in range(2):
                        ps = Ib[cnt["ii"] % 2]
                        r = rb[cnt["ii"] % 2]
                        cnt["ii"] += 1
                        P.mm(ps[:, :], lhsT=iqT[:, c, tsl], rhs=ikbd[:, kt].rearrange("p a b -> p (a b)"))
                        P.act(r[:], ps[:, :], AF.Relu)
                        for hh in range(4):
                            h = 4 * c + hh
                            if h == 0:
                                P.ts("dve", a_kt, r[:, 0:128], iw[:, qt, 0:1], None, op0=ALU.mult)
                            else:
                                P.stt("dve", a_kt, r[:, hh * 128:(hh + 1) * 128], iw[:, qt, h:h + 1], a_kt,
                                      ALU.mult, ALU.add)
                d_sl = slice(qt * 128, (qt + 1) * 128)
                P.tt("dve", acc[:, d_sl], acc[:, d_sl], C["cnegtm"][:], ALU.add)
                cur = acc
                for r_ in range(32):
                    P.op("dve", lambda e: e.max(out=mx[:], in_=cur[:, 0:nk * 128]),
                         [cur[:, 0:nk * 128]], [mx[:]])
                    P.op("dve", lambda e: e.match_replace(out=work[:, 0:nk * 128], in_to_replace=mx[:],
                                                          in_values=cur[:, 0:nk * 128], imm_value=-1e30),
                         [mx[:], cur[:, 0:nk * 128]], [work[:, 0:nk * 128]])
                    cur = work
                P.ts("dve", negm[:, 0:nk * 128], work[:, 0:nk * 128], -1e29, 1.0, op0=ALU.is_le,
                     op1=ALU.subtract)

            def stage_b(qt):
                nk = qt + 1
                nT = negmT[qt % 2]
                for k0 in range(0, nk, 8):
                    k1 = min(nk, k0 + 8)
                    t_ = tb[0]
                    for kt in range(k0, k1):
                        P.tr(t_[:, kt - k0, :], negm[:, kt * 128:(kt + 1) * 128], C["ident"][:],
                             inc=(kt == k1 - 1))
                    P.amul(nT[:, k0:k1, :], t_[:, 0:k1 - k0, :], -NEGM)
                P.tt("pool", nT[:, qt, :], nT[:, qt, :], C["cneg"][:], ALU.add)

            def stage_c(qt):
                nk = qt + 1
                tl = qt % 4
                tsl = slice(tl * 128, (tl + 1) * 128)
                nT = negmT[qt % 2]
                for hg in range(4):
                    O = Ob[hg % 2]
                    O3 = O[:, 0:260].rearrange("p (h e) -> p h e", e=65)
                    for kt in range(nk):
                        Sp = Sb[cnt["si"] % 2]
                        pT = pTb[cnt["si"] % 2]
                        cnt["si"] += 1
                        P.mm(Sp[:, :], lhsT=C["ident"][:], rhs=nT[:, kt, :].unsqueeze(1).to_broadcast([128, 4, 128]),
                             start=True, stop=False, inc=False)
                        P.mm(Sp[:, :], lhsT=kT[:, kt * 128:(kt + 1) * 128], rhs=qz[:, 4 * hg:4 * hg + 4, tsl],
                             start=False, stop=True)
                        P.act(pT[:], Sp[:, :], AF.Exp, scale=0.125)
                        for hh in range(4):
                            P.mm(O3[:, hh, :], lhsT=pT[:, hh * 128:(hh + 1) * 128], rhs=v1[:, kt, :],
                                 start=(kt == 0 and hh == 0), stop=(kt == nk - 1), inc=(hh == 3), sgc=True)
                    P.recip(rc[:], O3[:, :, 64])
                    P.tt("dve", ytm[:, hg * 256:(hg + 1) * 256].rearrange("p (h e) -> p h e", e=64),
                         O3[:, :, 0:64], rc[:].unsqueeze(2).to_broadcast([128, 4, 64]), ALU.mult)
                t_ = tb[1]
                for c in range(8):
                    P.tr(t_[:, c, :], ytm[:, c * 128:(c + 1) * 128], C["ident"][:], inc=(c == 7))
                P.cp("act", yT[:, :, tsl], t_[:])
                if tl == 3:
                    Q = qt // 4
                    P.dma(ya_d[:, :, Q * 512:(Q + 1) * 512], yT[:], q="sp")

            nq = min(NT, self.lim)
            stage_a(0)
            stage_b(0)
            for qt in range(nq):
                P.maybe_barrier()
                if qt + 1 < nq:
                    stage_a(qt + 1)
                if qt % 4 == 0:
                    stage_q(qt // 4)
                stage_c(qt)
                if qt + 1 < nq:
                    stage_b(qt + 1)

    def _nsa_kside(self, l, hT, st, env):
        P, C = self.P, self.C
        I = self.inp
        Wkv, Wks, Wkw = env["Wkv"], env["Wks"], env["Wkw"]
        gks, gkw, gkc = env["gks"], env["gkw"], env["gkc"]
        ksT, kwT, vs1, vw1, kcmpT, vext = env["ksT"], env["kwT"], env["vs1"], env["vw1"], env["kcmpT"], env["vext"]
        sq, rs, ss, Mb, tb = env["sq"], env["rs"], env["ss"], env["Mb"], env["tb"]
        cmpw = []
        posB = []
        for kv in range(2):
            cw = P.sb("cmpw", [128, 32, 64], BF16, st)
            stg = self.wstage[self.wsi % 2]
            self.wsi += 1
            sv = stg[:, 0:2048].rearrange("p (l e) -> p l e", e=64)
            src = I["nsa_cmp_w"][l, kv].rearrange("l d e -> d l e")
            P.dma(sv[0:64], src)
            P.dma(sv[64:128], src)
            P.cp("pool", cw[:], sv)
            cmpw.append(cw)
            pt_ = P.sb("posT", [128, 32], F32, st)
            psrc = I["nsa_cmp_pos"][l, kv].rearrange("l d -> d l")
            P.dma(pt_[0:64, :], psrc, allow_slow_non_contiguous=True)
            P.dma(pt_[64:128, :], psrc, allow_slow_non_contiguous=True)
            pb = P.sb("posB", [128, 32, 127], BF16, st)
            P.cp("pool", pb[:], pt_[:].unsqueeze(2).to_broadcast([128, 32, 127]))
            posB.append(pb)

        kcT = P.sb("kcT", [128, S], BF16, st)
        vcT = P.sb("vcT", [128, S], BF16, st)
        kd = P.sb("kd", [128, 128], BF16, st)
        P.memset("pool", vs1[:, :, :, 64:65], 1.0)
        P.memset("pool", vw1[:, :, :, 64:65], 1.0)
        P.memset("pool", kd[:], 0.0)
        for g in range(2):
            P.memset("pool", vext[g][:], 0.0)
        for Q in range(4):
            t0 = Q * 512
            self.proj_fm(Mb[0], Wkv, 0, 128, hT, t0, 512)
            P.cp("act", kcT[:, t0:t0 + 512], Mb[0][:, :])
            self.proj_fm(Mb[1], Wkv, 128, 128, hT, t0, 512)
            P.cp("act", vcT[:, t0:t0 + 512], Mb[1][:, :])
            for g in range(2):
                self.proj_fm(Mb[0], Wks, g * 128, 128, hT, t0, 512)
                self.norm_fm(Mb[0], Mb[1], 128, 512, gks[:, 0:1], ksT[g][:, t0:t0 + 512], sq, rs)
                self.proj_fm(Mb[0], Wkw, g * 128, 128, hT, t0, 512)
                self.norm_fm(Mb[0], Mb[1], 128, 512, gkw[:, 0:1], kwT[g][:, t0:t0 + 512], sq, rs)
        for tt in range(NT):
            ps = Mb[tt % 2]
            self.proj_tm(ps, Wkv, 384, 128, hT, tt)
            P.cp("act", vs1[:, tt, :, 0:64], ps[:, 0:128].rearrange("p (g e) -> p g e", e=64))
            self.proj_tm(ps, Wkv, 640, 128, hT, tt)
            P.cp("act", vw1[:, tt, :, 0:64], ps[:, 0:128].rearrange("p (g e) -> p g e", e=64))
        P.maybe_barrier()
        def strided(t, lo, off):
            return bass.AP(t[:].tensor, lo * S + off, [[S, 64], [16, 127]])
        for g in range(2):
            lo = g * 64
            for kv, src in ((0, kcT), (1, vcT)):
                ps = Mb[kv]
                for li in range(32):
                    P.mm(ps[0:127, 0:64], lhsT=strided(src, lo, li), rhs=cmpw[kv][lo:lo + 64, li, :],
                         start=(li == 0), stop=False, inc=False)
                for li in range(32):
                    P.mm(ps[0:127, 0:64], lhsT=posB[kv][lo:lo + 64, li, :], rhs=cmpw[kv][lo:lo + 64, li, :],
                         start=False, stop=(li == 31), inc=(li == 31))
            P.act(sq[0:127, 0:64], Mb[0][0:127, 0:64], AF.Square, accum=ss[0:127, :])
            P.act(ss[0:127, :], ss[0:127, :], AF.Sqrt, bias=EPS, scale=1.0 / 64)
            P.recip(ss[0:127, :], ss[0:127, :])
            P.stt("dve", kd[0:127, 0:64], Mb[0][0:127, 0:64], ss[0:127, 0:1], gkc[0:127, :], ALU.mult, ALU.mult)
            P.cp("pool", kd[0:127, 64:128], kd[0:127, 0:64])
            P.tr(tb[0][:, 0, :], kd[:], C["ident"][:])
            P.cp("act", kcmpT[g][:], tb[0][:, 0, :])
            P.cp("act", vext[g][0:127, 0:64], Mb[1][0:127, 0:64])
            P.cp("pool", vext[g][0:127, 64:97], C["cover"][0:127, :])
        P.maybe_barrier()


    def phase_nsa(self, l, hT, yb_d):
        P, C = self.P, self.C
        I = self.inp
        w_in = I["w_in"]
        KV = OFF["nkv"]
        with P.scope() as st:
            Wq = P.sb("Wnq", [128, 8, 1024], BF16, st)
            self.wload(Wq, w_in, l, OFF["nq"], 1024)
            Wkv = P.sb("Wkv", [128, 8, 768], BF16, st)
            self.wload(Wkv, w_in, l, KV, 768)
            Wks = P.sb("Wks", [128, 8, 256], BF16, st)
            Wkw = P.sb("Wkw", [128, 8, 256], BF16, st)
            for g in range(2):
                for r in range(2):
                    self.wload(Wks[:, :, g * 128 + r * 64:g * 128 + r * 64 + 64], w_in, l, KV + 256 + g * 64, 64)
                    self.wload(Wkw[:, :, g * 128 + r * 64:g * 128 + r * 64 + 64], w_in, l, KV + 512 + g * 64, 64)
            Wng = P.sb("Wng", [128, 8, 48], BF16, st)
            self.wload(Wng, w_in, l, OFF["ng"], 48)
            gq = self.gain_col("gnq", I["nsa_q_norm"][l], st)
            gks = self.gain_col("gks", I["nsa_k_norm"][l, 1], st)
            gkw = self.gain_col("gkw", I["nsa_k_norm"][l, 2], st)
            gkc = P.sb("gkc", [128, 64], F32, st)
            P.dma(gkc[:], I["nsa_k_norm"][l, 0].partition_broadcast(128))
            ksT = [P.sb("ksT", [128, S], BF16, st) for _ in range(2)]
            kwT = [P.sb("kwT", [128, S], BF16, st) for _ in range(2)]
            vs1 = P.sb("vs1", [128, NT, 2, 65], BF16, st)
            vw1 = P.sb("vw1", [128, NT, 2, 65], BF16, st)
            kcmpT = [P.sb("kcmpT", [128, 128], BF16, st) for _ in range(2)]
            vext = [P.sb("vext", [128, 97], BF16, st) for _ in range(2)]
            sq = P.sb("sq", [128, 512], F32, st)
            rs = P.sb("rs", [128, 512], F32, st)
            ss = P.sb("ss1", [128, 1], F32, st)
            bank = [P.ps("bk", [128, 512], F32, st) for _ in range(6)]
            tb = [P.ps("tb", [128, 8, 128], BF16, st) for _ in range(2)]
            Sb, Ob, Mb = bank[0:2], bank[2:4], bank[4:6]
            with P.scope() as st2:
                self._nsa_kside(l, hT, st2, locals())
            nqz = P.sb("nqz", [128, 16, 512], BF16, st)
            P.memset("pool", nqz[:], 0.0)
            gs = P.sb("gs", [128, 48], F32, st)
            yacc = P.sb("yacc", [128, 16, 64], F32, st)
            ytmp = P.sb("ytmp", [128, 4, 64], F32, st)
            impn = P.sb("impn", [128, 16, 32], F32, st)
            imp = P.sb("imp", [128, 32], F32, st)
            mx = P.sb("mx8", [128, 8], F32, st)
            negblk = P.sb("negblk", [128, 32], BF16, st)
            negblkT = [P.sb("negblkT", [32, 128], BF16, st) for _ in range(2)]
            pTb = [P.sb("pTb", [128, 512], BF16, st) for _ in range(2)]
            ytm = P.sb("ytm", [128, 1024], BF16, st)
            yT = P.sb("yT", [128, 8, 512], BF16, st)
            rc = P.sb("rc", [128, 4], F32, st)
            cf = P.sb("cf", [128, 4], F32, st)

            si = 0
            oi = 0
            for Q in range(4):
                t0 = Q * 512
                for c in range(8):
                    self.proj_fm(Mb[0], Wq, c * 128, 128, hT, t0, 512)
                    P.act(sq[:, :], Mb[0][:, :], AF.Square)
                    P.mm(Mb[1][:, :], lhsT=C["blk64_f"][:], rhs=sq[:, :])
                    P.act(rs[:, :], Mb[1][:, :], AF.Sqrt, bias=EPS, scale=1.0 / 64)
                    P.recip(rs[:, :], rs[:, :])
                    for hf in range(2):
                        psl = slice(64 * hf, 64 * hf + 64)
                        P.stt("dve", nqz[psl, 2 * c + hf, :], Mb[0][psl, :], gq[psl, 0:1], rs[psl, :], ALU.mult, ALU.mult)
                for tl in range(4):
                    qt = Q * 4 + tl
                    if qt >= self.lim:
                        continue
                    P.maybe_barrier()
                    tsl = slice(tl * 128, (tl + 1) * 128)
                    self.proj_tm(Mb[0], Wng, 0, 48, hT, qt)
                    P.act(gs[:], Mb[0][:, 0:48], AF.Sigmoid)
                    gs3 = gs[:].rearrange("p (h b) -> p h b", b=3)

                    def scores(Sp, sub, kT_g, kt_cols, masks):
                        first = True
                        for (ml, mr) in masks:
                            kk = mr.shape[0]
                            P.mm(Sp[:, :], lhsT=ml, rhs=mr.unsqueeze(1).to_broadcast([kk, 4, 128]),
                                 start=first, stop=False, inc=False)
                            first = False
                        P.mm(Sp[:, :], lhsT=kT_g[:, kt_cols], rhs=nqz[:, 4 * sub:4 * sub + 4, tsl],
                             start=first, stop=True)

                    def finalize(O3, sub, br, first):
                        hs = slice(4 * sub, 4 * sub + 4)
                        P.ts("dve", rc[:], O3[:, :, 64], 1e-30, None, op0=ALU.max)
                        P.recip(rc[:], rc[:])
                        if br == 0:
                            P.tt("dve", impn[:, hs, :], O3[:, :, 65:97], rc[:].unsqueeze(2).to_broadcast([128, 4, 32]),
                                 ALU.mult)
                        P.tt("dve", cf[:], rc[:], gs3[:, hs, br], ALU.mult)
                        if first:
                            P.tt("dve", yacc[:, hs, :], O3[:, :, 0:64], cf[:].unsqueeze(2).to_broadcast([128, 4, 64]),
                                 ALU.mult)
                        else:
                            P.tt("dve", ytmp[:], O3[:, :, 0:64], cf[:].unsqueeze(2).to_broadcast([128, 4, 64]),
                                 ALU.mult)
                            P.tt("pool", yacc[:, hs, :], yacc[:, hs, :], ytmp[:], ALU.add)

                    for sub in range(4):
                        g = sub // 2
                        Sp = Sb[si % 2]
                        pT = pTb[si % 2]
                        si += 1
                        scores(Sp, sub, kcmpT[g], slice(0, 128), [(C["ident"][:], C["visneg"][:, qt * 128:(qt + 1) * 128])])
                        P.act(pT[:], Sp[:, :], AF.Exp, scale=0.125)
                        O = Ob[oi % 2]
                        oi += 1
                        O3 = O[:, 0:388].rearrange("p (h e) -> p h e", e=97)
                        for hh in range(4):
                            P.mm(O3[:, hh, :], lhsT=pT[:, hh * 128:(hh + 1) * 128], rhs=vext[g][:, :],
                                 start=(hh == 0), stop=True, inc=(hh == 3), sgc=True)
                        finalize(O3, sub, 0, True)
                    for g in range(2):
                        P.op("dve", lambda e: e.tensor_reduce(out=imp[:], in_=impn[:, 8 * g:8 * g + 8, :].rearrange("p h j -> p j h"),
                                                              axis=AX.X, op=ALU.add),
                             [impn[:, 8 * g:8 * g + 8, :]], [imp[:]])
                        P.tt("dve", imp[:], imp[:], C["selkeep"][:, qt, :], ALU.mult)
                        P.tt("dve", imp[:], imp[:], C["selbias"][:, qt, :], ALU.add)
                        P.op("dve", lambda e: e.max(out=mx[:], in_=imp[:]), [imp[:]], [mx[:]])
                        P.ts("dve", negblk[:], imp[:], mx[:, 3:4], 1.0, op0=ALU.is_ge, op1=ALU.subtract)
                        P.tr(tb[0][0:32, 0, :], negblk[:], C["ident"][:])
                        P.amul(negblkT[g][:], tb[0][0:32, 0, :], -NEGM)
                    for br in (1, 2):
                        if str(br) not in os.environ.get("NSABR", "12"):
                            continue
                        kts = list(range(0, qt + 1)) if br == 1 else list(range(max(0, qt - 4), qt + 1))
                        kT_l = ksT if br == 1 else kwT
                        v_l = vs1 if br == 1 else vw1
                        for sub in range(4):
                            g = sub // 2
                            O = Ob[oi % 2]
                            oi += 1
                            O3 = O[:, 0:260].rearrange("p (h e) -> p h e", e=65)
                            for ki, kt in enumerate(kts):
                                Sp = Sb[si % 2]
                                pT = pTb[si % 2]
                                si += 1
                                masks = []
                                if br == 1:
                                    masks.append((C["E"][:, kt * 128:(kt + 1) * 128], negblkT[g][:]))
                                if kt == qt:
                                    masks.append((C["ident"][:], C["cneg"][:]))
                                if br == 2 and kt == qt - 4:
                                    masks.append((C["ident"][:], C["wneg"][:]))
                                scores(Sp, sub, kT_l[g], slice(kt * 128, (kt + 1) * 128), masks)
                                P.act(pT[:], Sp[:, :], AF.Exp, scale=0.125)
                                for hh in range(4):
                                    P.mm(O3[:, hh, :], lhsT=pT[:, hh * 128:(hh + 1) * 128], rhs=v_l[:, kt, g, :],
                                         start=(ki == 0 and hh == 0), stop=(ki == len(kts) - 1), inc=(hh == 3), sgc=True)
                            finalize(O3, sub, br, False)
                    P.cp("act", ytm[:], yacc[:].rearrange("p h e -> p (h e)"))
                    t_ = tb[1]
                    for c in range(8):
                        P.tr(t_[:, c, :], ytm[:, c * 128:(c + 1) * 128], C["ident"][:], inc=(c == 7))
                    P.cp("act", yT[:, :, tsl], t_[:])
                P.dma(yb_d[:, :, t0:t0 + 512], yT[:], q="sp")

    def phase_ssd(self, l, hT, yc_d, zs_d, xact_d):
        P, C = self.P, self.C
        I = self.inp
        w_in = I["w_in"]
        with P.scope() as st:
            Wz = P.sb("Wz", [128, 8, 2048], BF16, st)
            self.wload(Wz, w_in, l, OFF["sz"], 2048)
            zt = [P.sb("zt", [128, 2048], F32, st) for _ in range(2)]
            bank = [P.ps("bk", [128, 512], F32, st) for _ in range(4)]
            bi = 0
            for tt in range(NT):
                z_t = zt[tt % 2]
                for nb in range(4):
                    ps = bank[bi % 4]
                    bi += 1
                    self.proj_tm(ps, Wz, nb * 512, 512, hT, tt)
                    P.act(z_t[:, nb * 512:(nb + 1) * 512], ps[:, :], AF.Silu)
                P.dma(zs_d[tt * 128:(tt + 1) * 128, :], z_t[:])
        with P.scope() as st:
            cw = P.sb("cw", [128, 24, 4], F32, st)
            cb = P.sb("cb", [128, 24], F32, st)
            for cc in range(24):
                P.dma(cw[:, cc, :], I["ssd_conv_w"][l][:, cc * 128:(cc + 1) * 128].rearrange("k c -> c k"),
                      allow_slow_non_contiguous=True)
                P.dma(cb[:, cc:cc + 1], I["ssd_conv_b"][l][cc * 128:(cc + 1) * 128].rearrange("(c o) -> c o", o=1))
            Wx = [P.sb("Wx", [128, 8, 128], BF16, st) for _ in range(2)]
            raw = [P.sb("raw", [128, 3 + S], F32, st) for _ in range(2)]
            accb = [P.sb("accb", [128, S], F32, st) for _ in range(2)]
            xa = [P.sb("xa", [128, S], BF16, st) for _ in range(2)]
            ctmp = P.sb("ctmp", [128, S], F32, st)
            bank = [P.ps("bk", [128, 512], F32, st) for _ in range(4)]
            for r_ in raw:
                P.memset("pool", r_[:, 0:3], 0.0)
            bi = 0
            for cc in range(24):
                W_ = Wx[cc % 2]
                self.wload(W_, w_in, l, OFF["sxbc"] + cc * 128, 128)
                rw = raw[cc % 2]
                for Q in range(4):
                    ps = bank[bi % 4]
                    bi += 1
                    self.proj_fm(ps, W_, 0, 128, hT, Q * 512, 512)
                    P.cp("act", rw[:, 3 + Q * 512:3 + (Q + 1) * 512], ps[:, :])
                eng = "dve"
                ac = accb[cc % 2]
                P.ts(eng, ac[:], rw[:, 3:3 + S], cw[:, cc, 3:4], None, op0=ALU.mult)
                for k in (2, 1, 0):
                    if eng == "dve":
                        P.stt(eng, ac[:], rw[:, k:k + S], cw[:, cc, k:k + 1], ac[:], ALU.mult, ALU.add)
                    else:
                        P.ts(eng, ctmp[:], rw[:, k:k + S], cw[:, cc, k:k + 1], None, op0=ALU.mult)
                        P.tt(eng, ac[:], ac[:], ctmp[:], ALU.add)
                P.act(xa[cc % 2][:], ac[:], AF.Silu, bias=cb[:, cc:cc + 1])
                P.dma(xact_d[:, cc, :], xa[cc % 2][:])
                P.maybe_barrier()
        with P.scope() as st:
            Wdt = P.sb("Wdt", [128, 8, 32], BF16, st)
            self.wload(Wdt, w_in, l, OFF["sdt"], 32)
            dtb = P.sb("dtb", [128, 32], F32, st)
            P.dma(dtb[:], I["ssd_dt_bias"][l].partition_broadcast(128))
            a_b = P.sb("a_b", [128, 32], F32, st)
            P.dma(a_b[:], I["ssd_a_log"][l].partition_broadcast(128))
            P.act(a_b[:], a_b[:], AF.Exp)
            P.ts("dve", a_b[:], a_b[:], -1.0, None, op0=ALU.mult)
            D_b = P.sb("D_b", [128, 32], F32, st)
            P.dma(D_b[:], I["ssd_d"][l].partition_broadcast(128))
            ng_b = P.sb("ng_b", [128, 2048], F32, st)
            P.dma(ng_b[:], I["ssd_norm_g"][l].partition_broadcast(128))
            xab = [P.sb("xab", [128, 24, 128], BF16, st) for _ in range(2)]
            zsb = [P.sb("zsb", [128, 2048], F32, st) for _ in range(2)]
            xs_tm = P.sb("xs_tm", [128, 2048], BF16, st)
            bm_tm = P.sb("bm_tm", [128, 512], BF16, st)
            xs_w = P.sb("xs_w", [128, 2048], BF16, st)
            rhs1 = P.sb("rhs1", [128, 8, 128], F32, st)
            rhs2 = P.sb("rhs2", [128, 8, 128], F32, st)
            Lg = P.sb("Lg", [128, 8, 128], BF16, st)
            MTg = P.sb("MTg", [128, 8, 128], BF16, st)
            cbT = P.sb("cbT", [128, 4, 128], BF16, st)
            H = [P.sb("H", [128, 512], F32, st) for _ in range(4)]
            Hbf = [P.sb("Hbf", [128, 512], BF16, st) for _ in range(4)]
            y = P.sb("y", [128, 2048], F32, st)
            tmp = P.sb("tmp", [128, 512], F32, st)
            ynb = P.sb("ynb", [128, 2048], BF16, st)
            ycT = P.sb("ycT", [128, 16, 128], BF16, st)
            sm = {n: P.sb(n, [128, 32], F32, st) for n in
                  ("dtr", "dtc", "lndt", "da", "acum", "alast", "nb", "ea", "wj", "dec")}
            ss = P.sb("ss2", [128, 1], F32, st)
            R = [P.ps("R", [128, 512], F32, st) for _ in range(2)]
            Yp = P.ps("Yp", [128, 512], F32, st)
            Yo = P.ps("Yo", [128, 512], F32, st)
            STp = P.ps("STp", [128, 512], F32, st)
            M0 = P.ps("M0", [128, 512], F32, st)
            M1 = P.ps("M1", [128, 512], F32, st)
            tb = P.ps("tb", [128, 8, 128], BF16, st)
            for g in range(4):
                P.memset("pool", H[g][:], 0.0)
                P.memset("pool", Hbf[g][:], 0.0)
            for c in range(NT):
                if c >= self.lim:
                    continue
                P.maybe_barrier()
                csl = slice(c * 128, (c + 1) * 128)
                xa_ = xab[c % 2]
                zs_ = zsb[c % 2]
                P.dma(xa_[:], xact_d[:, :, csl])
                P.dma(zs_[:], zs_d[csl, :])
                for k0 in (0, 8, 16):
                    n = 8 if k0 < 16 else 4
                    for j in range(n):
                        P.tr(tb[:, j, :], xa_[:, k0 + j, :], C["ident"][:], inc=(j == n - 1))
                    if k0 < 16:
                        P.cp("act", xs_tm[:, k0 * 128:(k0 + 8) * 128], tb[:].rearrange("p a b -> p (a b)"))
                    else:
                        P.cp("act", bm_tm[:], tb[:, 0:4, :].rearrange("p a b -> p (a b)"))
                self.proj_tm(M1, Wdt, 0, 32, hT, c)
                P.tt("dve", sm["dtr"][:], M1[:, 0:32], dtb[:], ALU.add)
                P.act(sm["dtc"][:], sm["dtr"][:], AF.Exp)
                P.act(sm["dtc"][:], sm["dtc"][:], AF.Ln, bias=1.0)
                P.act(sm["lndt"][:], sm["dtc"][:], AF.Ln)
                P.tt("dve", sm["da"][:], sm["dtc"][:], a_b[:], ALU.mult)
                P.mm(M1[:, 64:96], lhsT=C["tri_f"][:], rhs=sm["da"][:])
                P.cp("dve", sm["acum"][:], M1[:, 64:96])
                P.mm(M1[:, 128:160], lhsT=C["ones_f"][:], rhs=sm["da"][:])
                P.cp("dve", sm["alast"][:], M1[:, 128:160])
                P.tt("dve", sm["nb"][:], sm["lndt"][:], sm["acum"][:], ALU.subtract)
                P.act(sm["ea"][:], sm["acum"][:], AF.Exp)
                P.tt("dve", sm["wj"][:], sm["alast"][:], sm["acum"][:], ALU.subtract)
                P.act(sm["wj"][:], sm["wj"][:], AF.Exp)
                P.tt("dve", sm["wj"][:], sm["wj"][:], sm["dtc"][:], ALU.mult)
                P.act(sm["dec"][:], sm["alast"][:], AF.Exp)
                for g in range(4):
                    P.mm(M0[:, g * 128:(g + 1) * 128], lhsT=xa_[:, 16 + g, :], rhs=xa_[:, 20 + g, :],
                         start=(g == 0), stop=True, inc=(g == 3), sgc=True)
                P.cp("act", cbT[:].rearrange("p a b -> p (a b)"), M0[:, :])
                P.tt("pool", xs_w[:].rearrange("p (h e) -> p h e", e=64), xs_tm[:].rearrange("p (h e) -> p h e", e=64),
                     sm["wj"][:].unsqueeze(2).to_broadcast([128, 32, 64]), ALU.mult)
                for g in range(4):
                    hs = slice(8 * g, 8 * g + 8)
                    gsl = slice(g * 512, (g + 1) * 512)
                    P.tt("dve", rhs1[:], C["tri_f"][:].unsqueeze(1).to_broadcast([128, 8, 128]),
                         sm["da"][:, hs].unsqueeze(2).to_broadcast([128, 8, 128]), ALU.mult)
                    P.tt("pool", rhs2[:], C["cneg_f"][:].unsqueeze(1).to_broadcast([128, 8, 128]),
                         sm["nb"][:, hs].unsqueeze(2).to_broadcast([128, 8, 128]), ALU.add)
                    for hf in range(2):
                        P.mm(R[hf][:, :], lhsT=C["ones_f"][:], rhs=rhs1[:, 4 * hf:4 * hf + 4, :].rearrange("p a b -> p (a b)"),
                             start=True, stop=False, inc=False)
                        P.mm(R[hf][:, :], lhsT=C["ident_f"][:], rhs=rhs2[:, 4 * hf:4 * hf + 4, :].rearrange("p a b -> p (a b)"),
                             start=False, stop=True)
                        P.act(Lg[:, 4 * hf:4 * hf + 4, :].rearrange("p a b -> p (a b)"), R[hf][:, :], AF.Exp)
                    P.tt("dve" if g % 2 else "pool", MTg[:], Lg[:], cbT[:, g, :].unsqueeze(1).to_broadcast([128, 8, 128]), ALU.mult)
                    for hh in range(8):
                        P.mm(Yp[:, hh * 64:(hh + 1) * 64], lhsT=MTg[:, hh, :], rhs=xs_tm[:, (8 * g + hh) * 64:(8 * g + hh + 1) * 64],
                             start=(hh == 0), stop=True, inc=(hh == 7), sgc=True)
                    P.mm(Yo[:, :], lhsT=xa_[:, 20 + g, :], rhs=Hbf[g][:])
                    yv = y[:, gsl].rearrange("p (h e) -> p h e", e=64)
                    P.tt("dve", yv, Yo[:, :].rearrange("p (h e) -> p h e", e=64),
                         sm["ea"][:, hs].unsqueeze(2).to_broadcast([128, 8, 64]), ALU.mult)
                    P.tt("dve", y[:, gsl], Yp[:, :], y[:, gsl], ALU.add)
                    P.tt("pool", tmp[:].rearrange("p (h e) -> p h e", e=64), xs_tm[:, gsl].rearrange("p (h e) -> p h e", e=64),
                         D_b[:, hs].unsqueeze(2).to_broadcast([128, 8, 64]), ALU.mult)
                    P.tt("pool", y[:, gsl], y[:, gsl], tmp[:], ALU.add)
                    P.mm(STp[:, :], lhsT=bm_tm[:, g * 128:(g + 1) * 128], rhs=xs_w[:, gsl])
                    Hv = H[g][:].rearrange("p (h e) -> p h e", e=64)
                    P.tt("pool", Hv, Hv, sm["dec"][:, hs].unsqueeze(2).to_broadcast([128, 8, 64]), ALU.mult)
                    P.tt("dve", H[g][:], STp[:, :], H[g][:], ALU.add)
                    P.cp("act", Hbf[g][:], H[g][:])
                P.tt("dve", y[:], y[:], zs_[:], ALU.mult)
                P.act(zs_[:], y[:], AF.Square, accum=ss[:])
                P.act(ss[:], ss[:], AF.Sqrt, bias=EPS, scale=1.0 / 2048)
                P.recip(ss[:], ss[:])
                P.stt("dve", ynb[:], y[:], ss[:, 0:1], ng_b[:], ALU.mult, ALU.mult)
                for k0 in (0, 8):
                    for j in range(8):
                        P.tr(tb[:, j, :], ynb[:, (k0 + j) * 128:(k0 + j + 1) * 128], C["ident"][:], inc=(j == 7))
                    P.cp("act", ycT[:, k0:k0 + 8, :], tb[:])
                P.dma(yc_d[:, :, csl], ycT[:])

    def phase_merge(self, l, hT, ya_d, yb_d, yc_d, mix_d):
        P, C = self.P, self.C
        I = self.inp
        with P.scope() as st:
            yh = [P.sb("yha", [128, 8, 1024], BF16, st), P.sb("yhb", [128, 8, 1024], BF16, st),
                  P.sb("yhc", [128, 16, 1024], BF16, st)]
            Wy = [[P.sb("Wya", [128, 8, 128], BF16, st), P.sb("Wyb", [128, 8, 128], BF16, st),
                   P.sb("Wyc", [128, 16, 128], BF16, st)] for _ in range(2)]
            Wg = [[P.sb("Wg", [128, 8, 128], BF16, st) for _ in range(3)] for _ in range(2)]
            sg = [P.sb("sg", [128, 512], F32, st) for _ in range(2)]
            macc = P.sb("macc", [128, 512], F32, st)
            mt = P.sb("mt", [128, 512], F32, st)
            mixh = P.sb("mixh", [128, 8, 1024], BF16, st)
            bank = [P.ps("bk", [128, 512], F32, st) for _ in range(6)]
            wsrc = [I["w_br_dsa"], I["w_br_nsa"], I["w_br_ssd"]]
            ysrc = [ya_d, yb_d, yc_d]
            bi = 0
            gi = 0
            it = 0
            for half in range(2):
                hsl = slice(half * 1024, (half + 1) * 1024)
                for br in range(3):
                    P.dma(yh[br][:], ysrc[br][:, :, hsl])
                for oc in range(8):
                    Wy_ = Wy[it % 2]
                    Wg_ = Wg[it % 2]
                    it += 1
                    for br in range(3):
                        self.wload(Wy_[br], wsrc[br], l, oc * 128, 128)
                        self.wload(Wg_[br], I["w_in"], l, OFF["mg"] + br * 1024 + oc * 128, 128)
                    for pc in range(2):
                        psl = slice(pc * 512, (pc + 1) * 512)
                        tok = half * 1024 + pc * 512
                        for br in range(3):
                            psy = bank[bi % 6]
                            bi += 1
                            psg = bank[bi % 6]
                            bi += 1
                            nk = 16 if br == 2 else 8
                            for kc in range(nk):
                                P.mm(psy[:, :], lhsT=Wy_[br][:, kc, :], rhs=yh[br][:, kc, psl],
                                     start=(kc == 0), stop=(kc == nk - 1))
                            self.proj_fm(psg, Wg_[br], 0, 128, hT, tok, 512)
                            s_ = sg[gi % 2]
                            gi += 1
                            P.act(s_[:], psg[:, :], AF.Sigmoid)
                            if br == 0:
                                P.tt("dve", macc[:], psy[:, :], s_[:], ALU.mult)
                            else:
                                P.tt("dve", mt[:], psy[:, :], s_[:], ALU.mult)
                                if br == 1:
                                    P.tt("pool", macc[:], macc[:], mt[:], ALU.add)
                                else:
                                    P.tt("pool", mixh[:, oc, psl], macc[:], mt[:], ALU.add)
                    P.maybe_barrier()
                P.dma(mix_d[:, :, hsl], mixh[:])

    def phase_out_norm(self, l, x_d, mix_d, x1_d, hT):
        P, C = self.P, self.C
        I = self.inp
        with P.scope() as st:
            Wo = P.sb("Wo", [128, 8, 1024], BF16, st)
            self.wload(Wo, I["w_out"], l, 0, 1024)
            gb = P.sb("gb2", [128, D], F32, st)
            P.dma(gb[:], I["norm2_g"][l].partition_broadcast(128))
            mq = [P.sb("mq", [128, 8, 128], BF16, st) for _ in range(2)]
            xb = [P.sb("xb", [128, D], F32, st) for _ in range(2)]
            x1b = [P.sb("x1b", [128, D], F32, st) for _ in range(2)]
            sq = P.sb("sq", [128, D], F32, st)
            hb = [P.sb("hb", [128, D], BF16, st) for _ in range(2)]
            ss = P.sb("ss", [128, 2], F32, st)
            bank = [P.ps("bk", [128, 512], F32, st) for _ in range(4)]
            pt = [P.ps("pt", [128, 8, 128], BF16, st) for _ in range(2)]
            for tt in range(NT):
                tsl = slice(tt * 128, (tt + 1) * 128)
                m_ = mq[tt % 2]
                P.dma(m_[:], mix_d[:, :, tsl])
                x_t = xb[tt % 2]
                P.dma(x_t[:], x_d[tsl, :])
                x1 = x1b[tt % 2]
                for hf in range(2):
                    ps = bank[(2 * tt + hf) % 4]
                    for oc in range(8):
                        P.mm(ps[:, :], lhsT=m_[:, oc, :], rhs=Wo[:, oc, hf * 512:(hf + 1) * 512],
                             start=(oc == 0), stop=(oc == 7))
                    P.tt("dve", x1[:, hf * 512:(hf + 1) * 512], ps[:, :], x_t[:, hf * 512:(hf + 1) * 512], ALU.add)
                P.dma(x1_d[tsl, :], x1[:])
                s1 = ss[:, tt % 2:tt % 2 + 1]
                P.act(sq[:], x1[:], AF.Square, accum=s1)
                P.act(s1, s1, AF.Sqrt, bias=EPS, scale=1.0 / D)
                P.recip(s1, s1)
                h_t = hb[tt % 2]
                P.stt("dve", h_t[:], x1[:], s1, gb[:], ALU.mult, ALU.mult)
                p_t = pt[tt % 2]
                for kc in range(8):
                    P.tr(p_t[:, kc, :], h_t[:, kc * 128:(kc + 1) * 128], C["ident"][:], inc=(kc == 7))
                P.cp("act", hT[:, :, tsl], p_t[:])

    def phase_ffn(self, l, hT, x1_d, xo_d):
        P, C = self.P, self.C
        I = self.inp
        with P.scope() as st:
            W1 = P.sb("W1", [128, 8, 4096], BF16, st)
            self.wload(W1, I["w_ff1"], l, 0, 4096)
            aT = P.sb("aT", [128, 32, 512], BF16, st)
            W2b = [P.sb("W2b", [128, 8, 512], BF16, st) for _ in range(2)]
            rb = [P.sb("rbf", [128, 512], F32, st) for _ in range(2)]
            xt = [P.sb("xt", [128, 512], F32, st) for _ in range(2)]
            xo = [P.sb("xo", [128, 512], F32, st) for _ in range(2)]
            fb = [P.ps("fb", [128, 512], F32, st) for _ in range(2)]
            ob = [P.ps("ob", [128, 512], F32, st) for _ in range(4)]
            wi = 0
            xi = 0
            for Q in range(4):
                for fc in range(32):
                    ps = fb[fc % 2]
                    self.proj_fm(ps, W1, fc * 128, 128, hT, Q * 512, 512)
                    r = rb[fc % 2]
                    P.act(r[:], ps[:, :], AF.Relu)
                    P.tt("pool" if fc % 2 else "dve", aT[:, fc, :], r[:], r[:], ALU.mult)
                for hf in range(2):
                    for blk in range(4):
                        W2_ = W2b[wi % 2]
                        wi += 1
                        self.wload(W2_, I["w_ff2"], l, hf * 512, 512, rows=(blk * 1024, (blk + 1) * 1024))
                        for tl in range(4):
                            for k in range(8):
                                P.mm(ob[tl][:, :], lhsT=aT[:, blk * 8 + k, tl * 128:(tl + 1) * 128], rhs=W2_[:, k, :],
                                     start=(blk == 0 and k == 0), stop=(blk == 3 and k == 7), inc=(k == 7))
                    for tl in range(4):
                        rows = slice((Q * 4 + tl) * 128, (Q * 4 + tl + 1) * 128)
                        cols = slice(hf * 512, (hf + 1) * 512)
                        x_ = xt[xi % 2]
                        o_ = xo[xi % 2]
                        xi += 1
                        P.dma(x_[:], x1_d[rows, cols])
                        P.tt("dve", o_[:], ob[tl][:, :], x_[:], ALU.add)
                        P.dma(xo_d[rows, cols], o_[:])
                P.maybe_barrier()

    def build(self):
        P = self.P
        I = self.inp
        self.din("x", [S, D])
        for name, shp in (("norm1_g", [DEPTH, D]), ("w_in", [DEPTH, D, D_IN]), ("dsa_q_norm", [DEPTH, 64]),
                          ("dsa_k_norm", [DEPTH, 64]), ("nsa_q_norm", [DEPTH, 64]), ("nsa_k_norm", [DEPTH, 3, 64]),
                          ("nsa_cmp_pos", [DEPTH, 2, 32, 64]), ("nsa_cmp_w", [DEPTH, 2, 32, 64, 64]),
                          ("ssd_conv_w", [DEPTH, 4, 3072]), ("ssd_conv_b", [DEPTH, 3072]),
                          ("ssd_dt_bias", [DEPTH, 32]), ("ssd_a_log", [DEPTH, 32]), ("ssd_d", [DEPTH, 32]),
                          ("ssd_norm_g", [DEPTH, 2048]), ("w_br_dsa", [DEPTH, D, D]), ("w_br_nsa", [DEPTH, D, D]),
                          ("w_br_ssd", [DEPTH, 2 * D, D]), ("w_out", [DEPTH, D, D]), ("norm2_g", [DEPTH, D]),
                          ("w_ff1", [DEPTH, D, 4 * D]), ("w_ff2", [DEPTH, 4 * D, D])):
            self.din(name, shp)
        self.load_consts()
        out = self.dout("out", [S, D])
        hT = P.sb("hT", [128, 8, S], BF16)
        ya_d = P.dram("ya", [128, 8, S], BF16)
        yb_d = P.dram("yb", [128, 8, S], BF16)
        yc_d = P.dram("yc", [128, 16, S], BF16)
        zs_d = P.dram("zs", [S, 2048], F32)
        xact_d = P.dram("xact", [128, 24, S], BF16)
        mix_d = P.dram("mix", [128, 8, S], BF16)
        x1_d = P.dram("x1", [S, D], F32)
        xm_d = P.dram("xm", [S, D], F32)
        dbg = self.debug
        x_cur = I["x"]
        for l in range(self.nlayers):
            x_nxt = out if l == self.nlayers - 1 else xm_d
            self.phase_norm(x_cur, I["norm1_g"][l], hT)
            if dbg == "hT":
                return self._dump(hT[:], [128, 8, S], BF16)
            if dbg in ("nsa", "nsa_k"):
                self.phase_nsa(l, hT, yb_d)
                return self._dump(yb_d, [128, 8, S], BF16)
            if dbg == "ssd":
                self.phase_ssd(l, hT, yc_d, zs_d, xact_d)
                return self._dump(yc_d, [128, 16, S], BF16)
            self.phase_dsa(l, hT, ya_d)
            if dbg is not None and dbg.startswith("dsa"):
                return self._dump(ya_d, [128, 8, S], BF16)
            self.phase_nsa(l, hT, yb_d)
            self.phase_ssd(l, hT, yc_d, zs_d, xact_d)
            self.phase_merge(l, hT, ya_d, yb_d, yc_d, mix_d)
            if dbg == "mix":
                return self._dump(mix_d, [128, 8, S], BF16)
            self.phase_out_norm(l, x_cur, mix_d, x1_d, hT)
            if dbg == "x1":
                return self._dump(x1_d, [S, D], F32)
            self.phase_ffn(l, hT, x1_d, x_nxt)
            x_cur = x_nxt
        P.finish()

    def _dump(self, src, shape, dt):
        if "dbg" not in self.out:
            o = self.dout("dbg", shape, dt)
            self.P.dma(o, src)
        self.P.finish()


_CACHE = {}


def kernel(**inputs):
    n = 8
    B = Builder()
    B.build()
    consts = _consts()
    shared = {}
    for name in B.inp:
        if name in consts:
            shared[name] = consts[name]
        elif name != "x":
            shared[name] = np.ascontiguousarray(np.asarray(inputs[name], dtype=np.float32))
    x = np.asarray(inputs["x"], dtype=np.float32)
    in_maps = []
    for b in range(n):
        m = dict(shared)
        m["x"] = np.ascontiguousarray(x[b])
        in_maps.append(m)
    res = run_bass_kernel_spmd(B.nc, in_maps, core_ids=list(range(n)))
    return np.stack([np.asarray(res.results[b]["out"], dtype=np.float32) for b in range(n)], axis=0)
```

```python
import contextlib
import os
import numpy as np
import concourse.bass as bass
import concourse.mybir as mybir
from concourse.bass_utils import run_bass_kernel_spmd

F32 = mybir.dt.float32
BF16 = mybir.dt.bfloat16
AF = mybir.ActivationFunctionType
ALU = mybir.AluOpType
AX = mybir.AxisListType

EPOCH = 240
NPOOL = 90

D = 1024
S = 2048
NT = 16
DEPTH = 2
EPS = 1e-6
OFF = dict(dq=0, dk=1024, dv=1088, iq=1152, ik=1408, iw=1440, nq=1448, nkv=2472, ng=3240,
           sz=3288, sxbc=5336, sdt=8408, mg=8440)
D_IN = 11512
NEGM = -30000.0


def _region(ap):
    t = ap.tensor
    name = t.name
    dims = [(int(s), int(c)) for s, c in ap.ap]
    off = int(ap.offset)
    if "DRam" in type(t).__name__:
        lo = hi = off
        for s, c in dims:
            if s >= 0:
                hi += s * (c - 1)
            else:
                lo += s * (c - 1)
        return (name, 0, 1, lo, hi + 1)
    if "PSum" in type(t).__name__:
        return (name, 0, 128, 0, 1 << 30)
    ps, pc = dims[0]
    if ps == 0:
        p0, f0 = 0, off
    else:
        p0 = off // ps
        f0 = off - p0 * ps
    lo = hi = f0
    for s, c in dims[1:]:
        if s >= 0:
            hi += s * (c - 1)
        else:
            lo += s * (c - 1)
    return (name, p0, p0 + pc, lo, hi + 1)


def _overlap(a, b):
    return a[1] < b[2] and b[1] < a[2] and a[3] < b[4] and b[3] < a[4]


def _contains(a, b):
    return a[1] <= b[1] and b[2] <= a[2] and a[3] <= b[3] and b[4] <= a[4]


class Prog:
    ENGS = ("pe", "act", "dve", "pool", "sp")

    def __init__(self, nc):
        self.nc = nc
        self.stack = contextlib.ExitStack()
        self.eng = {"pe": nc.tensor, "act": nc.scalar, "dve": nc.vector, "pool": nc.gpsimd, "sp": nc.sync}
        self.pool = [self.stack.enter_context(nc.semaphore(f"sp{i}")) for i in range(NPOOL)]
        self.bsem = [self.stack.enter_context(nc.semaphore(f"bar{i}")) for i in range(4)]
        self.nbar = 0
        self.uid = 0
        self.n_wait = 0
        self.n_ins = 0
        self.max_used = 0
        self._reset()

    def _reset(self):
        self.free = list(range(NPOOL))
        self.esem = {e: None for e in self.ENGS}
        self.ecnt = {e: 0 for e in self.ENGS}
        self.pend = {e: False for e in self.ENGS}
        self.seen = {e: {} for e in self.ENGS}
        self.dset = []
        self.dall = {}
        self.dnext = 0
        self.trk = {}

    def _alloc(self):
        assert self.free, "semaphore pool exhausted; add a barrier"
        i = self.free.pop()
        self.max_used = max(self.max_used, NPOOL - len(self.free))
        return i

    def sems_left(self):
        return len(self.free)

    def sb(self, name, shape, dtype, stack=None):
        self.uid += 1
        return (stack or self.stack).enter_context(
            self.nc.sbuf_tensor(f"{name}_{self.uid}", list(shape), dtype))

    def ps(self, name, shape, dtype, stack=None):
        self.uid += 1
        return (stack or self.stack).enter_context(
            self.nc.psum_tensor(f"{name}_{self.uid}", list(shape), dtype))

    def dram(self, name, shape, dtype):
        self.uid += 1
        return self.nc.dram_tensor(f"{name}_{self.uid}", list(shape), dtype).ap()

    @contextlib.contextmanager
    def scope(self):
        st = contextlib.ExitStack()
        try:
            yield st
        finally:
            self.barrier()
            st.close()

    def _deps(self, reads, writes):
        deps = []
        for ap in reads:
            r = _region(ap)
            t = self.trk.get(r[0])
            if t is None:
                continue
            for (wr, ev) in t["w"]:
                if _overlap(wr, r):
                    deps.append(ev)
        for ap in writes:
            r = _region(ap)
            t = self.trk.get(r[0])
            if t is None:
                continue
            for (wr, ev) in t["w"]:
                if _overlap(wr, r):
                    deps.append(ev)
            for (sk, rr), val in t["r"].items():
                if _overlap(rr, r):
                    deps.append((sk, val))
        return deps

    def _record(self, reads, writes, ev):
        for ap in reads:
            r = _region(ap)
            t = self.trk.setdefault(r[0], {"w": [], "r": {}})
            k = (ev[0], r)
            if t["r"].get(k, -1) < ev[1]:
                t["r"][k] = ev[1]
        for ap in writes:
            r = _region(ap)
            t = self.trk.setdefault(r[0], {"w": [], "r": {}})
            t["w"] = [(wr, e) for (wr, e) in t["w"] if not _contains(r, wr)]
            t["w"].append((r, ev))
            t["r"] = {k: v for k, v in t["r"].items() if not _contains(r, k[1])}

    def _wait(self, e, deps):
        eng = self.eng[e]
        need = {}
        for sk, val in deps:
            if self.seen[e].get(sk, 0) >= val:
                continue
            if need.get(sk, 0) < val:
                need[sk] = val
        for sk, val in need.items():
            assert 0 < val <= 255
            eng.wait_ge(self.pool[sk[1]], val)
            self.seen[e][sk] = val
            self.n_wait += 1

    def _next_ev(self, e, commit):
        if self.esem[e] is None or self.ecnt[e] >= EPOCH:
            self.esem[e] = self._alloc()
            self.ecnt[e] = 0
        ev = ((e, self.esem[e]), self.ecnt[e] + 1)
        if commit:
            self.ecnt[e] += 1
        return ev

    def op(self, e, fn, reads, writes, inc=True):
        deps = self._deps(reads, writes)
        if e == "pe":
            deps = [d for d in deps if d[0][0] != "pe"]
        self._wait(e, deps)
        ins = fn(self.eng[e])
        self.n_ins += 1
        ev = self._next_ev(e, inc)
        if inc:
            ins.then_inc(self.pool[ev[0][1]], 1)
            self.pend[e] = False
        else:
            self.pend[e] = True
        self._record(reads, writes, ev)
        return ins

    def dma(self, out, in_, q="sp", **kw):
        deps = self._deps([in_], [out])
        if q == "pool":
            ent = [self._alloc(), 0]
        else:
            if len(self.dset) < 8:
                self.dset.append([self._alloc(), 0])
            k = self.dnext % len(self.dset)
            self.dnext += 1
            if self.dset[k][1] >= 15:
                self.dset[k] = [self._alloc(), 0]
            ent = self.dset[k]
        if ent[1] > 0:
            deps.append((("d", ent[0]), 16 * ent[1]))
        self._wait(q, deps)
        ins = self.eng[q].dma_start(out=out, in_=in_, **kw)
        ins.then_inc(self.pool[ent[0]], 16)
        ent[1] += 1
        self.dall[ent[0]] = ent[1]
        self.n_ins += 1
        self._record([in_], [out], (("d", ent[0]), 16 * ent[1]))
        return ins

    def barrier(self):
        evs = []
        for e in self.ENGS:
            assert not self.pend[e], f"pending non-inc op on {e} at barrier"
            if self.esem[e] is not None and self.ecnt[e] > 0:
                evs.append(((e, self.esem[e]), self.ecnt[e]))
        for k, c in self.dall.items():
            evs.append((("d", k), 16 * c))
        for e in self.ENGS:
            self._wait(e, evs)
        b0 = self.bsem[2 * (self.nbar % 2)]
        b1 = self.bsem[2 * (self.nbar % 2) + 1]
        p0 = self.bsem[2 * ((self.nbar + 1) % 2)]
        p1 = self.bsem[2 * ((self.nbar + 1) % 2) + 1]
        for e in self.ENGS:
            if e != "sp":
                self.eng[e].sem_inc(b0, 1)
        sp = self.eng["sp"]
        sp.wait_ge(b0, 4)
        used = [i for i in range(NPOOL) if i not in set(self.free)]
        for i in used:
            sp.sem_clear(self.pool[i])
        sp.sem_clear(p0)
        sp.sem_clear(p1)
        sp.sem_inc(b1, 1)
        for e in self.ENGS:
            if e != "sp":
                self.eng[e].wait_ge(b1, 1)
        self.nbar += 1
        self._reset()

    def maybe_barrier(self, min_free=35):
        if len(self.free) < min_free:
            self.barrier()

    def finish(self):
        self.barrier()
        self.stack.close()

    def mm(self, out, lhsT, rhs, start=True, stop=True, inc=None, sgc=False):
        if inc is None:
            inc = stop
        return self.op("pe", lambda e: e.matmul(out, lhsT=lhsT, rhs=rhs, start=start, stop=stop,
                                                skip_group_check=sgc),
                       [lhsT, rhs], [out], inc=inc)

    def tr(self, out, in_, ident, inc=True):
        return self.op("pe", lambda e: e.transpose(out=out, in_=in_, identity=ident), [in_, ident], [out], inc=inc)

    def act(self, out, in_, func, bias=None, scale=None, accum=None):
        kw = {}
        reads = [in_]
        writes = [out]
        if bias is not None:
            kw["bias"] = bias
            if not isinstance(bias, (int, float)):
                reads.append(bias)
        if scale is not None:
            kw["scale"] = scale
            if not isinstance(scale, (int, float)):
                reads.append(scale)
        if accum is not None:
            kw["accum_out"] = accum
            writes.append(accum)
        return self.op("act", lambda e: e.activation(out=out, in_=in_, func=func, **kw), reads, writes)

    def cp(self, eng, out, in_):
        if eng == "act":
            return self.op("act", lambda e: e.copy(out=out, in_=in_), [in_], [out])
        return self.op(eng, lambda e: e.tensor_copy(out=out, in_=in_), [in_], [out])

    def tt(self, eng, out, in0, in1, op):
        return self.op(eng, lambda e: e.tensor_tensor(out=out, in0=in0, in1=in1, op=op), [in0, in1], [out])

    def ts(self, eng, out, in0, s1, s2=None, op0=ALU.mult, op1=None, accum=None):
        reads = [in0] + [s for s in (s1, s2) if s is not None and not isinstance(s, (int, float))]
        writes = [out] + ([accum] if accum is not None else [])
        kw = {}
        if op1 is not None:
            kw["op1"] = op1
        if accum is not None:
            kw["accum_out"] = accum
        return self.op(eng, lambda e: e.tensor_scalar(out=out, in0=in0, scalar1=s1, scalar2=s2, op0=op0, **kw),
                       reads, writes)

    def stt(self, eng, out, in0, scalar, in1, op0, op1):
        reads = [in0, in1] + ([scalar] if not isinstance(scalar, (int, float)) else [])
        return self.op(eng, lambda e: e.scalar_tensor_tensor(out=out, in0=in0, scalar=scalar, in1=in1,
                                                             op0=op0, op1=op1), reads, [out])

    def amul(self, out, in_, val):
        return self.op("act", lambda e: e.mul(out=out, in_=in_, mul=val), [in_], [out])

    def memset(self, eng, ap, val):
        return self.op(eng, lambda e: e.memset(ap, val), [], [ap])

    def recip(self, out, in_):
        return self.op("dve", lambda e: e.reciprocal(out=out, in_=in_), [in_], [out])


def _consts():
    c = {}
    p = np.arange(128)
    c["c_ident"] = np.eye(128, dtype=np.float32)
    c["c_ones"] = np.ones((128, 128), np.float32)
    c["c_blk64"] = (p[:, None] // 64 == p[None, :] // 64).astype(np.float32)
    c["c_tri"] = (p[:, None] <= p[None, :]).astype(np.float32)
    c["c_cneg"] = np.where(p[:, None] > p[None, :], NEGM, 0.0).astype(np.float32)
    c["c_wneg"] = np.where(p[:, None] <= p[None, :], NEGM, 0.0).astype(np.float32)
    c["c_cnegtm"] = np.where(p[None, :] > p[:, None], -3e30, 0.0).astype(np.float32)
    s = np.arange(S)
    c["c_E"] = (s[None, :] // 64 == np.arange(32)[:, None]).astype(np.float32)
    n = np.arange(128)
    vis = (16 * n[:, None] + 31 <= s[None, :])
    c["c_visneg"] = np.where(vis, 0.0, NEGM).astype(np.float32)
    j = np.arange(32)
    cover = ((16 * n[:, None] < 64 * j[None, :] + 64) & (16 * n[:, None] + 32 > 64 * j[None, :]))
    ce = np.zeros((128, 33), np.float32)
    ce[:, 0] = 1.0
    ce[:, 1:] = cover
    ce[127] = 0.0
    c["c_cover"] = ce
    t = (np.arange(NT)[None, :, None] * 128 + p[:, None, None])
    cur = t // 64
    jj = j[None, None, :]
    forced = (jj == cur) | (jj == 0)
    future = jj > cur
    c["c_selkeep"] = (~(forced | future)).astype(np.float32)
    c["c_selbias"] = np.where(forced, 1e9, np.where(future, -1e9, 0.0)).astype(np.float32)
    return c


class Builder:
    def __init__(self, debug=None, nlayers=DEPTH):
        self.debug = debug
        self.nlayers = nlayers
        import os
        self.lim = int(os.environ.get("KLIM", "16"))
        nc = bass.Bass("TRN2", target_bir_lowering=False)
        self.nc = nc
        self.P = Prog(nc)
        self.inp = {}
        self.out = {}

    def din(self, name, shape, dt=F32):
        self.inp[name] = self.nc.dram_tensor(name, list(shape), dt, kind="ExternalInput").ap()
        return self.inp[name]

    def dout(self, name, shape, dt=F32):
        self.out[name] = self.nc.dram_tensor(name, list(shape), dt, kind="ExternalOutput").ap()
        return self.out[name]

    def load_consts(self):
        P = self.P
        C = {}
        shapes = {k: v.shape for k, v in _consts().items()}
        for k, shp in shapes.items():
            self.din(k, shp)
        self.wstage = [P.sb("wstage", [128, 4096], F32) for _ in range(2)]
        self.wsi = 0

        def ld(name, key, shape, dt, rows=None):
            t = P.sb(name, shape, dt)
            src = self.inp[key]
            if dt == F32:
                P.dma(t[:], src)
            else:
                n = int(np.prod(shape[1:]))
                flat = "p a b -> p (a b)" if len(shape) == 3 else None
                for a0 in range(0, n, 4096):
                    b0 = min(n, a0 + 4096)
                    stg = self.wstage[self.wsi % 2]
                    self.wsi += 1
                    P.dma(stg[0:shape[0], 0:b0 - a0], src[:, a0:b0])
                    P.cp("pool", t[:, a0:b0], stg[0:shape[0], 0:b0 - a0])
            return t
        C["ident_f"] = ld("ident_f", "c_ident", [128, 128], F32)
        C["ident"] = ld("ident", "c_ident", [128, 128], BF16)
        C["ones_f"] = ld("ones_f", "c_ones", [128, 128], F32)
        C["blk64_f"] = ld("blk64_f", "c_blk64", [128, 128], F32)
        C["tri_f"] = ld("tri_f", "c_tri", [128, 128], F32)
        C["cneg"] = ld("cneg", "c_cneg", [128, 128], BF16)
        C["zero"] = P.sb("zero", [128, 128], BF16)
        P.memset("pool", C["zero"][:], 0.0)
        C["cneg_f"] = ld("cneg_f", "c_cneg", [128, 128], F32)
        C["wneg"] = ld("wneg", "c_wneg", [128, 128], BF16)
        C["cnegtm"] = ld("cnegtm", "c_cnegtm", [128, 128], F32)
        C["E"] = ld("E", "c_E", [32, S], BF16)
        C["visneg"] = ld("visneg", "c_visneg", [128, S], BF16)
        C["cover"] = ld("cover", "c_cover", [128, 33], BF16)
        C["selkeep"] = ld("selkeep", "c_selkeep", [128, NT, 32], F32)
        C["selbias"] = ld("selbias", "c_selbias", [128, NT, 32], F32)
        self.C = C

    def wload(self, dst, wdram, l, c0, n, rows=None):
        w2 = wdram[l] if rows is None else wdram[l][rows[0]:rows[1]]
        src = w2.rearrange("(kc p) n -> p kc n", p=128)[:, :, c0:c0 + n]
        nk = src.shape[1]
        step = max(1, 4096 // nk)
        for a in range(0, n, step):
            b = min(n, a + step)
            stg = self.wstage[self.wsi % 2]
            self.wsi += 1
            v = stg[:, 0:nk * (b - a)].rearrange("p (k n) -> p k n", k=nk)
            self.P.dma(v, src[:, :, a:b], q="sp")
            self.P.cp("pool", dst[:, :, a:b], v)

    def pipelined(self, items, Sb, pTb, cnt):
        P = self.P
        n = len(items)
        slots = []
        for i in range(n + 1):
            if i < n:
                k = cnt["si"] % 2
                cnt["si"] += 1
                Sp, pT = Sb[k], pTb[k]
                items[i][0](Sp)
                P.act(pT[:], Sp[:, :], AF.Exp, scale=0.125)
                slots.append(pT)
            if i >= 1:
                items[i - 1][1](slots[i - 1])

    def proj_fm(self, ps, Wt, c0, M, hT, t0, N):
        for kc in range(8):
            self.P.mm(ps[0:M, 0:N], lhsT=Wt[:, kc, c0:c0 + M], rhs=hT[:, kc, t0:t0 + N],
                      start=(kc == 0), stop=(kc == 7))

    def proj_tm(self, ps, Wt, c0, n, hT, tile):
        for kc in range(8):
            self.P.mm(ps[:, 0:n], lhsT=hT[:, kc, tile * 128:(tile + 1) * 128], rhs=Wt[:, kc, c0:c0 + n],
                      start=(kc == 0), stop=(kc == 7))

    def norm_fm(self, ps, ps2, M, N, gcol, out, sq, rs):
        P, C = self.P, self.C
        P.act(sq[0:M, 0:N], ps[0:M, 0:N], AF.Square)
        P.mm(ps2[0:M, 0:N], lhsT=C["blk64_f"][0:M, 0:M], rhs=sq[0:M, 0:N])
        P.act(rs[0:M, 0:N], ps2[0:M, 0:N], AF.Sqrt, bias=EPS, scale=1.0 / 64)
        P.recip(rs[0:M, 0:N], rs[0:M, 0:N])
        P.stt("dve", out, ps[0:M, 0:N], gcol, rs[0:M, 0:N], ALU.mult, ALU.mult)

    def gain_col(self, name, vec64, stack):
        t = self.P.sb(name, [128, 1], F32, stack)
        src = vec64.rearrange("(p o) -> p o", o=1)
        self.P.dma(t[0:64, :], src)
        self.P.dma(t[64:128, :], src)
        return t

    def phase_norm(self, xd, gvec, hT, store_T=None):
        P, C = self.P, self.C
        with P.scope() as st:
            gb = P.sb("gb", [128, D], F32, st)
            P.dma(gb[:], gvec.partition_broadcast(128))
            xb = [P.sb("xb", [128, D], F32, st) for _ in range(2)]
            sq = P.sb("sq", [128, D], F32, st)
            hb = [P.sb("hb", [128, D], BF16, st) for _ in range(2)]
            ss = P.sb("ss", [128, 2], F32, st)
            pt = [P.ps("pt", [128, 8, 128], BF16, st) for _ in range(2)]
            for tt in range(NT):
                x_t = xb[tt % 2]
                P.dma(x_t[:], xd[tt * 128:(tt + 1) * 128, :])
                s1 = ss[:, tt % 2:tt % 2 + 1]
                P.act(sq[:], x_t[:], AF.Square, accum=s1)
                P.act(s1, s1, AF.Sqrt, bias=EPS, scale=1.0 / D)
                P.recip(s1, s1)
                h_t = hb[tt % 2]
                P.stt("dve", h_t[:], x_t[:], s1, gb[:], ALU.mult, ALU.mult)
                p_t = pt[tt % 2]
                for kc in range(8):
                    P.tr(p_t[:, kc, :], h_t[:, kc * 128:(kc + 1) * 128], C["ident"][:], inc=(kc == 7))
                P.cp("act" if tt % 2 else "dve", hT[:, :, tt * 128:(tt + 1) * 128], p_t[:])

    def phase_dsa(self, l, hT, ya_d):
        P, C = self.P, self.C
        I = self.inp
        w_in = I["w_in"]
        with P.scope() as st:
            Wq = P.sb("Wq", [128, 8, 1024], BF16, st)
            self.wload(Wq, w_in, l, OFF["dq"], 1024)
            Wk = P.sb("Wk", [128, 8, 128], BF16, st)
            self.wload(Wk[:, :, 0:64], w_in, l, OFF["dk"], 64)
            self.wload(Wk[:, :, 64:128], w_in, l, OFF["dk"], 64)
            Wv = P.sb("Wv", [128, 8, 64], BF16, st)
            self.wload(Wv, w_in, l, OFF["dv"], 64)
            Wiq = P.sb("Wiq", [128, 8, 256], BF16, st)
            self.wload(Wiq, w_in, l, OFF["iq"], 256)
            Wik = P.sb("Wik", [128, 8, 128], BF16, st)
            for r in range(4):
                self.wload(Wik[:, :, r * 32:(r + 1) * 32], w_in, l, OFF["ik"], 32)
            Wiw = P.sb("Wiw", [128, 8, 8], BF16, st)
            self.wload(Wiw, w_in, l, OFF["iw"], 8)
            gq = self.gain_col("gq", I["dsa_q_norm"][l], st)
            gk = self.gain_col("gk", I["dsa_k_norm"][l], st)

            kT = P.sb("kT", [128, S], BF16, st)
            v1 = P.sb("v1", [128, NT, 65], BF16, st)
            ikT4 = P.sb("ikT4", [128, S], BF16, st)
            ikbd = P.sb("ikbd", [128, NT, 4, 128], BF16, st)
            qz = P.sb("qz", [128, 16, 512], BF16, st)
            iqT = P.sb("iqT", [128, 2, S], BF16, st)
            iw = P.sb("iw", [128, NT, 8], F32, st)
            acc = P.sb("acc", [128, S], F32, st)
            work = P.sb("work", [128, S], F32, st)
            mx = P.sb("mx", [128, 8], F32, st)
            negm = P.sb("negm", [128, S], BF16, st)
            negmT = [P.sb("negmT", [128, NT, 128], BF16, st) for _ in range(2)]
            rb = [P.sb("rb", [128, 512], F32, st) for _ in range(2)]
            pTb = [P.sb("pTb", [128, 512], BF16, st) for _ in range(2)]
            ytm = P.sb("ytm", [128, 1024], BF16, st)
            yT = P.sb("yT", [128, 8, 512], BF16, st)
            sq = P.sb("sq", [128, 512], F32, st)
            rs = P.sb("rs", [128, 512], F32, st)
            rc = P.sb("rc", [128, 4], F32, st)
            bank = [P.ps("bk", [128, 512], F32, st) for _ in range(6)]
            tb = [P.ps("tb", [128, 8, 128], BF16, st) for _ in range(2)]
            Sb, Ob, Ib = bank[0:2], bank[2:4], bank[4:6]

            P.memset("pool", v1[:, :, 64:65], 1.0)
            P.memset("pool", ikbd[:], 0.0)
            P.memset("pool", qz[:], 0.0)
            for Q in range(4):
                t0 = Q * 512
                self.proj_fm(Ib[0], Wk, 0, 128, hT, t0, 512)
                self.norm_fm(Ib[0], Ib[1], 128, 512, gk[:, 0:1], kT[:, t0:t0 + 512], sq, rs)
                self.proj_fm(Ib[0], Wik, 0, 128, hT, t0, 512)
                P.cp("act", ikT4[:, t0:t0 + 512], Ib[0][:, :])
                for c in range(2):
                    self.proj_fm(Ib[c], Wiq, c * 128, 128, hT, t0, 512)
                    P.cp("act", iqT[:, c, t0:t0 + 512], Ib[c][:, :])
            for hh in range(4):
                P.dma(ikbd[32 * hh:32 * hh + 32, :, hh, :],
                      ikT4[32 * hh:32 * hh + 32, :].rearrange("p (k s) -> p k s", s=128))
            for tt in range(NT):
                ps = Ib[tt % 2]
                self.proj_tm(ps, Wv, 0, 64, hT, tt)
                P.cp("act", v1[:, tt, 0:64], ps[:, 0:64])
                self.proj_tm(ps, Wiw, 64, 8, hT, tt) if False else None
            for tt in range(NT):
                ps = Ib[tt % 2]
                self.proj_tm(ps, Wiw, 0, 8, hT, tt)
                P.cp("act", iw[:, tt, :], ps[:, 0:8])

            cnt = {"si": 0, "ii": 0}

            def stage_q(Q):
                t0 = Q * 512
                for c in range(8):
                    self.proj_fm(Ib[0], Wq, c * 128, 128, hT, t0, 512)
                    P.act(sq[:, :], Ib[0][:, :], AF.Square)
                    P.mm(Ib[1][:, :], lhsT=C["blk64_f"][:], rhs=sq[:, :])
                    P.act(rs[:, :], Ib[1][:, :], AF.Sqrt, bias=EPS, scale=1.0 / 64)
                    P.recip(rs[:, :], rs[:, :])
                    for hf in range(2):
                        psl = slice(64 * hf, 64 * hf + 64)
                        P.stt("dve", qz[psl, 2 * c + hf, :], Ib[0][psl, :], gq[psl, 0:1], rs[psl, :], ALU.mult, ALU.mult)

            def stage_a(qt):
                nk = qt + 1
                tsl = slice(qt * 128, (qt + 1) * 128)
                if qt < 2:
                    P.memset("pool", negm[:, 0:nk * 128], 0.0)
                    return
                for kt in range(nk):
                    a_kt = acc[:, kt * 128:(kt + 1) * 128]
                    for c in range(2):
                        ps = Ib[cnt["ii"] % 2]
                        r = rb[cnt["ii"] % 2]
                        cnt["ii"] += 1
                        P.mm(ps[:, :], lhsT=iqT[:, c, tsl], rhs=ikbd[:, kt].rearrange("p a b -> p (a b)"))
                        P.act(r[:], ps[:, :], AF.Relu)
                        for hh in range(4):
                            h = 4 * c + hh
                            if h == 0:
                                P.ts("dve", a_kt, r[:, 0:128], iw[:, qt, 0:1], None, op0=ALU.mult)
                            else:
                                P.stt("dve", a_kt, r[:, hh * 128:(hh + 1) * 128], iw[:, qt, h:h + 1], a_kt,
                                      ALU.mult, ALU.add)
                d_sl = slice(qt * 128, (qt + 1) * 128)
                P.tt("dve", acc[:, d_sl], acc[:, d_sl], C["cnegtm"][:], ALU.add)
                cur = acc
                for r_ in range(32):
                    P.op("dve", lambda e: e.max(out=mx[:], in_=cur[:, 0:nk * 128]),
                         [cur[:, 0:nk * 128]], [mx[:]])
                    P.op("dve", lambda e: e.match_replace(out=work[:, 0:nk * 128], in_to_replace=mx[:],
                                                          in_values=cur[:, 0:nk * 128], imm_value=-1e30),
                         [mx[:], cur[:, 0:nk * 128]], [work[:, 0:nk * 128]])
                    cur = work
                P.ts("dve", negm[:, 0:nk * 128], work[:, 0:nk * 128], -1e29, 1.0, op0=ALU.is_le,
                     op1=ALU.subtract)

            def stage_b(qt):
                nk = qt + 1
                nT = negmT[qt % 2]
                for k0 in range(0, nk, 8):
                    k1 = min(nk, k0 + 8)
                    t_ = tb[0]
                    for kt in range(k0, k1):
                        P.tr(t_[:, kt - k0, :], negm[:, kt * 128:(kt + 1) * 128], C["ident"][:],
                             inc=(kt == k1 - 1))
                    P.amul(nT[:, k0:k1, :], t_[:, 0:k1 - k0, :], -NEGM)
                P.tt("pool", nT[:, qt, :], nT[:, qt, :], C["cneg"][:], ALU.add)

            def stage_c(qt):
                nk = qt + 1
                tl = qt % 4
                tsl = slice(tl * 128, (tl + 1) * 128)
                nT = negmT[qt % 2]
                items = []
                for hg in range(4):
                    O = Ob[hg % 2]
                    O3 = O[:, 0:260].rearrange("p (h e) -> p h e", e=65)
                    for kt in range(nk):
                        def em_s(Sp, hg=hg, kt=kt):
                            P.mm(Sp[:, :], lhsT=C["ident"][:], rhs=nT[:, kt, :].unsqueeze(1).to_broadcast([128, 4, 128]),
                                 start=True, stop=False, inc=False)
                            P.mm(Sp[:, :], lhsT=kT[:, kt * 128:(kt + 1) * 128], rhs=qz[:, 4 * hg:4 * hg + 4, tsl],
                                 start=False, stop=True)

                        def em_pv(pT, hg=hg, kt=kt, O3=O3):
                            for hh in range(4):
                                P.mm(O3[:, hh, :], lhsT=pT[:, hh * 128:(hh + 1) * 128], rhs=v1[:, kt, :],
                                     start=(kt == 0 and hh == 0), stop=(kt == nk - 1), inc=(hh == 3), sgc=True)
                            if kt == nk - 1:
                                P.recip(rc[:], O3[:, :, 64])
                                P.tt("dve", ytm[:, hg * 256:(hg + 1) * 256].rearrange("p (h e) -> p h e", e=64),
                                     O3[:, :, 0:64], rc[:].unsqueeze(2).to_broadcast([128, 4, 64]), ALU.mult)
                        items.append((em_s, em_pv))
                self.pipelined(items, Sb, pTb, cnt)
                t_ = tb[1]
                for c in range(8):
                    P.tr(t_[:, c, :], ytm[:, c * 128:(c + 1) * 128], C["ident"][:], inc=(c == 7))
                P.cp("act", yT[:, :, tsl], t_[:])
                if tl == 3:
                    Q = qt // 4
                    P.dma(ya_d[:, :, Q * 512:(Q + 1) * 512], yT[:], q="sp")

            nq = min(NT, self.lim)
            stage_a(0)
            stage_b(0)
            for qt in range(nq):
                P.maybe_barrier()
                if qt + 1 < nq:
                    stage_a(qt + 1)
                if qt % 4 == 0:
                    stage_q(qt // 4)
                stage_c(qt)
                if qt + 1 < nq:
                    stage_b(qt + 1)

    def _nsa_kside(self, l, hT, st, env):
        P, C = self.P, self.C
        I = self.inp
        Wkv, Wks, Wkw = env["Wkv"], env["Wks"], env["Wkw"]
        gks, gkw, gkc = env["gks"], env["gkw"], env["gkc"]
        ksT, kwT, vs1, vw1, kcmpT, vext = env["ksT"], env["kwT"], env["vs1"], env["vw1"], env["kcmpT"], env["vext"]
        sq, rs, ss, Mb, tb = env["sq"], env["rs"], env["ss"], env["Mb"], env["tb"]
        cmpw = []
        posB = []
        for kv in range(2):
            cw = P.sb("cmpw", [128, 32, 64], BF16, st)
            stg = self.wstage[self.wsi % 2]
            self.wsi += 1
            sv = stg[:, 0:2048].rearrange("p (l e) -> p l e", e=64)
            src = I["nsa_cmp_w"][l, kv].rearrange("l d e -> d l e")
            P.dma(sv[0:64], src)
            P.dma(sv[64:128], src)
            P.cp("pool", cw[:], sv)
            cmpw.append(cw)
            pt_ = P.sb("posT", [128, 32], F32, st)
            psrc = I["nsa_cmp_pos"][l, kv].rearrange("l d -> d l")
            P.dma(pt_[0:64, :], psrc, allow_slow_non_contiguous=True)
            P.dma(pt_[64:128, :], psrc, allow_slow_non_contiguous=True)
            pb = P.sb("posB", [128, 32, 127], BF16, st)
            P.cp("pool", pb[:], pt_[:].unsqueeze(2).to_broadcast([128, 32, 127]))
            posB.append(pb)

        kcT = P.sb("kcT", [128, S], BF16, st)
        vcT = P.sb("vcT", [128, S], BF16, st)
        kd = P.sb("kd", [128, 128], BF16, st)
        P.memset("pool", vs1[:, :, :, 64:65], 1.0)
        P.memset("pool", vw1[:, :, :, 64:65], 1.0)
        P.memset("pool", kd[:], 0.0)
        for g in range(2):
            P.memset("pool", vext[g][:], 0.0)
        for Q in range(4):
            t0 = Q * 512
            self.proj_fm(Mb[0], Wkv, 0, 128, hT, t0, 512)
            P.cp("act", kcT[:, t0:t0 + 512], Mb[0][:, :])
            self.proj_fm(Mb[1], Wkv, 128, 128, hT, t0, 512)
            P.cp("act", vcT[:, t0:t0 + 512], Mb[1][:, :])
            for g in range(2):
                self.proj_fm(Mb[0], Wks, g * 128, 128, hT, t0, 512)
                self.norm_fm(Mb[0], Mb[1], 128, 512, gks[:, 0:1], ksT[g][:, t0:t0 + 512], sq, rs)
                self.proj_fm(Mb[0], Wkw, g * 128, 128, hT, t0, 512)
                self.norm_fm(Mb[0], Mb[1], 128, 512, gkw[:, 0:1], kwT[g][:, t0:t0 + 512], sq, rs)
        for tt in range(NT):
            ps = Mb[tt % 2]
            self.proj_tm(ps, Wkv, 384, 128, hT, tt)
            P.cp("act", vs1[:, tt, :, 0:64], ps[:, 0:128].rearrange("p (g e) -> p g e", e=64))
            self.proj_tm(ps, Wkv, 640, 128, hT, tt)
            P.cp("act", vw1[:, tt, :, 0:64], ps[:, 0:128].rearrange("p (g e) -> p g e", e=64))
        P.maybe_barrier()
        def strided(t, lo, off):
            return bass.AP(t[:].tensor, lo * S + off, [[S, 64], [16, 127]])
        for g in range(2):
            lo = g * 64
            for kv, src in ((0, kcT), (1, vcT)):
                ps = Mb[kv]
                for li in range(32):
                    P.mm(ps[0:127, 0:64], lhsT=strided(src, lo, li), rhs=cmpw[kv][lo:lo + 64, li, :],
                         start=(li == 0), stop=False, inc=False)
                for li in range(32):
                    P.mm(ps[0:127, 0:64], lhsT=posB[kv][lo:lo + 64, li, :], rhs=cmpw[kv][lo:lo + 64, li, :],
                         start=False, stop=(li == 31), inc=(li == 31))
            P.act(sq[0:127, 0:64], Mb[0][0:127, 0:64], AF.Square, accum=ss[0:127, :])
            P.act(ss[0:127, :], ss[0:127, :], AF.Sqrt, bias=EPS, scale=1.0 / 64)
            P.recip(ss[0:127, :], ss[0:127, :])
            P.stt("dve", kd[0:127, 0:64], Mb[0][0:127, 0:64], ss[0:127, 0:1], gkc[0:127, :], ALU.mult, ALU.mult)
            P.cp("pool", kd[0:127, 64:128], kd[0:127, 0:64])
            P.tr(tb[0][:, 0, :], kd[:], C["ident"][:])
            P.cp("act", kcmpT[g][:], tb[0][:, 0, :])
            P.cp("act", vext[g][0:127, 0:64], Mb[1][0:127, 0:64])
            P.cp("pool", vext[g][0:127, 64:97], C["cover"][0:127, :])
        P.maybe_barrier()


    def phase_nsa(self, l, hT, yb_d):
        P, C = self.P, self.C
        I = self.inp
        w_in = I["w_in"]
        KV = OFF["nkv"]
        with P.scope() as st:
            Wq = P.sb("Wnq", [128, 8, 1024], BF16, st)
            self.wload(Wq, w_in, l, OFF["nq"], 1024)
            Wkv = P.sb("Wkv", [128, 8, 768], BF16, st)
            self.wload(Wkv, w_in, l, KV, 768)
            Wks = P.sb("Wks", [128, 8, 256], BF16, st)
            Wkw = P.sb("Wkw", [128, 8, 256], BF16, st)
            for g in range(2):
                for r in range(2):
                    self.wload(Wks[:, :, g * 128 + r * 64:g * 128 + r * 64 + 64], w_in, l, KV + 256 + g * 64, 64)
                    self.wload(Wkw[:, :, g * 128 + r * 64:g * 128 + r * 64 + 64], w_in, l, KV + 512 + g * 64, 64)
            Wng = P.sb("Wng", [128, 8, 48], BF16, st)
            self.wload(Wng, w_in, l, OFF["ng"], 48)
            gq = self.gain_col("gnq", I["nsa_q_norm"][l], st)
            gks = self.gain_col("gks", I["nsa_k_norm"][l, 1], st)
            gkw = self.gain_col("gkw", I["nsa_k_norm"][l, 2], st)
            gkc = P.sb("gkc", [128, 64], F32, st)
            P.dma(gkc[:], I["nsa_k_norm"][l, 0].partition_broadcast(128))
            ksT = [P.sb("ksT", [128, S], BF16, st) for _ in range(2)]
            kwT = [P.sb("kwT", [128, S], BF16, st) for _ in range(2)]
            vs1 = P.sb("vs1", [128, NT, 2, 65], BF16, st)
            vw1 = P.sb("vw1", [128, NT, 2, 65], BF16, st)
            kcmpT = [P.sb("kcmpT", [128, 128], BF16, st) for _ in range(2)]
            vext = [P.sb("vext", [128, 97], BF16, st) for _ in range(2)]
            sq = P.sb("sq", [128, 512], F32, st)
            rs = P.sb("rs", [128, 512], F32, st)
            ss = P.sb("ss1", [128, 1], F32, st)
            bank = [P.ps("bk", [128, 512], F32, st) for _ in range(6)]
            tb = [P.ps("tb", [128, 8, 128], BF16, st) for _ in range(2)]
            Sb, Ob, Mb = bank[0:2], bank[2:4], bank[4:6]
            with P.scope() as st2:
                self._nsa_kside(l, hT, st2, locals())
            nqz = P.sb("nqz", [128, 16, 512], BF16, st)
            P.memset("pool", nqz[:], 0.0)
            gs = P.sb("gs", [128, 48], F32, st)
            yacc = P.sb("yacc", [128, 16, 64], F32, st)
            ytmp = P.sb("ytmp", [128, 4, 64], F32, st)
            impn = P.sb("impn", [128, 16, 32], F32, st)
            imp = P.sb("imp", [128, 32], F32, st)
            mx = P.sb("mx8", [128, 8], F32, st)
            negblk = P.sb("negblk", [128, 32], BF16, st)
            negblkT = [P.sb("negblkT", [32, 128], BF16, st) for _ in range(2)]
            pTb = [P.sb("pTb", [128, 512], BF16, st) for _ in range(2)]
            ytm = P.sb("ytm", [128, 1024], BF16, st)
            yT = P.sb("yT", [128, 8, 512], BF16, st)
            rc = P.sb("rc", [128, 4], F32, st)
            cf = P.sb("cf", [128, 4], F32, st)

            cnt = {"si": 0}
            oi = 0
            for Q in range(4):
                t0 = Q * 512
                for c in range(8):
                    self.proj_fm(Mb[0], Wq, c * 128, 128, hT, t0, 512)
                    P.act(sq[:, :], Mb[0][:, :], AF.Square)
                    P.mm(Mb[1][:, :], lhsT=C["blk64_f"][:], rhs=sq[:, :])
                    P.act(rs[:, :], Mb[1][:, :], AF.Sqrt, bias=EPS, scale=1.0 / 64)
                    P.recip(rs[:, :], rs[:, :])
                    for hf in range(2):
                        psl = slice(64 * hf, 64 * hf + 64)
                        P.stt("dve", nqz[psl, 2 * c + hf, :], Mb[0][psl, :], gq[psl, 0:1], rs[psl, :], ALU.mult, ALU.mult)
                for tl in range(4):
                    qt = Q * 4 + tl
                    if qt >= self.lim:
                        continue
                    P.maybe_barrier()
                    tsl = slice(tl * 128, (tl + 1) * 128)
                    self.proj_tm(Mb[0], Wng, 0, 48, hT, qt)
                    P.act(gs[:], Mb[0][:, 0:48], AF.Sigmoid)
                    gs3 = gs[:].rearrange("p (h b) -> p h b", b=3)

                    def scores(Sp, sub, kT_g, kt_cols, masks):
                        first = True
                        for (ml, mr) in masks:
                            kk = mr.shape[0]
                            P.mm(Sp[:, :], lhsT=ml, rhs=mr.unsqueeze(1).to_broadcast([kk, 4, 128]),
                                 start=first, stop=False, inc=False)
                            first = False
                        P.mm(Sp[:, :], lhsT=kT_g[:, kt_cols], rhs=nqz[:, 4 * sub:4 * sub + 4, tsl],
                             start=first, stop=True)

                    def finalize(O3, sub, br, first):
                        hs = slice(4 * sub, 4 * sub + 4)
                        P.ts("dve", rc[:], O3[:, :, 64], 1e-30, None, op0=ALU.max)
                        P.recip(rc[:], rc[:])
                        if br == 0:
                            P.tt("dve", impn[:, hs, :], O3[:, :, 65:97], rc[:].unsqueeze(2).to_broadcast([128, 4, 32]),
                                 ALU.mult)
                        P.tt("dve", cf[:], rc[:], gs3[:, hs, br], ALU.mult)
                        if first:
                            P.tt("dve", yacc[:, hs, :], O3[:, :, 0:64], cf[:].unsqueeze(2).to_broadcast([128, 4, 64]),
                                 ALU.mult)
                        else:
                            P.tt("dve", ytmp[:], O3[:, :, 0:64], cf[:].unsqueeze(2).to_broadcast([128, 4, 64]),
                                 ALU.mult)
                            P.tt("pool", yacc[:, hs, :], yacc[:, hs, :], ytmp[:], ALU.add)

                    items = []
                    for sub in range(4):
                        g = sub // 2
                        O = Ob[oi % 2]
                        oi += 1
                        O3 = O[:, 0:388].rearrange("p (h e) -> p h e", e=97)

                        def em_s(Sp, sub=sub, g=g):
                            scores(Sp, sub, kcmpT[g], slice(0, 128),
                                   [(C["ident"][:], C["visneg"][:, qt * 128:(qt + 1) * 128])])

                        def em_pv(pT, sub=sub, g=g, O3=O3):
                            for hh in range(4):
                                P.mm(O3[:, hh, :], lhsT=pT[:, hh * 128:(hh + 1) * 128], rhs=vext[g][:, :],
                                     start=(hh == 0), stop=True, inc=(hh == 3), sgc=True)
                            finalize(O3, sub, 0, True)
                        items.append((em_s, em_pv))
                    self.pipelined(items, Sb, pTb, cnt)
                    for g in range(2):
                        P.op("dve", lambda e: e.tensor_reduce(out=imp[:], in_=impn[:, 8 * g:8 * g + 8, :].rearrange("p h j -> p j h"),
                                                              axis=AX.X, op=ALU.add),
                             [impn[:, 8 * g:8 * g + 8, :]], [imp[:]])
                        P.tt("dve", imp[:], imp[:], C["selkeep"][:, qt, :], ALU.mult)
                        P.tt("dve", imp[:], imp[:], C["selbias"][:, qt, :], ALU.add)
                        P.op("dve", lambda e: e.max(out=mx[:], in_=imp[:]), [imp[:]], [mx[:]])
                        P.ts("dve", negblk[:], imp[:], mx[:, 3:4], 1.0, op0=ALU.is_ge, op1=ALU.subtract)
                        P.tr(tb[0][0:32, 0, :], negblk[:], C["ident"][:])
                        P.amul(negblkT[g][:], tb[0][0:32, 0, :], -NEGM)
                    items = []
                    for br in (2, 1):
                        kts = list(range(0, qt + 1)) if br == 1 else list(range(max(0, qt - 4), qt + 1))
                        kT_l = ksT if br == 1 else kwT
                        v_l = vs1 if br == 1 else vw1
                        for sub in range(4):
                            g = sub // 2
                            O = Ob[oi % 2]
                            oi += 1
                            O3 = O[:, 0:260].rearrange("p (h e) -> p h e", e=65)
                            for ki, kt in enumerate(kts):
                                masks = []
                                if br == 1:
                                    masks.append((C["E"][:, kt * 128:(kt + 1) * 128], negblkT[g][:]))
                                if kt == qt:
                                    masks.append((C["ident"][:], C["cneg"][:]))
                                if br == 2 and kt == qt - 4:
                                    masks.append((C["ident"][:], C["wneg"][:]))

                                def em_s(Sp, sub=sub, g=g, kt=kt, masks=masks, kT_l=kT_l):
                                    scores(Sp, sub, kT_l[g], slice(kt * 128, (kt + 1) * 128), masks)

                                def em_pv(pT, sub=sub, g=g, kt=kt, ki=ki, nkt=len(kts), O3=O3, v_l=v_l, br=br):
                                    for hh in range(4):
                                        P.mm(O3[:, hh, :], lhsT=pT[:, hh * 128:(hh + 1) * 128], rhs=v_l[:, kt, g, :],
                                             start=(ki == 0 and hh == 0), stop=(ki == nkt - 1), inc=(hh == 3), sgc=True)
                                    if ki == nkt - 1:
                                        finalize(O3, sub, br, False)
                                items.append((em_s, em_pv))
                    self.pipelined(items, Sb, pTb, cnt)
                    P.cp("act", ytm[:], yacc[:].rearrange("p h e -> p (h e)"))
                    t_ = tb[1]
                    for c in range(8):
                        P.tr(t_[:, c, :], ytm[:, c * 128:(c + 1) * 128], C["ident"][:], inc=(c == 7))
                    P.cp("act", yT[:, :, tsl], t_[:])
                P.dma(yb_d[:, :, t0:t0 + 512], yT[:], q="sp")

    def phase_ssd(self, l, hT, yc_d, zs_d, xact_d):
        P, C = self.P, self.C
        I = self.inp
        w_in = I["w_in"]
        with P.scope() as st:
            Wz = P.sb("Wz", [128, 8, 2048], BF16, st)
            self.wload(Wz, w_in, l, OFF["sz"], 2048)
            zt = [P.sb("zt", [128, 2048], F32, st) for _ in range(2)]
            bank = [P.ps("bk", [128, 512], F32, st) for _ in range(4)]
            bi = 0
            for tt in range(NT):
                z_t = zt[tt % 2]
                for nb in range(4):
                    ps = bank[bi % 4]
                    bi += 1
                    self.proj_tm(ps, Wz, nb * 512, 512, hT, tt)
                    P.act(z_t[:, nb * 512:(nb + 1) * 512], ps[:, :], AF.Silu)
                P.dma(zs_d[tt * 128:(tt + 1) * 128, :], z_t[:])
        with P.scope() as st:
            cw = P.sb("cw", [128, 24, 4], F32, st)
            cb = P.sb("cb", [128, 24], F32, st)
            for cc in range(24):
                P.dma(cw[:, cc, :], I["ssd_conv_w"][l][:, cc * 128:(cc + 1) * 128].rearrange("k c -> c k"),
                      allow_slow_non_contiguous=True)
                P.dma(cb[:, cc:cc + 1], I["ssd_conv_b"][l][cc * 128:(cc + 1) * 128].rearrange("(c o) -> c o", o=1))
            Wx = [P.sb("Wx", [128, 8, 128], BF16, st) for _ in range(2)]
            raw = [P.sb("raw", [128, 3 + S], F32, st) for _ in range(2)]
            accb = [P.sb("accb", [128, S], F32, st) for _ in range(2)]
            xa = [P.sb("xa", [128, S], BF16, st) for _ in range(2)]
            ctmp = P.sb("ctmp", [128, S], F32, st)
            bank = [P.ps("bk", [128, 512], F32, st) for _ in range(4)]
            for r_ in raw:
                P.memset("pool", r_[:, 0:3], 0.0)
            bi = 0
            for cc in range(24):
                W_ = Wx[cc % 2]
                self.wload(W_, w_in, l, OFF["sxbc"] + cc * 128, 128)
                rw = raw[cc % 2]
                for Q in range(4):
                    ps = bank[bi % 4]
                    bi += 1
                    self.proj_fm(ps, W_, 0, 128, hT, Q * 512, 512)
                    P.cp("act", rw[:, 3 + Q * 512:3 + (Q + 1) * 512], ps[:, :])
                eng = "dve"
                ac = accb[cc % 2]
                P.ts(eng, ac[:], rw[:, 3:3 + S], cw[:, cc, 3:4], None, op0=ALU.mult)
                for k in (2, 1, 0):
                    if eng == "dve":
                        P.stt(eng, ac[:], rw[:, k:k + S], cw[:, cc, k:k + 1], ac[:], ALU.mult, ALU.add)
                    else:
                        P.ts(eng, ctmp[:], rw[:, k:k + S], cw[:, cc, k:k + 1], None, op0=ALU.mult)
                        P.tt(eng, ac[:], ac[:], ctmp[:], ALU.add)
                P.act(xa[cc % 2][:], ac[:], AF.Silu, bias=cb[:, cc:cc + 1])
                P.dma(xact_d[:, cc, :], xa[cc % 2][:])
                P.maybe_barrier()
        with P.scope() as st:
            Wdt = P.sb("Wdt", [128, 8, 32], BF16, st)
            self.wload(Wdt, w_in, l, OFF["sdt"], 32)
            dtb = P.sb("dtb", [128, 32], F32, st)
            P.dma(dtb[:], I["ssd_dt_bias"][l].partition_broadcast(128))
            a_b = P.sb("a_b", [128, 32], F32, st)
            P.dma(a_b[:], I["ssd_a_log"][l].partition_broadcast(128))
            P.act(a_b[:], a_b[:], AF.Exp)
            P.ts("dve", a_b[:], a_b[:], -1.0, None, op0=ALU.mult)
            D_b = P.sb("D_b", [128, 32], F32, st)
            P.dma(D_b[:], I["ssd_d"][l].partition_broadcast(128))
            ng_b = P.sb("ng_b", [128, 2048], F32, st)
            P.dma(ng_b[:], I["ssd_norm_g"][l].partition_broadcast(128))
            xab = [P.sb("xab", [128, 24, 128], BF16, st) for _ in range(2)]
            zsb = [P.sb("zsb", [128, 2048], F32, st) for _ in range(2)]
            xs_tm = P.sb("xs_tm", [128, 2048], BF16, st)
            bm_tm = P.sb("bm_tm", [128, 512], BF16, st)
            xs_w = P.sb("xs_w", [128, 2048], BF16, st)
            rhs1 = P.sb("rhs1", [128, 8, 128], F32, st)
            rhs2 = P.sb("rhs2", [128, 8, 128], F32, st)
            Lg = P.sb("Lg", [128, 8, 128], BF16, st)
            MTg = P.sb("MTg", [128, 8, 128], BF16, st)
            cbT = P.sb("cbT", [128, 4, 128], BF16, st)
            H = [P.sb("H", [128, 512], F32, st) for _ in range(4)]
            Hbf = [P.sb("Hbf", [128, 512], BF16, st) for _ in range(4)]
            y = P.sb("y", [128, 2048], F32, st)
            tmp = P.sb("tmp", [128, 512], F32, st)
            ynb = P.sb("ynb", [128, 2048], BF16, st)
            ycT = P.sb("ycT", [128, 16, 128], BF16, st)
            sm = {n: P.sb(n, [128, 32], F32, st) for n in
                  ("dtr", "dtc", "lndt", "da", "acum", "alast", "nb", "ea", "wj", "dec")}
            ss = P.sb("ss2", [128, 1], F32, st)
            R = [P.ps("R", [128, 512], F32, st) for _ in range(2)]
            Yp = P.ps("Yp", [128, 512], F32, st)
            Yo = P.ps("Yo", [128, 512], F32, st)
            STp = P.ps("STp", [128, 512], F32, st)
            M0 = P.ps("M0", [128, 512], F32, st)
            M1 = P.ps("M1", [128, 512], F32, st)
            tb = P.ps("tb", [128, 8, 128], BF16, st)
            for g in range(4):
                P.memset("pool", H[g][:], 0.0)
                P.memset("pool", Hbf[g][:], 0.0)
            for c in range(NT):
                if c >= self.lim:
                    continue
                P.maybe_barrier()
                csl = slice(c * 128, (c + 1) * 128)
                xa_ = xab[c % 2]
                zs_ = zsb[c % 2]
                P.dma(xa_[:], xact_d[:, :, csl])
                P.dma(zs_[:], zs_d[csl, :])
                for k0 in (0, 8, 16):
                    n = 8 if k0 < 16 else 4
                    for j in range(n):
                        P.tr(tb[:, j, :], xa_[:, k0 + j, :], C["ident"][:], inc=(j == n - 1))
                    if k0 < 16:
                        P.cp("act", xs_tm[:, k0 * 128:(k0 + 8) * 128], tb[:].rearrange("p a b -> p (a b)"))
                    else:
                        P.cp("act", bm_tm[:], tb[:, 0:4, :].rearrange("p a b -> p (a b)"))
                self.proj_tm(M1, Wdt, 0, 32, hT, c)
                P.tt("dve", sm["dtr"][:], M1[:, 0:32], dtb[:], ALU.add)
                P.act(sm["dtc"][:], sm["dtr"][:], AF.Exp)
                P.act(sm["dtc"][:], sm["dtc"][:], AF.Ln, bias=1.0)
                P.act(sm["lndt"][:], sm["dtc"][:], AF.Ln)
                P.tt("dve", sm["da"][:], sm["dtc"][:], a_b[:], ALU.mult)
                P.mm(M1[:, 64:96], lhsT=C["tri_f"][:], rhs=sm["da"][:])
                P.cp("dve", sm["acum"][:], M1[:, 64:96])
                P.mm(M1[:, 128:160], lhsT=C["ones_f"][:], rhs=sm["da"][:])
                P.cp("dve", sm["alast"][:], M1[:, 128:160])
                P.tt("dve", sm["nb"][:], sm["lndt"][:], sm["acum"][:], ALU.subtract)
                P.act(sm["ea"][:], sm["acum"][:], AF.Exp)
                P.tt("dve", sm["wj"][:], sm["alast"][:], sm["acum"][:], ALU.subtract)
                P.act(sm["wj"][:], sm["wj"][:], AF.Exp)
                P.tt("dve", sm["wj"][:], sm["wj"][:], sm["dtc"][:], ALU.mult)
                P.act(sm["dec"][:], sm["alast"][:], AF.Exp)
                for g in range(4):
                    P.mm(M0[:, g * 128:(g + 1) * 128], lhsT=xa_[:, 16 + g, :], rhs=xa_[:, 20 + g, :],
                         start=(g == 0), stop=True, inc=(g == 3), sgc=True)
                P.cp("act", cbT[:].rearrange("p a b -> p (a b)"), M0[:, :])
                P.tt("pool", xs_w[:].rearrange("p (h e) -> p h e", e=64), xs_tm[:].rearrange("p (h e) -> p h e", e=64),
                     sm["wj"][:].unsqueeze(2).to_broadcast([128, 32, 64]), ALU.mult)
                for g in range(4):
                    hs = slice(8 * g, 8 * g + 8)
                    gsl = slice(g * 512, (g + 1) * 512)
                    P.tt("dve", rhs1[:], C["tri_f"][:].unsqueeze(1).to_broadcast([128, 8, 128]),
                         sm["da"][:, hs].unsqueeze(2).to_broadcast([128, 8, 128]), ALU.mult)
                    P.tt("pool", rhs2[:], C["cneg_f"][:].unsqueeze(1).to_broadcast([128, 8, 128]),
                         sm["nb"][:, hs].unsqueeze(2).to_broadcast([128, 8, 128]), ALU.add)
                    for hf in range(2):
                        P.mm(R[hf][:, :], lhsT=C["ones_f"][:], rhs=rhs1[:, 4 * hf:4 * hf + 4, :].rearrange("p a b -> p (a b)"),
                             start=True, stop=False, inc=False)
                        P.mm(R[hf][:, :], lhsT=C["ident_f"][:], rhs=rhs2[:, 4 * hf:4 * hf + 4, :].rearrange("p a b -> p (a b)"),
                             start=False, stop=True)
                        P.act(Lg[:, 4 * hf:4 * hf + 4, :].rearrange("p a b -> p (a b)"), R[hf][:, :], AF.Exp)
                    P.tt("dve" if g % 2 else "pool", MTg[:], Lg[:], cbT[:, g, :].unsqueeze(1).to_broadcast([128, 8, 128]), ALU.mult)
                    for hh in range(8):
                        P.mm(Yp[:, hh * 64:(hh + 1) * 64], lhsT=MTg[:, hh, :], rhs=xs_tm[:, (8 * g + hh) * 64:(8 * g + hh + 1) * 64],
                             start=(hh == 0), stop=True, inc=(hh == 7), sgc=True)
                    P.mm(Yo[:, :], lhsT=xa_[:, 20 + g, :], rhs=Hbf[g][:])
                    yv = y[:, gsl].rearrange("p (h e) -> p h e", e=64)
                    P.tt("dve", yv, Yo[:, :].rearrange("p (h e) -> p h e", e=64),
                         sm["ea"][:, hs].unsqueeze(2).to_broadcast([128, 8, 64]), ALU.mult)
                    P.tt("dve", y[:, gsl], Yp[:, :], y[:, gsl], ALU.add)
                    P.tt("pool", tmp[:].rearrange("p (h e) -> p h e", e=64), xs_tm[:, gsl].rearrange("p (h e) -> p h e", e=64),
                         D_b[:, hs].unsqueeze(2).to_broadcast([128, 8, 64]), ALU.mult)
                    P.tt("pool", y[:, gsl], y[:, gsl], tmp[:], ALU.add)
                    P.mm(STp[:, :], lhsT=bm_tm[:, g * 128:(g + 1) * 128], rhs=xs_w[:, gsl])
                    Hv = H[g][:].rearrange("p (h e) -> p h e", e=64)
                    P.tt("pool", Hv, Hv, sm["dec"][:, hs].unsqueeze(2).to_broadcast([128, 8, 64]), ALU.mult)
                    P.tt("dve", H[g][:], STp[:, :], H[g][:], ALU.add)
                    P.cp("act", Hbf[g][:], H[g][:])
                P.tt("dve", y[:], y[:], zs_[:], ALU.mult)
                P.act(zs_[:], y[:], AF.Square, accum=ss[:])
                P.act(ss[:], ss[:], AF.Sqrt, bias=EPS, scale=1.0 / 2048)
                P.recip(ss[:], ss[:])
                P.stt("dve", ynb[:], y[:], ss[:, 0:1], ng_b[:], ALU.mult, ALU.mult)
                for k0 in (0, 8):
                    for j in range(8):
                        P.tr(tb[:, j, :], ynb[:, (k0 + j) * 128:(k0 + j + 1) * 128], C["ident"][:], inc=(j == 7))
                    P.cp("act", ycT[:, k0:k0 + 8, :], tb[:])
                P.dma(yc_d[:, :, csl], ycT[:])

    def phase_merge(self, l, hT, ya_d, yb_d, yc_d, mix_d):
        P, C = self.P, self.C
        I = self.inp
        with P.scope() as st:
            yh = [P.sb("yha", [128, 8, 1024], BF16, st), P.sb("yhb", [128, 8, 1024], BF16, st),
                  P.sb("yhc", [128, 16, 1024], BF16, st)]
            Wy = [[P.sb("Wya", [128, 8, 128], BF16, st), P.sb("Wyb", [128, 8, 128], BF16, st),
                   P.sb("Wyc", [128, 16, 128], BF16, st)] for _ in range(2)]
            Wg = [[P.sb("Wg", [128, 8, 128], BF16, st) for _ in range(3)] for _ in range(2)]
            sg = [P.sb("sg", [128, 512], F32, st) for _ in range(2)]
            macc = P.sb("macc", [128, 512], F32, st)
            mt = P.sb("mt", [128, 512], F32, st)
            mixh = P.sb("mixh", [128, 8, 1024], BF16, st)
            bank = [P.ps("bk", [128, 512], F32, st) for _ in range(6)]
            wsrc = [I["w_br_dsa"], I["w_br_nsa"], I["w_br_ssd"]]
            ysrc = [ya_d, yb_d, yc_d]
            bi = 0
            gi = 0
            it = 0
            for half in range(2):
                hsl = slice(half * 1024, (half + 1) * 1024)
                for br in range(3):
                    P.dma(yh[br][:], ysrc[br][:, :, hsl])
                for oc in range(8):
                    Wy_ = Wy[it % 2]
                    Wg_ = Wg[it % 2]
                    it += 1
                    for br in range(3):
                        self.wload(Wy_[br], wsrc[br], l, oc * 128, 128)
                        self.wload(Wg_[br], I["w_in"], l, OFF["mg"] + br * 1024 + oc * 128, 128)
                    for pc in range(2):
                        psl = slice(pc * 512, (pc + 1) * 512)
                        tok = half * 1024 + pc * 512
                        for br in range(3):
                            psy = bank[bi % 6]
                            bi += 1
                            psg = bank[bi % 6]
                            bi += 1
                            nk = 16 if br == 2 else 8
                            for kc in range(nk):
                                P.mm(psy[:, :], lhsT=Wy_[br][:, kc, :], rhs=yh[br][:, kc, psl],
                                     start=(kc == 0), stop=(kc == nk - 1))
                            self.proj_fm(psg, Wg_[br], 0, 128, hT, tok, 512)
                            s_ = sg[gi % 2]
                            gi += 1
                            P.act(s_[:], psg[:, :], AF.Sigmoid)
                            if br == 0:
                                P.tt("dve", macc[:], psy[:, :], s_[:], ALU.mult)
                            else:
                                P.tt("dve", mt[:], psy[:, :], s_[:], ALU.mult)
                                if br == 1:
                                    P.tt("pool", macc[:], macc[:], mt[:], ALU.add)
                                else:
                                    P.tt("pool", mixh[:, oc, psl], macc[:], mt[:], ALU.add)
                    P.maybe_barrier()
                P.dma(mix_d[:, :, hsl], mixh[:])

    def phase_out_norm(self, l, x_d, mix_d, x1_d, hT):
        P, C = self.P, self.C
        I = self.inp
        with P.scope() as st:
            Wo = P.sb("Wo", [128, 8, 1024], BF16, st)
            self.wload(Wo, I["w_out"], l, 0, 1024)
            gb = P.sb("gb2", [128, D], F32, st)
            P.dma(gb[:], I["norm2_g"][l].partition_broadcast(128))
            mq = [P.sb("mq", [128, 8, 128], BF16, st) for _ in range(2)]
            xb = [P.sb("xb", [128, D], F32, st) for _ in range(2)]
            x1b = [P.sb("x1b", [128, D], F32, st) for _ in range(2)]
            sq = P.sb("sq", [128, D], F32, st)
            hb = [P.sb("hb", [128, D], BF16, st) for _ in range(2)]
            ss = P.sb("ss", [128, 2], F32, st)
            bank = [P.ps("bk", [128, 512], F32, st) for _ in range(4)]
            pt = [P.ps("pt", [128, 8, 128], BF16, st) for _ in range(2)]
            for tt in range(NT):
                tsl = slice(tt * 128, (tt + 1) * 128)
                m_ = mq[tt % 2]
                P.dma(m_[:], mix_d[:, :, tsl])
                x_t = xb[tt % 2]
                P.dma(x_t[:], x_d[tsl, :])
                x1 = x1b[tt % 2]
                for hf in range(2):
                    ps = bank[(2 * tt + hf) % 4]
                    for oc in range(8):
                        P.mm(ps[:, :], lhsT=m_[:, oc, :], rhs=Wo[:, oc, hf * 512:(hf + 1) * 512],
                             start=(oc == 0), stop=(oc == 7))
                    P.tt("dve", x1[:, hf * 512:(hf + 1) * 512], ps[:, :], x_t[:, hf * 512:(hf + 1) * 512], ALU.add)
                P.dma(x1_d[tsl, :], x1[:])
                s1 = ss[:, tt % 2:tt % 2 + 1]
                P.act(sq[:], x1[:], AF.Square, accum=s1)
                P.act(s1, s1, AF.Sqrt, bias=EPS, scale=1.0 / D)
                P.recip(s1, s1)
                h_t = hb[tt % 2]
                P.stt("dve", h_t[:], x1[:], s1, gb[:], ALU.mult, ALU.mult)
                p_t = pt[tt % 2]
                for kc in range(8):
                    P.tr(p_t[:, kc, :], h_t[:, kc * 128:(kc + 1) * 128], C["ident"][:], inc=(kc == 7))
                P.cp("act", hT[:, :, tsl], p_t[:])

    def phase_ffn(self, l, hT, x1_d, xo_d):
        P, C = self.P, self.C
        I = self.inp
        with P.scope() as st:
            W1 = P.sb("W1", [128, 8, 4096], BF16, st)
            self.wload(W1, I["w_ff1"], l, 0, 4096)
            aT = P.sb("aT", [128, 32, 512], BF16, st)
            W2b = [P.sb("W2b", [128, 8, 512], BF16, st) for _ in range(2)]
            rb = [P.sb("rbf", [128, 512], F32, st) for _ in range(2)]
            xt = [P.sb("xt", [128, 512], F32, st) for _ in range(2)]
            xo = [P.sb("xo", [128, 512], F32, st) for _ in range(2)]
            fb = [P.ps("fb", [128, 512], F32, st) for _ in range(2)]
            ob = [P.ps("ob", [128, 512], F32, st) for _ in range(4)]
            wi = 0
            xi = 0
            for Q in range(4):
                for fc in range(32):
                    ps = fb[fc % 2]
                    self.proj_fm(ps, W1, fc * 128, 128, hT, Q * 512, 512)
                    r = rb[fc % 2]
                    P.act(r[:], ps[:, :], AF.Relu)
                    P.tt("pool" if fc % 2 else "dve", aT[:, fc, :], r[:], r[:], ALU.mult)
                for hf in range(2):
                    for blk in range(4):
                        W2_ = W2b[wi % 2]
                        wi += 1
                        self.wload(W2_, I["w_ff2"], l, hf * 512, 512, rows=(blk * 1024, (blk + 1) * 1024))
                        for tl in range(4):
                            for k in range(8):
                                P.mm(ob[tl][:, :], lhsT=aT[:, blk * 8 + k, tl * 128:(tl + 1) * 128], rhs=W2_[:, k, :],
                                     start=(blk == 0 and k == 0), stop=(blk == 3 and k == 7), inc=(k == 7))
                    for tl in range(4):
                        rows = slice((Q * 4 + tl) * 128, (Q * 4 + tl + 1) * 128)
                        cols = slice(hf * 512, (hf + 1) * 512)
                        x_ = xt[xi % 2]
                        o_ = xo[xi % 2]
                        xi += 1
                        P.dma(x_[:], x1_d[rows, cols])
                        P.tt("dve", o_[:], ob[tl][:, :], x_[:], ALU.add)
                        P.dma(xo_d[rows, cols], o_[:])
                P.maybe_barrier()

    def build(self):
        P = self.P
        I = self.inp
        self.din("x", [S, D])
        for name, shp in (("norm1_g", [DEPTH, D]), ("w_in", [DEPTH, D, D_IN]), ("dsa_q_norm", [DEPTH, 64]),
                          ("dsa_k_norm", [DEPTH, 64]), ("nsa_q_norm", [DEPTH, 64]), ("nsa_k_norm", [DEPTH, 3, 64]),
                          ("nsa_cmp_pos", [DEPTH, 2, 32, 64]), ("nsa_cmp_w", [DEPTH, 2, 32, 64, 64]),
                          ("ssd_conv_w", [DEPTH, 4, 3072]), ("ssd_conv_b", [DEPTH, 3072]),
                          ("ssd_dt_bias", [DEPTH, 32]), ("ssd_a_log", [DEPTH, 32]), ("ssd_d", [DEPTH, 32]),
                          ("ssd_norm_g", [DEPTH, 2048]), ("w_br_dsa", [DEPTH, D, D]), ("w_br_nsa", [DEPTH, D, D]),
                          ("w_br_ssd", [DEPTH, 2 * D, D]), ("w_out", [DEPTH, D, D]), ("norm2_g", [DEPTH, D]),
                          ("w_ff1", [DEPTH, D, 4 * D]), ("w_ff2", [DEPTH, 4 * D, D])):
            self.din(name, shp)
        self.load_consts()
        out = self.dout("out", [S, D])
        hT = P.sb("hT", [128, 8, S], BF16)
        ya_d = P.dram("ya", [128, 8, S], BF16)
        yb_d = P.dram("yb", [128, 8, S], BF16)
        yc_d = P.dram("yc", [128, 16, S], BF16)
        zs_d = P.dram("zs", [S, 2048], F32)
        xact_d = P.dram("xact", [128, 24, S], BF16)
        mix_d = P.dram("mix", [128, 8, S], BF16)
        x1_d = P.dram("x1", [S, D], F32)
        xm_d = P.dram("xm", [S, D], F32)
        dbg = self.debug
        x_cur = I["x"]
        for l in range(self.nlayers):
            x_nxt = out if l == self.nlayers - 1 else xm_d
            self.phase_norm(x_cur, I["norm1_g"][l], hT)
            if dbg == "hT":
                return self._dump(hT[:], [128, 8, S], BF16)
            if dbg in ("nsa", "nsa_k"):
                self.phase_nsa(l, hT, yb_d)
                return self._dump(yb_d, [128, 8, S], BF16)
            if dbg == "ssd":
                self.phase_ssd(l, hT, yc_d, zs_d, xact_d)
                return self._dump(yc_d, [128, 16, S], BF16)
            self.phase_dsa(l, hT, ya_d)
            if dbg is not None and dbg.startswith("dsa"):
                return self._dump(ya_d, [128, 8, S], BF16)
            self.phase_nsa(l, hT, yb_d)
            self.phase_ssd(l, hT, yc_d, zs_d, xact_d)
            self.phase_merge(l, hT, ya_d, yb_d, yc_d, mix_d)
            if dbg == "mix":
                return self._dump(mix_d, [128, 8, S], BF16)
            self.phase_out_norm(l, x_cur, mix_d, x1_d, hT)
            if dbg == "x1":
                return self._dump(x1_d, [S, D], F32)
            self.phase_ffn(l, hT, x1_d, x_nxt)
            x_cur = x_nxt
        P.finish()

    def _dump(self, src, shape, dt):
        if "dbg" not in self.out:
            o = self.dout("dbg", shape, dt)
            self.P.dma(o, src)
        self.P.finish()


_CACHE = {}


def kernel(**inputs):
    n = 8
    B = Builder()
    B.build()
    consts = _consts()
    shared = {}
    for name in B.inp:
        if name in consts:
            shared[name] = consts[name]
        elif name != "x":
            shared[name] = np.ascontiguousarray(np.asarray(inputs[name], dtype=np.float32))
    x = np.asarray(inputs["x"], dtype=np.float32)
    in_maps = []
    for b in range(n):
        m = dict(shared)
        m["x"] = np.ascontiguousarray(x[b])
        in_maps.append(m)
    res = run_bass_kernel_spmd(B.nc, in_maps, core_ids=list(range(n)))
    return np.stack([np.asarray(res.results[b]["out"], dtype=np.float32) for b in range(n)], axis=0)
```

```python
import contextlib
import os
import numpy as np
import concourse.bass as bass
import concourse.mybir as mybir
from concourse.bass_utils import run_bass_kernel_spmd

F32 = mybir.dt.float32
BF16 = mybir.dt.bfloat16
AF = mybir.ActivationFunctionType
ALU = mybir.AluOpType
AX = mybir.AxisListType

EPOCH = 240
NPOOL = 90

D = 1024
S = 2048
NT = 16
DEPTH = 2
EPS = 1e-6
OFF = dict(dq=0, dk=1024, dv=1088, iq=1152, ik=1408, iw=1440, nq=1448, nkv=2472, ng=3240,
           sz=3288, sxbc=5336, sdt=8408, mg=8440)
D_IN = 11512
NEGM = -30000.0


def _region(ap):
    t = ap.tensor
    name = t.name
    dims = [(int(s), int(c)) for s, c in ap.ap]
    off = int(ap.offset)
    if "DRam" in type(t).__name__:
        lo = hi = off
        for s, c in dims:
            if s >= 0:
                hi += s * (c - 1)
            else:
                lo += s * (c - 1)
        return (name, 0, 1, lo, hi + 1)
    if "PSum" in type(t).__name__:
        return (name, 0, 128, 0, 1 << 30)
    ps, pc = dims[0]
    if ps == 0:
        p0, f0 = 0, off
    else:
        p0 = off // ps
        f0 = off - p0 * ps
    lo = hi = f0
    for s, c in dims[1:]:
        if s >= 0:
            hi += s * (c - 1)
        else:
            lo += s * (c - 1)
    return (name, p0, p0 + pc, lo, hi + 1)


def _overlap(a, b):
    return a[1] < b[2] and b[1] < a[2] and a[3] < b[4] and b[3] < a[4]


def _contains(a, b):
    return a[1] <= b[1] and b[2] <= a[2] and a[3] <= b[3] and b[4] <= a[4]


class Prog:
    ENGS = ("pe", "act", "dve", "pool", "sp")

    def __init__(self, nc):
        self.nc = nc
        self.stack = contextlib.ExitStack()
        self.eng = {"pe": nc.tensor, "act": nc.scalar, "dve": nc.vector, "pool": nc.gpsimd, "sp": nc.sync}
        self.pool = [self.stack.enter_context(nc.semaphore(f"sp{i}")) for i in range(NPOOL)]
        self.bsem = [self.stack.enter_context(nc.semaphore(f"bar{i}")) for i in range(4)]
        self.nbar = 0
        self.uid = 0
        self.n_wait = 0
        self.n_ins = 0
        self.max_used = 0
        self._reset()

    def _reset(self):
        self.free = list(range(NPOOL))
        self.esem = {e: None for e in self.ENGS}
        self.ecnt = {e: 0 for e in self.ENGS}
        self.pend = {e: False for e in self.ENGS}
        self.seen = {e: {} for e in self.ENGS}
        self.dset = []
        self.dall = {}
        self.dnext = 0
        self.trk = {}

    def _alloc(self):
        assert self.free, "semaphore pool exhausted; add a barrier"
        i = self.free.pop()
        self.max_used = max(self.max_used, NPOOL - len(self.free))
        return i

    def sems_left(self):
        return len(self.free)

    def sb(self, name, shape, dtype, stack=None):
        self.uid += 1
        return (stack or self.stack).enter_context(
            self.nc.sbuf_tensor(f"{name}_{self.uid}", list(shape), dtype))

    def ps(self, name, shape, dtype, stack=None):
        self.uid += 1
        return (stack or self.stack).enter_context(
            self.nc.psum_tensor(f"{name}_{self.uid}", list(shape), dtype))

    def dram(self, name, shape, dtype):
        self.uid += 1
        return self.nc.dram_tensor(f"{name}_{self.uid}", list(shape), dtype).ap()

    @contextlib.contextmanager
    def scope(self):
        st = contextlib.ExitStack()
        try:
            yield st
        finally:
            self.barrier()
            st.close()

    def _deps(self, reads, writes):
        deps = []
        for ap in reads:
            r = _region(ap)
            t = self.trk.get(r[0])
            if t is None:
                continue
            for (wr, ev) in t["w"]:
                if _overlap(wr, r):
                    deps.append(ev)
        for ap in writes:
            r = _region(ap)
            t = self.trk.get(r[0])
            if t is None:
                continue
            for (wr, ev) in t["w"]:
                if _overlap(wr, r):
                    deps.append(ev)
            for (sk, rr), val in t["r"].items():
                if _overlap(rr, r):
                    deps.append((sk, val))
        return deps

    def _record(self, reads, writes, ev):
        for ap in reads:
            r = _region(ap)
            t = self.trk.setdefault(r[0], {"w": [], "r": {}})
            k = (ev[0], r)
            if t["r"].get(k, -1) < ev[1]:
                t["r"][k] = ev[1]
        for ap in writes:
            r = _region(ap)
            t = self.trk.setdefault(r[0], {"w": [], "r": {}})
            t["w"] = [(wr, e) for (wr, e) in t["w"] if not _contains(r, wr)]
            t["w"].append((r, ev))
            t["r"] = {k: v for k, v in t["r"].items() if not _contains(r, k[1])}

    def _wait(self, e, deps):
        eng = self.eng[e]
        need = {}
        for sk, val in deps:
            if self.seen[e].get(sk, 0) >= val:
                continue
            if need.get(sk, 0) < val:
                need[sk] = val
        for sk, val in need.items():
            assert 0 < val <= 255
            eng.wait_ge(self.pool[sk[1]], val)
            self.seen[e][sk] = val
            self.n_wait += 1

    def _next_ev(self, e, commit):
        if self.esem[e] is None or self.ecnt[e] >= EPOCH:
            self.esem[e] = self._alloc()
            self.ecnt[e] = 0
        ev = ((e, self.esem[e]), self.ecnt[e] + 1)
        if commit:
            self.ecnt[e] += 1
        return ev

    def op(self, e, fn, reads, writes, inc=True):
        deps = self._deps(reads, writes)
        if e == "pe":
            deps = [d for d in deps if d[0][0] != "pe"]
        self._wait(e, deps)
        ins = fn(self.eng[e])
        self.n_ins += 1
        ev = self._next_ev(e, inc)
        if inc:
            ins.then_inc(self.pool[ev[0][1]], 1)
            self.pend[e] = False
        else:
            self.pend[e] = True
        self._record(reads, writes, ev)
        return ins

    def dma(self, out, in_, q="sp", **kw):
        deps = self._deps([in_], [out])
        if q == "pool":
            ent = [self._alloc(), 0]
        else:
            if len(self.dset) < 8:
                self.dset.append([self._alloc(), 0])
            k = self.dnext % len(self.dset)
            self.dnext += 1
            if self.dset[k][1] >= 15:
                self.dset[k] = [self._alloc(), 0]
            ent = self.dset[k]
        if ent[1] > 0:
            deps.append((("d", ent[0]), 16 * ent[1]))
        self._wait(q, deps)
        ins = self.eng[q].dma_start(out=out, in_=in_, **kw)
        ins.then_inc(self.pool[ent[0]], 16)
        ent[1] += 1
        self.dall[ent[0]] = ent[1]
        self.n_ins += 1
        self._record([in_], [out], (("d", ent[0]), 16 * ent[1]))
        return ins

    def barrier(self):
        evs = []
        for e in self.ENGS:
            assert not self.pend[e], f"pending non-inc op on {e} at barrier"
            if self.esem[e] is not None and self.ecnt[e] > 0:
                evs.append(((e, self.esem[e]), self.ecnt[e]))
        for k, c in self.dall.items():
            evs.append((("d", k), 16 * c))
        for e in self.ENGS:
            self._wait(e, evs)
        b0 = self.bsem[2 * (self.nbar % 2)]
        b1 = self.bsem[2 * (self.nbar % 2) + 1]
        p0 = self.bsem[2 * ((self.nbar + 1) % 2)]
        p1 = self.bsem[2 * ((self.nbar + 1) % 2) + 1]
        for e in self.ENGS:
            if e != "sp":
                self.eng[e].sem_inc(b0, 1)
        sp = self.eng["sp"]
        sp.wait_ge(b0, 4)
        used = [i for i in range(NPOOL) if i not in set(self.free)]
        for i in used:
            sp.sem_clear(self.pool[i])
        sp.sem_clear(p0)
        sp.sem_clear(p1)
        sp.sem_inc(b1, 1)
        for e in self.ENGS:
            if e != "sp":
                self.eng[e].wait_ge(b1, 1)
        self.nbar += 1
        self._reset()

    def maybe_barrier(self, min_free=35):
        if len(self.free) < min_free:
            self.barrier()

    def finish(self):
        self.barrier()
        self.stack.close()

    def mm(self, out, lhsT, rhs, start=True, stop=True, inc=None, sgc=False):
        if inc is None:
            inc = stop
        return self.op("pe", lambda e: e.matmul(out, lhsT=lhsT, rhs=rhs, start=start, stop=stop,
                                                skip_group_check=sgc),
                       [lhsT, rhs], [out], inc=inc)

    def tr(self, out, in_, ident, inc=True):
        return self.op("pe", lambda e: e.transpose(out=out, in_=in_, identity=ident), [in_, ident], [out], inc=inc)

    def act(self, out, in_, func, bias=None, scale=None, accum=None):
        kw = {}
        reads = [in_]
        writes = [out]
        if bias is not None:
            kw["bias"] = bias
            if not isinstance(bias, (int, float)):
                reads.append(bias)
        if scale is not None:
            kw["scale"] = scale
            if not isinstance(scale, (int, float)):
                reads.append(scale)
        if accum is not None:
            kw["accum_out"] = accum
            writes.append(accum)
        return self.op("act", lambda e: e.activation(out=out, in_=in_, func=func, **kw), reads, writes)

    def cp(self, eng, out, in_):
        if eng == "act":
            return self.op("act", lambda e: e.copy(out=out, in_=in_), [in_], [out])
        return self.op(eng, lambda e: e.tensor_copy(out=out, in_=in_), [in_], [out])

    def tt(self, eng, out, in0, in1, op):
        return self.op(eng, lambda e: e.tensor_tensor(out=out, in0=in0, in1=in1, op=op), [in0, in1], [out])

    def ts(self, eng, out, in0, s1, s2=None, op0=ALU.mult, op1=None, accum=None):
        reads = [in0] + [s for s in (s1, s2) if s is not None and not isinstance(s, (int, float))]
        writes = [out] + ([accum] if accum is not None else [])
        kw = {}
        if op1 is not None:
            kw["op1"] = op1
        if accum is not None:
            kw["accum_out"] = accum
        return self.op(eng, lambda e: e.tensor_scalar(out=out, in0=in0, scalar1=s1, scalar2=s2, op0=op0, **kw),
                       reads, writes)

    def stt(self, eng, out, in0, scalar, in1, op0, op1):
        reads = [in0, in1] + ([scalar] if not isinstance(scalar, (int, float)) else [])
        return self.op(eng, lambda e: e.scalar_tensor_tensor(out=out, in0=in0, scalar=scalar, in1=in1,
                                                             op0=op0, op1=op1), reads, [out])

    def amul(self, out, in_, val):
        return self.op("act", lambda e: e.mul(out=out, in_=in_, mul=val), [in_], [out])

    def memset(self, eng, ap, val):
        return self.op(eng, lambda e: e.memset(ap, val), [], [ap])

    def recip(self, out, in_):
        return self.op("dve", lambda e: e.reciprocal(out=out, in_=in_), [in_], [out])


def _consts():
    c = {}
    p = np.arange(128)
    c["c_ident"] = np.eye(128, dtype=np.float32)
    c["c_ones"] = np.ones((128, 128), np.float32)
    c["c_blk64"] = (p[:, None] // 64 == p[None, :] // 64).astype(np.float32)
    c["c_tri"] = (p[:, None] <= p[None, :]).astype(np.float32)
    c["c_cneg"] = np.where(p[:, None] > p[None, :], NEGM, 0.0).astype(np.float32)
    c["c_wneg"] = np.where(p[:, None] <= p[None, :], NEGM, 0.0).astype(np.float32)
    c["c_cnegtm"] = np.where(p[None, :] > p[:, None], -3e30, 0.0).astype(np.float32)
    s = np.arange(S)
    c["c_E"] = (s[None, :] // 64 == np.arange(32)[:, None]).astype(np.float32)
    n = np.arange(128)
    vis = (16 * n[:, None] + 31 <= s[None, :])
    c["c_visneg"] = np.where(vis, 0.0, NEGM).astype(np.float32)
    j = np.arange(32)
    cover = ((16 * n[:, None] < 64 * j[None, :] + 64) & (16 * n[:, None] + 32 > 64 * j[None, :]))
    ce = np.zeros((128, 33), np.float32)
    ce[:, 0] = 1.0
    ce[:, 1:] = cover
    ce[127] = 0.0
    c["c_cover"] = ce
    t = (np.arange(NT)[None, :, None] * 128 + p[:, None, None])
    cur = t // 64
    jj = j[None, None, :]
    forced = (jj == cur) | (jj == 0)
    future = jj > cur
    c["c_selkeep"] = (~(forced | future)).astype(np.float32)
    c["c_selbias"] = np.where(forced, 1e9, np.where(future, -1e9, 0.0)).astype(np.float32)
    return c


class Builder:
    def __init__(self, debug=None, nlayers=DEPTH):
        self.debug = debug
        self.nlayers = nlayers
        import os
        self.lim = int(os.environ.get("KLIM", "16"))
        nc = bass.Bass("TRN2", target_bir_lowering=False)
        self.nc = nc
        self.P = Prog(nc)
        self.inp = {}
        self.out = {}

    def din(self, name, shape, dt=F32):
        self.inp[name] = self.nc.dram_tensor(name, list(shape), dt, kind="ExternalInput").ap()
        return self.inp[name]

    def dout(self, name, shape, dt=F32):
        self.out[name] = self.nc.dram_tensor(name, list(shape), dt, kind="ExternalOutput").ap()
        return self.out[name]

    def load_consts(self):
        P = self.P
        C = {}
        shapes = {k: v.shape for k, v in _consts().items()}
        for k, shp in shapes.items():
            self.din(k, shp)
        self.wstage = [P.sb("wstage", [128, 4096], F32) for _ in range(2)]
        self.wsi = 0

        def ld(name, key, shape, dt, rows=None):
            t = P.sb(name, shape, dt)
            src = self.inp[key]
            if dt == F32:
                P.dma(t[:], src)
            else:
                n = int(np.prod(shape[1:]))
                flat = "p a b -> p (a b)" if len(shape) == 3 else None
                for a0 in range(0, n, 4096):
                    b0 = min(n, a0 + 4096)
                    stg = self.wstage[self.wsi % 2]
                    self.wsi += 1
                    P.dma(stg[0:shape[0], 0:b0 - a0], src[:, a0:b0])
                    P.cp("pool", t[:, a0:b0], stg[0:shape[0], 0:b0 - a0])
            return t
        C["ident_f"] = ld("ident_f", "c_ident", [128, 128], F32)
        C["ident"] = ld("ident", "c_ident", [128, 128], BF16)
        C["ones_f"] = ld("ones_f", "c_ones", [128, 128], F32)
        C["blk64_f"] = ld("blk64_f", "c_blk64", [128, 128], F32)
        C["tri_f"] = ld("tri_f", "c_tri", [128, 128], F32)
        C["cneg"] = ld("cneg", "c_cneg", [128, 128], BF16)
        C["zero"] = P.sb("zero", [128, 128], BF16)
        P.memset("pool", C["zero"][:], 0.0)
        C["cneg_f"] = ld("cneg_f", "c_cneg", [128, 128], F32)
        C["wneg"] = ld("wneg", "c_wneg", [128, 128], BF16)
        C["cnegtm"] = ld("cnegtm", "c_cnegtm", [128, 128], F32)
        C["E"] = ld("E", "c_E", [32, S], BF16)
        C["visneg"] = ld("visneg", "c_visneg", [128, S], BF16)
        C["cover"] = ld("cover", "c_cover", [128, 33], BF16)
        C["selkeep"] = ld("selkeep", "c_selkeep", [128, NT, 32], F32)
        C["selbias"] = ld("selbias", "c_selbias", [128, NT, 32], F32)
        self.C = C

    def wload(self, dst, wdram, l, c0, n, rows=None):
        w2 = wdram[l] if rows is None else wdram[l][rows[0]:rows[1]]
        src = w2.rearrange("(kc p) n -> p kc n", p=128)[:, :, c0:c0 + n]
        nk = src.shape[1]
        step = max(1, 4096 // nk)
        for a in range(0, n, step):
            b = min(n, a + step)
            stg = self.wstage[self.wsi % 2]
            self.wsi += 1
            v = stg[:, 0:nk * (b - a)].rearrange("p (k n) -> p k n", k=nk)
            self.P.dma(v, src[:, :, a:b], q="sp")
            self.P.cp("pool", dst[:, :, a:b], v)

    def pipelined(self, items, Sb, pTb, cnt):
        P = self.P
        n = len(items)
        slots = []
        for i in range(n + 1):
            if i < n:
                k = cnt["si"] % 2
                cnt["si"] += 1
                Sp, pT = Sb[k], pTb[k]
                items[i][0](Sp)
                P.act(pT[:], Sp[:, :], AF.Exp, scale=0.125)
                slots.append(pT)
            if i >= 1:
                items[i - 1][1](slots[i - 1])

    def proj_fm(self, ps, Wt, c0, M, hT, t0, N):
        for kc in range(8):
            self.P.mm(ps[0:M, 0:N], lhsT=Wt[:, kc, c0:c0 + M], rhs=hT[:, kc, t0:t0 + N],
                      start=(kc == 0), stop=(kc == 7))

    def proj_tm(self, ps, Wt, c0, n, hT, tile):
        for kc in range(8):
            self.P.mm(ps[:, 0:n], lhsT=hT[:, kc, tile * 128:(tile + 1) * 128], rhs=Wt[:, kc, c0:c0 + n],
                      start=(kc == 0), stop=(kc == 7))

    def norm_fm(self, ps, ps2, M, N, gcol, out, sq, rs):
        P, C = self.P, self.C
        P.act(sq[0:M, 0:N], ps[0:M, 0:N], AF.Square)
        P.mm(ps2[0:M, 0:N], lhsT=C["blk64_f"][0:M, 0:M], rhs=sq[0:M, 0:N])
        P.act(rs[0:M, 0:N], ps2[0:M, 0:N], AF.Sqrt, bias=EPS, scale=1.0 / 64)
        P.recip(rs[0:M, 0:N], rs[0:M, 0:N])
        P.stt("dve", out, ps[0:M, 0:N], gcol, rs[0:M, 0:N], ALU.mult, ALU.mult)

    def gain_col(self, name, vec64, stack):
        t = self.P.sb(name, [128, 1], F32, stack)
        src = vec64.rearrange("(p o) -> p o", o=1)
        self.P.dma(t[0:64, :], src)
        self.P.dma(t[64:128, :], src)
        return t

    def phase_norm(self, xd, gvec, hT, store_T=None):
        P, C = self.P, self.C
        with P.scope() as st:
            gb = P.sb("gb", [128, D], F32, st)
            P.dma(gb[:], gvec.partition_broadcast(128))
            xb = [P.sb("xb", [128, D], F32, st) for _ in range(2)]
            sq = P.sb("sq", [128, D], F32, st)
            hb = [P.sb("hb", [128, D], BF16, st) for _ in range(2)]
            ss = P.sb("ss", [128, 2], F32, st)
            pt = [P.ps("pt", [128, 8, 128], BF16, st) for _ in range(2)]
            for tt in range(NT):
                x_t = xb[tt % 2]
                P.dma(x_t[:], xd[tt * 128:(tt + 1) * 128, :])
                s1 = ss[:, tt % 2:tt % 2 + 1]
                P.act(sq[:], x_t[:], AF.Square, accum=s1)
                P.act(s1, s1, AF.Sqrt, bias=EPS, scale=1.0 / D)
                P.recip(s1, s1)
                h_t = hb[tt % 2]
                P.stt("dve", h_t[:], x_t[:], s1, gb[:], ALU.mult, ALU.mult)
                p_t = pt[tt % 2]
                for kc in range(8):
                    P.tr(p_t[:, kc, :], h_t[:, kc * 128:(kc + 1) * 128], C["ident"][:], inc=(kc == 7))
                P.cp("act" if tt % 2 else "dve", hT[:, :, tt * 128:(tt + 1) * 128], p_t[:])

    def phase_dsa(self, l, hT, ya_d):
        P, C = self.P, self.C
        I = self.inp
        w_in = I["w_in"]
        with P.scope() as st:
            Wq = P.sb("Wq", [128, 8, 1024], BF16, st)
            self.wload(Wq, w_in, l, OFF["dq"], 1024)
            Wk = P.sb("Wk", [128, 8, 128], BF16, st)
            self.wload(Wk[:, :, 0:64], w_in, l, OFF["dk"], 64)
            self.wload(Wk[:, :, 64:128], w_in, l, OFF["dk"], 64)
            Wv = P.sb("Wv", [128, 8, 64], BF16, st)
            self.wload(Wv, w_in, l, OFF["dv"], 64)
            Wiq = P.sb("Wiq", [128, 8, 256], BF16, st)
            self.wload(Wiq, w_in, l, OFF["iq"], 256)
            Wik = P.sb("Wik", [128, 8, 128], BF16, st)
            for r in range(4):
                self.wload(Wik[:, :, r * 32:(r + 1) * 32], w_in, l, OFF["ik"], 32)
            Wiw = P.sb("Wiw", [128, 8, 8], BF16, st)
            self.wload(Wiw, w_in, l, OFF["iw"], 8)
            gq = self.gain_col("gq", I["dsa_q_norm"][l], st)
            gk = self.gain_col("gk", I["dsa_k_norm"][l], st)

            kT = P.sb("kT", [128, S], BF16, st)
            v1 = P.sb("v1", [128, NT, 65], BF16, st)
            ikT4 = P.sb("ikT4", [128, S], BF16, st)
            ikbd = P.sb("ikbd", [128, NT, 4, 128], BF16, st)
            qz = P.sb("qz", [128, 16, 512], BF16, st)
            iqT = P.sb("iqT", [128, 2, S], BF16, st)
            iw = P.sb("iw", [128, NT, 8], F32, st)
            acc = P.sb("acc", [128, S], F32, st)
            work = P.sb("work", [128, S], F32, st)
            mx = P.sb("mx", [128, 8], F32, st)
            negm = P.sb("negm", [128, S], BF16, st)
            negmT = [P.sb("negmT", [128, NT, 128], BF16, st) for _ in range(2)]
            rb = [P.sb("rb", [128, 512], F32, st) for _ in range(2)]
            pTb = [P.sb("pTb", [128, 512], BF16, st) for _ in range(2)]
            ytm = P.sb("ytm", [128, 1024], BF16, st)
            yT = P.sb("yT", [128, 8, 512], BF16, st)
            sq = P.sb("sq", [128, 512], F32, st)
            rs = P.sb("rs", [128, 512], F32, st)
            rc = P.sb("rc", [128, 4], F32, st)
            bank = [P.ps("bk", [128, 512], F32, st) for _ in range(6)]
            tb = [P.ps("tb", [128, 8, 128], BF16, st) for _ in range(2)]
            Sb, Ob, Ib = bank[0:2], bank[2:4], bank[4:6]

            P.memset("pool", v1[:, :, 64:65], 1.0)
            P.memset("pool", ikbd[:], 0.0)
            P.memset("pool", qz[:], 0.0)
            for Q in range(4):
                t0 = Q * 512
                self.proj_fm(Ib[0], Wk, 0, 128, hT, t0, 512)
                self.norm_fm(Ib[0], Ib[1], 128, 512, gk[:, 0:1], kT[:, t0:t0 + 512], sq, rs)
                self.proj_fm(Ib[0], Wik, 0, 128, hT, t0, 512)
                P.cp("act", ikT4[:, t0:t0 + 512], Ib[0][:, :])
                for c in range(2):
                    self.proj_fm(Ib[c], Wiq, c * 128, 128, hT, t0, 512)
                    P.cp("act", iqT[:, c, t0:t0 + 512], Ib[c][:, :])
            for hh in range(4):
                P.dma(ikbd[32 * hh:32 * hh + 32, :, hh, :],
                      ikT4[32 * hh:32 * hh + 32, :].rearrange("p (k s) -> p k s", s=128))
            for tt in range(NT):
                ps = Ib[tt % 2]
                self.proj_tm(ps, Wv, 0, 64, hT, tt)
                P.cp("act", v1[:, tt, 0:64], ps[:, 0:64])
                self.proj_tm(ps, Wiw, 64, 8, hT, tt) if False else None
            for tt in range(NT):
                ps = Ib[tt % 2]
                self.proj_tm(ps, Wiw, 0, 8, hT, tt)
                P.cp("act", iw[:, tt, :], ps[:, 0:8])

            cnt = {"si": 0, "ii": 0}

            def stage_q(Q):
                t0 = Q * 512
                for c in range(8):
                    self.proj_fm(Ib[0], Wq, c * 128, 128, hT, t0, 512)
                    P.act(sq[:, :], Ib[0][:, :], AF.Square)
                    P.mm(Ib[1][:, :], lhsT=C["blk64_f"][:], rhs=sq[:, :])
                    P.act(rs[:, :], Ib[1][:, :], AF.Sqrt, bias=EPS, scale=1.0 / 64)
                    P.recip(rs[:, :], rs[:, :])
                    for hf in range(2):
                        psl = slice(64 * hf, 64 * hf + 64)
                        P.stt("dve", qz[psl, 2 * c + hf, :], Ib[0][psl, :], gq[psl, 0:1], rs[psl, :], ALU.mult, ALU.mult)

            def stage_a(qt):
                nk = qt + 1
                tsl = slice(qt * 128, (qt + 1) * 128)
                if qt < 2:
                    P.memset("pool", negm[:, 0:nk * 128], 0.0)
                    return
                for kt in range(nk):
                    a_kt = acc[:, kt * 128:(kt + 1) * 128]
                    for c in range(2):
                        ps = Ib[cnt["ii"] % 2]
                        r = rb[cnt["ii"] % 2]
                        cnt["ii"] += 1
                        P.mm(ps[:, :], lhsT=iqT[:, c, tsl], rhs=ikbd[:, kt].rearrange("p a b -> p (a b)"))
                        P.act(r[:], ps[:, :], AF.Relu)
                        for hh in range(4):
                            h = 4 * c + hh
                            if h == 0:
                                P.ts("dve", a_kt, r[:, 0:128], iw[:, qt, 0:1], None, op0=ALU.mult)
                            else:
                                P.stt("dve", a_kt, r[:, hh * 128:(hh + 1) * 128], iw[:, qt, h:h + 1], a_kt,
                                      ALU.mult, ALU.add)
                d_sl = slice(qt * 128, (qt + 1) * 128)
                P.tt("dve", acc[:, d_sl], acc[:, d_sl], C["cnegtm"][:], ALU.add)
                cur = acc
                for r_ in range(32):
                    P.op("dve", lambda e: e.max(out=mx[:], in_=cur[:, 0:nk * 128]),
                         [cur[:, 0:nk * 128]], [mx[:]])
                    P.op("dve", lambda e: e.match_replace(out=work[:, 0:nk * 128], in_to_replace=mx[:],
                                                          in_values=cur[:, 0:nk * 128], imm_value=-1e30),
                         [mx[:], cur[:, 0:nk * 128]], [work[:, 0:nk * 128]])
                    cur = work
                P.ts("dve", negm[:, 0:nk * 128], work[:, 0:nk * 128], -1e29, 1.0, op0=ALU.is_le,
                     op1=ALU.subtract)

            def stage_b(qt):
                nk = qt + 1
                nT = negmT[qt % 2]
                for k0 in range(0, nk, 8):
                    k1 = min(nk, k0 + 8)
                    t_ = tb[0]
                    for kt in range(k0, k1):
                        P.tr(t_[:, kt - k0, :], negm[:, kt * 128:(kt + 1) * 128], C["ident"][:],
                             inc=(kt == k1 - 1))
                    P.amul(nT[:, k0:k1, :], t_[:, 0:k1 - k0, :], -NEGM)
                P.tt("pool", nT[:, qt, :], nT[:, qt, :], C["cneg"][:], ALU.add)

            def stage_c(qt):
                nk = qt + 1
                tl = qt % 4
                tsl = slice(tl * 128, (tl + 1) * 128)
                nT = negmT[qt % 2]
                items = []
                for hg in range(4):
                    O = Ob[hg % 2]
                    O3 = O[:, 0:260].rearrange("p (h e) -> p h e", e=65)
                    for kt in range(nk):
                        def em_s(Sp, hg=hg, kt=kt):
                            P.mm(Sp[:, :], lhsT=C["ident"][:], rhs=nT[:, kt, :].unsqueeze(1).to_broadcast([128, 4, 128]),
                                 start=True, stop=False, inc=False)
                            P.mm(Sp[:, :], lhsT=kT[:, kt * 128:(kt + 1) * 128], rhs=qz[:, 4 * hg:4 * hg + 4, tsl],
                                 start=False, stop=True)

                        def em_pv(pT, hg=hg, kt=kt, O3=O3):
                            for hh in range(4):
                                P.mm(O3[:, hh, :], lhsT=pT[:, hh * 128:(hh + 1) * 128], rhs=v1[:, kt, :],
                                     start=(kt == 0 and hh == 0), stop=(kt == nk - 1), inc=(hh == 3), sgc=True)
                            if kt == nk - 1:
                                P.recip(rc[:], O3[:, :, 64])
                                P.tt("dve", ytm[:, hg * 256:(hg + 1) * 256].rearrange("p (h e) -> p h e", e=64),
                                     O3[:, :, 0:64], rc[:].unsqueeze(2).to_broadcast([128, 4, 64]), ALU.mult)
                        items.append((em_s, em_pv))
                self.pipelined(items, Sb, pTb, cnt)
                t_ = tb[1]
                for c in range(8):
                    P.tr(t_[:, c, :], ytm[:, c * 128:(c + 1) * 128], C["ident"][:], inc=(c == 7))
                P.cp("act", yT[:, :, tsl], t_[:])
                if tl == 3:
                    Q = qt // 4
                    P.dma(ya_d[:, :, Q * 512:(Q + 1) * 512], yT[:], q="sp")

            nq = min(NT, self.lim)
            stage_a(0)
            stage_b(0)
            for qt in range(nq):
                P.maybe_barrier()
                if qt + 1 < nq:
                    stage_a(qt + 1)
                if qt % 4 == 0:
                    stage_q(qt // 4)
                stage_c(qt)
                if qt + 1 < nq:
                    stage_b(qt + 1)

    def _nsa_kside(self, l, hT, st, env):
        P, C = self.P, self.C
        I = self.inp
        Wkv, Wks, Wkw = env["Wkv"], env["Wks"], env["Wkw"]
        gks, gkw, gkc = env["gks"], env["gkw"], env["gkc"]
        ksT, kwT, vs1, vw1, kcmpT, vext = env["ksT"], env["kwT"], env["vs1"], env["vw1"], env["kcmpT"], env["vext"]
        sq, rs, ss, Mb, tb = env["sq"], env["rs"], env["ss"], env["Mb"], env["tb"]
        cmpw = []
        posB = []
        for kv in range(2):
            cw = P.sb("cmpw", [128, 32, 64], BF16, st)
            stg = self.wstage[self.wsi % 2]
            self.wsi += 1
            sv = stg[:, 0:2048].rearrange("p (l e) -> p l e", e=64)
            src = I["nsa_cmp_w"][l, kv].rearrange("l d e -> d l e")
            P.dma(sv[0:64], src)
            P.dma(sv[64:128], src)
            P.cp("pool", cw[:], sv)
            cmpw.append(cw)
            pt_ = P.sb("posT", [128, 32], F32, st)
            psrc = I["nsa_cmp_pos"][l, kv].rearrange("l d -> d l")
            P.dma(pt_[0:64, :], psrc, allow_slow_non_contiguous=True)
            P.dma(pt_[64:128, :], psrc, allow_slow_non_contiguous=True)
            pb = P.sb("posB", [128, 32, 127], BF16, st)
            P.cp("pool", pb[:], pt_[:].unsqueeze(2).to_broadcast([128, 32, 127]))
            posB.append(pb)

        kcT = P.sb("kcT", [128, S], BF16, st)
        vcT = P.sb("vcT", [128, S], BF16, st)
        kd = P.sb("kd", [128, 128], BF16, st)
        P.memset("pool", vs1[:, :, :, 64:65], 1.0)
        P.memset("pool", vw1[:, :, :, 64:65], 1.0)
        P.memset("pool", kd[:], 0.0)
        for g in range(2):
            P.memset("pool", vext[g][:], 0.0)
        for Q in range(4):
            t0 = Q * 512
            self.proj_fm(Mb[0], Wkv, 0, 128, hT, t0, 512)
            P.cp("act", kcT[:, t0:t0 + 512], Mb[0][:, :])
            self.proj_fm(Mb[1], Wkv, 128, 128, hT, t0, 512)
            P.cp("act", vcT[:, t0:t0 + 512], Mb[1][:, :])
            for g in range(2):
                self.proj_fm(Mb[0], Wks, g * 128, 128, hT, t0, 512)
                self.norm_fm(Mb[0], Mb[1], 128, 512, gks[:, 0:1], ksT[g][:, t0:t0 + 512], sq, rs)
                self.proj_fm(Mb[0], Wkw, g * 128, 128, hT, t0, 512)
                self.norm_fm(Mb[0], Mb[1], 128, 512, gkw[:, 0:1], kwT[g][:, t0:t0 + 512], sq, rs)
        for tt in range(NT):
            ps = Mb[tt % 2]
            self.proj_tm(ps, Wkv, 384, 128, hT, tt)
            P.cp("act", vs1[:, tt, :, 0:64], ps[:, 0:128].rearrange("p (g e) -> p g e", e=64))
            self.proj_tm(ps, Wkv, 640, 128, hT, tt)
            P.cp("act", vw1[:, tt, :, 0:64], ps[:, 0:128].rearrange("p (g e) -> p g e", e=64))
        P.maybe_barrier()
        def strided(t, lo, off):
            return bass.AP(t[:].tensor, lo * S + off, [[S, 64], [16, 127]])
        for g in range(2):
            lo = g * 64
            for kv, src in ((0, kcT), (1, vcT)):
                ps = Mb[kv]
                for li in range(32):
                    P.mm(ps[0:127, 0:64], lhsT=strided(src, lo, li), rhs=cmpw[kv][lo:lo + 64, li, :],
                         start=(li == 0), stop=False, inc=False)
                for li in range(32):
                    P.mm(ps[0:127, 0:64], lhsT=posB[kv][lo:lo + 64, li, :], rhs=cmpw[kv][lo:lo + 64, li, :],
                         start=False, stop=(li == 31), inc=(li == 31))
            P.act(sq[0:127, 0:64], Mb[0][0:127, 0:64], AF.Square, accum=ss[0:127, :])
            P.act(ss[0:127, :], ss[0:127, :], AF.Sqrt, bias=EPS, scale=1.0 / 64)
            P.recip(ss[0:127, :], ss[0:127, :])
            P.stt("dve", kd[0:127, 0:64], Mb[0][0:127, 0:64], ss[0:127, 0:1], gkc[0:127, :], ALU.mult, ALU.mult)
            P.cp("pool", kd[0:127, 64:128], kd[0:127, 0:64])
            P.tr(tb[0][:, 0, :], kd[:], C["ident"][:])
            P.cp("act", kcmpT[g][:], tb[0][:, 0, :])
            P.cp("act", vext[g][0:127, 0:64], Mb[1][0:127, 0:64])
            P.cp("pool", vext[g][0:127, 64:97], C["cover"][0:127, :])
        P.maybe_barrier()


    def phase_nsa(self, l, hT, yb_d):
        P, C = self.P, self.C
        I = self.inp
        w_in = I["w_in"]
        KV = OFF["nkv"]
        with P.scope() as st:
            Wq = P.sb("Wnq", [128, 8, 1024], BF16, st)
            self.wload(Wq, w_in, l, OFF["nq"], 1024)
            Wkv = P.sb("Wkv", [128, 8, 768], BF16, st)
            self.wload(Wkv, w_in, l, KV, 768)
            Wks = P.sb("Wks", [128, 8, 256], BF16, st)
            Wkw = P.sb("Wkw", [128, 8, 256], BF16, st)
            for g in range(2):
                for r in range(2):
                    self.wload(Wks[:, :, g * 128 + r * 64:g * 128 + r * 64 + 64], w_in, l, KV + 256 + g * 64, 64)
                    self.wload(Wkw[:, :, g * 128 + r * 64:g * 128 + r * 64 + 64], w_in, l, KV + 512 + g * 64, 64)
            Wng = P.sb("Wng", [128, 8, 48], BF16, st)
            self.wload(Wng, w_in, l, OFF["ng"], 48)
            gq = self.gain_col("gnq", I["nsa_q_norm"][l], st)
            gks = self.gain_col("gks", I["nsa_k_norm"][l, 1], st)
            gkw = self.gain_col("gkw", I["nsa_k_norm"][l, 2], st)
            gkc = P.sb("gkc", [128, 64], F32, st)
            P.dma(gkc[:], I["nsa_k_norm"][l, 0].partition_broadcast(128))
            ksT = [P.sb("ksT", [128, S], BF16, st) for _ in range(2)]
            kwT = [P.sb("kwT", [128, S], BF16, st) for _ in range(2)]
            vs1 = P.sb("vs1", [128, NT, 2, 65], BF16, st)
            vw1 = P.sb("vw1", [128, NT, 2, 65], BF16, st)
            kcmpT = [P.sb("kcmpT", [128, 128], BF16, st) for _ in range(2)]
            vext = [P.sb("vext", [128, 97], BF16, st) for _ in range(2)]
            sq = P.sb("sq", [128, 512], F32, st)
            rs = P.sb("rs", [128, 512], F32, st)
            ss = P.sb("ss1", [128, 1], F32, st)
            bank = [P.ps("bk", [128, 512], F32, st) for _ in range(6)]
            tb = [P.ps("tb", [128, 8, 128], BF16, st) for _ in range(2)]
            Sb, Ob, Mb = bank[0:2], bank[2:4], bank[4:6]
            with P.scope() as st2:
                self._nsa_kside(l, hT, st2, locals())
            nqz = P.sb("nqz", [128, 16, 512], BF16, st)
            P.memset("pool", nqz[:], 0.0)
            gs = P.sb("gs", [128, 48], F32, st)
            yacc = P.sb("yacc", [128, 16, 64], F32, st)
            ytmp = P.sb("ytmp", [128, 4, 64], F32, st)
            impn = P.sb("impn", [128, 16, 32], F32, st)
            imp = P.sb("imp", [128, 32], F32, st)
            mx = P.sb("mx8", [128, 8], F32, st)
            negblk = P.sb("negblk", [128, 32], BF16, st)
            negblkT = [P.sb("negblkT", [32, 128], BF16, st) for _ in range(2)]
            pTb = [P.sb("pTb", [128, 512], BF16, st) for _ in range(2)]
            ytm = P.sb("ytm", [128, 1024], BF16, st)
            yT = P.sb("yT", [128, 8, 512], BF16, st)
            rc = P.sb("rc", [128, 4], F32, st)
            cf = P.sb("cf", [128, 4], F32, st)

            cnt = {"si": 0}
            oi = 0
            for Q in range(4):
                t0 = Q * 512
                for c in range(8):
                    self.proj_fm(Mb[0], Wq, c * 128, 128, hT, t0, 512)
                    P.act(sq[:, :], Mb[0][:, :], AF.Square)
                    P.mm(Mb[1][:, :], lhsT=C["blk64_f"][:], rhs=sq[:, :])
                    P.act(rs[:, :], Mb[1][:, :], AF.Sqrt, bias=EPS, scale=1.0 / 64)
                    P.recip(rs[:, :], rs[:, :])
                    for hf in range(2):
                        psl = slice(64 * hf, 64 * hf + 64)
                        P.stt("dve", nqz[psl, 2 * c + hf, :], Mb[0][psl, :], gq[psl, 0:1], rs[psl, :], ALU.mult, ALU.mult)
                for tl in range(4):
                    qt = Q * 4 + tl
                    if qt >= self.lim:
                        continue
                    P.maybe_barrier()
                    tsl = slice(tl * 128, (tl + 1) * 128)
                    self.proj_tm(Mb[0], Wng, 0, 48, hT, qt)
                    P.act(gs[:], Mb[0][:, 0:48], AF.Sigmoid)
                    gs3 = gs[:].rearrange("p (h b) -> p h b", b=3)

                    def scores(Sp, sub, kT_g, kt_cols, masks):
                        first = True
                        for (ml, mr) in masks:
                            kk = mr.shape[0]
                            P.mm(Sp[:, :], lhsT=ml, rhs=mr.unsqueeze(1).to_broadcast([kk, 4, 128]),
                                 start=first, stop=False, inc=False)
                            first = False
                        P.mm(Sp[:, :], lhsT=kT_g[:, kt_cols], rhs=nqz[:, 4 * sub:4 * sub + 4, tsl],
                             start=first, stop=True)

                    def finalize(O3, sub, br, first):
                        hs = slice(4 * sub, 4 * sub + 4)
                        P.ts("dve", rc[:], O3[:, :, 64], 1e-30, None, op0=ALU.max)
                        P.recip(rc[:], rc[:])
                        if br == 0:
                            P.tt("dve", impn[:, hs, :], O3[:, :, 65:97], rc[:].unsqueeze(2).to_broadcast([128, 4, 32]),
                                 ALU.mult)
                        P.tt("dve", cf[:], rc[:], gs3[:, hs, br], ALU.mult)
                        if first:
                            P.tt("dve", yacc[:, hs, :], O3[:, :, 0:64], cf[:].unsqueeze(2).to_broadcast([128, 4, 64]),
                                 ALU.mult)
                        else:
                            P.tt("dve", ytmp[:], O3[:, :, 0:64], cf[:].unsqueeze(2).to_broadcast([128, 4, 64]),
                                 ALU.mult)
                            P.tt("pool", yacc[:, hs, :], yacc[:, hs, :], ytmp[:], ALU.add)

                    items = []
                    for sub in range(4):
                        g = sub // 2
                        O = Ob[oi % 2]
                        oi += 1
                        O3 = O[:, 0:388].rearrange("p (h e) -> p h e", e=97)

                        def em_s(Sp, sub=sub, g=g):
                            scores(Sp, sub, kcmpT[g], slice(0, 128),
                                   [(C["ident"][:], C["visneg"][:, qt * 128:(qt + 1) * 128])])

                        def em_pv(pT, sub=sub, g=g, O3=O3):
                            for hh in range(4):
                                P.mm(O3[:, hh, :], lhsT=pT[:, hh * 128:(hh + 1) * 128], rhs=vext[g][:, :],
                                     start=(hh == 0), stop=True, inc=(hh == 3), sgc=True)
                            finalize(O3, sub, 0, True)
                        items.append((em_s, em_pv))
                    self.pipelined(items, Sb, pTb, cnt)
                    for g in range(2):
                        P.op("dve", lambda e: e.tensor_reduce(out=imp[:], in_=impn[:, 8 * g:8 * g + 8, :].rearrange("p h j -> p j h"),
                                                              axis=AX.X, op=ALU.add),
                             [impn[:, 8 * g:8 * g + 8, :]], [imp[:]])
                        P.tt("dve", imp[:], imp[:], C["selkeep"][:, qt, :], ALU.mult)
                        P.tt("dve", imp[:], imp[:], C["selbias"][:, qt, :], ALU.add)
                        P.op("dve", lambda e: e.max(out=mx[:], in_=imp[:]), [imp[:]], [mx[:]])
                        P.ts("dve", negblk[:], imp[:], mx[:, 3:4], 1.0, op0=ALU.is_ge, op1=ALU.subtract)
                        P.tr(tb[0][0:32, 0, :], negblk[:], C["ident"][:])
                        P.amul(negblkT[g][:], tb[0][0:32, 0, :], -NEGM)
                    items = []
                    for br in (2, 1):
                        kts = list(range(0, qt + 1)) if br == 1 else list(range(max(0, qt - 4), qt + 1))
                        kT_l = ksT if br == 1 else kwT
                        v_l = vs1 if br == 1 else vw1
                        for sub in range(4):
                            g = sub // 2
                            O = Ob[oi % 2]
                            oi += 1
                            O3 = O[:, 0:260].rearrange("p (h e) -> p h e", e=65)
                            for ki, kt in enumerate(kts):
                                masks = []
                                if br == 1:
                                    masks.append((C["E"][:, kt * 128:(kt + 1) * 128], negblkT[g][:]))
                                if kt == qt:
                                    masks.append((C["ident"][:], C["cneg"][:]))
                                if br == 2 and kt == qt - 4:
                                    masks.append((C["ident"][:], C["wneg"][:]))

                                def em_s(Sp, sub=sub, g=g, kt=kt, masks=masks, kT_l=kT_l):
                                    scores(Sp, sub, kT_l[g], slice(kt * 128, (kt + 1) * 128), masks)

                                def em_pv(pT, sub=sub, g=g, kt=kt, ki=ki, nkt=len(kts), O3=O3, v_l=v_l, br=br):
                                    for hh in range(4):
                                        P.mm(O3[:, hh, :], lhsT=pT[:, hh * 128:(hh + 1) * 128], rhs=v_l[:, kt, g, :],
                                             start=(ki == 0 and hh == 0), stop=(ki == nkt - 1), inc=(hh == 3), sgc=True)
                                    if ki == nkt - 1:
                                        finalize(O3, sub, br, False)
                                items.append((em_s, em_pv))
                    self.pipelined(items, Sb, pTb, cnt)
                    P.cp("act", ytm[:], yacc[:].rearrange("p h e -> p (h e)"))
                    t_ = tb[1]
                    for c in range(8):
                        P.tr(t_[:, c, :], ytm[:, c * 128:(c + 1) * 128], C["ident"][:], inc=(c == 7))
                    P.cp("act", yT[:, :, tsl], t_[:])
                P.dma(yb_d[:, :, t0:t0 + 512], yT[:], q="sp")

    def phase_ssd(self, l, hT, yc_d, zs_d, xact_d):
        P, C = self.P, self.C
        I = self.inp
        w_in = I["w_in"]
        with P.scope() as st:
            Wz = P.sb("Wz", [128, 8, 2048], BF16, st)
            self.wload(Wz, w_in, l, OFF["sz"], 2048)
            zt = [P.sb("zt", [128, 2048], F32, st) for _ in range(2)]
            bank = [P.ps("bk", [128, 512], F32, st) for _ in range(4)]
            bi = 0
            for tt in range(NT):
                z_t = zt[tt % 2]
                for nb in range(4):
                    ps = bank[bi % 4]
                    bi += 1
                    self.proj_tm(ps, Wz, nb * 512, 512, hT, tt)
                    P.act(z_t[:, nb * 512:(nb + 1) * 512], ps[:, :], AF.Silu)
                P.dma(zs_d[tt * 128:(tt + 1) * 128, :], z_t[:])
        with P.scope() as st:
            cw = P.sb("cw", [128, 24, 4], F32, st)
            cb = P.sb("cb", [128, 24], F32, st)
            for cc in range(24):
                P.dma(cw[:, cc, :], I["ssd_conv_w"][l][:, cc * 128:(cc + 1) * 128].rearrange("k c -> c k"),
                      allow_slow_non_contiguous=True)
                P.dma(cb[:, cc:cc + 1], I["ssd_conv_b"][l][cc * 128:(cc + 1) * 128].rearrange("(c o) -> c o", o=1))
            Wx = [P.sb("Wx", [128, 8, 512], BF16, st) for _ in range(2)]
            raw = [P.sb("raw", [128, 3 + S], F32, st) for _ in range(2)]
            accb = [P.sb("accb", [128, S], F32, st) for _ in range(2)]
            xa = [P.sb("xa", [128, S], BF16, st) for _ in range(2)]
            ctmp = P.sb("ctmp", [128, S], F32, st)
            bank = [P.ps("bk", [128, 512], F32, st) for _ in range(4)]
            for r_ in raw:
                P.memset("pool", r_[:, 0:3], 0.0)
            bi = 0
            for cc in range(24):
                W_ = Wx[(cc // 4) % 2]
                if cc % 4 == 0:
                    self.wload(W_, w_in, l, OFF["sxbc"] + cc * 128, 512)
                rw = raw[cc % 2]
                for Q in range(4):
                    ps = bank[bi % 4]
                    bi += 1
                    self.proj_fm(ps, W_, (cc % 4) * 128, 128, hT, Q * 512, 512)
                    P.cp("act", rw[:, 3 + Q * 512:3 + (Q + 1) * 512], ps[:, :])
                eng = "dve"
                ac = accb[cc % 2]
                P.ts(eng, ac[:], rw[:, 3:3 + S], cw[:, cc, 3:4], None, op0=ALU.mult)
                for k in (2, 1, 0):
                    if eng == "dve":
                        P.stt(eng, ac[:], rw[:, k:k + S], cw[:, cc, k:k + 1], ac[:], ALU.mult, ALU.add)
                    else:
                        P.ts(eng, ctmp[:], rw[:, k:k + S], cw[:, cc, k:k + 1], None, op0=ALU.mult)
                        P.tt(eng, ac[:], ac[:], ctmp[:], ALU.add)
                P.act(xa[cc % 2][:], ac[:], AF.Silu, bias=cb[:, cc:cc + 1])
                P.dma(xact_d[:, cc, :], xa[cc % 2][:])
                P.maybe_barrier()
        with P.scope() as st:
            Wdt = P.sb("Wdt", [128, 8, 32], BF16, st)
            self.wload(Wdt, w_in, l, OFF["sdt"], 32)
            dtb = P.sb("dtb", [128, 32], F32, st)
            P.dma(dtb[:], I["ssd_dt_bias"][l].partition_broadcast(128))
            a_b = P.sb("a_b", [128, 32], F32, st)
            P.dma(a_b[:], I["ssd_a_log"][l].partition_broadcast(128))
            P.act(a_b[:], a_b[:], AF.Exp)
            P.ts("dve", a_b[:], a_b[:], -1.0, None, op0=ALU.mult)
            D_b = P.sb("D_b", [128, 32], F32, st)
            P.dma(D_b[:], I["ssd_d"][l].partition_broadcast(128))
            ng_b = P.sb("ng_b", [128, 2048], F32, st)
            P.dma(ng_b[:], I["ssd_norm_g"][l].partition_broadcast(128))
            xab = [P.sb("xab", [128, 24, 128], BF16, st) for _ in range(2)]
            zsb = [P.sb("zsb", [128, 2048], F32, st) for _ in range(2)]
            xs_tm = P.sb("xs_tm", [128, 2048], BF16, st)
            bm_b = [P.sb("bm_tm", [128, 512], BF16, st) for _ in range(2)]
            xsw_b = [P.sb("xs_w", [128, 2048], BF16, st) for _ in range(2)]
            rhs1_b = [P.sb("rhs1", [128, 8, 128], F32, st) for _ in range(2)]
            rhs2_b = [P.sb("rhs2", [128, 8, 128], F32, st)] * 2
            Lg_b = [P.sb("Lg", [128, 8, 128], BF16, st) for _ in range(2)]
            MTg_b = [P.sb("MTg", [128, 8, 128], BF16, st) for _ in range(2)]
            tmp_b = [P.sb("tmpb", [128, 512], F32, st) for _ in range(2)]
            cbT = P.sb("cbT", [128, 4, 128], BF16, st)
            H = [P.sb("H", [128, 512], F32, st) for _ in range(4)]
            Hbf = [P.sb("Hbf", [128, 512], BF16, st) for _ in range(4)]
            y_b = [P.sb("y", [128, 2048], F32, st) for _ in range(2)]
            ynb = P.sb("ynb", [128, 2048], BF16, st)
            ycT = P.sb("ycT", [128, 16, 128], BF16, st)
            sma = {n: P.sb(n, [128, NT, 32], F32, st) for n in ("dtc", "da", "nb", "ea", "wj", "dec")}
            sma["lndt"] = sma["nb"]
            sma["acum"] = sma["ea"]
            sma["alast"] = sma["dec"]
            ss = P.sb("ss2", [128, 1], F32, st)
            R = [P.ps("R", [128, 512], F32, st) for _ in range(2)]
            Yp_b = [P.ps("Yp", [128, 512], F32, st) for _ in range(2)]
            Yo = P.ps("Yo", [128, 512], F32, st)
            STp = P.ps("STp", [128, 512], F32, st)
            M0 = P.ps("M0", [128, 512], F32, st)
            M1 = M0
            tb = P.ps("tb", [128, 8, 128], BF16, st)
            for g in range(4):
                P.memset("pool", H[g][:], 0.0)
                P.memset("pool", Hbf[g][:], 0.0)
            nch = min(NT, self.lim)
            fl = lambda t: t[:].rearrange("p c h -> p (c h)")
            for c in range(NT):
                for kc in range(8):
                    P.mm(M1[:, c * 32:(c + 1) * 32], lhsT=hT[:, kc, c * 128:(c + 1) * 128], rhs=Wdt[:, kc, :],
                         start=(c == 0 and kc == 0), stop=(kc == 7), inc=(kc == 7), sgc=True)
            P.tt("dve", sma["dtc"][:], M1[:, :].rearrange("p (c h) -> p c h", h=32),
                 dtb[:].unsqueeze(1).to_broadcast([128, NT, 32]), ALU.add)
            P.act(fl(sma["dtc"]), fl(sma["dtc"]), AF.Exp)
            P.act(fl(sma["dtc"]), fl(sma["dtc"]), AF.Ln, bias=1.0)
            P.act(fl(sma["lndt"]), fl(sma["dtc"]), AF.Ln)
            P.tt("dve", sma["da"][:], sma["dtc"][:], a_b[:].unsqueeze(1).to_broadcast([128, NT, 32]), ALU.mult)
            P.mm(M1[:, :], lhsT=C["tri_f"][:], rhs=fl(sma["da"]))
            P.cp("dve", fl(sma["acum"]), M1[:, :])
            P.mm(M1[:, :], lhsT=C["ones_f"][:], rhs=fl(sma["da"]))
            P.cp("dve", fl(sma["alast"]), M1[:, :])
            P.tt("dve", sma["nb"][:], sma["lndt"][:], sma["acum"][:], ALU.subtract)
            P.tt("dve", sma["wj"][:], sma["alast"][:], sma["acum"][:], ALU.subtract)
            P.act(fl(sma["wj"]), fl(sma["wj"]), AF.Exp)
            P.tt("dve", sma["wj"][:], sma["wj"][:], sma["dtc"][:], ALU.mult)
            P.act(fl(sma["ea"]), fl(sma["acum"]), AF.Exp)
            P.act(fl(sma["dec"]), fl(sma["alast"]), AF.Exp)
            sma = {k: sma[k] for k in ("da", "nb", "ea", "wj", "dec")}

            def load(c):
                csl = slice(c * 128, (c + 1) * 128)
                P.dma(xab[c % 2][:], xact_d[:, :, csl])
                P.dma(zsb[c % 2][:], zs_d[csl, :])

            def part1(c):
                p = c % 2
                xa_ = xab[p]
                sm = {k: v[:, c, :] for k, v in sma.items()}
                bm_tm, xs_w, y = bm_b[p], xsw_b[p], y_b[p]
                for k0 in (0, 8, 16):
                    n = 8 if k0 < 16 else 4
                    for j in range(n):
                        P.tr(tb[:, j, :], xa_[:, k0 + j, :], C["ident"][:], inc=(j == n - 1))
                    if k0 < 16:
                        P.cp("act", xs_tm[:, k0 * 128:(k0 + 8) * 128], tb[:].rearrange("p a b -> p (a b)"))
                    else:
                        P.cp("act", bm_tm[:], tb[:, 0:4, :].rearrange("p a b -> p (a b)"))
                for g in range(4):
                    P.mm(M0[:, g * 128:(g + 1) * 128], lhsT=xa_[:, 16 + g, :], rhs=xa_[:, 20 + g, :],
                         start=(g == 0), stop=True, inc=(g == 3), sgc=True)
                P.cp("act", cbT[:].rearrange("p a b -> p (a b)"), M0[:, :])
                P.tt("dve", xs_w[:].rearrange("p (h e) -> p h e", e=64), xs_tm[:].rearrange("p (h e) -> p h e", e=64),
                     sm["wj"][:].unsqueeze(2).to_broadcast([128, 32, 64]), ALU.mult)
                for g in range(4):
                    hs = slice(8 * g, 8 * g + 8)
                    gsl = slice(g * 512, (g + 1) * 512)
                    rhs1, rhs2, Lg, MTg, Yp, tmp = rhs1_b[g % 2], rhs2_b[g % 2], Lg_b[g % 2], MTg_b[g % 2], Yp_b[g % 2], tmp_b[g % 2]
                    P.tt("dve", rhs1[:], C["tri_f"][:].unsqueeze(1).to_broadcast([128, 8, 128]),
                         sm["da"][:, hs].unsqueeze(2).to_broadcast([128, 8, 128]), ALU.mult)
                    P.tt("dve", rhs2[:], C["cneg_f"][:].unsqueeze(1).to_broadcast([128, 8, 128]),
                         sm["nb"][:, hs].unsqueeze(2).to_broadcast([128, 8, 128]), ALU.add)
                    for hf in range(2):
                        P.mm(R[hf][:, :], lhsT=C["ones_f"][:], rhs=rhs1[:, 4 * hf:4 * hf + 4, :].rearrange("p a b -> p (a b)"),
                             start=True, stop=False, inc=False)
                        P.mm(R[hf][:, :], lhsT=C["ident_f"][:], rhs=rhs2[:, 4 * hf:4 * hf + 4, :].rearrange("p a b -> p (a b)"),
                             start=False, stop=True)
                        P.act(Lg[:, 4 * hf:4 * hf + 4, :].rearrange("p a b -> p (a b)"), R[hf][:, :], AF.Exp)
                    P.tt("dve" if g % 2 else "pool", MTg[:], Lg[:], cbT[:, g, :].unsqueeze(1).to_broadcast([128, 8, 128]), ALU.mult)
                    for hh in range(8):
                        P.mm(Yp[:, hh * 64:(hh + 1) * 64], lhsT=MTg[:, hh, :], rhs=xs_tm[:, (8 * g + hh) * 64:(8 * g + hh + 1) * 64],
                             start=(hh == 0), stop=True, inc=(hh == 7), sgc=True)
                    P.tt("pool", tmp[:].rearrange("p (h e) -> p h e", e=64), xs_tm[:, gsl].rearrange("p (h e) -> p h e", e=64),
                         D_b[:, hs].unsqueeze(2).to_broadcast([128, 8, 64]), ALU.mult)
                    P.tt("dve", y[:, gsl], Yp[:, :], tmp[:], ALU.add)

            def part2(c):
                p = c % 2
                csl = slice(c * 128, (c + 1) * 128)
                xa_ = xab[p]
                zs_ = zsb[p]
                sm = {k: v[:, c, :] for k, v in sma.items()}
                bm_tm, xs_w, y = bm_b[p], xsw_b[p], y_b[p]
                for g in range(4):
                    hs = slice(8 * g, 8 * g + 8)
                    gsl = slice(g * 512, (g + 1) * 512)
                    tmp = tmp_b[g % 2]
                    P.mm(Yo[:, :], lhsT=xa_[:, 20 + g, :], rhs=Hbf[g][:])
                    P.tt("dve", tmp[:].rearrange("p (h e) -> p h e", e=64), Yo[:, :].rearrange("p (h e) -> p h e", e=64),
                         sm["ea"][:, hs].unsqueeze(2).to_broadcast([128, 8, 64]), ALU.mult)
                    P.tt("pool", y[:, gsl], y[:, gsl], tmp[:], ALU.add)
                    P.mm(STp[:, :], lhsT=bm_tm[:, g * 128:(g + 1) * 128], rhs=xs_w[:, gsl])
                    Hv = H[g][:].rearrange("p (h e) -> p h e", e=64)
                    P.tt("pool", Hv, Hv, sm["dec"][:, hs].unsqueeze(2).to_broadcast([128, 8, 64]), ALU.mult)
                    P.tt("dve", H[g][:], STp[:, :], H[g][:], ALU.add)
                    P.cp("act", Hbf[g][:], H[g][:])
                P.tt("dve", y[:], y[:], zs_[:], ALU.mult)
                P.act(zs_[:], y[:], AF.Square, accum=ss[:])
                P.act(ss[:], ss[:], AF.Sqrt, bias=EPS, scale=1.0 / 2048)
                P.recip(ss[:], ss[:])
                P.stt("dve", ynb[:], y[:], ss[:, 0:1], ng_b[:], ALU.mult, ALU.mult)
                for k0 in (0, 8):
                    for j in range(8):
                        P.tr(tb[:, j, :], ynb[:, (k0 + j) * 128:(k0 + j + 1) * 128], C["ident"][:], inc=(j == 7))
                    P.cp("act", ycT[:, k0:k0 + 8, :], tb[:])
                P.dma(yc_d[:, :, csl], ycT[:])

            if nch > 0:
                load(0)
                if nch > 1:
                    load(1)
                part1(0)
            for c in range(nch):
                if P.sems_left() < 40:
                    P.barrier()
                if c + 1 < nch:
                    part1(c + 1)
                part2(c)
                if c + 2 < nch:
                    load(c + 2)

    def phase_merge(self, l, hT, ya_d, yb_d, yc_d, mix_d):
        P, C = self.P, self.C
        I = self.inp
        with P.scope() as st:
            yh = [P.sb("yha", [128, 8, 1024], BF16, st), P.sb("yhb", [128, 8, 1024], BF16, st),
                  P.sb("yhc", [128, 16, 1024], BF16, st)]
            Wy = [[P.sb("Wya", [128, 8, 128], BF16, st), P.sb("Wyb", [128, 8, 128], BF16, st),
                   P.sb("Wyc", [128, 16, 128], BF16, st)] for _ in range(2)]
            Wg = [[P.sb("Wg", [128, 8, 128], BF16, st) for _ in range(3)] for _ in range(2)]
            sg = [P.sb("sg", [128, 512], F32, st) for _ in range(2)]
            macc = P.sb("macc", [128, 512], F32, st)
            mt = P.sb("mt", [128, 512], F32, st)
            mixh = P.sb("mixh", [128, 8, 1024], BF16, st)
            bank = [P.ps("bk", [128, 512], F32, st) for _ in range(6)]
            wsrc = [I["w_br_dsa"], I["w_br_nsa"], I["w_br_ssd"]]
            ysrc = [ya_d, yb_d, yc_d]
            bi = 0
            gi = 0
            it = 0
            for half in range(2):
                hsl = slice(half * 1024, (half + 1) * 1024)
                for br in range(3):
                    P.dma(yh[br][:], ysrc[br][:, :, hsl])
                for oc in range(8):
                    Wy_ = Wy[it % 2]
                    Wg_ = Wg[it % 2]
                    it += 1
                    for br in range(3):
                        self.wload(Wy_[br], wsrc[br], l, oc * 128, 128)
                        self.wload(Wg_[br], I["w_in"], l, OFF["mg"] + br * 1024 + oc * 128, 128)
                    for pc in range(2):
                        psl = slice(pc * 512, (pc + 1) * 512)
                        tok = half * 1024 + pc * 512
                        for br in range(3):
                            psy = bank[bi % 6]
                            bi += 1
                            psg = bank[bi % 6]
                            bi += 1
                            nk = 16 if br == 2 else 8
                            for kc in range(nk):
                                P.mm(psy[:, :], lhsT=Wy_[br][:, kc, :], rhs=yh[br][:, kc, psl],
                                     start=(kc == 0), stop=(kc == nk - 1))
                            self.proj_fm(psg, Wg_[br], 0, 128, hT, tok, 512)
                            s_ = sg[gi % 2]
                            gi += 1
                            P.act(s_[:], psg[:, :], AF.Sigmoid)
                            if br == 0:
                                P.tt("dve", macc[:], psy[:, :], s_[:], ALU.mult)
                            else:
                                P.tt("dve", mt[:], psy[:, :], s_[:], ALU.mult)
                                if br == 1:
                                    P.tt("pool", macc[:], macc[:], mt[:], ALU.add)
                                else:
                                    P.tt("pool", mixh[:, oc, psl], macc[:], mt[:], ALU.add)
                    P.maybe_barrier()
                P.dma(mix_d[:, :, hsl], mixh[:])

    def phase_out_norm(self, l, x_d, mix_d, x1_d, hT):
        P, C = self.P, self.C
        I = self.inp
        with P.scope() as st:
            Wo = P.sb("Wo", [128, 8, 1024], BF16, st)
            self.wload(Wo, I["w_out"], l, 0, 1024)
            gb = P.sb("gb2", [128, D], F32, st)
            P.dma(gb[:], I["norm2_g"][l].partition_broadcast(128))
            mq = [P.sb("mq", [128, 8, 128], BF16, st) for _ in range(2)]
            xb = [P.sb("xb", [128, D], F32, st) for _ in range(2)]
            x1b = [P.sb("x1b", [128, D], F32, st) for _ in range(2)]
            sq = P.sb("sq", [128, D], F32, st)
            hb = [P.sb("hb", [128, D], BF16, st) for _ in range(2)]
            ss = P.sb("ss", [128, 2], F32, st)
            bank = [P.ps("bk", [128, 512], F32, st) for _ in range(4)]
            pt = [P.ps("pt", [128, 8, 128], BF16, st) for _ in range(2)]
            for tt in range(NT):
                tsl = slice(tt * 128, (tt + 1) * 128)
                m_ = mq[tt % 2]
                P.dma(m_[:], mix_d[:, :, tsl])
                x_t = xb[tt % 2]
                P.dma(x_t[:], x_d[tsl, :])
                x1 = x1b[tt % 2]
                for hf in range(2):
                    ps = bank[(2 * tt + hf) % 4]
                    for oc in range(8):
                        P.mm(ps[:, :], lhsT=m_[:, oc, :], rhs=Wo[:, oc, hf * 512:(hf + 1) * 512],
                             start=(oc == 0), stop=(oc == 7))
                    P.tt("dve", x1[:, hf * 512:(hf + 1) * 512], ps[:, :], x_t[:, hf * 512:(hf + 1) * 512], ALU.add)
                P.dma(x1_d[tsl, :], x1[:])
                s1 = ss[:, tt % 2:tt % 2 + 1]
                P.act(sq[:], x1[:], AF.Square, accum=s1)
                P.act(s1, s1, AF.Sqrt, bias=EPS, scale=1.0 / D)
                P.recip(s1, s1)
                h_t = hb[tt % 2]
                P.stt("dve", h_t[:], x1[:], s1, gb[:], ALU.mult, ALU.mult)
                p_t = pt[tt % 2]
                for kc in range(8):
                    P.tr(p_t[:, kc, :], h_t[:, kc * 128:(kc + 1) * 128], C["ident"][:], inc=(kc == 7))
                P.cp("act", hT[:, :, tsl], p_t[:])

    def phase_ffn(self, l, hT, x1_d, xo_d, xp_d):
        P, C = self.P, self.C
        I = self.inp
        with P.scope() as st:
            W1h = P.sb("W1h", [128, 8, 2048], BF16, st)
            W2h = P.sb("W2h", [128, 16, 1024], BF16, st)
            aT = P.sb("aT", [128, 16, 512], BF16, st)
            rb = [P.sb("rbf", [128, 512], F32, st) for _ in range(2)]
            xt = [P.sb("xt", [128, 512], F32, st) for _ in range(4)]
            xo = [P.sb("xo", [128, 512], F32, st) for _ in range(4)]
            fb = [P.ps("fb", [128, 512], F32, st) for _ in range(2)]
            ob = [P.ps("ob", [128, 512], F32, st) for _ in range(4)]
            xi = 0
            for hp in range(2):
                self.wload(W1h, I["w_ff1"], l, hp * 2048, 2048)
                self.wload(W2h, I["w_ff2"], l, 0, 1024, rows=(hp * 2048, (hp + 1) * 2048))
                src_d = x1_d if hp == 0 else xp_d
                dst_d = xp_d if hp == 0 else xo_d
                for Q in range(4):
                    for fc in range(16):
                        ps = fb[fc % 2]
                        self.proj_fm(ps, W1h, fc * 128, 128, hT, Q * 512, 512)
                        r = rb[fc % 2]
                        P.act(r[:], ps[:, :], AF.Relu)
                        P.tt("dve", aT[:, fc, :], r[:], r[:], ALU.mult)
                    for hf in range(2):
                        cols = slice(hf * 512, (hf + 1) * 512)
                        xs_ = []
                        for tl in range(4):
                            rows = slice((Q * 4 + tl) * 128, (Q * 4 + tl + 1) * 128)
                            x_ = xt[xi % 4]
                            o_ = xo[xi % 4]
                            xi += 1
                            P.dma(x_[:], src_d[rows, cols])
                            xs_.append((x_, o_, rows))
                            for k in range(16):
                                P.mm(ob[tl][:, :], lhsT=aT[:, k, tl * 128:(tl + 1) * 128], rhs=W2h[:, k, cols],
                                     start=(k == 0), stop=(k == 15))
                        for tl in range(4):
                            x_, o_, rows = xs_[tl]
                            P.tt("dve", o_[:], ob[tl][:, :], x_[:], ALU.add)
                            P.dma(dst_d[rows, cols], o_[:])
                    P.maybe_barrier()

    def build(self):
        P = self.P
        I = self.inp
        self.din("x", [S, D])
        for name, shp in (("norm1_g", [DEPTH, D]), ("w_in", [DEPTH, D, D_IN]), ("dsa_q_norm", [DEPTH, 64]),
                          ("dsa_k_norm", [DEPTH, 64]), ("nsa_q_norm", [DEPTH, 64]), ("nsa_k_norm", [DEPTH, 3, 64]),
                          ("nsa_cmp_pos", [DEPTH, 2, 32, 64]), ("nsa_cmp_w", [DEPTH, 2, 32, 64, 64]),
                          ("ssd_conv_w", [DEPTH, 4, 3072]), ("ssd_conv_b", [DEPTH, 3072]),
                          ("ssd_dt_bias", [DEPTH, 32]), ("ssd_a_log", [DEPTH, 32]), ("ssd_d", [DEPTH, 32]),
                          ("ssd_norm_g", [DEPTH, 2048]), ("w_br_dsa", [DEPTH, D, D]), ("w_br_nsa", [DEPTH, D, D]),
                          ("w_br_ssd", [DEPTH, 2 * D, D]), ("w_out", [DEPTH, D, D]), ("norm2_g", [DEPTH, D]),
                          ("w_ff1", [DEPTH, D, 4 * D]), ("w_ff2", [DEPTH, 4 * D, D])):
            self.din(name, shp)
        self.load_consts()
        out = self.dout("out", [S, D])
        hT = P.sb("hT", [128, 8, S], BF16)
        ya_d = P.dram("ya", [128, 8, S], BF16)
        yb_d = P.dram("yb", [128, 8, S], BF16)
        yc_d = P.dram("yc", [128, 16, S], BF16)
        zs_d = P.dram("zs", [S, 2048], F32)
        xact_d = P.dram("xact", [128, 24, S], BF16)
        mix_d = P.dram("mix", [128, 8, S], BF16)
        x1_d = P.dram("x1", [S, D], F32)
        xm_d = P.dram("xm", [S, D], F32)
        xp_d = P.dram("xp", [S, D], F32)
        dbg = self.debug
        x_cur = I["x"]
        for l in range(self.nlayers):
            x_nxt = out if l == self.nlayers - 1 else xm_d
            self.phase_norm(x_cur, I["norm1_g"][l], hT)
            if dbg == "hT":
                return self._dump(hT[:], [128, 8, S], BF16)
            if dbg in ("nsa", "nsa_k"):
                self.phase_nsa(l, hT, yb_d)
                return self._dump(yb_d, [128, 8, S], BF16)
            if dbg == "ssd":
                self.phase_ssd(l, hT, yc_d, zs_d, xact_d)
                return self._dump(yc_d, [128, 16, S], BF16)
            self.phase_dsa(l, hT, ya_d)
            if dbg is not None and dbg.startswith("dsa"):
                return self._dump(ya_d, [128, 8, S], BF16)
            self.phase_nsa(l, hT, yb_d)
            self.phase_ssd(l, hT, yc_d, zs_d, xact_d)
            self.phase_merge(l, hT, ya_d, yb_d, yc_d, mix_d)
            if dbg == "mix":
                return self._dump(mix_d, [128, 8, S], BF16)
            self.phase_out_norm(l, x_cur, mix_d, x1_d, hT)
            if dbg == "x1":
                return self._dump(x1_d, [S, D], F32)
            self.phase_ffn(l, hT, x1_d, x_nxt, xp_d)
            x_cur = x_nxt
        P.finish()

    def _dump(self, src, shape, dt):
        if "dbg" not in self.out:
            o = self.dout("dbg", shape, dt)
            self.P.dma(o, src)
        self.P.finish()


_CACHE = {}


def kernel(**inputs):
    n = 8
    B = Builder()
    B.build()
    consts = _consts()
    shared = {}
    for name in B.inp:
        if name in consts:
            shared[name] = consts[name]
        elif name != "x":
            shared[name] = np.ascontiguousarray(np.asarray(inputs[name], dtype=np.float32))
    x = np.asarray(inputs["x"], dtype=np.float32)
    in_maps = []
    for b in range(n):
        m = dict(shared)
        m["x"] = np.ascontiguousarray(x[b])
        in_maps.append(m)
    res = run_bass_kernel_spmd(B.nc, in_maps, core_ids=list(range(n)))
    return np.stack([np.asarray(res.results[b]["out"], dtype=np.float32) for b in range(n)], axis=0)
```

```python
import contextlib
import os
import numpy as np
import concourse.bass as bass
import concourse.mybir as mybir
from concourse.bass_utils import run_bass_kernel_spmd

F32 = mybir.dt.float32
BF16 = mybir.dt.bfloat16
AF = mybir.ActivationFunctionType
ALU = mybir.AluOpType
AX = mybir.AxisListType

EPOCH = 240
NPOOL = 90

D = 1024
S = 2048
NT = 16
DEPTH = 2
EPS = 1e-6
OFF = dict(dq=0, dk=1024, dv=1088, iq=1152, ik=1408, iw=1440, nq=1448, nkv=2472, ng=3240,
           sz=3288, sxbc=5336, sdt=8408, mg=8440)
D_IN = 11512
NEGM = -30000.0


def _region(ap):
    t = ap.tensor
    name = t.name
    dims = [(int(s), int(c)) for s, c in ap.ap]
    off = int(ap.offset)
    if "DRam" in type(t).__name__:
        lo = hi = off
        for s, c in dims:
            if s >= 0:
                hi += s * (c - 1)
            else:
                lo += s * (c - 1)
        return (name, 0, 1, lo, hi + 1)
    if "PSum" in type(t).__name__:
        return (name, 0, 128, 0, 1 << 30)
    ps, pc = dims[0]
    if ps == 0:
        p0, f0 = 0, off
    else:
        p0 = off // ps
        f0 = off - p0 * ps
    lo = hi = f0
    for s, c in dims[1:]:
        if s >= 0:
            hi += s * (c - 1)
        else:
            lo += s * (c - 1)
    return (name, p0, p0 + pc, lo, hi + 1)


def _overlap(a, b):
    return a[1] < b[2] and b[1] < a[2] and a[3] < b[4] and b[3] < a[4]


def _contains(a, b):
    return a[1] <= b[1] and b[2] <= a[2] and a[3] <= b[3] and b[4] <= a[4]


class Prog:
    ENGS = ("pe", "act", "dve", "pool", "sp")

    def __init__(self, nc):
        self.nc = nc
        self.stack = contextlib.ExitStack()
        self.eng = {"pe": nc.tensor, "act": nc.scalar, "dve": nc.vector, "pool": nc.gpsimd, "sp": nc.sync}
        self.pool = [self.stack.enter_context(nc.semaphore(f"sp{i}")) for i in range(NPOOL)]
        self.bsem = [self.stack.enter_context(nc.semaphore(f"bar{i}")) for i in range(4)]
        self.nbar = 0
        self.uid = 0
        self.n_wait = 0
        self.n_ins = 0
        self.max_used = 0
        self._reset()

    def _reset(self):
        self.free = list(range(NPOOL))
        self.esem = {e: None for e in self.ENGS}
        self.ecnt = {e: 0 for e in self.ENGS}
        self.pend = {e: False for e in self.ENGS}
        self.seen = {e: {} for e in self.ENGS}
        self.dset = []
        self.dall = {}
        self.dnext = 0
        self.trk = {}

    def _alloc(self):
        assert self.free, "semaphore pool exhausted; add a barrier"
        i = self.free.pop()
        self.max_used = max(self.max_used, NPOOL - len(self.free))
        return i

    def sems_left(self):
        return len(self.free)

    def sb(self, name, shape, dtype, stack=None):
        self.uid += 1
        return (stack or self.stack).enter_context(
            self.nc.sbuf_tensor(f"{name}_{self.uid}", list(shape), dtype))

    def ps(self, name, shape, dtype, stack=None):
        self.uid += 1
        return (stack or self.stack).enter_context(
            self.nc.psum_tensor(f"{name}_{self.uid}", list(shape), dtype))

    def dram(self, name, shape, dtype):
        self.uid += 1
        return self.nc.dram_tensor(f"{name}_{self.uid}", list(shape), dtype).ap()

    @contextlib.contextmanager
    def scope(self):
        st = contextlib.ExitStack()
        try:
            yield st
        finally:
            self.barrier()
            st.close()

    def _deps(self, reads, writes):
        deps = []
        for ap in reads:
            r = _region(ap)
            t = self.trk.get(r[0])
            if t is None:
                continue
            for (wr, ev) in t["w"]:
                if _overlap(wr, r):
                    deps.append(ev)
        for ap in writes:
            r = _region(ap)
            t = self.trk.get(r[0])
            if t is None:
                continue
            for (wr, ev) in t["w"]:
                if _overlap(wr, r):
                    deps.append(ev)
            for (sk, rr), val in t["r"].items():
                if _overlap(rr, r):
                    deps.append((sk, val))
        return deps

    def _record(self, reads, writes, ev):
        for ap in reads:
            r = _region(ap)
            t = self.trk.setdefault(r[0], {"w": [], "r": {}})
            k = (ev[0], r)
            if t["r"].get(k, -1) < ev[1]:
                t["r"][k] = ev[1]
        for ap in writes:
            r = _region(ap)
            t = self.trk.setdefault(r[0], {"w": [], "r": {}})
            t["w"] = [(wr, e) for (wr, e) in t["w"] if not _contains(r, wr)]
            t["w"].append((r, ev))
            t["r"] = {k: v for k, v in t["r"].items() if not _contains(r, k[1])}

    def _wait(self, e, deps):
        eng = self.eng[e]
        need = {}
        for sk, val in deps:
            if self.seen[e].get(sk, 0) >= val:
                continue
            if need.get(sk, 0) < val:
                need[sk] = val
        for sk, val in need.items():
            assert 0 < val <= 255
            eng.wait_ge(self.pool[sk[1]], val)
            self.seen[e][sk] = val
            self.n_wait += 1

    def _next_ev(self, e, commit):
        if self.esem[e] is None or self.ecnt[e] >= EPOCH:
            self.esem[e] = self._alloc()
            self.ecnt[e] = 0
        ev = ((e, self.esem[e]), self.ecnt[e] + 1)
        if commit:
            self.ecnt[e] += 1
        return ev

    def op(self, e, fn, reads, writes, inc=True):
        deps = self._deps(reads, writes)
        if e == "pe":
            deps = [d for d in deps if d[0][0] != "pe"]
        self._wait(e, deps)
        ins = fn(self.eng[e])
        self.n_ins += 1
        ev = self._next_ev(e, inc)
        if inc:
            ins.then_inc(self.pool[ev[0][1]], 1)
            self.pend[e] = False
        else:
            self.pend[e] = True
        self._record(reads, writes, ev)
        return ins

    def dma(self, out, in_, q="sp", **kw):
        deps = self._deps([in_], [out])
        if q == "pool":
            ent = [self._alloc(), 0]
        else:
            if len(self.dset) < 8:
                self.dset.append([self._alloc(), 0])
            k = self.dnext % len(self.dset)
            self.dnext += 1
            if self.dset[k][1] >= 15:
                self.dset[k] = [self._alloc(), 0]
            ent = self.dset[k]
        if ent[1] > 0:
            deps.append((("d", ent[0]), 16 * ent[1]))
        self._wait(q, deps)
        ins = self.eng[q].dma_start(out=out, in_=in_, **kw)
        ins.then_inc(self.pool[ent[0]], 16)
        ent[1] += 1
        self.dall[ent[0]] = ent[1]
        self.n_ins += 1
        self._record([in_], [out], (("d", ent[0]), 16 * ent[1]))
        return ins

    def barrier(self):
        evs = []
        for e in self.ENGS:
            assert not self.pend[e], f"pending non-inc op on {e} at barrier"
            if self.esem[e] is not None and self.ecnt[e] > 0:
                evs.append(((e, self.esem[e]), self.ecnt[e]))
        for k, c in self.dall.items():
            evs.append((("d", k), 16 * c))
        for e in self.ENGS:
            self._wait(e, evs)
        b0 = self.bsem[2 * (self.nbar % 2)]
        b1 = self.bsem[2 * (self.nbar % 2) + 1]
        p0 = self.bsem[2 * ((self.nbar + 1) % 2)]
        p1 = self.bsem[2 * ((self.nbar + 1) % 2) + 1]
        for e in self.ENGS:
            if e != "sp":
                self.eng[e].sem_inc(b0, 1)
        sp = self.eng["sp"]
        sp.wait_ge(b0, 4)
        used = [i for i in range(NPOOL) if i not in set(self.free)]
        for i in used:
            sp.sem_clear(self.pool[i])
        sp.sem_clear(p0)
        sp.sem_clear(p1)
        sp.sem_inc(b1, 1)
        for e in self.ENGS:
            if e != "sp":
                self.eng[e].wait_ge(b1, 1)
        self.nbar += 1
        self._reset()

    def maybe_barrier(self, min_free=35):
        if len(self.free) < min_free:
            self.barrier()

    def finish(self):
        self.barrier()
        self.stack.close()

    def mm(self, out, lhsT, rhs, start=True, stop=True, inc=None, sgc=False):
        if inc is None:
            inc = stop
        return self.op("pe", lambda e: e.matmul(out, lhsT=lhsT, rhs=rhs, start=start, stop=stop,
                                                skip_group_check=sgc),
                       [lhsT, rhs], [out], inc=inc)

    def tr(self, out, in_, ident, inc=True):
        return self.op("pe", lambda e: e.transpose(out=out, in_=in_, identity=ident), [in_, ident], [out], inc=inc)

    def act(self, out, in_, func, bias=None, scale=None, accum=None):
        kw = {}
        reads = [in_]
        writes = [out]
        if bias is not None:
            kw["bias"] = bias
            if not isinstance(bias, (int, float)):
                reads.append(bias)
        if scale is not None:
            kw["scale"] = scale
            if not isinstance(scale, (int, float)):
                reads.append(scale)
        if accum is not None:
            kw["accum_out"] = accum
            writes.append(accum)
        return self.op("act", lambda e: e.activation(out=out, in_=in_, func=func, **kw), reads, writes)

    def cp(self, eng, out, in_):
        if eng == "act":
            return self.op("act", lambda e: e.copy(out=out, in_=in_), [in_], [out])
        return self.op(eng, lambda e: e.tensor_copy(out=out, in_=in_), [in_], [out])

    def tt(self, eng, out, in0, in1, op):
        return self.op(eng, lambda e: e.tensor_tensor(out=out, in0=in0, in1=in1, op=op), [in0, in1], [out])

    def ts(self, eng, out, in0, s1, s2=None, op0=ALU.mult, op1=None, accum=None):
        reads = [in0] + [s for s in (s1, s2) if s is not None and not isinstance(s, (int, float))]
        writes = [out] + ([accum] if accum is not None else [])
        kw = {}
        if op1 is not None:
            kw["op1"] = op1
        if accum is not None:
            kw["accum_out"] = accum
        return self.op(eng, lambda e: e.tensor_scalar(out=out, in0=in0, scalar1=s1, scalar2=s2, op0=op0, **kw),
                       reads, writes)

    def stt(self, eng, out, in0, scalar, in1, op0, op1):
        reads = [in0, in1] + ([scalar] if not isinstance(scalar, (int, float)) else [])
        return self.op(eng, lambda e: e.scalar_tensor_tensor(out=out, in0=in0, scalar=scalar, in1=in1,
                                                             op0=op0, op1=op1), reads, [out])

    def amul(self, out, in_, val):
        return self.op("act", lambda e: e.mul(out=out, in_=in_, mul=val), [in_], [out])

    def memset(self, eng, ap, val):
        return self.op(eng, lambda e: e.memset(ap, val), [], [ap])

    def recip(self, out, in_):
        return self.op("dve", lambda e: e.reciprocal(out=out, in_=in_), [in_], [out])


def _consts():
    c = {}
    p = np.arange(128)
    c["c_ident"] = np.eye(128, dtype=np.float32)
    c["c_ones"] = np.ones((128, 128), np.float32)
    c["c_blk64"] = (p[:, None] // 64 == p[None, :] // 64).astype(np.float32)
    c["c_tri"] = (p[:, None] <= p[None, :]).astype(np.float32)
    c["c_cneg"] = np.where(p[:, None] > p[None, :], NEGM, 0.0).astype(np.float32)
    c["c_wneg"] = np.where(p[:, None] <= p[None, :], NEGM, 0.0).astype(np.float32)
    c["c_cnegtm"] = np.where(p[None, :] > p[:, None], -3e30, 0.0).astype(np.float32)
    s = np.arange(S)
    c["c_E"] = (s[None, :] // 64 == np.arange(32)[:, None]).astype(np.float32)
    n = np.arange(128)
    vis = (16 * n[:, None] + 31 <= s[None, :])
    c["c_visneg"] = np.where(vis, 0.0, NEGM).astype(np.float32)
    j = np.arange(32)
    cover = ((16 * n[:, None] < 64 * j[None, :] + 64) & (16 * n[:, None] + 32 > 64 * j[None, :]))
    ce = np.zeros((128, 33), np.float32)
    ce[:, 0] = 1.0
    ce[:, 1:] = cover
    ce[127] = 0.0
    c["c_cover"] = ce
    t = (np.arange(NT)[None, :, None] * 128 + p[:, None, None])
    cur = t // 64
    jj = j[None, None, :]
    forced = (jj == cur) | (jj == 0)
    future = jj > cur
    c["c_selkeep"] = (~(forced | future)).astype(np.float32)
    c["c_selbias"] = np.where(forced, 1e9, np.where(future, -1e9, 0.0)).astype(np.float32)
    return c


class Builder:
    def __init__(self, debug=None, nlayers=DEPTH):
        self.debug = debug
        self.nlayers = nlayers
        import os
        self.lim = int(os.environ.get("KLIM", "16"))
        nc = bass.Bass("TRN2", target_bir_lowering=False)
        self.nc = nc
        self.P = Prog(nc)
        self.inp = {}
        self.out = {}

    def din(self, name, shape, dt=F32):
        self.inp[name] = self.nc.dram_tensor(name, list(shape), dt, kind="ExternalInput").ap()
        return self.inp[name]

    def dout(self, name, shape, dt=F32):
        self.out[name] = self.nc.dram_tensor(name, list(shape), dt, kind="ExternalOutput").ap()
        return self.out[name]

    def load_consts(self):
        P = self.P
        C = {}
        shapes = {k: v.shape for k, v in _consts().items()}
        for k, shp in shapes.items():
            self.din(k, shp)
        self.wstage = [P.sb("wstage", [128, 4096], F32) for _ in range(2)]
        self.wsi = 0
        self.cast_engs = ["pool"]

        def ld(name, key, shape, dt, rows=None):
            t = P.sb(name, shape, dt)
            src = self.inp[key]
            if dt == F32:
                P.dma(t[:], src)
            else:
                n = int(np.prod(shape[1:]))
                flat = "p a b -> p (a b)" if len(shape) == 3 else None
                for a0 in range(0, n, 4096):
                    b0 = min(n, a0 + 4096)
                    stg = self.wstage[self.wsi % 2]
                    self.wsi += 1
                    P.dma(stg[0:shape[0], 0:b0 - a0], src[:, a0:b0])
                    P.cp("pool", t[:, a0:b0], stg[0:shape[0], 0:b0 - a0])
            return t
        C["ident_f"] = ld("ident_f", "c_ident", [128, 128], F32)
        C["ident"] = ld("ident", "c_ident", [128, 128], BF16)
        C["ones_f"] = ld("ones_f", "c_ones", [128, 128], F32)
        C["blk64_f"] = ld("blk64_f", "c_blk64", [128, 128], F32)
        C["tri_f"] = ld("tri_f", "c_tri", [128, 128], F32)
        C["cneg"] = ld("cneg", "c_cneg", [128, 128], BF16)
        C["zero"] = P.sb("zero", [128, 128], BF16)
        P.memset("pool", C["zero"][:], 0.0)
        C["cneg_f"] = ld("cneg_f", "c_cneg", [128, 128], F32)
        C["wneg"] = ld("wneg", "c_wneg", [128, 128], BF16)
        C["cnegtm"] = ld("cnegtm", "c_cnegtm", [128, 128], F32)
        C["E"] = ld("E", "c_E", [32, S], BF16)
        C["visneg"] = ld("visneg", "c_visneg", [128, S], BF16)
        C["cover"] = ld("cover", "c_cover", [128, 33], BF16)
        C["selkeep"] = ld("selkeep", "c_selkeep", [128, NT, 32], F32)
        C["selbias"] = ld("selbias", "c_selbias", [128, NT, 32], F32)
        self.C = C

    def wload(self, dst, wdram, l, c0, n, rows=None):
        w2 = wdram[l] if rows is None else wdram[l][rows[0]:rows[1]]
        src = w2.rearrange("(kc p) n -> p kc n", p=128)[:, :, c0:c0 + n]
        nk = src.shape[1]
        step = max(1, 4096 // nk)
        for a in range(0, n, step):
            b = min(n, a + step)
            stg = self.wstage[self.wsi % 2]
            self.wsi += 1
            v = stg[:, 0:nk * (b - a)].rearrange("p (k n) -> p k n", k=nk)
            self.P.dma(v, src[:, :, a:b], q="sp")
            eng = self.cast_engs[self.wsi % len(self.cast_engs)]
            self.P.cp(eng, dst[:, :, a:b], v)

    def pipelined(self, items, Sb, pTb, cnt):
        P = self.P
        n = len(items)
        slots = []
        for i in range(n + 1):
            if i < n:
                k = cnt["si"] % 2
                cnt["si"] += 1
                Sp, pT = Sb[k], pTb[k]
                items[i][0](Sp)
                P.act(pT[:], Sp[:, :], AF.Exp, scale=0.125)
                slots.append(pT)
            if i >= 1:
                items[i - 1][1](slots[i - 1])

    def proj_fm(self, ps, Wt, c0, M, hT, t0, N):
        for kc in range(8):
            self.P.mm(ps[0:M, 0:N], lhsT=Wt[:, kc, c0:c0 + M], rhs=hT[:, kc, t0:t0 + N],
                      start=(kc == 0), stop=(kc == 7))

    def proj_tm(self, ps, Wt, c0, n, hT, tile):
        for kc in range(8):
            self.P.mm(ps[:, 0:n], lhsT=hT[:, kc, tile * 128:(tile + 1) * 128], rhs=Wt[:, kc, c0:c0 + n],
                      start=(kc == 0), stop=(kc == 7))

    def norm_fm(self, ps, ps2, M, N, gcol, out, sq, rs):
        P, C = self.P, self.C
        P.act(sq[0:M, 0:N], ps[0:M, 0:N], AF.Square)
        P.mm(ps2[0:M, 0:N], lhsT=C["blk64_f"][0:M, 0:M], rhs=sq[0:M, 0:N])
        P.act(rs[0:M, 0:N], ps2[0:M, 0:N], AF.Sqrt, bias=EPS, scale=1.0 / 64)
        P.recip(rs[0:M, 0:N], rs[0:M, 0:N])
        P.stt("dve", out, ps[0:M, 0:N], gcol, rs[0:M, 0:N], ALU.mult, ALU.mult)

    def gain_col(self, name, vec64, stack):
        t = self.P.sb(name, [128, 1], F32, stack)
        src = vec64.rearrange("(p o) -> p o", o=1)
        self.P.dma(t[0:64, :], src)
        self.P.dma(t[64:128, :], src)
        return t

    def phase_norm(self, xd, gvec, hT, store_T=None):
        P, C = self.P, self.C
        with P.scope() as st:
            gb = P.sb("gb", [128, D], F32, st)
            P.dma(gb[:], gvec.partition_broadcast(128))
            xb = [P.sb("xb", [128, D], F32, st) for _ in range(2)]
            sq = P.sb("sq", [128, D], F32, st)
            hb = [P.sb("hb", [128, D], BF16, st) for _ in range(2)]
            ss = P.sb("ss", [128, 2], F32, st)
            pt = [P.ps("pt", [128, 8, 128], BF16, st) for _ in range(2)]
            for tt in range(NT):
                x_t = xb[tt % 2]
                P.dma(x_t[:], xd[tt * 128:(tt + 1) * 128, :])
                s1 = ss[:, tt % 2:tt % 2 + 1]
                P.act(sq[:], x_t[:], AF.Square, accum=s1)
                P.act(s1, s1, AF.Sqrt, bias=EPS, scale=1.0 / D)
                P.recip(s1, s1)
                h_t = hb[tt % 2]
                P.stt("dve", h_t[:], x_t[:], s1, gb[:], ALU.mult, ALU.mult)
                p_t = pt[tt % 2]
                for kc in range(8):
                    P.tr(p_t[:, kc, :], h_t[:, kc * 128:(kc + 1) * 128], C["ident"][:], inc=(kc == 7))
                P.cp("act" if tt % 2 else "dve", hT[:, :, tt * 128:(tt + 1) * 128], p_t[:])

    def phase_dsa(self, l, hT, ya_d):
        P, C = self.P, self.C
        I = self.inp
        self.cast_engs = ["pool", "act"]
        w_in = I["w_in"]
        with P.scope() as st:
            Wq = P.sb("Wq", [128, 8, 1024], BF16, st)
            self.wload(Wq, w_in, l, OFF["dq"], 1024)
            Wk = P.sb("Wk", [128, 8, 128], BF16, st)
            self.wload(Wk[:, :, 0:64], w_in, l, OFF["dk"], 64)
            self.wload(Wk[:, :, 64:128], w_in, l, OFF["dk"], 64)
            Wv = P.sb("Wv", [128, 8, 64], BF16, st)
            self.wload(Wv, w_in, l, OFF["dv"], 64)
            Wiq = P.sb("Wiq", [128, 8, 256], BF16, st)
            self.wload(Wiq, w_in, l, OFF["iq"], 256)
            Wik = P.sb("Wik", [128, 8, 128], BF16, st)
            for r in range(4):
                self.wload(Wik[:, :, r * 32:(r + 1) * 32], w_in, l, OFF["ik"], 32)
            Wiw = P.sb("Wiw", [128, 8, 8], BF16, st)
            self.wload(Wiw, w_in, l, OFF["iw"], 8)
            gq = self.gain_col("gq", I["dsa_q_norm"][l], st)
            gk = self.gain_col("gk", I["dsa_k_norm"][l], st)

            kT = P.sb("kT", [128, S], BF16, st)
            v1 = P.sb("v1", [128, NT, 65], BF16, st)
            ikbd = P.sb("ikbd", [128, NT, 4, 128], BF16, st)
            qz = P.sb("qz", [128, 16, 512], BF16, st)
            iqT = P.sb("iqT", [128, 2, S], BF16, st)
            iw = P.sb("iw", [128, NT, 8], F32, st)
            acc = P.sb("acc", [128, S], F32, st)
            work = P.sb("work", [128, S], F32, st)
            mx = P.sb("mx", [128, 8], F32, st)
            negm = P.sb("negm", [128, S], BF16, st)
            ikT4 = negm
            negmT = [P.sb("negmT", [128, NT, 128], BF16, st) for _ in range(2)]
            rb = [P.sb("rb", [128, 512], F32, st) for _ in range(2)]
            pTb = [P.sb("pTb", [128, 512], BF16, st) for _ in range(2)]
            ytm = P.sb("ytm", [128, 1024], BF16, st)
            yT = P.sb("yT", [128, 8, 512], BF16, st)
            sq = P.sb("sq", [128, 512], F32, st)
            rs = P.sb("rs", [128, 512], F32, st)
            rc = P.sb("rc", [128, 4], F32, st)
            bank = [P.ps("bk", [128, 512], F32, st) for _ in range(6)]
            tb = [P.ps("tb", [128, 8, 128], BF16, st) for _ in range(2)]
            Sb, Ob, Ib = bank[0:2], bank[2:4], bank[4:6]

            P.memset("pool", v1[:, :, 64:65], 1.0)
            P.memset("pool", ikbd[:], 0.0)
            P.memset("pool", qz[:], 0.0)
            for Q in range(4):
                t0 = Q * 512
                self.proj_fm(Ib[0], Wk, 0, 128, hT, t0, 512)
                self.norm_fm(Ib[0], Ib[1], 128, 512, gk[:, 0:1], kT[:, t0:t0 + 512], sq, rs)
                self.proj_fm(Ib[0], Wik, 0, 128, hT, t0, 512)
                P.cp("act", ikT4[:, t0:t0 + 512], Ib[0][:, :])
                for c in range(2):
                    self.proj_fm(Ib[c], Wiq, c * 128, 128, hT, t0, 512)
                    P.cp("act", iqT[:, c, t0:t0 + 512], Ib[c][:, :])
            for hh in range(4):
                P.dma(ikbd[32 * hh:32 * hh + 32, :, hh, :],
                      ikT4[32 * hh:32 * hh + 32, :].rearrange("p (k s) -> p k s", s=128))
            for tt in range(NT):
                ps = Ib[tt % 2]
                self.proj_tm(ps, Wv, 0, 64, hT, tt)
                P.cp("act", v1[:, tt, 0:64], ps[:, 0:64])
                self.proj_tm(ps, Wiw, 64, 8, hT, tt) if False else None
            for tt in range(NT):
                ps = Ib[tt % 2]
                self.proj_tm(ps, Wiw, 0, 8, hT, tt)
                P.cp("act", iw[:, tt, :], ps[:, 0:8])

            cnt = {"si": 0, "ii": 0}

            def stage_q(Q):
                t0 = Q * 512
                for c in range(8):
                    self.proj_fm(Ib[0], Wq, c * 128, 128, hT, t0, 512)
                    P.act(sq[:, :], Ib[0][:, :], AF.Square)
                    P.mm(Ib[1][:, :], lhsT=C["blk64_f"][:], rhs=sq[:, :])
                    P.act(rs[:, :], Ib[1][:, :], AF.Sqrt, bias=EPS, scale=1.0 / 64)
                    P.recip(rs[:, :], rs[:, :])
                    for hf in range(2):
                        psl = slice(64 * hf, 64 * hf + 64)
                        P.stt("dve", qz[psl, 2 * c + hf, :], Ib[0][psl, :], gq[psl, 0:1], rs[psl, :], ALU.mult, ALU.mult)

            def stage_a(qt):
                nk = qt + 1
                tsl = slice(qt * 128, (qt + 1) * 128)
                if qt < 2:
                    P.memset("pool", negm[:, 0:nk * 128], 0.0)
                    return
                for kt in range(nk):
                    a_kt = acc[:, kt * 128:(kt + 1) * 128]
                    for c in range(2):
                        ps = Ib[cnt["ii"] % 2]
                        r = rb[cnt["ii"] % 2]
                        cnt["ii"] += 1
                        P.mm(ps[:, :], lhsT=iqT[:, c, tsl], rhs=ikbd[:, kt].rearrange("p a b -> p (a b)"))
                        P.act(r[:], ps[:, :], AF.Relu)
                        for hh in range(4):
                            h = 4 * c + hh
                            if h == 0:
                                P.ts("dve", a_kt, r[:, 0:128], iw[:, qt, 0:1], None, op0=ALU.mult)
                            else:
                                P.stt("dve", a_kt, r[:, hh * 128:(hh + 1) * 128], iw[:, qt, h:h + 1], a_kt,
                                      ALU.mult, ALU.add)
                d_sl = slice(qt * 128, (qt + 1) * 128)
                P.tt("dve", acc[:, d_sl], acc[:, d_sl], C["cnegtm"][:], ALU.add)
                cur = acc
                for r_ in range(32):
                    P.op("dve", lambda e: e.max(out=mx[:], in_=cur[:, 0:nk * 128]),
                         [cur[:, 0:nk * 128]], [mx[:]])
                    P.op("dve", lambda e: e.match_replace(out=work[:, 0:nk * 128], in_to_replace=mx[:],
                                                          in_values=cur[:, 0:nk * 128], imm_value=-1e30),
                         [mx[:], cur[:, 0:nk * 128]], [work[:, 0:nk * 128]])
                    cur = work
                P.ts("dve", negm[:, 0:nk * 128], work[:, 0:nk * 128], -1e29, 1.0, op0=ALU.is_le,
                     op1=ALU.subtract)

            def stage_b(qt):
                nk = qt + 1
                nT = negmT[qt % 2]
                for k0 in range(0, nk, 8):
                    k1 = min(nk, k0 + 8)
                    t_ = tb[0]
                    for kt in range(k0, k1):
                        P.tr(t_[:, kt - k0, :], negm[:, kt * 128:(kt + 1) * 128], C["ident"][:],
                             inc=(kt == k1 - 1))
                    P.amul(nT[:, k0:k1, :], t_[:, 0:k1 - k0, :], -NEGM)
                P.tt("pool", nT[:, qt, :], nT[:, qt, :], C["cneg"][:], ALU.add)

            def stage_c(qt):
                nk = qt + 1
                tl = qt % 4
                tsl = slice(tl * 128, (tl + 1) * 128)
                nT = negmT[qt % 2]
                items = []
                for hg in range(4):
                    O = Ob[hg % 2]
                    O3 = O[:, 0:260].rearrange("p (h e) -> p h e", e=65)
                    for kt in range(nk):
                        def em_s(Sp, hg=hg, kt=kt):
                            P.mm(Sp[:, :], lhsT=C["ident"][:], rhs=nT[:, kt, :].unsqueeze(1).to_broadcast([128, 4, 128]),
                                 start=True, stop=False, inc=False)
                            P.mm(Sp[:, :], lhsT=kT[:, kt * 128:(kt + 1) * 128], rhs=qz[:, 4 * hg:4 * hg + 4, tsl],
                                 start=False, stop=True)

                        def em_pv(pT, hg=hg, kt=kt, O3=O3):
                            for hh in range(4):
                                P.mm(O3[:, hh, :], lhsT=pT[:, hh * 128:(hh + 1) * 128], rhs=v1[:, kt, :],
                                     start=(kt == 0 and hh == 0), stop=(kt == nk - 1), inc=(hh == 3), sgc=True)
                            if kt == nk - 1:
                                P.recip(rc[:], O3[:, :, 64])
                                P.tt("dve", ytm[:, hg * 256:(hg + 1) * 256].rearrange("p (h e) -> p h e", e=64),
                                     O3[:, :, 0:64], rc[:].unsqueeze(2).to_broadcast([128, 4, 64]), ALU.mult)
                        items.append((em_s, em_pv))
                self.pipelined(items, Sb, pTb, cnt)
                t_ = tb[1]
                for c in range(8):
                    P.tr(t_[:, c, :], ytm[:, c * 128:(c + 1) * 128], C["ident"][:], inc=(c == 7))
                P.cp("act", yT[:, :, tsl], t_[:])
                if tl == 3:
                    Q = qt // 4
                    P.dma(ya_d[:, :, Q * 512:(Q + 1) * 512], yT[:], q="sp")

            nq = min(NT, self.lim)
            stage_a(0)
            stage_b(0)
            for qt in range(nq):
                P.maybe_barrier()
                if qt + 1 < nq:
                    stage_a(qt + 1)
                if qt % 4 == 0:
                    stage_q(qt // 4)
                stage_c(qt)
                if qt + 1 < nq:
                    stage_b(qt + 1)

    def _nsa_kside(self, l, hT, st, env):
        P, C = self.P, self.C
        I = self.inp
        Wkv, Wks, Wkw = env["Wkv"], env["Wks"], env["Wkw"]
        gks, gkw, gkc = env["gks"], env["gkw"], env["gkc"]
        ksT, kwT, vs1, vw1, kcmpT, vext = env["ksT"], env["kwT"], env["vs1"], env["vw1"], env["kcmpT"], env["vext"]
        sq, rs, ss, Mb, tb = env["sq"], env["rs"], env["ss"], env["Mb"], env["tb"]
        cmpw = []
        posB = []
        for kv in range(2):
            cw = P.sb("cmpw", [128, 32, 64], BF16, st)
            stg = self.wstage[self.wsi % 2]
            self.wsi += 1
            sv = stg[:, 0:2048].rearrange("p (l e) -> p l e", e=64)
            src = I["nsa_cmp_w"][l, kv].rearrange("l d e -> d l e")
            P.dma(sv[0:64], src)
            P.dma(sv[64:128], src)
            P.cp("pool", cw[:], sv)
            cmpw.append(cw)
            pt_ = P.sb("posT", [128, 32], F32, st)
            psrc = I["nsa_cmp_pos"][l, kv].rearrange("l d -> d l")
            P.dma(pt_[0:64, :], psrc, allow_slow_non_contiguous=True)
            P.dma(pt_[64:128, :], psrc, allow_slow_non_contiguous=True)
            pb = P.sb("posB", [128, 32, 127], BF16, st)
            P.cp("pool", pb[:], pt_[:].unsqueeze(2).to_broadcast([128, 32, 127]))
            posB.append(pb)

        kcT = P.sb("kcT", [128, S], BF16, st)
        vcT = P.sb("vcT", [128, S], BF16, st)
        kd = P.sb("kd", [128, 128], BF16, st)
        P.memset("pool", vs1[:, :, :, 64:65], 1.0)
        P.memset("pool", vw1[:, :, :, 64:65], 1.0)
        P.memset("pool", kd[:], 0.0)
        for g in range(2):
            P.memset("pool", vext[g][:], 0.0)
        for Q in range(4):
            t0 = Q * 512
            self.proj_fm(Mb[0], Wkv, 0, 128, hT, t0, 512)
            P.cp("act", kcT[:, t0:t0 + 512], Mb[0][:, :])
            self.proj_fm(Mb[1], Wkv, 128, 128, hT, t0, 512)
            P.cp("act", vcT[:, t0:t0 + 512], Mb[1][:, :])
            for g in range(2):
                self.proj_fm(Mb[0], Wks, g * 128, 128, hT, t0, 512)
                self.norm_fm(Mb[0], Mb[1], 128, 512, gks[:, 0:1], ksT[g][:, t0:t0 + 512], sq, rs)
                self.proj_fm(Mb[0], Wkw, g * 128, 128, hT, t0, 512)
                self.norm_fm(Mb[0], Mb[1], 128, 512, gkw[:, 0:1], kwT[g][:, t0:t0 + 512], sq, rs)
        for tt in range(NT):
            ps = Mb[tt % 2]
            self.proj_tm(ps, Wkv, 384, 128, hT, tt)
            P.cp("act", vs1[:, tt, :, 0:64], ps[:, 0:128].rearrange("p (g e) -> p g e", e=64))
            self.proj_tm(ps, Wkv, 640, 128, hT, tt)
            P.cp("act", vw1[:, tt, :, 0:64], ps[:, 0:128].rearrange("p (g e) -> p g e", e=64))
        P.maybe_barrier()
        def strided(t, lo, off):
            return bass.AP(t[:].tensor, lo * S + off, [[S, 64], [16, 127]])
        for g in range(2):
            lo = g * 64
            for kv, src in ((0, kcT), (1, vcT)):
                ps = Mb[kv]
                for li in range(32):
                    P.mm(ps[0:127, 0:64], lhsT=strided(src, lo, li), rhs=cmpw[kv][lo:lo + 64, li, :],
                         start=(li == 0), stop=False, inc=False)
                for li in range(32):
                    P.mm(ps[0:127, 0:64], lhsT=posB[kv][lo:lo + 64, li, :], rhs=cmpw[kv][lo:lo + 64, li, :],
                         start=False, stop=(li == 31), inc=(li == 31))
            P.act(sq[0:127, 0:64], Mb[0][0:127, 0:64], AF.Square, accum=ss[0:127, :])
            P.act(ss[0:127, :], ss[0:127, :], AF.Sqrt, bias=EPS, scale=1.0 / 64)
            P.recip(ss[0:127, :], ss[0:127, :])
            P.stt("dve", kd[0:127, 0:64], Mb[0][0:127, 0:64], ss[0:127, 0:1], gkc[0:127, :], ALU.mult, ALU.mult)
            P.cp("pool", kd[0:127, 64:128], kd[0:127, 0:64])
            P.tr(tb[0][:, 0, :], kd[:], C["ident"][:])
            P.cp("act", kcmpT[g][:], tb[0][:, 0, :])
            P.cp("act", vext[g][0:127, 0:64], Mb[1][0:127, 0:64])
            P.cp("pool", vext[g][0:127, 64:97], C["cover"][0:127, :])
        P.maybe_barrier()


    def phase_nsa(self, l, hT, yb_d):
        P, C = self.P, self.C
        I = self.inp
        self.cast_engs = ["pool", "dve"]
        w_in = I["w_in"]
        KV = OFF["nkv"]
        with P.scope() as st:
            Wq = P.sb("Wnq", [128, 8, 1024], BF16, st)
            self.wload(Wq, w_in, l, OFF["nq"], 1024)
            Wkv = P.sb("Wkv", [128, 8, 768], BF16, st)
            self.wload(Wkv, w_in, l, KV, 768)
            Wks = P.sb("Wks", [128, 8, 256], BF16, st)
            Wkw = P.sb("Wkw", [128, 8, 256], BF16, st)
            for g in range(2):
                for r in range(2):
                    self.wload(Wks[:, :, g * 128 + r * 64:g * 128 + r * 64 + 64], w_in, l, KV + 256 + g * 64, 64)
                    self.wload(Wkw[:, :, g * 128 + r * 64:g * 128 + r * 64 + 64], w_in, l, KV + 512 + g * 64, 64)
            Wng = P.sb("Wng", [128, 8, 48], BF16, st)
            self.wload(Wng, w_in, l, OFF["ng"], 48)
            gq = self.gain_col("gnq", I["nsa_q_norm"][l], st)
            gks = self.gain_col("gks", I["nsa_k_norm"][l, 1], st)
            gkw = self.gain_col("gkw", I["nsa_k_norm"][l, 2], st)
            gkc = P.sb("gkc", [128, 64], F32, st)
            P.dma(gkc[:], I["nsa_k_norm"][l, 0].partition_broadcast(128))
            ksT = [P.sb("ksT", [128, S], BF16, st) for _ in range(2)]
            kwT = [P.sb("kwT", [128, S], BF16, st) for _ in range(2)]
            vs1 = P.sb("vs1", [128, NT, 2, 65], BF16, st)
            vw1 = P.sb("vw1", [128, NT, 2, 65], BF16, st)
            kcmpT = [P.sb("kcmpT", [128, 128], BF16, st) for _ in range(2)]
            vext = [P.sb("vext", [128, 97], BF16, st) for _ in range(2)]
            sq = P.sb("sq", [128, 512], F32, st)
            rs = P.sb("rs", [128, 512], F32, st)
            ss = P.sb("ss1", [128, 1], F32, st)
            bank = [P.ps("bk", [128, 512], F32, st) for _ in range(6)]
            tb = [P.ps("tb", [128, 8, 128], BF16, st) for _ in range(2)]
            Sb, Ob, Mb = bank[0:2], bank[2:4], bank[4:6]
            with P.scope() as st2:
                self._nsa_kside(l, hT, st2, locals())
            nqz = P.sb("nqz", [128, 16, 512], BF16, st)
            P.memset("pool", nqz[:], 0.0)
            gs = P.sb("gs", [128, 48], F32, st)
            yacc = P.sb("yacc", [128, 16, 64], F32, st)
            ytmp = P.sb("ytmp", [128, 4, 64], F32, st)
            impn = P.sb("impn", [128, 16, 32], F32, st)
            imp = P.sb("imp", [128, 32], F32, st)
            mx = P.sb("mx8", [128, 8], F32, st)
            negblk = P.sb("negblk", [128, 32], BF16, st)
            negblkT = [P.sb("negblkT", [32, 128], BF16, st) for _ in range(2)]
            pTb = [P.sb("pTb", [128, 512], BF16, st) for _ in range(2)]
            ytm = P.sb("ytm", [128, 1024], BF16, st)
            yT = P.sb("yT", [128, 8, 512], BF16, st)
            rc = P.sb("rc", [128, 4], F32, st)
            cf = P.sb("cf", [128, 4], F32, st)

            cnt = {"si": 0}
            oi = 0
            for Q in range(4):
                t0 = Q * 512
                for c in range(8):
                    self.proj_fm(Mb[0], Wq, c * 128, 128, hT, t0, 512)
                    P.act(sq[:, :], Mb[0][:, :], AF.Square)
                    P.mm(Mb[1][:, :], lhsT=C["blk64_f"][:], rhs=sq[:, :])
                    P.act(rs[:, :], Mb[1][:, :], AF.Sqrt, bias=EPS, scale=1.0 / 64)
                    P.recip(rs[:, :], rs[:, :])
                    for hf in range(2):
                        psl = slice(64 * hf, 64 * hf + 64)
                        P.stt("dve", nqz[psl, 2 * c + hf, :], Mb[0][psl, :], gq[psl, 0:1], rs[psl, :], ALU.mult, ALU.mult)
                for tl in range(4):
                    qt = Q * 4 + tl
                    if qt >= self.lim:
                        continue
                    P.maybe_barrier()
                    tsl = slice(tl * 128, (tl + 1) * 128)
                    self.proj_tm(Mb[0], Wng, 0, 48, hT, qt)
                    P.act(gs[:], Mb[0][:, 0:48], AF.Sigmoid)
                    gs3 = gs[:].rearrange("p (h b) -> p h b", b=3)

                    def scores(Sp, sub, kT_g, kt_cols, masks):
                        first = True
                        for (ml, mr) in masks:
                            kk = mr.shape[0]
                            P.mm(Sp[:, :], lhsT=ml, rhs=mr.unsqueeze(1).to_broadcast([kk, 4, 128]),
                                 start=first, stop=False, inc=False)
                            first = False
                        P.mm(Sp[:, :], lhsT=kT_g[:, kt_cols], rhs=nqz[:, 4 * sub:4 * sub + 4, tsl],
                             start=first, stop=True)

                    def finalize(O3, sub, br, first):
                        hs = slice(4 * sub, 4 * sub + 4)
                        P.ts("dve", rc[:], O3[:, :, 64], 1e-30, None, op0=ALU.max)
                        P.recip(rc[:], rc[:])
                        if br == 0:
                            P.tt("dve", impn[:, hs, :], O3[:, :, 65:97], rc[:].unsqueeze(2).to_broadcast([128, 4, 32]),
                                 ALU.mult)
                        P.tt("dve", cf[:], rc[:], gs3[:, hs, br], ALU.mult)
                        if first:
                            P.tt("dve", yacc[:, hs, :], O3[:, :, 0:64], cf[:].unsqueeze(2).to_broadcast([128, 4, 64]),
                                 ALU.mult)
                        else:
                            P.tt("dve", ytmp[:], O3[:, :, 0:64], cf[:].unsqueeze(2).to_broadcast([128, 4, 64]),
                                 ALU.mult)
                            P.tt("pool", yacc[:, hs, :], yacc[:, hs, :], ytmp[:], ALU.add)

                    items = []
                    for sub in range(4):
                        g = sub // 2
                        O = Ob[oi % 2]
                        oi += 1
                        O3 = O[:, 0:388].rearrange("p (h e) -> p h e", e=97)

                        def em_s(Sp, sub=sub, g=g):
                            scores(Sp, sub, kcmpT[g], slice(0, 128),
                                   [(C["ident"][:], C["visneg"][:, qt * 128:(qt + 1) * 128])])

                        def em_pv(pT, sub=sub, g=g, O3=O3):
                            for hh in range(4):
                                P.mm(O3[:, hh, :], lhsT=pT[:, hh * 128:(hh + 1) * 128], rhs=vext[g][:, :],
                                     start=(hh == 0), stop=True, inc=(hh == 3), sgc=True)
                            finalize(O3, sub, 0, True)
                        items.append((em_s, em_pv))
                    self.pipelined(items, Sb, pTb, cnt)
                    for g in range(2):
                        P.op("dve", lambda e: e.tensor_reduce(out=imp[:], in_=impn[:, 8 * g:8 * g + 8, :].rearrange("p h j -> p j h"),
                                                              axis=AX.X, op=ALU.add),
                             [impn[:, 8 * g:8 * g + 8, :]], [imp[:]])
                        P.tt("dve", imp[:], imp[:], C["selkeep"][:, qt, :], ALU.mult)
                        P.tt("dve", imp[:], imp[:], C["selbias"][:, qt, :], ALU.add)
                        P.op("dve", lambda e: e.max(out=mx[:], in_=imp[:]), [imp[:]], [mx[:]])
                        P.ts("dve", negblk[:], imp[:], mx[:, 3:4], 1.0, op0=ALU.is_ge, op1=ALU.subtract)
                        P.tr(tb[0][0:32, 0, :], negblk[:], C["ident"][:])
                        P.amul(negblkT[g][:], tb[0][0:32, 0, :], -NEGM)
                    items = []
                    for br in (2, 1):
                        kts = list(range(0, qt + 1)) if br == 1 else list(range(max(0, qt - 4), qt + 1))
                        kT_l = ksT if br == 1 else kwT
                        v_l = vs1 if br == 1 else vw1
                        for sub in range(4):
                            g = sub // 2
                            O = Ob[oi % 2]
                            oi += 1
                            O3 = O[:, 0:260].rearrange("p (h e) -> p h e", e=65)
                            for ki, kt in enumerate(kts):
                                masks = []
                                if br == 1:
                                    masks.append((C["E"][:, kt * 128:(kt + 1) * 128], negblkT[g][:]))
                                if kt == qt:
                                    masks.append((C["ident"][:], C["cneg"][:]))
                                if br == 2 and kt == qt - 4:
                                    masks.append((C["ident"][:], C["wneg"][:]))

                                def em_s(Sp, sub=sub, g=g, kt=kt, masks=masks, kT_l=kT_l):
                                    scores(Sp, sub, kT_l[g], slice(kt * 128, (kt + 1) * 128), masks)

                                def em_pv(pT, sub=sub, g=g, kt=kt, ki=ki, nkt=len(kts), O3=O3, v_l=v_l, br=br):
                                    for hh in range(4):
                                        P.mm(O3[:, hh, :], lhsT=pT[:, hh * 128:(hh + 1) * 128], rhs=v_l[:, kt, g, :],
                                             start=(ki == 0 and hh == 0), stop=(ki == nkt - 1), inc=(hh == 3), sgc=True)
                                    if ki == nkt - 1:
                                        finalize(O3, sub, br, False)
                                items.append((em_s, em_pv))
                    self.pipelined(items, Sb, pTb, cnt)
                    P.cp("act", ytm[:], yacc[:].rearrange("p h e -> p (h e)"))
                    t_ = tb[1]
                    for c in range(8):
                        P.tr(t_[:, c, :], ytm[:, c * 128:(c + 1) * 128], C["ident"][:], inc=(c == 7))
                    P.cp("act", yT[:, :, tsl], t_[:])
                P.dma(yb_d[:, :, t0:t0 + 512], yT[:], q="sp")

    def phase_ssd(self, l, hT, yc_d, zs_d, xact_d):
        P, C = self.P, self.C
        I = self.inp
        self.cast_engs = ["pool", "dve"]
        w_in = I["w_in"]
        with P.scope() as st:
            Wz = P.sb("Wz", [128, 8, 2048], BF16, st)
            self.wload(Wz, w_in, l, OFF["sz"], 2048)
            zt = [P.sb("zt", [128, 2048], F32, st) for _ in range(2)]
            bank = [P.ps("bk", [128, 512], F32, st) for _ in range(4)]
            bi = 0
            for tt in range(NT):
                z_t = zt[tt % 2]
                for nb in range(4):
                    ps = bank[bi % 4]
                    bi += 1
                    self.proj_tm(ps, Wz, nb * 512, 512, hT, tt)
                    P.act(z_t[:, nb * 512:(nb + 1) * 512], ps[:, :], AF.Silu)
                P.dma(zs_d[tt * 128:(tt + 1) * 128, :], z_t[:])
        with P.scope() as st:
            cw = P.sb("cw", [128, 24, 4], F32, st)
            cb = P.sb("cb", [128, 24], F32, st)
            for cc in range(24):
                P.dma(cw[:, cc, :], I["ssd_conv_w"][l][:, cc * 128:(cc + 1) * 128].rearrange("k c -> c k"),
                      allow_slow_non_contiguous=True)
                P.dma(cb[:, cc:cc + 1], I["ssd_conv_b"][l][cc * 128:(cc + 1) * 128].rearrange("(c o) -> c o", o=1))
            Wx = [P.sb("Wx", [128, 8, 512], BF16, st) for _ in range(2)]
            raw = [P.sb("raw", [128, 3 + S], F32, st) for _ in range(2)]
            accb = [P.sb("accb", [128, S], F32, st) for _ in range(2)]
            xa = [P.sb("xa", [128, S], BF16, st) for _ in range(2)]
            ctmp = P.sb("ctmp", [128, S], F32, st)
            bank = [P.ps("bk", [128, 512], F32, st) for _ in range(4)]
            for r_ in raw:
                P.memset("pool", r_[:, 0:3], 0.0)
            bi = 0
            for cc in range(24):
                W_ = Wx[(cc // 4) % 2]
                if cc % 4 == 0:
                    self.wload(W_, w_in, l, OFF["sxbc"] + cc * 128, 512)
                rw = raw[cc % 2]
                for Q in range(4):
                    ps = bank[bi % 4]
                    bi += 1
                    self.proj_fm(ps, W_, (cc % 4) * 128, 128, hT, Q * 512, 512)
                    P.cp("act", rw[:, 3 + Q * 512:3 + (Q + 1) * 512], ps[:, :])
                eng = "dve"
                ac = accb[cc % 2]
                P.ts(eng, ac[:], rw[:, 3:3 + S], cw[:, cc, 3:4], None, op0=ALU.mult)
                for k in (2, 1, 0):
                    if eng == "dve":
                        P.stt(eng, ac[:], rw[:, k:k + S], cw[:, cc, k:k + 1], ac[:], ALU.mult, ALU.add)
                    else:
                        P.ts(eng, ctmp[:], rw[:, k:k + S], cw[:, cc, k:k + 1], None, op0=ALU.mult)
                        P.tt(eng, ac[:], ac[:], ctmp[:], ALU.add)
                P.act(xa[cc % 2][:], ac[:], AF.Silu, bias=cb[:, cc:cc + 1])
                P.dma(xact_d[:, cc, :], xa[cc % 2][:])
                P.maybe_barrier()
        with P.scope() as st:
            Wdt = P.sb("Wdt", [128, 8, 32], BF16, st)
            self.wload(Wdt, w_in, l, OFF["sdt"], 32)
            dtb = P.sb("dtb", [128, 32], F32, st)
            P.dma(dtb[:], I["ssd_dt_bias"][l].partition_broadcast(128))
            a_b = P.sb("a_b", [128, 32], F32, st)
            P.dma(a_b[:], I["ssd_a_log"][l].partition_broadcast(128))
            P.act(a_b[:], a_b[:], AF.Exp)
            P.ts("dve", a_b[:], a_b[:], -1.0, None, op0=ALU.mult)
            D_b = P.sb("D_b", [128, 32], F32, st)
            P.dma(D_b[:], I["ssd_d"][l].partition_broadcast(128))
            ng_b = P.sb("ng_b", [128, 2048], F32, st)
            P.dma(ng_b[:], I["ssd_norm_g"][l].partition_broadcast(128))
            xab = [P.sb("xab", [128, 24, 128], BF16, st) for _ in range(2)]
            zsb = [P.sb("zsb", [128, 2048], F32, st) for _ in range(2)]
            xs_tm = P.sb("xs_tm", [128, 2048], BF16, st)
            bm_b = [P.sb("bm_tm", [128, 512], BF16, st) for _ in range(2)]
            xsw_b = [P.sb("xs_w", [128, 2048], BF16, st) for _ in range(2)]
            rhs1_b = [P.sb("rhs1", [128, 8, 128], F32, st) for _ in range(2)]
            rhs2_b = [P.sb("rhs2", [128, 8, 128], F32, st)] * 2
            Lg_b = [P.sb("Lg", [128, 8, 128], BF16, st) for _ in range(2)]
            MTg_b = [P.sb("MTg", [128, 8, 128], BF16, st) for _ in range(2)]
            tmp_b = [P.sb("tmpb", [128, 512], F32, st) for _ in range(2)]
            cbT = P.sb("cbT", [128, 4, 128], BF16, st)
            H = [P.sb("H", [128, 512], F32, st) for _ in range(4)]
            Hbf = [P.sb("Hbf", [128, 512], BF16, st) for _ in range(4)]
            y_b = [P.sb("y", [128, 2048], F32, st) for _ in range(2)]
            ynb = P.sb("ynb", [128, 2048], BF16, st)
            ycT = P.sb("ycT", [128, 16, 128], BF16, st)
            sma = {n: P.sb(n, [128, NT, 32], F32, st) for n in ("dtc", "da", "nb", "ea", "wj", "dec")}
            sma["lndt"] = sma["nb"]
            sma["acum"] = sma["ea"]
            sma["alast"] = sma["dec"]
            ss = P.sb("ss2", [128, 1], F32, st)
            R = [P.ps("R", [128, 512], F32, st) for _ in range(2)]
            Yp_b = [P.ps("Yp", [128, 512], F32, st) for _ in range(2)]
            Yo = P.ps("Yo", [128, 512], F32, st)
            STp = P.ps("STp", [128, 512], F32, st)
            M0 = P.ps("M0", [128, 512], F32, st)
            M1 = M0
            tb = P.ps("tb", [128, 8, 128], BF16, st)
            for g in range(4):
                P.memset("pool", H[g][:], 0.0)
                P.memset("pool", Hbf[g][:], 0.0)
            nch = min(NT, self.lim)
            fl = lambda t: t[:].rearrange("p c h -> p (c h)")
            for c in range(NT):
                for kc in range(8):
                    P.mm(M1[:, c * 32:(c + 1) * 32], lhsT=hT[:, kc, c * 128:(c + 1) * 128], rhs=Wdt[:, kc, :],
                         start=(c == 0 and kc == 0), stop=(kc == 7), inc=(kc == 7), sgc=True)
            P.tt("dve", sma["dtc"][:], M1[:, :].rearrange("p (c h) -> p c h", h=32),
                 dtb[:].unsqueeze(1).to_broadcast([128, NT, 32]), ALU.add)
            P.act(fl(sma["dtc"]), fl(sma["dtc"]), AF.Exp)
            P.act(fl(sma["dtc"]), fl(sma["dtc"]), AF.Ln, bias=1.0)
            P.act(fl(sma["lndt"]), fl(sma["dtc"]), AF.Ln)
            P.tt("dve", sma["da"][:], sma["dtc"][:], a_b[:].unsqueeze(1).to_broadcast([128, NT, 32]), ALU.mult)
            P.mm(M1[:, :], lhsT=C["tri_f"][:], rhs=fl(sma["da"]))
            P.cp("dve", fl(sma["acum"]), M1[:, :])
            P.mm(M1[:, :], lhsT=C["ones_f"][:], rhs=fl(sma["da"]))
            P.cp("dve", fl(sma["alast"]), M1[:, :])
            P.tt("dve", sma["nb"][:], sma["lndt"][:], sma["acum"][:], ALU.subtract)
            P.tt("dve", sma["wj"][:], sma["alast"][:], sma["acum"][:], ALU.subtract)
            P.act(fl(sma["wj"]), fl(sma["wj"]), AF.Exp)
            P.tt("dve", sma["wj"][:], sma["wj"][:], sma["dtc"][:], ALU.mult)
            P.act(fl(sma["ea"]), fl(sma["acum"]), AF.Exp)
            P.act(fl(sma["dec"]), fl(sma["alast"]), AF.Exp)
            sma = {k: sma[k] for k in ("da", "nb", "ea", "wj", "dec")}

            def load(c):
                csl = slice(c * 128, (c + 1) * 128)
                P.dma(xab[c % 2][:], xact_d[:, :, csl])
                P.dma(zsb[c % 2][:], zs_d[csl, :])

            def part1(c):
                p = c % 2
                xa_ = xab[p]
                sm = {k: v[:, c, :] for k, v in sma.items()}
                bm_tm, xs_w, y = bm_b[p], xsw_b[p], y_b[p]
                for k0 in (0, 8, 16):
                    n = 8 if k0 < 16 else 4
                    for j in range(n):
                        P.tr(tb[:, j, :], xa_[:, k0 + j, :], C["ident"][:], inc=(j == n - 1))
                    if k0 < 16:
                        P.cp("act", xs_tm[:, k0 * 128:(k0 + 8) * 128], tb[:].rearrange("p a b -> p (a b)"))
                    else:
                        P.cp("act", bm_tm[:], tb[:, 0:4, :].rearrange("p a b -> p (a b)"))
                for g in range(4):
                    P.mm(M0[:, g * 128:(g + 1) * 128], lhsT=xa_[:, 16 + g, :], rhs=xa_[:, 20 + g, :],
                         start=(g == 0), stop=True, inc=(g == 3), sgc=True)
                P.cp("act", cbT[:].rearrange("p a b -> p (a b)"), M0[:, :])
                P.tt("dve", xs_w[:].rearrange("p (h e) -> p h e", e=64), xs_tm[:].rearrange("p (h e) -> p h e", e=64),
                     sm["wj"][:].unsqueeze(2).to_broadcast([128, 32, 64]), ALU.mult)
                for g in range(4):
                    hs = slice(8 * g, 8 * g + 8)
                    gsl = slice(g * 512, (g + 1) * 512)
                    rhs1, rhs2, Lg, MTg, Yp, tmp = rhs1_b[g % 2], rhs2_b[g % 2], Lg_b[g % 2], MTg_b[g % 2], Yp_b[g % 2], tmp_b[g % 2]
                    P.tt("dve", rhs1[:], C["tri_f"][:].unsqueeze(1).to_broadcast([128, 8, 128]),
                         sm["da"][:, hs].unsqueeze(2).to_broadcast([128, 8, 128]), ALU.mult)
                    P.tt("dve", rhs2[:], C["cneg_f"][:].unsqueeze(1).to_broadcast([128, 8, 128]),
                         sm["nb"][:, hs].unsqueeze(2).to_broadcast([128, 8, 128]), ALU.add)
                    for hf in range(2):
                        P.mm(R[hf][:, :], lhsT=C["ones_f"][:], rhs=rhs1[:, 4 * hf:4 * hf + 4, :].rearrange("p a b -> p (a b)"),
                             start=True, stop=False, inc=False)
                        P.mm(R[hf][:, :], lhsT=C["ident_f"][:], rhs=rhs2[:, 4 * hf:4 * hf + 4, :].rearrange("p a b -> p (a b)"),
                             start=False, stop=True)
                        P.act(Lg[:, 4 * hf:4 * hf + 4, :].rearrange("p a b -> p (a b)"), R[hf][:, :], AF.Exp)
                    P.tt("dve" if g % 2 else "pool", MTg[:], Lg[:], cbT[:, g, :].unsqueeze(1).to_broadcast([128, 8, 128]), ALU.mult)
                    for hh in range(8):
                        P.mm(Yp[:, hh * 64:(hh + 1) * 64], lhsT=MTg[:, hh, :], rhs=xs_tm[:, (8 * g + hh) * 64:(8 * g + hh + 1) * 64],
                             start=(hh == 0), stop=True, inc=(hh == 7), sgc=True)
                    P.tt("pool", tmp[:].rearrange("p (h e) -> p h e", e=64), xs_tm[:, gsl].rearrange("p (h e) -> p h e", e=64),
                         D_b[:, hs].unsqueeze(2).to_broadcast([128, 8, 64]), ALU.mult)
                    P.tt("dve", y[:, gsl], Yp[:, :], tmp[:], ALU.add)

            def part2(c):
                p = c % 2
                csl = slice(c * 128, (c + 1) * 128)
                xa_ = xab[p]
                zs_ = zsb[p]
                sm = {k: v[:, c, :] for k, v in sma.items()}
                bm_tm, xs_w, y = bm_b[p], xsw_b[p], y_b[p]
                for g in range(4):
                    hs = slice(8 * g, 8 * g + 8)
                    gsl = slice(g * 512, (g + 1) * 512)
                    tmp = tmp_b[g % 2]
                    P.mm(Yo[:, :], lhsT=xa_[:, 20 + g, :], rhs=Hbf[g][:])
                    P.tt("dve", tmp[:].rearrange("p (h e) -> p h e", e=64), Yo[:, :].rearrange("p (h e) -> p h e", e=64),
                         sm["ea"][:, hs].unsqueeze(2).to_broadcast([128, 8, 64]), ALU.mult)
                    P.tt("pool", y[:, gsl], y[:, gsl], tmp[:], ALU.add)
                    P.mm(STp[:, :], lhsT=bm_tm[:, g * 128:(g + 1) * 128], rhs=xs_w[:, gsl])
                    Hv = H[g][:].rearrange("p (h e) -> p h e", e=64)
                    P.tt("pool", Hv, Hv, sm["dec"][:, hs].unsqueeze(2).to_broadcast([128, 8, 64]), ALU.mult)
                    P.tt("dve", H[g][:], STp[:, :], H[g][:], ALU.add)
                    P.cp("act", Hbf[g][:], H[g][:])
                P.tt("dve", y[:], y[:], zs_[:], ALU.mult)
                P.act(zs_[:], y[:], AF.Square, accum=ss[:])
                P.act(ss[:], ss[:], AF.Sqrt, bias=EPS, scale=1.0 / 2048)
                P.recip(ss[:], ss[:])
                P.stt("dve", ynb[:], y[:], ss[:, 0:1], ng_b[:], ALU.mult, ALU.mult)
                for k0 in (0, 8):
                    for j in range(8):
                        P.tr(tb[:, j, :], ynb[:, (k0 + j) * 128:(k0 + j + 1) * 128], C["ident"][:], inc=(j == 7))
                    P.cp("act", ycT[:, k0:k0 + 8, :], tb[:])
                P.dma(yc_d[:, :, csl], ycT[:])

            if nch > 0:
                load(0)
                if nch > 1:
                    load(1)
                part1(0)
            for c in range(nch):
                if P.sems_left() < 40:
                    P.barrier()
                if c + 1 < nch:
                    part1(c + 1)
                part2(c)
                if c + 2 < nch:
                    load(c + 2)

    def phase_merge(self, l, hT, ya_d, yb_d, yc_d, mix_d):
        P, C = self.P, self.C
        I = self.inp
        self.cast_engs = ["pool", "dve"]
        with P.scope() as st:
            yh = [P.sb("yha", [128, 8, 1024], BF16, st), P.sb("yhb", [128, 8, 1024], BF16, st),
                  P.sb("yhc", [128, 16, 1024], BF16, st)]
            Wy = [[P.sb("Wya", [128, 8, 128], BF16, st), P.sb("Wyb", [128, 8, 128], BF16, st),
                   P.sb("Wyc", [128, 16, 128], BF16, st)] for _ in range(2)]
            Wg = [[P.sb("Wg", [128, 8, 128], BF16, st) for _ in range(3)] for _ in range(2)]
            sg = [P.sb("sg", [128, 512], F32, st) for _ in range(2)]
            macc = P.sb("macc", [128, 512], F32, st)
            mt = P.sb("mt", [128, 512], F32, st)
            mixh = P.sb("mixh", [128, 8, 1024], BF16, st)
            bank = [P.ps("bk", [128, 512], F32, st) for _ in range(6)]
            wsrc = [I["w_br_dsa"], I["w_br_nsa"], I["w_br_ssd"]]
            ysrc = [ya_d, yb_d, yc_d]
            bi = 0
            gi = 0
            it = 0
            for half in range(2):
                hsl = slice(half * 1024, (half + 1) * 1024)
                for br in range(3):
                    P.dma(yh[br][:], ysrc[br][:, :, hsl])
                for oc in range(8):
                    Wy_ = Wy[it % 2]
                    Wg_ = Wg[it % 2]
                    it += 1
                    for br in range(3):
                        self.wload(Wy_[br], wsrc[br], l, oc * 128, 128)
                        self.wload(Wg_[br], I["w_in"], l, OFF["mg"] + br * 1024 + oc * 128, 128)
                    for pc in range(2):
                        psl = slice(pc * 512, (pc + 1) * 512)
                        tok = half * 1024 + pc * 512
                        for br in range(3):
                            psy = bank[bi % 6]
                            bi += 1
                            psg = bank[bi % 6]
                            bi += 1
                            nk = 16 if br == 2 else 8
                            for kc in range(nk):
                                P.mm(psy[:, :], lhsT=Wy_[br][:, kc, :], rhs=yh[br][:, kc, psl],
                                     start=(kc == 0), stop=(kc == nk - 1))
                            self.proj_fm(psg, Wg_[br], 0, 128, hT, tok, 512)
                            s_ = sg[gi % 2]
                            gi += 1
                            P.act(s_[:], psg[:, :], AF.Sigmoid)
                            if br == 0:
                                P.tt("dve", macc[:], psy[:, :], s_[:], ALU.mult)
                            else:
                                P.tt("dve", mt[:], psy[:, :], s_[:], ALU.mult)
                                if br == 1:
                                    P.tt("pool", macc[:], macc[:], mt[:], ALU.add)
                                else:
                                    P.tt("pool", mixh[:, oc, psl], macc[:], mt[:], ALU.add)
                    P.maybe_barrier()
                P.dma(mix_d[:, :, hsl], mixh[:])

    def phase_out_norm(self, l, x_d, mix_d, x1_d, hT):
        P, C = self.P, self.C
        I = self.inp
        self.cast_engs = ["pool", "dve"]
        with P.scope() as st:
            Wo = P.sb("Wo", [128, 8, 1024], BF16, st)
            self.wload(Wo, I["w_out"], l, 0, 1024)
            gb = P.sb("gb2", [128, D], F32, st)
            P.dma(gb[:], I["norm2_g"][l].partition_broadcast(128))
            mq = [P.sb("mq", [128, 8, 128], BF16, st) for _ in range(2)]
            xb = [P.sb("xb", [128, D], F32, st) for _ in range(2)]
            x1b = [P.sb("x1b", [128, D], F32, st) for _ in range(2)]
            sq = P.sb("sq", [128, D], F32, st)
            hb = [P.sb("hb", [128, D], BF16, st) for _ in range(2)]
            ss = P.sb("ss", [128, 2], F32, st)
            bank = [P.ps("bk", [128, 512], F32, st) for _ in range(4)]
            pt = [P.ps("pt", [128, 8, 128], BF16, st) for _ in range(2)]
            for tt in range(NT):
                tsl = slice(tt * 128, (tt + 1) * 128)
                m_ = mq[tt % 2]
                P.dma(m_[:], mix_d[:, :, tsl])
                x_t = xb[tt % 2]
                P.dma(x_t[:], x_d[tsl, :])
                x1 = x1b[tt % 2]
                for hf in range(2):
                    ps = bank[(2 * tt + hf) % 4]
                    for oc in range(8):
                        P.mm(ps[:, :], lhsT=m_[:, oc, :], rhs=Wo[:, oc, hf * 512:(hf + 1) * 512],
                             start=(oc == 0), stop=(oc == 7))
                    P.tt("dve", x1[:, hf * 512:(hf + 1) * 512], ps[:, :], x_t[:, hf * 512:(hf + 1) * 512], ALU.add)
                P.dma(x1_d[tsl, :], x1[:])
                s1 = ss[:, tt % 2:tt % 2 + 1]
                P.act(sq[:], x1[:], AF.Square, accum=s1)
                P.act(s1, s1, AF.Sqrt, bias=EPS, scale=1.0 / D)
                P.recip(s1, s1)
                h_t = hb[tt % 2]
                P.stt("dve", h_t[:], x1[:], s1, gb[:], ALU.mult, ALU.mult)
                p_t = pt[tt % 2]
                for kc in range(8):
                    P.tr(p_t[:, kc, :], h_t[:, kc * 128:(kc + 1) * 128], C["ident"][:], inc=(kc == 7))
                P.cp("act", hT[:, :, tsl], p_t[:])

    def phase_ffn(self, l, hT, x1_d, xo_d, xp_d):
        P, C = self.P, self.C
        I = self.inp
        self.cast_engs = ["pool", "dve", "act", "dve"]
        with P.scope() as st:
            W1h = P.sb("W1h", [128, 8, 2048], BF16, st)
            W2h = P.sb("W2h", [128, 16, 1024], BF16, st)
            aT = P.sb("aT", [128, 16, 512], BF16, st)
            rb = [P.sb("rbf", [128, 512], F32, st) for _ in range(2)]
            xt = [P.sb("xt", [128, 512], F32, st) for _ in range(4)]
            xo = [P.sb("xo", [128, 512], F32, st) for _ in range(4)]
            fb = [P.ps("fb", [128, 512], F32, st) for _ in range(2)]
            ob = [P.ps("ob", [128, 512], F32, st) for _ in range(4)]
            xi = 0
            for hp in range(2):
                self.wload(W1h, I["w_ff1"], l, hp * 2048, 2048)
                self.wload(W2h, I["w_ff2"], l, 0, 1024, rows=(hp * 2048, (hp + 1) * 2048))
                src_d = x1_d if hp == 0 else xp_d
                dst_d = xp_d if hp == 0 else xo_d
                for Q in range(4):
                    for fc in range(16):
                        ps = fb[fc % 2]
                        self.proj_fm(ps, W1h, fc * 128, 128, hT, Q * 512, 512)
                        r = rb[fc % 2]
                        P.act(r[:], ps[:, :], AF.Relu)
                        P.tt("dve", aT[:, fc, :], r[:], r[:], ALU.mult)
                    for hf in range(2):
                        cols = slice(hf * 512, (hf + 1) * 512)
                        xs_ = []
                        for tl in range(4):
                            rows = slice((Q * 4 + tl) * 128, (Q * 4 + tl + 1) * 128)
                            x_ = xt[xi % 4]
                            o_ = xo[xi % 4]
                            xi += 1
                            P.dma(x_[:], src_d[rows, cols])
                            xs_.append((x_, o_, rows))
                            for k in range(16):
                                P.mm(ob[tl][:, :], lhsT=aT[:, k, tl * 128:(tl + 1) * 128], rhs=W2h[:, k, cols],
                                     start=(k == 0), stop=(k == 15))
                        for tl in range(4):
                            x_, o_, rows = xs_[tl]
                            P.tt("dve", o_[:], ob[tl][:, :], x_[:], ALU.add)
                            P.dma(dst_d[rows, cols], o_[:])
                    P.maybe_barrier()

    def build(self):
        P = self.P
        I = self.inp
        self.din("x", [S, D])
        for name, shp in (("norm1_g", [DEPTH, D]), ("w_in", [DEPTH, D, D_IN]), ("dsa_q_norm", [DEPTH, 64]),
                          ("dsa_k_norm", [DEPTH, 64]), ("nsa_q_norm", [DEPTH, 64]), ("nsa_k_norm", [DEPTH, 3, 64]),
                          ("nsa_cmp_pos", [DEPTH, 2, 32, 64]), ("nsa_cmp_w", [DEPTH, 2, 32, 64, 64]),
                          ("ssd_conv_w", [DEPTH, 4, 3072]), ("ssd_conv_b", [DEPTH, 3072]),
                          ("ssd_dt_bias", [DEPTH, 32]), ("ssd_a_log", [DEPTH, 32]), ("ssd_d", [DEPTH, 32]),
                          ("ssd_norm_g", [DEPTH, 2048]), ("w_br_dsa", [DEPTH, D, D]), ("w_br_nsa", [DEPTH, D, D]),
                          ("w_br_ssd", [DEPTH, 2 * D, D]), ("w_out", [DEPTH, D, D]), ("norm2_g", [DEPTH, D]),
                          ("w_ff1", [DEPTH, D, 4 * D]), ("w_ff2", [DEPTH, 4 * D, D])):
            self.din(name, shp)
        self.load_consts()
        out = self.dout("out", [S, D])
        hT = P.sb("hT", [128, 8, S], BF16)
        ya_d = P.dram("ya", [128, 8, S], BF16)
        yb_d = P.dram("yb", [128, 8, S], BF16)
        yc_d = P.dram("yc", [128, 16, S], BF16)
        zs_d = P.dram("zs", [S, 2048], F32)
        xact_d = P.dram("xact", [128, 24, S], BF16)
        mix_d = P.dram("mix", [128, 8, S], BF16)
        x1_d = P.dram("x1", [S, D], F32)
        xm_d = P.dram("xm", [S, D], F32)
        xp_d = P.dram("xp", [S, D], F32)
        dbg = self.debug
        x_cur = I["x"]
        for l in range(self.nlayers):
            x_nxt = out if l == self.nlayers - 1 else xm_d
            self.phase_norm(x_cur, I["norm1_g"][l], hT)
            if dbg == "hT":
                return self._dump(hT[:], [128, 8, S], BF16)
            if dbg in ("nsa", "nsa_k"):
                self.phase_nsa(l, hT, yb_d)
                return self._dump(yb_d, [128, 8, S], BF16)
            if dbg == "ssd":
                self.phase_ssd(l, hT, yc_d, zs_d, xact_d)
                return self._dump(yc_d, [128, 16, S], BF16)
            self.phase_dsa(l, hT, ya_d)
            if dbg is not None and dbg.startswith("dsa"):
                return self._dump(ya_d, [128, 8, S], BF16)
            self.phase_nsa(l, hT, yb_d)
            self.phase_ssd(l, hT, yc_d, zs_d, xact_d)
            self.phase_merge(l, hT, ya_d, yb_d, yc_d, mix_d)
            if dbg == "mix":
                return self._dump(mix_d, [128, 8, S], BF16)
            self.phase_out_norm(l, x_cur, mix_d, x1_d, hT)
            if dbg == "x1":
                return self._dump(x1_d, [S, D], F32)
            self.phase_ffn(l, hT, x1_d, x_nxt, xp_d)
            x_cur = x_nxt
        P.finish()

    def _dump(self, src, shape, dt):
        if "dbg" not in self.out:
            o = self.dout("dbg", shape, dt)
            self.P.dma(o, src)
        self.P.finish()


_CACHE = {}


def kernel(**inputs):
    n = 8
    B = Builder()
    B.build()
    consts = _consts()
    shared = {}
    for name in B.inp:
        if name in consts:
            shared[name] = consts[name]
        elif name != "x":
            shared[name] = np.ascontiguousarray(np.asarray(inputs[name], dtype=np.float32))
    x = np.asarray(inputs["x"], dtype=np.float32)
    in_maps = []
    for b in range(n):
        m = dict(shared)
        m["x"] = np.ascontiguousarray(x[b])
        in_maps.append(m)
    res = run_bass_kernel_spmd(B.nc, in_maps, core_ids=list(range(n)))
    return np.stack([np.asarray(res.results[b]["out"], dtype=np.float32) for b in range(n)], axis=0)
```

```python
import contextlib
import os
import numpy as np
import concourse.bass as bass
import concourse.mybir as mybir
from concourse.bass_utils import run_bass_kernel_spmd

F32 = mybir.dt.float32
BF16 = mybir.dt.bfloat16
AF = mybir.ActivationFunctionType
ALU = mybir.AluOpType
AX = mybir.AxisListType

EPOCH = 240
NPOOL = 90

D = 1024
S = 2048
NT = 16
DEPTH = 2
EPS = 1e-6
OFF = dict(dq=0, dk=1024, dv=1088, iq=1152, ik=1408, iw=1440, nq=1448, nkv=2472, ng=3240,
           sz=3288, sxbc=5336, sdt=8408, mg=8440)
D_IN = 11512
NEGM = -30000.0


def _region(ap):
    t = ap.tensor
    name = t.name
    dims = [(int(s), int(c)) for s, c in ap.ap]
    off = int(ap.offset)
    if "DRam" in type(t).__name__:
        lo = hi = off
        for s, c in dims:
            if s >= 0:
                hi += s * (c - 1)
            else:
                lo += s * (c - 1)
        return (name, 0, 1, lo, hi + 1)
    if "PSum" in type(t).__name__:
        return (name, 0, 128, 0, 1 << 30)
    ps, pc = dims[0]
    if ps == 0:
        p0, f0 = 0, off
    else:
        p0 = off // ps
        f0 = off - p0 * ps
    lo = hi = f0
    for s, c in dims[1:]:
        if s >= 0:
            hi += s * (c - 1)
        else:
            lo += s * (c - 1)
    return (name, p0, p0 + pc, lo, hi + 1)


def _overlap(a, b):
    return a[1] < b[2] and b[1] < a[2] and a[3] < b[4] and b[3] < a[4]


def _contains(a, b):
    return a[1] <= b[1] and b[2] <= a[2] and a[3] <= b[3] and b[4] <= a[4]


class Prog:
    ENGS = ("pe", "act", "dve", "pool", "sp")

    def __init__(self, nc):
        self.nc = nc
        self.stack = contextlib.ExitStack()
        self.eng = {"pe": nc.tensor, "act": nc.scalar, "dve": nc.vector, "pool": nc.gpsimd, "sp": nc.sync}
        self.pool = [self.stack.enter_context(nc.semaphore(f"sp{i}")) for i in range(NPOOL)]
        self.bsem = [self.stack.enter_context(nc.semaphore(f"bar{i}")) for i in range(4)]
        self.nbar = 0
        self.uid = 0
        self.n_wait = 0
        self.n_ins = 0
        self.max_used = 0
        self._reset()

    def _reset(self):
        self.free = list(range(NPOOL))
        self.esem = {e: None for e in self.ENGS}
        self.ecnt = {e: 0 for e in self.ENGS}
        self.pend = {e: False for e in self.ENGS}
        self.seen = {e: {} for e in self.ENGS}
        self.dset = []
        self.dall = {}
        self.dnext = 0
        self.trk = {}

    def _alloc(self):
        assert self.free, "semaphore pool exhausted; add a barrier"
        i = self.free.pop()
        self.max_used = max(self.max_used, NPOOL - len(self.free))
        return i

    def sems_left(self):
        return len(self.free)

    def sb(self, name, shape, dtype, stack=None):
        self.uid += 1
        return (stack or self.stack).enter_context(
            self.nc.sbuf_tensor(f"{name}_{self.uid}", list(shape), dtype))

    def ps(self, name, shape, dtype, stack=None):
        self.uid += 1
        return (stack or self.stack).enter_context(
            self.nc.psum_tensor(f"{name}_{self.uid}", list(shape), dtype))

    def dram(self, name, shape, dtype):
        self.uid += 1
        return self.nc.dram_tensor(f"{name}_{self.uid}", list(shape), dtype).ap()

    @contextlib.contextmanager
    def scope(self):
        st = contextlib.ExitStack()
        try:
            yield st
        finally:
            self.barrier()
            st.close()

    def _deps(self, reads, writes):
        deps = []
        for ap in reads:
            r = _region(ap)
            t = self.trk.get(r[0])
            if t is None:
                continue
            for (wr, ev) in t["w"]:
                if _overlap(wr, r):
                    deps.append(ev)
        for ap in writes:
            r = _region(ap)
            t = self.trk.get(r[0])
            if t is None:
                continue
            for (wr, ev) in t["w"]:
                if _overlap(wr, r):
                    deps.append(ev)
            for (sk, rr), val in t["r"].items():
                if _overlap(rr, r):
                    deps.append((sk, val))
        return deps

    def _record(self, reads, writes, ev):
        for ap in reads:
            r = _region(ap)
            t = self.trk.setdefault(r[0], {"w": [], "r": {}})
            k = (ev[0], r)
            if t["r"].get(k, -1) < ev[1]:
                t["r"][k] = ev[1]
        for ap in writes:
            r = _region(ap)
            t = self.trk.setdefault(r[0], {"w": [], "r": {}})
            t["w"] = [(wr, e) for (wr, e) in t["w"] if not _contains(r, wr)]
            t["w"].append((r, ev))
            t["r"] = {k: v for k, v in t["r"].items() if not _contains(r, k[1])}

    def _wait(self, e, deps):
        eng = self.eng[e]
        need = {}
        for sk, val in deps:
            if self.seen[e].get(sk, 0) >= val:
                continue
            if need.get(sk, 0) < val:
                need[sk] = val
        for sk, val in need.items():
            assert 0 < val <= 255
            eng.wait_ge(self.pool[sk[1]], val)
            self.seen[e][sk] = val
            self.n_wait += 1

    def _next_ev(self, e, commit):
        if self.esem[e] is None or self.ecnt[e] >= EPOCH:
            self.esem[e] = self._alloc()
            self.ecnt[e] = 0
        ev = ((e, self.esem[e]), self.ecnt[e] + 1)
        if commit:
            self.ecnt[e] += 1
        return ev

    def op(self, e, fn, reads, writes, inc=True):
        deps = self._deps(reads, writes)
        if e == "pe":
            deps = [d for d in deps if d[0][0] != "pe"]
        self._wait(e, deps)
        ins = fn(self.eng[e])
        self.n_ins += 1
        ev = self._next_ev(e, inc)
        if inc:
            ins.then_inc(self.pool[ev[0][1]], 1)
            self.pend[e] = False
        else:
            self.pend[e] = True
        self._record(reads, writes, ev)
        return ins

    def dma(self, out, in_, q="sp", **kw):
        deps = self._deps([in_], [out])
        if q == "pool":
            ent = [self._alloc(), 0]
        else:
            if len(self.dset) < 8:
                self.dset.append([self._alloc(), 0])
            k = self.dnext % len(self.dset)
            self.dnext += 1
            if self.dset[k][1] >= 15:
                self.dset[k] = [self._alloc(), 0]
            ent = self.dset[k]
        if ent[1] > 0:
            deps.append((("d", ent[0]), 16 * ent[1]))
        self._wait(q, deps)
        ins = self.eng[q].dma_start(out=out, in_=in_, **kw)
        ins.then_inc(self.pool[ent[0]], 16)
        ent[1] += 1
        self.dall[ent[0]] = ent[1]
        self.n_ins += 1
        self._record([in_], [out], (("d", ent[0]), 16 * ent[1]))
        return ins

    def barrier(self):
        evs = []
        for e in self.ENGS:
            assert not self.pend[e], f"pending non-inc op on {e} at barrier"
            if self.esem[e] is not None and self.ecnt[e] > 0:
                evs.append(((e, self.esem[e]), self.ecnt[e]))
        for k, c in self.dall.items():
            evs.append((("d", k), 16 * c))
        for e in self.ENGS:
            self._wait(e, evs)
        b0 = self.bsem[2 * (self.nbar % 2)]
        b1 = self.bsem[2 * (self.nbar % 2) + 1]
        p0 = self.bsem[2 * ((self.nbar + 1) % 2)]
        p1 = self.bsem[2 * ((self.nbar + 1) % 2) + 1]
        for e in self.ENGS:
            if e != "sp":
                self.eng[e].sem_inc(b0, 1)
        sp = self.eng["sp"]
        sp.wait_ge(b0, 4)
        used = [i for i in range(NPOOL) if i not in set(self.free)]
        for i in used:
            sp.sem_clear(self.pool[i])
        sp.sem_clear(p0)
        sp.sem_clear(p1)
        sp.sem_inc(b1, 1)
        for e in self.ENGS:
            if e != "sp":
                self.eng[e].wait_ge(b1, 1)
        self.nbar += 1
        self._reset()

    def maybe_barrier(self, min_free=35):
        if len(self.free) < min_free:
            self.barrier()

    def finish(self):
        self.barrier()
        self.stack.close()

    def mm(self, out, lhsT, rhs, start=True, stop=True, inc=None, sgc=False):
        if inc is None:
            inc = stop
        return self.op("pe", lambda e: e.matmul(out, lhsT=lhsT, rhs=rhs, start=start, stop=stop,
                                                skip_group_check=sgc),
                       [lhsT, rhs], [out], inc=inc)

    def tr(self, out, in_, ident, inc=True):
        return self.op("pe", lambda e: e.transpose(out=out, in_=in_, identity=ident), [in_, ident], [out], inc=inc)

    def act(self, out, in_, func, bias=None, scale=None, accum=None):
        kw = {}
        reads = [in_]
        writes = [out]
        if bias is not None:
            kw["bias"] = bias
            if not isinstance(bias, (int, float)):
                reads.append(bias)
        if scale is not None:
            kw["scale"] = scale
            if not isinstance(scale, (int, float)):
                reads.append(scale)
        if accum is not None:
            kw["accum_out"] = accum
            writes.append(accum)
        return self.op("act", lambda e: e.activation(out=out, in_=in_, func=func, **kw), reads, writes)

    def cp(self, eng, out, in_):
        if eng == "act":
            return self.op("act", lambda e: e.copy(out=out, in_=in_), [in_], [out])
        return self.op(eng, lambda e: e.tensor_copy(out=out, in_=in_), [in_], [out])

    def tt(self, eng, out, in0, in1, op):
        return self.op(eng, lambda e: e.tensor_tensor(out=out, in0=in0, in1=in1, op=op), [in0, in1], [out])

    def ts(self, eng, out, in0, s1, s2=None, op0=ALU.mult, op1=None, accum=None):
        reads = [in0] + [s for s in (s1, s2) if s is not None and not isinstance(s, (int, float))]
        writes = [out] + ([accum] if accum is not None else [])
        kw = {}
        if op1 is not None:
            kw["op1"] = op1
        if accum is not None:
            kw["accum_out"] = accum
        return self.op(eng, lambda e: e.tensor_scalar(out=out, in0=in0, scalar1=s1, scalar2=s2, op0=op0, **kw),
                       reads, writes)

    def stt(self, eng, out, in0, scalar, in1, op0, op1):
        reads = [in0, in1] + ([scalar] if not isinstance(scalar, (int, float)) else [])
        return self.op(eng, lambda e: e.scalar_tensor_tensor(out=out, in0=in0, scalar=scalar, in1=in1,
                                                             op0=op0, op1=op1), reads, [out])

    def amul(self, out, in_, val):
        return self.op("act", lambda e: e.mul(out=out, in_=in_, mul=val), [in_], [out])

    def memset(self, eng, ap, val):
        return self.op(eng, lambda e: e.memset(ap, val), [], [ap])

    def recip(self, out, in_):
        return self.op("dve", lambda e: e.reciprocal(out=out, in_=in_), [in_], [out])


def _consts():
    c = {}
    p = np.arange(128)
    c["c_ident"] = np.eye(128, dtype=np.float32)
    c["c_ones"] = np.ones((128, 128), np.float32)
    c["c_blk64"] = (p[:, None] // 64 == p[None, :] // 64).astype(np.float32)
    c["c_tri"] = (p[:, None] <= p[None, :]).astype(np.float32)
    c["c_cneg"] = np.where(p[:, None] > p[None, :], NEGM, 0.0).astype(np.float32)
    c["c_wneg"] = np.where(p[:, None] <= p[None, :], NEGM, 0.0).astype(np.float32)
    c["c_cnegtm"] = np.where(p[None, :] > p[:, None], -3e30, 0.0).astype(np.float32)
    s = np.arange(S)
    c["c_E"] = (s[None, :] // 64 == np.arange(32)[:, None]).astype(np.float32)
    n = np.arange(128)
    vis = (16 * n[:, None] + 31 <= s[None, :])
    c["c_visneg"] = np.where(vis, 0.0, NEGM).astype(np.float32)
    j = np.arange(32)
    cover = ((16 * n[:, None] < 64 * j[None, :] + 64) & (16 * n[:, None] + 32 > 64 * j[None, :]))
    ce = np.zeros((128, 33), np.float32)
    ce[:, 0] = 1.0
    ce[:, 1:] = cover
    ce[127] = 0.0
    c["c_cover"] = ce
    t = (np.arange(NT)[None, :, None] * 128 + p[:, None, None])
    cur = t // 64
    jj = j[None, None, :]
    forced = (jj == cur) | (jj == 0)
    future = jj > cur
    c["c_selkeep"] = (~(forced | future)).astype(np.float32)
    c["c_selbias"] = np.where(forced, 1e9, np.where(future, -1e9, 0.0)).astype(np.float32)
    return c


class Builder:
    def __init__(self, debug=None, nlayers=DEPTH):
        self.debug = debug
        self.nlayers = nlayers
        import os
        self.lim = int(os.environ.get("KLIM", "16"))
        nc = bass.Bass("TRN2", target_bir_lowering=False)
        self.nc = nc
        self.P = Prog(nc)
        self.inp = {}
        self.out = {}

    def din(self, name, shape, dt=F32):
        self.inp[name] = self.nc.dram_tensor(name, list(shape), dt, kind="ExternalInput").ap()
        return self.inp[name]

    def dout(self, name, shape, dt=F32):
        self.out[name] = self.nc.dram_tensor(name, list(shape), dt, kind="ExternalOutput").ap()
        return self.out[name]

    def load_consts(self):
        P = self.P
        C = {}
        shapes = {k: v.shape for k, v in _consts().items()}
        for k, shp in shapes.items():
            self.din(k, shp)
        self.wstage = [P.sb("wstage", [128, 4096], F32) for _ in range(2)]
        self.wsi = 0
        self.cast_engs = ["pool"]

        def ld(name, key, shape, dt, rows=None):
            t = P.sb(name, shape, dt)
            src = self.inp[key]
            if dt == F32:
                P.dma(t[:], src)
            else:
                n = int(np.prod(shape[1:]))
                flat = "p a b -> p (a b)" if len(shape) == 3 else None
                for a0 in range(0, n, 4096):
                    b0 = min(n, a0 + 4096)
                    stg = self.wstage[self.wsi % 2]
                    self.wsi += 1
                    P.dma(stg[0:shape[0], 0:b0 - a0], src[:, a0:b0])
                    P.cp("pool", t[:, a0:b0], stg[0:shape[0], 0:b0 - a0])
            return t
        C["ident_f"] = ld("ident_f", "c_ident", [128, 128], F32)
        C["ident"] = ld("ident", "c_ident", [128, 128], BF16)
        C["ones_f"] = ld("ones_f", "c_ones", [128, 128], F32)
        C["blk64_f"] = ld("blk64_f", "c_blk64", [128, 128], F32)
        C["tri_f"] = ld("tri_f", "c_tri", [128, 128], F32)
        C["cneg"] = ld("cneg", "c_cneg", [128, 128], BF16)
        C["zero"] = P.sb("zero", [128, 128], BF16)
        P.memset("pool", C["zero"][:], 0.0)
        C["cneg_f"] = ld("cneg_f", "c_cneg", [128, 128], F32)
        C["wneg"] = ld("wneg", "c_wneg", [128, 128], BF16)
        C["cnegtm"] = ld("cnegtm", "c_cnegtm", [128, 128], F32)
        C["E"] = ld("E", "c_E", [32, S], BF16)
        C["visneg"] = ld("visneg", "c_visneg", [128, S], BF16)
        C["cover"] = ld("cover", "c_cover", [128, 33], BF16)
        C["selkeep"] = ld("selkeep", "c_selkeep", [128, NT, 32], F32)
        C["selbias"] = ld("selbias", "c_selbias", [128, NT, 32], F32)
        self.C = C

    def wload(self, dst, wdram, l, c0, n, rows=None):
        w2 = wdram[l] if rows is None else wdram[l][rows[0]:rows[1]]
        src = w2.rearrange("(kc p) n -> p kc n", p=128)[:, :, c0:c0 + n]
        nk = src.shape[1]
        step = max(1, 4096 // nk)
        for a in range(0, n, step):
            b = min(n, a + step)
            stg = self.wstage[self.wsi % 2]
            self.wsi += 1
            v = stg[:, 0:nk * (b - a)].rearrange("p (k n) -> p k n", k=nk)
            self.P.dma(v, src[:, :, a:b], q="sp")
            eng = self.cast_engs[self.wsi % len(self.cast_engs)]
            self.P.cp(eng, dst[:, :, a:b], v)

    def pipelined(self, items, Sb, pTb, cnt, depth=1):
        P = self.P
        n = len(items)
        nb = len(Sb)
        assert nb >= depth + 1 and len(pTb) >= depth + 1
        slots = []
        for i in range(n + depth):
            if i < n:
                k = cnt["si"] % nb
                cnt["si"] += 1
                Sp, pT = Sb[k], pTb[k]
                items[i][0](Sp)
                P.act(pT[:], Sp[:, :], AF.Exp, scale=0.125)
                slots.append(pT)
            if i >= depth:
                items[i - depth][1](slots[i - depth])

    def proj_fm(self, ps, Wt, c0, M, hT, t0, N):
        for kc in range(8):
            self.P.mm(ps[0:M, 0:N], lhsT=Wt[:, kc, c0:c0 + M], rhs=hT[:, kc, t0:t0 + N],
                      start=(kc == 0), stop=(kc == 7))

    def proj_tm(self, ps, Wt, c0, n, hT, tile):
        for kc in range(8):
            self.P.mm(ps[:, 0:n], lhsT=hT[:, kc, tile * 128:(tile + 1) * 128], rhs=Wt[:, kc, c0:c0 + n],
                      start=(kc == 0), stop=(kc == 7))

    def norm_fm(self, ps, ps2, M, N, gcol, out, sq, rs):
        P, C = self.P, self.C
        P.act(sq[0:M, 0:N], ps[0:M, 0:N], AF.Square)
        P.mm(ps2[0:M, 0:N], lhsT=C["blk64_f"][0:M, 0:M], rhs=sq[0:M, 0:N])
        P.act(rs[0:M, 0:N], ps2[0:M, 0:N], AF.Sqrt, bias=EPS, scale=1.0 / 64)
        P.recip(rs[0:M, 0:N], rs[0:M, 0:N])
        P.stt("dve", out, ps[0:M, 0:N], gcol, rs[0:M, 0:N], ALU.mult, ALU.mult)

    def gain_col(self, name, vec64, stack):
        t = self.P.sb(name, [128, 1], F32, stack)
        src = vec64.rearrange("(p o) -> p o", o=1)
        self.P.dma(t[0:64, :], src)
        self.P.dma(t[64:128, :], src)
        return t

    def phase_norm(self, xd, gvec, hT, store_T=None):
        P, C = self.P, self.C
        with P.scope() as st:
            gb = P.sb("gb", [128, D], F32, st)
            P.dma(gb[:], gvec.partition_broadcast(128))
            xb = [P.sb("xb", [128, D], F32, st) for _ in range(2)]
            sq = P.sb("sq", [128, D], F32, st)
            hb = [P.sb("hb", [128, D], BF16, st) for _ in range(2)]
            ss = P.sb("ss", [128, 2], F32, st)
            pt = [P.ps("pt", [128, 8, 128], BF16, st) for _ in range(2)]
            for tt in range(NT):
                x_t = xb[tt % 2]
                P.dma(x_t[:], xd[tt * 128:(tt + 1) * 128, :])
                s1 = ss[:, tt % 2:tt % 2 + 1]
                P.act(sq[:], x_t[:], AF.Square, accum=s1)
                P.act(s1, s1, AF.Sqrt, bias=EPS, scale=1.0 / D)
                P.recip(s1, s1)
                h_t = hb[tt % 2]
                P.stt("dve", h_t[:], x_t[:], s1, gb[:], ALU.mult, ALU.mult)
                p_t = pt[tt % 2]
                for kc in range(8):
                    P.tr(p_t[:, kc, :], h_t[:, kc * 128:(kc + 1) * 128], C["ident"][:], inc=(kc == 7))
                P.cp("act" if tt % 2 else "dve", hT[:, :, tt * 128:(tt + 1) * 128], p_t[:])

    def phase_dsa(self, l, hT, ya_d):
        P, C = self.P, self.C
        I = self.inp
        self.cast_engs = ["pool", "act"]
        w_in = I["w_in"]
        with P.scope() as st:
            Wq = P.sb("Wq", [128, 8, 1024], BF16, st)
            self.wload(Wq, w_in, l, OFF["dq"], 1024)
            Wk = P.sb("Wk", [128, 8, 128], BF16, st)
            self.wload(Wk[:, :, 0:64], w_in, l, OFF["dk"], 64)
            self.wload(Wk[:, :, 64:128], w_in, l, OFF["dk"], 64)
            Wv = P.sb("Wv", [128, 8, 64], BF16, st)
            self.wload(Wv, w_in, l, OFF["dv"], 64)
            Wiq = P.sb("Wiq", [128, 8, 256], BF16, st)
            self.wload(Wiq, w_in, l, OFF["iq"], 256)
            Wik = P.sb("Wik", [128, 8, 128], BF16, st)
            for r in range(4):
                self.wload(Wik[:, :, r * 32:(r + 1) * 32], w_in, l, OFF["ik"], 32)
            Wiw = P.sb("Wiw", [128, 8, 8], BF16, st)
            self.wload(Wiw, w_in, l, OFF["iw"], 8)
            gq = self.gain_col("gq", I["dsa_q_norm"][l], st)
            gk = self.gain_col("gk", I["dsa_k_norm"][l], st)

            kT = P.sb("kT", [128, S], BF16, st)
            v1 = P.sb("v1", [128, NT, 65], BF16, st)
            ikbd = P.sb("ikbd", [128, NT, 4, 128], BF16, st)
            qz = P.sb("qz", [128, 16, 512], BF16, st)
            iqT = P.sb("iqT", [128, 2, S], BF16, st)
            iw = P.sb("iw", [128, NT, 8], F32, st)
            acc = P.sb("acc", [128, S], F32, st)
            work = P.sb("work", [128, S], F32, st)
            mx = P.sb("mx", [128, 8], F32, st)
            negm = P.sb("negm", [128, S], BF16, st)
            ikT4 = negm
            negmT = [P.sb("negmT", [128, NT, 128], BF16, st) for _ in range(2)]
            rb = [P.sb("rb", [128, 512], F32, st) for _ in range(2)]
            pTb = [P.sb("pTb", [128, 512], BF16, st) for _ in range(2)]
            ytm = P.sb("ytm", [128, 1024], BF16, st)
            yT = P.sb("yT", [128, 8, 512], BF16, st)
            sq = P.sb("sq", [128, 512], F32, st)
            rs = P.sb("rs", [128, 512], F32, st)
            rc = P.sb("rc", [128, 4], F32, st)
            bank = [P.ps("bk", [128, 512], F32, st) for _ in range(6)]
            tb = [P.ps("tb", [128, 8, 128], BF16, st) for _ in range(2)]
            Sb, Ob, Ib = bank[0:2], bank[2:4], bank[4:6]

            P.memset("pool", v1[:, :, 64:65], 1.0)
            P.memset("pool", ikbd[:], 0.0)
            P.memset("pool", qz[:], 0.0)
            for Q in range(4):
                t0 = Q * 512
                self.proj_fm(Ib[0], Wk, 0, 128, hT, t0, 512)
                self.norm_fm(Ib[0], Ib[1], 128, 512, gk[:, 0:1], kT[:, t0:t0 + 512], sq, rs)
                self.proj_fm(Ib[0], Wik, 0, 128, hT, t0, 512)
                P.cp("act", ikT4[:, t0:t0 + 512], Ib[0][:, :])
                for c in range(2):
                    self.proj_fm(Ib[c], Wiq, c * 128, 128, hT, t0, 512)
                    P.cp("act", iqT[:, c, t0:t0 + 512], Ib[c][:, :])
            for hh in range(4):
                P.dma(ikbd[32 * hh:32 * hh + 32, :, hh, :],
                      ikT4[32 * hh:32 * hh + 32, :].rearrange("p (k s) -> p k s", s=128))
            for tt in range(NT):
                ps = Ib[tt % 2]
                self.proj_tm(ps, Wv, 0, 64, hT, tt)
                P.cp("act", v1[:, tt, 0:64], ps[:, 0:64])
                self.proj_tm(ps, Wiw, 64, 8, hT, tt) if False else None
            for tt in range(NT):
                ps = Ib[tt % 2]
                self.proj_tm(ps, Wiw, 0, 8, hT, tt)
                P.cp("act", iw[:, tt, :], ps[:, 0:8])

            cnt = {"si": 0, "ii": 0}

            def stage_q(Q):
                t0 = Q * 512
                for c in range(8):
                    self.proj_fm(Ib[0], Wq, c * 128, 128, hT, t0, 512)
                    P.act(sq[:, :], Ib[0][:, :], AF.Square)
                    P.mm(Ib[1][:, :], lhsT=C["blk64_f"][:], rhs=sq[:, :])
                    P.act(rs[:, :], Ib[1][:, :], AF.Sqrt, bias=EPS, scale=1.0 / 64)
                    P.recip(rs[:, :], rs[:, :])
                    for hf in range(2):
                        psl = slice(64 * hf, 64 * hf + 64)
                        P.stt("dve", qz[psl, 2 * c + hf, :], Ib[0][psl, :], gq[psl, 0:1], rs[psl, :], ALU.mult, ALU.mult)

            def stage_a(qt):
                nk = qt + 1
                tsl = slice(qt * 128, (qt + 1) * 128)
                if qt < 2:
                    P.memset("pool", negm[:, 0:nk * 128], 0.0)
                    return
                for kt in range(nk):
                    a_kt = acc[:, kt * 128:(kt + 1) * 128]
                    for c in range(2):
                        ps = Ib[cnt["ii"] % 2]
                        r = rb[cnt["ii"] % 2]
                        cnt["ii"] += 1
                        P.mm(ps[:, :], lhsT=iqT[:, c, tsl], rhs=ikbd[:, kt].rearrange("p a b -> p (a b)"))
                        P.act(r[:], ps[:, :], AF.Relu)
                        for hh in range(4):
                            h = 4 * c + hh
                            if h == 0:
                                P.ts("dve", a_kt, r[:, 0:128], iw[:, qt, 0:1], None, op0=ALU.mult)
                            else:
                                P.stt("dve", a_kt, r[:, hh * 128:(hh + 1) * 128], iw[:, qt, h:h + 1], a_kt,
                                      ALU.mult, ALU.add)
                d_sl = slice(qt * 128, (qt + 1) * 128)
                P.tt("dve", acc[:, d_sl], acc[:, d_sl], C["cnegtm"][:], ALU.add)
                cur = acc
                for r_ in range(32):
                    P.op("dve", lambda e: e.max(out=mx[:], in_=cur[:, 0:nk * 128]),
                         [cur[:, 0:nk * 128]], [mx[:]])
                    P.op("dve", lambda e: e.match_replace(out=work[:, 0:nk * 128], in_to_replace=mx[:],
                                                          in_values=cur[:, 0:nk * 128], imm_value=-1e30),
                         [mx[:], cur[:, 0:nk * 128]], [work[:, 0:nk * 128]])
                    cur = work
                P.ts("dve", negm[:, 0:nk * 128], work[:, 0:nk * 128], -1e29, 1.0, op0=ALU.is_le,
                     op1=ALU.subtract)

            def stage_b(qt):
                nk = qt + 1
                nT = negmT[qt % 2]
                for k0 in range(0, nk, 8):
                    k1 = min(nk, k0 + 8)
                    t_ = tb[0]
                    for kt in range(k0, k1):
                        P.tr(t_[:, kt - k0, :], negm[:, kt * 128:(kt + 1) * 128], C["ident"][:],
                             inc=(kt == k1 - 1))
                    P.amul(nT[:, k0:k1, :], t_[:, 0:k1 - k0, :], -NEGM)
                P.tt("pool", nT[:, qt, :], nT[:, qt, :], C["cneg"][:], ALU.add)

            def stage_c(qt):
                nk = qt + 1
                tl = qt % 4
                tsl = slice(tl * 128, (tl + 1) * 128)
                nT = negmT[qt % 2]
                items = []
                for hg in range(4):
                    O = Ob[hg % 2]
                    O3 = O[:, 0:260].rearrange("p (h e) -> p h e", e=65)
                    for kt in range(nk):
                        def em_s(Sp, hg=hg, kt=kt):
                            P.mm(Sp[:, :], lhsT=C["ident"][:], rhs=nT[:, kt, :].unsqueeze(1).to_broadcast([128, 4, 128]),
                                 start=True, stop=False, inc=False)
                            P.mm(Sp[:, :], lhsT=kT[:, kt * 128:(kt + 1) * 128], rhs=qz[:, 4 * hg:4 * hg + 4, tsl],
                                 start=False, stop=True)

                        def em_pv(pT, hg=hg, kt=kt, O3=O3):
                            for hh in range(4):
                                P.mm(O3[:, hh, :], lhsT=pT[:, hh * 128:(hh + 1) * 128], rhs=v1[:, kt, :],
                                     start=(kt == 0 and hh == 0), stop=(kt == nk - 1), inc=(hh == 3), sgc=True)
                            if kt == nk - 1:
                                P.recip(rc[:], O3[:, :, 64])
                                P.tt("dve", ytm[:, hg * 256:(hg + 1) * 256].rearrange("p (h e) -> p h e", e=64),
                                     O3[:, :, 0:64], rc[:].unsqueeze(2).to_broadcast([128, 4, 64]), ALU.mult)
                        items.append((em_s, em_pv))
                self.pipelined(items, Sb, pTb, cnt)
                t_ = tb[1]
                for c in range(8):
                    P.tr(t_[:, c, :], ytm[:, c * 128:(c + 1) * 128], C["ident"][:], inc=(c == 7))
                P.cp("act", yT[:, :, tsl], t_[:])
                if tl == 3:
                    Q = qt // 4
                    P.dma(ya_d[:, :, Q * 512:(Q + 1) * 512], yT[:], q="sp")

            nq = min(NT, self.lim)
            stage_a(0)
            stage_b(0)
            for qt in range(nq):
                P.maybe_barrier()
                if qt + 1 < nq:
                    stage_a(qt + 1)
                if qt % 4 == 0:
                    stage_q(qt // 4)
                stage_c(qt)
                if qt + 1 < nq:
                    stage_b(qt + 1)

    def _nsa_kside(self, l, hT, st, env):
        P, C = self.P, self.C
        I = self.inp
        Wkv, Wks, Wkw = env["Wkv"], env["Wks"], env["Wkw"]
        gks, gkw, gkc = env["gks"], env["gkw"], env["gkc"]
        ksT, kwT, vs1, vw1, kcmpT, vext = env["ksT"], env["kwT"], env["vs1"], env["vw1"], env["kcmpT"], env["vext"]
        sq, rs, ss, Mb, tb = env["sq"], env["rs"], env["ss"], env["Mb"], env["tb"]
        cmpw = []
        posB = []
        for kv in range(2):
            cw = P.sb("cmpw", [128, 32, 64], BF16, st)
            stg = self.wstage[self.wsi % 2]
            self.wsi += 1
            sv = stg[:, 0:2048].rearrange("p (l e) -> p l e", e=64)
            src = I["nsa_cmp_w"][l, kv].rearrange("l d e -> d l e")
            P.dma(sv[0:64], src)
            P.dma(sv[64:128], src)
            P.cp("pool", cw[:], sv)
            cmpw.append(cw)
            pt_ = P.sb("posT", [128, 32], F32, st)
            psrc = I["nsa_cmp_pos"][l, kv].rearrange("l d -> d l")
            P.dma(pt_[0:64, :], psrc, allow_slow_non_contiguous=True)
            P.dma(pt_[64:128, :], psrc, allow_slow_non_contiguous=True)
            pb = P.sb("posB", [128, 32, 127], BF16, st)
            P.cp("pool", pb[:], pt_[:].unsqueeze(2).to_broadcast([128, 32, 127]))
            posB.append(pb)

        kcT = P.sb("kcT", [128, S], BF16, st)
        vcT = P.sb("vcT", [128, S], BF16, st)
        kd = P.sb("kd", [128, 128], BF16, st)
        P.memset("pool", vs1[:, :, :, 64:65], 1.0)
        P.memset("pool", vw1[:, :, :, 64:65], 1.0)
        P.memset("pool", kd[:], 0.0)
        for g in range(2):
            P.memset("pool", vext[g][:], 0.0)
        for Q in range(4):
            t0 = Q * 512
            self.proj_fm(Mb[0], Wkv, 0, 128, hT, t0, 512)
            P.cp("act", kcT[:, t0:t0 + 512], Mb[0][:, :])
            self.proj_fm(Mb[1], Wkv, 128, 128, hT, t0, 512)
            P.cp("act", vcT[:, t0:t0 + 512], Mb[1][:, :])
            for g in range(2):
                self.proj_fm(Mb[0], Wks, g * 128, 128, hT, t0, 512)
                self.norm_fm(Mb[0], Mb[1], 128, 512, gks[:, 0:1], ksT[g][:, t0:t0 + 512], sq, rs)
                self.proj_fm(Mb[0], Wkw, g * 128, 128, hT, t0, 512)
                self.norm_fm(Mb[0], Mb[1], 128, 512, gkw[:, 0:1], kwT[g][:, t0:t0 + 512], sq, rs)
        for tt in range(NT):
            ps = Mb[tt % 2]
            self.proj_tm(ps, Wkv, 384, 128, hT, tt)
            P.cp("act", vs1[:, tt, :, 0:64], ps[:, 0:128].rearrange("p (g e) -> p g e", e=64))
            self.proj_tm(ps, Wkv, 640, 128, hT, tt)
            P.cp("act", vw1[:, tt, :, 0:64], ps[:, 0:128].rearrange("p (g e) -> p g e", e=64))
        P.maybe_barrier()
        def strided(t, lo, off):
            return bass.AP(t[:].tensor, lo * S + off, [[S, 64], [16, 127]])
        for g in range(2):
            lo = g * 64
            for kv, src in ((0, kcT), (1, vcT)):
                ps = Mb[kv]
                for li in range(32):
                    P.mm(ps[0:127, 0:64], lhsT=strided(src, lo, li), rhs=cmpw[kv][lo:lo + 64, li, :],
                         start=(li == 0), stop=False, inc=False)
                for li in range(32):
                    P.mm(ps[0:127, 0:64], lhsT=posB[kv][lo:lo + 64, li, :], rhs=cmpw[kv][lo:lo + 64, li, :],
                         start=False, stop=(li == 31), inc=(li == 31))
            P.act(sq[0:127, 0:64], Mb[0][0:127, 0:64], AF.Square, accum=ss[0:127, :])
            P.act(ss[0:127, :], ss[0:127, :], AF.Sqrt, bias=EPS, scale=1.0 / 64)
            P.recip(ss[0:127, :], ss[0:127, :])
            P.stt("dve", kd[0:127, 0:64], Mb[0][0:127, 0:64], ss[0:127, 0:1], gkc[0:127, :], ALU.mult, ALU.mult)
            P.cp("pool", kd[0:127, 64:128], kd[0:127, 0:64])
            P.tr(tb[0][:, 0, :], kd[:], C["ident"][:])
            P.cp("act", kcmpT[g][:], tb[0][:, 0, :])
            P.cp("act", vext[g][0:127, 0:64], Mb[1][0:127, 0:64])
            P.cp("pool", vext[g][0:127, 64:97], C["cover"][0:127, :])
        P.maybe_barrier()


    def phase_nsa(self, l, hT, yb_d):
        P, C = self.P, self.C
        I = self.inp
        self.cast_engs = ["pool", "dve"]
        w_in = I["w_in"]
        KV = OFF["nkv"]
        with P.scope() as st:
            Wq = P.sb("Wnq", [128, 8, 1024], BF16, st)
            self.wload(Wq, w_in, l, OFF["nq"], 1024)
            Wkv = P.sb("Wkv", [128, 8, 768], BF16, st)
            self.wload(Wkv, w_in, l, KV, 768)
            Wks = P.sb("Wks", [128, 8, 256], BF16, st)
            Wkw = P.sb("Wkw", [128, 8, 256], BF16, st)
            for g in range(2):
                for r in range(2):
                    self.wload(Wks[:, :, g * 128 + r * 64:g * 128 + r * 64 + 64], w_in, l, KV + 256 + g * 64, 64)
                    self.wload(Wkw[:, :, g * 128 + r * 64:g * 128 + r * 64 + 64], w_in, l, KV + 512 + g * 64, 64)
            Wng = P.sb("Wng", [128, 8, 48], BF16, st)
            self.wload(Wng, w_in, l, OFF["ng"], 48)
            gq = self.gain_col("gnq", I["nsa_q_norm"][l], st)
            gks = self.gain_col("gks", I["nsa_k_norm"][l, 1], st)
            gkw = self.gain_col("gkw", I["nsa_k_norm"][l, 2], st)
            gkc = P.sb("gkc", [128, 64], F32, st)
            P.dma(gkc[:], I["nsa_k_norm"][l, 0].partition_broadcast(128))
            ksT = [P.sb("ksT", [128, S], BF16, st) for _ in range(2)]
            kwT = [P.sb("kwT", [128, S], BF16, st) for _ in range(2)]
            vs1 = P.sb("vs1", [128, NT, 2, 65], BF16, st)
            vw1 = P.sb("vw1", [128, NT, 2, 65], BF16, st)
            kcmpT = [P.sb("kcmpT", [128, 128], BF16, st) for _ in range(2)]
            vext = [P.sb("vext", [128, 97], BF16, st) for _ in range(2)]
            sq = P.sb("sq", [128, 512], F32, st)
            rs = P.sb("rs", [128, 512], F32, st)
            ss = P.sb("ss1", [128, 1], F32, st)
            bank = [P.ps("bk", [128, 512], F32, st) for _ in range(6)]
            tb = [P.ps("tb", [128, 8, 128], BF16, st) for _ in range(2)]
            Sb, Ob, Mb = bank[0:2], bank[2:4], bank[4:6]
            with P.scope() as st2:
                self._nsa_kside(l, hT, st2, locals())
            nqz = P.sb("nqz", [128, 16, 512], BF16, st)
            P.memset("pool", nqz[:], 0.0)
            gs = P.sb("gs", [128, 48], F32, st)
            yacc = P.sb("yacc", [128, 16, 64], F32, st)
            ytmp = P.sb("ytmp", [128, 4, 64], F32, st)
            impn = P.sb("impn", [128, 16, 32], F32, st)
            imp = P.sb("imp", [128, 32], F32, st)
            mx = P.sb("mx8", [128, 8], F32, st)
            negblk = P.sb("negblk", [128, 32], BF16, st)
            negblkT = [P.sb("negblkT", [32, 128], BF16, st) for _ in range(2)]
            pTb = [P.sb("pTb", [128, 512], BF16, st) for _ in range(3)]
            Sb = [bank[0], bank[1], bank[5]]
            ytm = P.sb("ytm", [128, 1024], BF16, st)
            yT = P.sb("yT", [128, 8, 512], BF16, st)
            rc = P.sb("rc", [128, 4], F32, st)
            cf = P.sb("cf", [128, 4], F32, st)
            Osb = [P.sb("Osb", [128, 4, 97], F32, st) for _ in range(2)]
            fin_cnt = [0]

            cnt = {"si": 0}
            oi = 0
            for Q in range(4):
                t0 = Q * 512
                for c in range(8):
                    self.proj_fm(Mb[0], Wq, c * 128, 128, hT, t0, 512)
                    P.act(sq[:, :], Mb[0][:, :], AF.Square)
                    P.mm(Mb[1][:, :], lhsT=C["blk64_f"][:], rhs=sq[:, :])
                    P.act(rs[:, :], Mb[1][:, :], AF.Sqrt, bias=EPS, scale=1.0 / 64)
                    P.recip(rs[:, :], rs[:, :])
                    for hf in range(2):
                        psl = slice(64 * hf, 64 * hf + 64)
                        P.stt("dve", nqz[psl, 2 * c + hf, :], Mb[0][psl, :], gq[psl, 0:1], rs[psl, :], ALU.mult, ALU.mult)
                for tl in range(4):
                    qt = Q * 4 + tl
                    if qt >= self.lim:
                        continue
                    P.maybe_barrier()
                    tsl = slice(tl * 128, (tl + 1) * 128)
                    self.proj_tm(Mb[0], Wng, 0, 48, hT, qt)
                    P.act(gs[:], Mb[0][:, 0:48], AF.Sigmoid)
                    gs3 = gs[:].rearrange("p (h b) -> p h b", b=3)

                    def scores(Sp, sub, kT_g, kt_cols, masks):
                        first = True
                        for (ml, mr) in masks:
                            kk = mr.shape[0]
                            P.mm(Sp[:, :], lhsT=ml, rhs=mr.unsqueeze(1).to_broadcast([kk, 4, 128]),
                                 start=first, stop=False, inc=False)
                            first = False
                        P.mm(Sp[:, :], lhsT=kT_g[:, kt_cols], rhs=nqz[:, 4 * sub:4 * sub + 4, tsl],
                             start=first, stop=True)

                    def finalize(O3p, sub, br, first):
                        w_ = 97 if br == 0 else 65
                        O3 = Osb[fin_cnt[0] % 2][:, :, 0:w_]
                        fin_cnt[0] += 1
                        P.cp("act", O3, O3p)
                        hs = slice(4 * sub, 4 * sub + 4)
                        P.ts("dve", rc[:], O3[:, :, 64], 1e-30, None, op0=ALU.max)
                        P.recip(rc[:], rc[:])
                        if br == 0:
                            P.tt("dve", impn[:, hs, :], O3[:, :, 65:97], rc[:].unsqueeze(2).to_broadcast([128, 4, 32]),
                                 ALU.mult)
                        P.tt("dve", cf[:], rc[:], gs3[:, hs, br], ALU.mult)
                        if first:
                            P.tt("dve", yacc[:, hs, :], O3[:, :, 0:64], cf[:].unsqueeze(2).to_broadcast([128, 4, 64]),
                                 ALU.mult)
                        else:
                            P.tt("dve", ytmp[:], O3[:, :, 0:64], cf[:].unsqueeze(2).to_broadcast([128, 4, 64]),
                                 ALU.mult)
                            P.tt("pool", yacc[:, hs, :], yacc[:, hs, :], ytmp[:], ALU.add)

                    items = []
                    for sub in range(4):
                        g = sub // 2
                        O = Ob[oi % 2]
                        oi += 1
                        O3 = O[:, 0:388].rearrange("p (h e) -> p h e", e=97)

                        def em_s(Sp, sub=sub, g=g):
                            scores(Sp, sub, kcmpT[g], slice(0, 128),
                                   [(C["ident"][:], C["visneg"][:, qt * 128:(qt + 1) * 128])])

                        def em_pv(pT, sub=sub, g=g, O3=O3):
                            for hh in range(4):
                                P.mm(O3[:, hh, :], lhsT=pT[:, hh * 128:(hh + 1) * 128], rhs=vext[g][:, :],
                                     start=(hh == 0), stop=True, inc=(hh == 3), sgc=True)
                            finalize(O3, sub, 0, True)
                        items.append((em_s, em_pv))
                    self.pipelined(items, Sb, pTb, cnt, depth=2)
                    for g in range(2):
                        P.op("dve", lambda e: e.tensor_reduce(out=imp[:], in_=impn[:, 8 * g:8 * g + 8, :].rearrange("p h j -> p j h"),
                                                              axis=AX.X, op=ALU.add),
                             [impn[:, 8 * g:8 * g + 8, :]], [imp[:]])
                        P.tt("dve", imp[:], imp[:], C["selkeep"][:, qt, :], ALU.mult)
                        P.tt("dve", imp[:], imp[:], C["selbias"][:, qt, :], ALU.add)
                        P.op("dve", lambda e: e.max(out=mx[:], in_=imp[:]), [imp[:]], [mx[:]])
                        P.ts("dve", negblk[:], imp[:], mx[:, 3:4], 1.0, op0=ALU.is_ge, op1=ALU.subtract)
                        P.tr(tb[0][0:32, 0, :], negblk[:], C["ident"][:])
                        P.amul(negblkT[g][:], tb[0][0:32, 0, :], -NEGM)
                    items = []
                    for br in (2, 1):
                        kts = list(range(0, qt + 1)) if br == 1 else list(range(max(0, qt - 4), qt + 1))
                        kT_l = ksT if br == 1 else kwT
                        v_l = vs1 if br == 1 else vw1
                        for sub in range(4):
                            g = sub // 2
                            O = Ob[oi % 2]
                            oi += 1
                            O3 = O[:, 0:260].rearrange("p (h e) -> p h e", e=65)
                            for ki, kt in enumerate(kts):
                                masks = []
                                if br == 1:
                                    masks.append((C["E"][:, kt * 128:(kt + 1) * 128], negblkT[g][:]))
                                if kt == qt:
                                    masks.append((C["ident"][:], C["cneg"][:]))
                                if br == 2 and kt == qt - 4:
                                    masks.append((C["ident"][:], C["wneg"][:]))

                                def em_s(Sp, sub=sub, g=g, kt=kt, masks=masks, kT_l=kT_l):
                                    scores(Sp, sub, kT_l[g], slice(kt * 128, (kt + 1) * 128), masks)

                                def em_pv(pT, sub=sub, g=g, kt=kt, ki=ki, nkt=len(kts), O3=O3, v_l=v_l, br=br):
                                    for hh in range(4):
                                        P.mm(O3[:, hh, :], lhsT=pT[:, hh * 128:(hh + 1) * 128], rhs=v_l[:, kt, g, :],
                                             start=(ki == 0 and hh == 0), stop=(ki == nkt - 1), inc=(hh == 3), sgc=True)
                                    if ki == nkt - 1:
                                        finalize(O3, sub, br, False)
                                items.append((em_s, em_pv))
                    self.pipelined(items, Sb, pTb, cnt, depth=2)
                    P.cp("act", ytm[:], yacc[:].rearrange("p h e -> p (h e)"))
                    t_ = tb[1]
                    for c in range(8):
                        P.tr(t_[:, c, :], ytm[:, c * 128:(c + 1) * 128], C["ident"][:], inc=(c == 7))
                    P.cp("act", yT[:, :, tsl], t_[:])
                P.dma(yb_d[:, :, t0:t0 + 512], yT[:], q="sp")

    def phase_ssd(self, l, hT, yc_d, zs_d, xact_d):
        P, C = self.P, self.C
        I = self.inp
        self.cast_engs = ["pool", "dve"]
        w_in = I["w_in"]
        with P.scope() as st:
            Wz = P.sb("Wz", [128, 8, 2048], BF16, st)
            self.wload(Wz, w_in, l, OFF["sz"], 2048)
            zt = [P.sb("zt", [128, 2048], F32, st) for _ in range(2)]
            bank = [P.ps("bk", [128, 512], F32, st) for _ in range(4)]
            bi = 0
            for tt in range(NT):
                z_t = zt[tt % 2]
                for nb in range(4):
                    ps = bank[bi % 4]
                    bi += 1
                    self.proj_tm(ps, Wz, nb * 512, 512, hT, tt)
                    P.act(z_t[:, nb * 512:(nb + 1) * 512], ps[:, :], AF.Silu)
                P.dma(zs_d[tt * 128:(tt + 1) * 128, :], z_t[:])
        with P.scope() as st:
            cw = P.sb("cw", [128, 24, 4], F32, st)
            cb = P.sb("cb", [128, 24], F32, st)
            for cc in range(24):
                P.dma(cw[:, cc, :], I["ssd_conv_w"][l][:, cc * 128:(cc + 1) * 128].rearrange("k c -> c k"),
                      allow_slow_non_contiguous=True)
                P.dma(cb[:, cc:cc + 1], I["ssd_conv_b"][l][cc * 128:(cc + 1) * 128].rearrange("(c o) -> c o", o=1))
            Wx = [P.sb("Wx", [128, 8, 512], BF16, st) for _ in range(2)]
            raw = [P.sb("raw", [128, 3 + S], F32, st) for _ in range(2)]
            accb = [P.sb("accb", [128, S], F32, st) for _ in range(2)]
            xa = [P.sb("xa", [128, S], BF16, st) for _ in range(2)]
            ctmp = P.sb("ctmp", [128, S], F32, st)
            bank = [P.ps("bk", [128, 512], F32, st) for _ in range(4)]
            for r_ in raw:
                P.memset("pool", r_[:, 0:3], 0.0)
            bi = 0
            for cc in range(24):
                W_ = Wx[(cc // 4) % 2]
                if cc % 4 == 0:
                    self.wload(W_, w_in, l, OFF["sxbc"] + cc * 128, 512)
                rw = raw[cc % 2]
                for Q in range(4):
                    ps = bank[bi % 4]
                    bi += 1
                    self.proj_fm(ps, W_, (cc % 4) * 128, 128, hT, Q * 512, 512)
                    P.cp("act", rw[:, 3 + Q * 512:3 + (Q + 1) * 512], ps[:, :])
                eng = "dve"
                ac = accb[cc % 2]
                P.ts(eng, ac[:], rw[:, 3:3 + S], cw[:, cc, 3:4], None, op0=ALU.mult)
                for k in (2, 1, 0):
                    if eng == "dve":
                        P.stt(eng, ac[:], rw[:, k:k + S], cw[:, cc, k:k + 1], ac[:], ALU.mult, ALU.add)
                    else:
                        P.ts(eng, ctmp[:], rw[:, k:k + S], cw[:, cc, k:k + 1], None, op0=ALU.mult)
                        P.tt(eng, ac[:], ac[:], ctmp[:], ALU.add)
                P.act(xa[cc % 2][:], ac[:], AF.Silu, bias=cb[:, cc:cc + 1])
                P.dma(xact_d[:, cc, :], xa[cc % 2][:])
                P.maybe_barrier()
        with P.scope() as st:
            Wdt = P.sb("Wdt", [128, 8, 32], BF16, st)
            self.wload(Wdt, w_in, l, OFF["sdt"], 32)
            dtb = P.sb("dtb", [128, 32], F32, st)
            P.dma(dtb[:], I["ssd_dt_bias"][l].partition_broadcast(128))
            a_b = P.sb("a_b", [128, 32], F32, st)
            P.dma(a_b[:], I["ssd_a_log"][l].partition_broadcast(128))
            P.act(a_b[:], a_b[:], AF.Exp)
            P.ts("dve", a_b[:], a_b[:], -1.0, None, op0=ALU.mult)
            D_b = P.sb("D_b", [128, 32], F32, st)
            P.dma(D_b[:], I["ssd_d"][l].partition_broadcast(128))
            ng_b = P.sb("ng_b", [128, 2048], F32, st)
            P.dma(ng_b[:], I["ssd_norm_g"][l].partition_broadcast(128))
            xab = [P.sb("xab", [128, 24, 128], BF16, st) for _ in range(2)]
            zsb = [P.sb("zsb", [128, 2048], F32, st) for _ in range(2)]
            xs_tm = P.sb("xs_tm", [128, 2048], BF16, st)
            bm_b = [P.sb("bm_tm", [128, 512], BF16, st) for _ in range(2)]
            xsw_b = [P.sb("xs_w", [128, 2048], BF16, st) for _ in range(2)]
            rhs1_b = [P.sb("rhs1", [128, 8, 128], F32, st) for _ in range(2)]
            rhs2_b = [P.sb("rhs2", [128, 8, 128], F32, st)] * 2
            Lg_b = [P.sb("Lg", [128, 8, 128], BF16, st) for _ in range(2)]
            MTg_b = [P.sb("MTg", [128, 8, 128], BF16, st) for _ in range(2)]
            tmp_b = [P.sb("tmpb", [128, 512], F32, st) for _ in range(2)]
            cbT = P.sb("cbT", [128, 4, 128], BF16, st)
            H = [P.sb("H", [128, 512], F32, st) for _ in range(4)]
            Hbf = [P.sb("Hbf", [128, 512], BF16, st) for _ in range(4)]
            y_b = [P.sb("y", [128, 2048], F32, st) for _ in range(2)]
            ynb = P.sb("ynb", [128, 2048], BF16, st)
            ycT = P.sb("ycT", [128, 16, 128], BF16, st)
            sma = {n: P.sb(n, [128, NT, 32], F32, st) for n in ("dtc", "da", "nb", "ea", "wj", "dec")}
            sma["lndt"] = sma["nb"]
            sma["acum"] = sma["ea"]
            sma["alast"] = sma["dec"]
            ss = P.sb("ss2", [128, 1], F32, st)
            R = [P.ps("R", [128, 512], F32, st) for _ in range(2)]
            Yp_b = [P.ps("Yp", [128, 512], F32, st) for _ in range(2)]
            Yo = P.ps("Yo", [128, 512], F32, st)
            STp = P.ps("STp", [128, 512], F32, st)
            M0 = P.ps("M0", [128, 512], F32, st)
            M1 = M0
            tb = P.ps("tb", [128, 8, 128], BF16, st)
            for g in range(4):
                P.memset("pool", H[g][:], 0.0)
                P.memset("pool", Hbf[g][:], 0.0)
            nch = min(NT, self.lim)
            fl = lambda t: t[:].rearrange("p c h -> p (c h)")
            for c in range(NT):
                for kc in range(8):
                    P.mm(M1[:, c * 32:(c + 1) * 32], lhsT=hT[:, kc, c * 128:(c + 1) * 128], rhs=Wdt[:, kc, :],
                         start=(c == 0 and kc == 0), stop=(kc == 7), inc=(kc == 7), sgc=True)
            P.tt("dve", sma["dtc"][:], M1[:, :].rearrange("p (c h) -> p c h", h=32),
                 dtb[:].unsqueeze(1).to_broadcast([128, NT, 32]), ALU.add)
            P.act(fl(sma["dtc"]), fl(sma["dtc"]), AF.Exp)
            P.act(fl(sma["dtc"]), fl(sma["dtc"]), AF.Ln, bias=1.0)
            P.act(fl(sma["lndt"]), fl(sma["dtc"]), AF.Ln)
            P.tt("dve", sma["da"][:], sma["dtc"][:], a_b[:].unsqueeze(1).to_broadcast([128, NT, 32]), ALU.mult)
            P.mm(M1[:, :], lhsT=C["tri_f"][:], rhs=fl(sma["da"]))
            P.cp("dve", fl(sma["acum"]), M1[:, :])
            P.mm(M1[:, :], lhsT=C["ones_f"][:], rhs=fl(sma["da"]))
            P.cp("dve", fl(sma["alast"]), M1[:, :])
            P.tt("dve", sma["nb"][:], sma["lndt"][:], sma["acum"][:], ALU.subtract)
            P.tt("dve", sma["wj"][:], sma["alast"][:], sma["acum"][:], ALU.subtract)
            P.act(fl(sma["wj"]), fl(sma["wj"]), AF.Exp)
            P.tt("dve", sma["wj"][:], sma["wj"][:], sma["dtc"][:], ALU.mult)
            P.act(fl(sma["ea"]), fl(sma["acum"]), AF.Exp)
            P.act(fl(sma["dec"]), fl(sma["alast"]), AF.Exp)
            sma = {k: sma[k] for k in ("da", "nb", "ea", "wj", "dec")}

            def load(c):
                csl = slice(c * 128, (c + 1) * 128)
                P.dma(xab[c % 2][:], xact_d[:, :, csl])
                P.dma(zsb[c % 2][:], zs_d[csl, :])

            def part1(c):
                p = c % 2
                xa_ = xab[p]
                sm = {k: v[:, c, :] for k, v in sma.items()}
                bm_tm, xs_w, y = bm_b[p], xsw_b[p], y_b[p]
                for k0 in (0, 8, 16):
                    n = 8 if k0 < 16 else 4
                    for j in range(n):
                        P.tr(tb[:, j, :], xa_[:, k0 + j, :], C["ident"][:], inc=(j == n - 1))
                    if k0 < 16:
                        P.cp("act", xs_tm[:, k0 * 128:(k0 + 8) * 128], tb[:].rearrange("p a b -> p (a b)"))
                    else:
                        P.cp("act", bm_tm[:], tb[:, 0:4, :].rearrange("p a b -> p (a b)"))
                for g in range(4):
                    P.mm(M0[:, g * 128:(g + 1) * 128], lhsT=xa_[:, 16 + g, :], rhs=xa_[:, 20 + g, :],
                         start=(g == 0), stop=True, inc=(g == 3), sgc=True)
                P.cp("act", cbT[:].rearrange("p a b -> p (a b)"), M0[:, :])
                P.tt("dve", xs_w[:].rearrange("p (h e) -> p h e", e=64), xs_tm[:].rearrange("p (h e) -> p h e", e=64),
                     sm["wj"][:].unsqueeze(2).to_broadcast([128, 32, 64]), ALU.mult)
                for g in range(4):
                    hs = slice(8 * g, 8 * g + 8)
                    gsl = slice(g * 512, (g + 1) * 512)
                    rhs1, rhs2, Lg, MTg, Yp, tmp = rhs1_b[g % 2], rhs2_b[g % 2], Lg_b[g % 2], MTg_b[g % 2], Yp_b[g % 2], tmp_b[g % 2]
                    P.tt("dve", rhs1[:], C["tri_f"][:].unsqueeze(1).to_broadcast([128, 8, 128]),
                         sm["da"][:, hs].unsqueeze(2).to_broadcast([128, 8, 128]), ALU.mult)
                    P.tt("dve", rhs2[:], C["cneg_f"][:].unsqueeze(1).to_broadcast([128, 8, 128]),
                         sm["nb"][:, hs].unsqueeze(2).to_broadcast([128, 8, 128]), ALU.add)
                    for hf in range(2):
                        P.mm(R[hf][:, :], lhsT=C["ones_f"][:], rhs=rhs1[:, 4 * hf:4 * hf + 4, :].rearrange("p a b -> p (a b)"),
                             start=True, stop=False, inc=False)
                        P.mm(R[hf][:, :], lhsT=C["ident_f"][:], rhs=rhs2[:, 4 * hf:4 * hf + 4, :].rearrange("p a b -> p (a b)"),
                             start=False, stop=True)
                        P.act(Lg[:, 4 * hf:4 * hf + 4, :].rearrange("p a b -> p (a b)"), R[hf][:, :], AF.Exp)
                    P.tt("dve" if g % 2 else "pool", MTg[:], Lg[:], cbT[:, g, :].unsqueeze(1).to_broadcast([128, 8, 128]), ALU.mult)
                    for hh in range(8):
                        P.mm(Yp[:, hh * 64:(hh + 1) * 64], lhsT=MTg[:, hh, :], rhs=xs_tm[:, (8 * g + hh) * 64:(8 * g + hh + 1) * 64],
                             start=(hh == 0), stop=True, inc=(hh == 7), sgc=True)
                    P.tt("pool", tmp[:].rearrange("p (h e) -> p h e", e=64), xs_tm[:, gsl].rearrange("p (h e) -> p h e", e=64),
                         D_b[:, hs].unsqueeze(2).to_broadcast([128, 8, 64]), ALU.mult)
                    P.tt("dve", y[:, gsl], Yp[:, :], tmp[:], ALU.add)

            def part2(c):
                p = c % 2
                csl = slice(c * 128, (c + 1) * 128)
                xa_ = xab[p]
                zs_ = zsb[p]
                sm = {k: v[:, c, :] for k, v in sma.items()}
                bm_tm, xs_w, y = bm_b[p], xsw_b[p], y_b[p]
                for g in range(4):
                    hs = slice(8 * g, 8 * g + 8)
                    gsl = slice(g * 512, (g + 1) * 512)
                    tmp = tmp_b[g % 2]
                    P.mm(Yo[:, :], lhsT=xa_[:, 20 + g, :], rhs=Hbf[g][:])
                    P.tt("dve", tmp[:].rearrange("p (h e) -> p h e", e=64), Yo[:, :].rearrange("p (h e) -> p h e", e=64),
                         sm["ea"][:, hs].unsqueeze(2).to_broadcast([128, 8, 64]), ALU.mult)
                    P.tt("pool", y[:, gsl], y[:, gsl], tmp[:], ALU.add)
                    P.mm(STp[:, :], lhsT=bm_tm[:, g * 128:(g + 1) * 128], rhs=xs_w[:, gsl])
                    Hv = H[g][:].rearrange("p (h e) -> p h e", e=64)
                    P.tt("pool", Hv, Hv, sm["dec"][:, hs].unsqueeze(2).to_broadcast([128, 8, 64]), ALU.mult)
                    P.tt("dve", H[g][:], STp[:, :], H[g][:], ALU.add)
                    P.cp("act", Hbf[g][:], H[g][:])
                P.tt("dve", y[:], y[:], zs_[:], ALU.mult)
                P.act(zs_[:], y[:], AF.Square, accum=ss[:])
                P.act(ss[:], ss[:], AF.Sqrt, bias=EPS, scale=1.0 / 2048)
                P.recip(ss[:], ss[:])
                P.stt("dve", ynb[:], y[:], ss[:, 0:1], ng_b[:], ALU.mult, ALU.mult)
                for k0 in (0, 8):
                    for j in range(8):
                        P.tr(tb[:, j, :], ynb[:, (k0 + j) * 128:(k0 + j + 1) * 128], C["ident"][:], inc=(j == 7))
                    P.cp("act", ycT[:, k0:k0 + 8, :], tb[:])
                P.dma(yc_d[:, :, csl], ycT[:])

            if nch > 0:
                load(0)
                if nch > 1:
                    load(1)
                part1(0)
            for c in range(nch):
                if P.sems_left() < 40:
                    P.barrier()
                if c + 1 < nch:
                    part1(c + 1)
                part2(c)
                if c + 2 < nch:
                    load(c + 2)

    def phase_merge(self, l, hT, ya_d, yb_d, yc_d, mix_d):
        P, C = self.P, self.C
        I = self.inp
        self.cast_engs = ["pool", "dve"]
        with P.scope() as st:
            yh = [P.sb("yha", [128, 8, 1024], BF16, st), P.sb("yhb", [128, 8, 1024], BF16, st),
                  P.sb("yhc", [128, 16, 1024], BF16, st)]
            Wy = [[P.sb("Wya", [128, 8, 128], BF16, st), P.sb("Wyb", [128, 8, 128], BF16, st),
                   P.sb("Wyc", [128, 16, 128], BF16, st)] for _ in range(2)]
            Wg = [[P.sb("Wg", [128, 8, 128], BF16, st) for _ in range(3)] for _ in range(2)]
            sg = [P.sb("sg", [128, 512], F32, st) for _ in range(2)]
            macc = P.sb("macc", [128, 512], F32, st)
            mt = P.sb("mt", [128, 512], F32, st)
            mixh = P.sb("mixh", [128, 8, 1024], BF16, st)
            bank = [P.ps("bk", [128, 512], F32, st) for _ in range(6)]
            wsrc = [I["w_br_dsa"], I["w_br_nsa"], I["w_br_ssd"]]
            ysrc = [ya_d, yb_d, yc_d]
            bi = 0
            gi = 0
            it = 0
            for half in range(2):
                hsl = slice(half * 1024, (half + 1) * 1024)
                for br in range(3):
                    P.dma(yh[br][:], ysrc[br][:, :, hsl])
                for oc in range(8):
                    Wy_ = Wy[it % 2]
                    Wg_ = Wg[it % 2]
                    it += 1
                    for br in range(3):
                        self.wload(Wy_[br], wsrc[br], l, oc * 128, 128)
                        self.wload(Wg_[br], I["w_in"], l, OFF["mg"] + br * 1024 + oc * 128, 128)
                    for pc in range(2):
                        psl = slice(pc * 512, (pc + 1) * 512)
                        tok = half * 1024 + pc * 512
                        for br in range(3):
                            psy = bank[bi % 6]
                            bi += 1
                            psg = bank[bi % 6]
                            bi += 1
                            nk = 16 if br == 2 else 8
                            for kc in range(nk):
                                P.mm(psy[:, :], lhsT=Wy_[br][:, kc, :], rhs=yh[br][:, kc, psl],
                                     start=(kc == 0), stop=(kc == nk - 1))
                            self.proj_fm(psg, Wg_[br], 0, 128, hT, tok, 512)
                            s_ = sg[gi % 2]
                            gi += 1
                            P.act(s_[:], psg[:, :], AF.Sigmoid)
                            if br == 0:
                                P.tt("dve", macc[:], psy[:, :], s_[:], ALU.mult)
                            else:
                                P.tt("dve", mt[:], psy[:, :], s_[:], ALU.mult)
                                if br == 1:
                                    P.tt("pool", macc[:], macc[:], mt[:], ALU.add)
                                else:
                                    P.tt("pool", mixh[:, oc, psl], macc[:], mt[:], ALU.add)
                    P.maybe_barrier()
                P.dma(mix_d[:, :, hsl], mixh[:])

    def phase_out_norm(self, l, x_d, mix_d, x1_d, hT):
        P, C = self.P, self.C
        I = self.inp
        self.cast_engs = ["pool", "dve"]
        with P.scope() as st:
            Wo = P.sb("Wo", [128, 8, 1024], BF16, st)
            self.wload(Wo, I["w_out"], l, 0, 1024)
            gb = P.sb("gb2", [128, D], F32, st)
            P.dma(gb[:], I["norm2_g"][l].partition_broadcast(128))
            mq = [P.sb("mq", [128, 8, 128], BF16, st) for _ in range(2)]
            xb = [P.sb("xb", [128, D], F32, st) for _ in range(2)]
            x1b = [P.sb("x1b", [128, D], F32, st) for _ in range(2)]
            sq = P.sb("sq", [128, D], F32, st)
            hb = [P.sb("hb", [128, D], BF16, st) for _ in range(2)]
            ss = P.sb("ss", [128, 2], F32, st)
            bank = [P.ps("bk", [128, 512], F32, st) for _ in range(4)]
            pt = [P.ps("pt", [128, 8, 128], BF16, st) for _ in range(2)]
            for tt in range(NT):
                tsl = slice(tt * 128, (tt + 1) * 128)
                m_ = mq[tt % 2]
                P.dma(m_[:], mix_d[:, :, tsl])
                x_t = xb[tt % 2]
                P.dma(x_t[:], x_d[tsl, :])
                x1 = x1b[tt % 2]
                for hf in range(2):
                    ps = bank[(2 * tt + hf) % 4]
                    for oc in range(8):
                        P.mm(ps[:, :], lhsT=m_[:, oc, :], rhs=Wo[:, oc, hf * 512:(hf + 1) * 512],
                             start=(oc == 0), stop=(oc == 7))
                    P.tt("dve", x1[:, hf * 512:(hf + 1) * 512], ps[:, :], x_t[:, hf * 512:(hf + 1) * 512], ALU.add)
                P.dma(x1_d[tsl, :], x1[:])
                s1 = ss[:, tt % 2:tt % 2 + 1]
                P.act(sq[:], x1[:], AF.Square, accum=s1)
                P.act(s1, s1, AF.Sqrt, bias=EPS, scale=1.0 / D)
                P.recip(s1, s1)
                h_t = hb[tt % 2]
                P.stt("dve", h_t[:], x1[:], s1, gb[:], ALU.mult, ALU.mult)
                p_t = pt[tt % 2]
                for kc in range(8):
                    P.tr(p_t[:, kc, :], h_t[:, kc * 128:(kc + 1) * 128], C["ident"][:], inc=(kc == 7))
                P.cp("act", hT[:, :, tsl], p_t[:])

    def phase_ffn(self, l, hT, x1_d, xo_d, xp_d):
        P, C = self.P, self.C
        I = self.inp
        self.cast_engs = ["pool", "dve", "act", "dve"]
        with P.scope() as st:
            W1h = P.sb("W1h", [128, 8, 2048], BF16, st)
            W2h = P.sb("W2h", [128, 16, 1024], BF16, st)
            aT = P.sb("aT", [128, 16, 512], BF16, st)
            rb = [P.sb("rbf", [128, 512], F32, st) for _ in range(2)]
            xt = [P.sb("xt", [128, 512], F32, st) for _ in range(4)]
            xo = [P.sb("xo", [128, 512], F32, st) for _ in range(4)]
            fb = [P.ps("fb", [128, 512], F32, st) for _ in range(2)]
            ob = [P.ps("ob", [128, 512], F32, st) for _ in range(4)]
            xi = 0
            for hp in range(2):
                self.wload(W1h, I["w_ff1"], l, hp * 2048, 2048)
                self.wload(W2h, I["w_ff2"], l, 0, 1024, rows=(hp * 2048, (hp + 1) * 2048))
                src_d = x1_d if hp == 0 else xp_d
                dst_d = xp_d if hp == 0 else xo_d
                for Q in range(4):
                    for fc in range(16):
                        ps = fb[fc % 2]
                        self.proj_fm(ps, W1h, fc * 128, 128, hT, Q * 512, 512)
                        r = rb[fc % 2]
                        P.act(r[:], ps[:, :], AF.Relu)
                        P.tt("dve", aT[:, fc, :], r[:], r[:], ALU.mult)
                    for hf in range(2):
                        cols = slice(hf * 512, (hf + 1) * 512)
                        xs_ = []
                        for tl in range(4):
                            rows = slice((Q * 4 + tl) * 128, (Q * 4 + tl + 1) * 128)
                            x_ = xt[xi % 4]
                            o_ = xo[xi % 4]
                            xi += 1
                            P.dma(x_[:], src_d[rows, cols])
                            xs_.append((x_, o_, rows))
                            for k in range(16):
                                P.mm(ob[tl][:, :], lhsT=aT[:, k, tl * 128:(tl + 1) * 128], rhs=W2h[:, k, cols],
                                     start=(k == 0), stop=(k == 15))
                        for tl in range(4):
                            x_, o_, rows = xs_[tl]
                            P.tt("dve", o_[:], ob[tl][:, :], x_[:], ALU.add)
                            P.dma(dst_d[rows, cols], o_[:])
                    P.maybe_barrier()

    def build(self):
        P = self.P
        I = self.inp
        self.din("x", [S, D])
        for name, shp in (("norm1_g", [DEPTH, D]), ("w_in", [DEPTH, D, D_IN]), ("dsa_q_norm", [DEPTH, 64]),
                          ("dsa_k_norm", [DEPTH, 64]), ("nsa_q_norm", [DEPTH, 64]), ("nsa_k_norm", [DEPTH, 3, 64]),
                          ("nsa_cmp_pos", [DEPTH, 2, 32, 64]), ("nsa_cmp_w", [DEPTH, 2, 32, 64, 64]),
                          ("ssd_conv_w", [DEPTH, 4, 3072]), ("ssd_conv_b", [DEPTH, 3072]),
                          ("ssd_dt_bias", [DEPTH, 32]), ("ssd_a_log", [DEPTH, 32]), ("ssd_d", [DEPTH, 32]),
                          ("ssd_norm_g", [DEPTH, 2048]), ("w_br_dsa", [DEPTH, D, D]), ("w_br_nsa", [DEPTH, D, D]),
                          ("w_br_ssd", [DEPTH, 2 * D, D]), ("w_out", [DEPTH, D, D]), ("norm2_g", [DEPTH, D]),
                          ("w_ff1", [DEPTH, D, 4 * D]), ("w_ff2", [DEPTH, 4 * D, D])):
            self.din(name, shp)
        self.load_consts()
        out = self.dout("out", [S, D])
        hT = P.sb("hT", [128, 8, S], BF16)
        ya_d = P.dram("ya", [128, 8, S], BF16)
        yb_d = P.dram("yb", [128, 8, S], BF16)
        yc_d = P.dram("yc", [128, 16, S], BF16)
        zs_d = P.dram("zs", [S, 2048], F32)
        xact_d = P.dram("xact", [128, 24, S], BF16)
        mix_d = P.dram("mix", [128, 8, S], BF16)
        x1_d = P.dram("x1", [S, D], F32)
        xm_d = P.dram("xm", [S, D], F32)
        xp_d = P.dram("xp", [S, D], F32)
        dbg = self.debug
        x_cur = I["x"]
        for l in range(self.nlayers):
            x_nxt = out if l == self.nlayers - 1 else xm_d
            self.phase_norm(x_cur, I["norm1_g"][l], hT)
            if dbg == "hT":
                return self._dump(hT[:], [128, 8, S], BF16)
            if dbg in ("nsa", "nsa_k"):
                self.phase_nsa(l, hT, yb_d)
                return self._dump(yb_d, [128, 8, S], BF16)
            if dbg == "ssd":
                self.phase_ssd(l, hT, yc_d, zs_d, xact_d)
                return self._dump(yc_d, [128, 16, S], BF16)
            self.phase_dsa(l, hT, ya_d)
            if dbg is not None and dbg.startswith("dsa"):
                return self._dump(ya_d, [128, 8, S], BF16)
            self.phase_nsa(l, hT, yb_d)
            self.phase_ssd(l, hT, yc_d, zs_d, xact_d)
            self.phase_merge(l, hT, ya_d, yb_d, yc_d, mix_d)
            if dbg == "mix":
                return self._dump(mix_d, [128, 8, S], BF16)
            self.phase_out_norm(l, x_cur, mix_d, x1_d, hT)
            if dbg == "x1":
                return self._dump(x1_d, [S, D], F32)
            self.phase_ffn(l, hT, x1_d, x_nxt, xp_d)
            x_cur = x_nxt
        P.finish()

    def _dump(self, src, shape, dt):
        if "dbg" not in self.out:
            o = self.dout("dbg", shape, dt)
            self.P.dma(o, src)
        self.P.finish()


_CACHE = {}


def kernel(**inputs):
    n = 8
    B = Builder()
    B.build()
    consts = _consts()
    shared = {}
    for name in B.inp:
        if name in consts:
            shared[name] = consts[name]
        elif name != "x":
            shared[name] = np.ascontiguousarray(np.asarray(inputs[name], dtype=np.float32))
    x = np.asarray(inputs["x"], dtype=np.float32)
    in_maps = []
    for b in range(n):
        m = dict(shared)
        m["x"] = np.ascontiguousarray(x[b])
        in_maps.append(m)
    res = run_bass_kernel_spmd(B.nc, in_maps, core_ids=list(range(n)))
    return np.stack([np.asarray(res.results[b]["out"], dtype=np.float32) for b in range(n)], axis=0)
```

```python
import contextlib
import os
import numpy as np
import concourse.bass as bass
import concourse.mybir as mybir
from concourse.bass_utils import run_bass_kernel_spmd

F32 = mybir.dt.float32
BF16 = mybir.dt.bfloat16
AF = mybir.ActivationFunctionType
ALU = mybir.AluOpType
AX = mybir.AxisListType

EPOCH = 240
NPOOL = 90

D = 1024
S = 2048
NT = 16
DEPTH = 2
EPS = 1e-6
OFF = dict(dq=0, dk=1024, dv=1088, iq=1152, ik=1408, iw=1440, nq=1448, nkv=2472, ng=3240,
           sz=3288, sxbc=5336, sdt=8408, mg=8440)
D_IN = 11512
NEGM = -30000.0


def _region(ap):
    t = ap.tensor
    name = t.name
    dims = [(int(s), int(c)) for s, c in ap.ap]
    off = int(ap.offset)
    if "DRam" in type(t).__name__:
        lo = hi = off
        for s, c in dims:
            if s >= 0:
                hi += s * (c - 1)
            else:
                lo += s * (c - 1)
        return (name, 0, 1, lo, hi + 1)
    if "PSum" in type(t).__name__:
        return (name, 0, 128, 0, 1 << 30)
    ps, pc = dims[0]
    if ps == 0:
        p0, f0 = 0, off
    else:
        p0 = off // ps
        f0 = off - p0 * ps
    lo = hi = f0
    for s, c in dims[1:]:
        if s >= 0:
            hi += s * (c - 1)
        else:
            lo += s * (c - 1)
    return (name, p0, p0 + pc, lo, hi + 1)


def _overlap(a, b):
    return a[1] < b[2] and b[1] < a[2] and a[3] < b[4] and b[3] < a[4]


def _contains(a, b):
    return a[1] <= b[1] and b[2] <= a[2] and a[3] <= b[3] and b[4] <= a[4]


class Prog:
    ENGS = ("pe", "act", "dve", "pool", "sp")

    def __init__(self, nc):
        self.nc = nc
        self.stack = contextlib.ExitStack()
        self.eng = {"pe": nc.tensor, "act": nc.scalar, "dve": nc.vector, "pool": nc.gpsimd, "sp": nc.sync}
        self.pool = [self.stack.enter_context(nc.semaphore(f"sp{i}")) for i in range(NPOOL)]
        self.bsem = [self.stack.enter_context(nc.semaphore(f"bar{i}")) for i in range(4)]
        self.nbar = 0
        self.uid = 0
        self.n_wait = 0
        self.n_ins = 0
        self.max_used = 0
        self._reset()

    def _reset(self):
        self.free = list(range(NPOOL))
        self.esem = {e: None for e in self.ENGS}
        self.ecnt = {e: 0 for e in self.ENGS}
        self.pend = {e: False for e in self.ENGS}
        self.seen = {e: {} for e in self.ENGS}
        self.dset = []
        self.dall = {}
        self.dnext = 0
        self.trk = {}

    def _alloc(self):
        assert self.free, "semaphore pool exhausted; add a barrier"
        i = self.free.pop()
        self.max_used = max(self.max_used, NPOOL - len(self.free))
        return i

    def sems_left(self):
        return len(self.free)

    def sb(self, name, shape, dtype, stack=None):
        self.uid += 1
        return (stack or self.stack).enter_context(
            self.nc.sbuf_tensor(f"{name}_{self.uid}", list(shape), dtype))

    def ps(self, name, shape, dtype, stack=None):
        self.uid += 1
        return (stack or self.stack).enter_context(
            self.nc.psum_tensor(f"{name}_{self.uid}", list(shape), dtype))

    def dram(self, name, shape, dtype):
        self.uid += 1
        return self.nc.dram_tensor(f"{name}_{self.uid}", list(shape), dtype).ap()

    @contextlib.contextmanager
    def scope(self):
        st = contextlib.ExitStack()
        try:
            yield st
        finally:
            self.barrier()
            st.close()

    def _deps(self, reads, writes):
        deps = []
        for ap in reads:
            r = _region(ap)
            t = self.trk.get(r[0])
            if t is None:
                continue
            for (wr, ev) in t["w"]:
                if _overlap(wr, r):
                    deps.append(ev)
        for ap in writes:
            r = _region(ap)
            t = self.trk.get(r[0])
            if t is None:
                continue
            for (wr, ev) in t["w"]:
                if _overlap(wr, r):
                    deps.append(ev)
            for (sk, rr), val in t["r"].items():
                if _overlap(rr, r):
                    deps.append((sk, val))
        return deps

    def _record(self, reads, writes, ev):
        for ap in reads:
            r = _region(ap)
            t = self.trk.setdefault(r[0], {"w": [], "r": {}})
            k = (ev[0], r)
            if t["r"].get(k, -1) < ev[1]:
                t["r"][k] = ev[1]
        for ap in writes:
            r = _region(ap)
            t = self.trk.setdefault(r[0], {"w": [], "r": {}})
            t["w"] = [(wr, e) for (wr, e) in t["w"] if not _contains(r, wr)]
            t["w"].append((r, ev))
            t["r"] = {k: v for k, v in t["r"].items() if not _contains(r, k[1])}

    def _wait(self, e, deps):
        eng = self.eng[e]
        need = {}
        for sk, val in deps:
            if self.seen[e].get(sk, 0) >= val:
                continue
            if need.get(sk, 0) < val:
                need[sk] = val
        for sk, val in need.items():
            assert 0 < val <= 255
            eng.wait_ge(self.pool[sk[1]], val)
            self.seen[e][sk] = val
            self.n_wait += 1

    def _next_ev(self, e, commit):
        if self.esem[e] is None or self.ecnt[e] >= EPOCH:
            self.esem[e] = self._alloc()
            self.ecnt[e] = 0
        ev = ((e, self.esem[e]), self.ecnt[e] + 1)
        if commit:
            self.ecnt[e] += 1
        return ev

    def op(self, e, fn, reads, writes, inc=True):
        deps = self._deps(reads, writes)
        if e == "pe":
            deps = [d for d in deps if d[0][0] != "pe"]
        self._wait(e, deps)
        ins = fn(self.eng[e])
        self.n_ins += 1
        ev = self._next_ev(e, inc)
        if inc:
            ins.then_inc(self.pool[ev[0][1]], 1)
            self.pend[e] = False
        else:
            self.pend[e] = True
        self._record(reads, writes, ev)
        return ins

    def dma(self, out, in_, q="sp", **kw):
        deps = self._deps([in_], [out])
        if q == "pool":
            ent = [self._alloc(), 0]
        else:
            if len(self.dset) < 8:
                self.dset.append([self._alloc(), 0])
            k = self.dnext % len(self.dset)
            self.dnext += 1
            if self.dset[k][1] >= 15:
                self.dset[k] = [self._alloc(), 0]
            ent = self.dset[k]
        if ent[1] > 0:
            deps.append((("d", ent[0]), 16 * ent[1]))
        self._wait(q, deps)
        ins = self.eng[q].dma_start(out=out, in_=in_, **kw)
        ins.then_inc(self.pool[ent[0]], 16)
        ent[1] += 1
        self.dall[ent[0]] = ent[1]
        self.n_ins += 1
        self._record([in_], [out], (("d", ent[0]), 16 * ent[1]))
        return ins

    def barrier(self):
        evs = []
        for e in self.ENGS:
            assert not self.pend[e], f"pending non-inc op on {e} at barrier"
            if self.esem[e] is not None and self.ecnt[e] > 0:
                evs.append(((e, self.esem[e]), self.ecnt[e]))
        for k, c in self.dall.items():
            evs.append((("d", k), 16 * c))
        for e in self.ENGS:
            self._wait(e, evs)
        b0 = self.bsem[2 * (self.nbar % 2)]
        b1 = self.bsem[2 * (self.nbar % 2) + 1]
        p0 = self.bsem[2 * ((self.nbar + 1) % 2)]
        p1 = self.bsem[2 * ((self.nbar + 1) % 2) + 1]
        for e in self.ENGS:
            if e != "sp":
                self.eng[e].sem_inc(b0, 1)
        sp = self.eng["sp"]
        sp.wait_ge(b0, 4)
        used = [i for i in range(NPOOL) if i not in set(self.free)]
        for i in used:
            sp.sem_clear(self.pool[i])
        sp.sem_clear(p0)
        sp.sem_clear(p1)
        sp.sem_inc(b1, 1)
        for e in self.ENGS:
            if e != "sp":
                self.eng[e].wait_ge(b1, 1)
        self.nbar += 1
        self._reset()

    def maybe_barrier(self, min_free=35):
        if len(self.free) < min_free:
            self.barrier()

    def finish(self):
        self.barrier()
        self.stack.close()

    def mm(self, out, lhsT, rhs, start=True, stop=True, inc=None, sgc=False):
        if inc is None:
            inc = stop
        return self.op("pe", lambda e: e.matmul(out, lhsT=lhsT, rhs=rhs, start=start, stop=stop,
                                                skip_group_check=sgc),
                       [lhsT, rhs], [out], inc=inc)

    def tr(self, out, in_, ident, inc=True):
        return self.op("pe", lambda e: e.transpose(out=out, in_=in_, identity=ident), [in_, ident], [out], inc=inc)

    def act(self, out, in_, func, bias=None, scale=None, accum=None):
        kw = {}
        reads = [in_]
        writes = [out]
        if bias is not None:
            kw["bias"] = bias
            if not isinstance(bias, (int, float)):
                reads.append(bias)
        if scale is not None:
            kw["scale"] = scale
            if not isinstance(scale, (int, float)):
                reads.append(scale)
        if accum is not None:
            kw["accum_out"] = accum
            writes.append(accum)
        return self.op("act", lambda e: e.activation(out=out, in_=in_, func=func, **kw), reads, writes)

    def cp(self, eng, out, in_):
        if eng == "act":
            return self.op("act", lambda e: e.copy(out=out, in_=in_), [in_], [out])
        return self.op(eng, lambda e: e.tensor_copy(out=out, in_=in_), [in_], [out])

    def tt(self, eng, out, in0, in1, op):
        return self.op(eng, lambda e: e.tensor_tensor(out=out, in0=in0, in1=in1, op=op), [in0, in1], [out])

    def ts(self, eng, out, in0, s1, s2=None, op0=ALU.mult, op1=None, accum=None):
        reads = [in0] + [s for s in (s1, s2) if s is not None and not isinstance(s, (int, float))]
        writes = [out] + ([accum] if accum is not None else [])
        kw = {}
        if op1 is not None:
            kw["op1"] = op1
        if accum is not None:
            kw["accum_out"] = accum
        return self.op(eng, lambda e: e.tensor_scalar(out=out, in0=in0, scalar1=s1, scalar2=s2, op0=op0, **kw),
                       reads, writes)

    def stt(self, eng, out, in0, scalar, in1, op0, op1):
        reads = [in0, in1] + ([scalar] if not isinstance(scalar, (int, float)) else [])
        return self.op(eng, lambda e: e.scalar_tensor_tensor(out=out, in0=in0, scalar=scalar, in1=in1,
                                                             op0=op0, op1=op1), reads, [out])

    def amul(self, out, in_, val):
        return self.op("act", lambda e: e.mul(out=out, in_=in_, mul=val), [in_], [out])

    def memset(self, eng, ap, val):
        return self.op(eng, lambda e: e.memset(ap, val), [], [ap])

    def recip(self, out, in_):
        return self.op("dve", lambda e: e.reciprocal(out=out, in_=in_), [in_], [out])


def _consts():
    c = {}
    p = np.arange(128)
    c["c_ident"] = np.eye(128, dtype=np.float32)
    c["c_ones"] = np.ones((128, 128), np.float32)
    c["c_blk64"] = (p[:, None] // 64 == p[None, :] // 64).astype(np.float32)
    c["c_tri"] = (p[:, None] <= p[None, :]).astype(np.float32)
    c["c_cneg"] = np.where(p[:, None] > p[None, :], NEGM, 0.0).astype(np.float32)
    c["c_wneg"] = np.where(p[:, None] <= p[None, :], NEGM, 0.0).astype(np.float32)
    c["c_cnegtm"] = np.where(p[None, :] > p[:, None], -3e30, 0.0).astype(np.float32)
    s = np.arange(S)
    c["c_E"] = (s[None, :] // 64 == np.arange(32)[:, None]).astype(np.float32)
    n = np.arange(128)
    vis = (16 * n[:, None] + 31 <= s[None, :])
    c["c_visneg"] = np.where(vis, 0.0, NEGM).astype(np.float32)
    j = np.arange(32)
    cover = ((16 * n[:, None] < 64 * j[None, :] + 64) & (16 * n[:, None] + 32 > 64 * j[None, :]))
    ce = np.zeros((128, 33), np.float32)
    ce[:, 0] = 1.0
    ce[:, 1:] = cover
    ce[127] = 0.0
    c["c_cover"] = ce
    t = (np.arange(NT)[None, :, None] * 128 + p[:, None, None])
    cur = t // 64
    jj = j[None, None, :]
    forced = (jj == cur) | (jj == 0)
    future = jj > cur
    c["c_selkeep"] = (~(forced | future)).astype(np.float32)
    c["c_selbias"] = np.where(forced, 1e9, np.where(future, -1e9, 0.0)).astype(np.float32)
    return c


class Builder:
    def __init__(self, debug=None, nlayers=DEPTH):
        self.debug = debug
        self.nlayers = nlayers
        import os
        self.lim = int(os.environ.get("KLIM", "16"))
        nc = bass.Bass("TRN2", target_bir_lowering=False)
        self.nc = nc
        self.P = Prog(nc)
        self.inp = {}
        self.out = {}

    def din(self, name, shape, dt=F32):
        self.inp[name] = self.nc.dram_tensor(name, list(shape), dt, kind="ExternalInput").ap()
        return self.inp[name]

    def dout(self, name, shape, dt=F32):
        self.out[name] = self.nc.dram_tensor(name, list(shape), dt, kind="ExternalOutput").ap()
        return self.out[name]

    def load_consts(self):
        P = self.P
        C = {}
        shapes = {k: v.shape for k, v in _consts().items()}
        for k, shp in shapes.items():
            self.din(k, shp)
        self.wstage = [P.sb("wstage", [128, 4096], F32) for _ in range(2)]
        self.wsi = 0
        self.cast_engs = ["pool"]

        def ld(name, key, shape, dt, rows=None):
            t = P.sb(name, shape, dt)
            src = self.inp[key]
            if dt == F32:
                P.dma(t[:], src)
            else:
                n = int(np.prod(shape[1:]))
                flat = "p a b -> p (a b)" if len(shape) == 3 else None
                for a0 in range(0, n, 4096):
                    b0 = min(n, a0 + 4096)
                    stg = self.wstage[self.wsi % 2]
                    self.wsi += 1
                    P.dma(stg[0:shape[0], 0:b0 - a0], src[:, a0:b0])
                    P.cp("pool", t[:, a0:b0], stg[0:shape[0], 0:b0 - a0])
            return t
        C["ident_f"] = ld("ident_f", "c_ident", [128, 128], F32)
        C["ident"] = ld("ident", "c_ident", [128, 128], BF16)
        C["ones_f"] = ld("ones_f", "c_ones", [128, 128], F32)
        C["blk64_f"] = ld("blk64_f", "c_blk64", [128, 128], F32)
        C["tri_f"] = ld("tri_f", "c_tri", [128, 128], F32)
        C["cneg"] = ld("cneg", "c_cneg", [128, 128], BF16)
        C["zero"] = P.sb("zero", [128, 128], BF16)
        P.memset("pool", C["zero"][:], 0.0)
        C["cneg_f"] = ld("cneg_f", "c_cneg", [128, 128], F32)
        C["wneg"] = ld("wneg", "c_wneg", [128, 128], BF16)
        C["cnegtm"] = ld("cnegtm", "c_cnegtm", [128, 128], F32)
        C["E"] = ld("E", "c_E", [32, S], BF16)
        C["visneg"] = ld("visneg", "c_visneg", [128, S], BF16)
        C["cover"] = ld("cover", "c_cover", [128, 33], BF16)
        C["selkeep"] = ld("selkeep", "c_selkeep", [128, NT, 32], F32)
        C["selbias"] = ld("selbias", "c_selbias", [128, NT, 32], F32)
        self.C = C

    def wload(self, dst, wdram, l, c0, n, rows=None):
        w2 = wdram[l] if rows is None else wdram[l][rows[0]:rows[1]]
        src = w2.rearrange("(kc p) n -> p kc n", p=128)[:, :, c0:c0 + n]
        nk = src.shape[1]
        step = max(1, 4096 // nk)
        for a in range(0, n, step):
            b = min(n, a + step)
            stg = self.wstage[self.wsi % 2]
            self.wsi += 1
            v = stg[:, 0:nk * (b - a)].rearrange("p (k n) -> p k n", k=nk)
            self.P.dma(v, src[:, :, a:b], q="sp")
            eng = self.cast_engs[self.wsi % len(self.cast_engs)]
            self.P.cp(eng, dst[:, :, a:b], v)

    def pipelined(self, items, Sb, pTb, cnt, depth=1):
        P = self.P
        n = len(items)
        nb = len(Sb)
        assert nb >= depth + 1 and len(pTb) >= depth + 1
        slots = []
        for i in range(n + depth):
            if i < n:
                k = cnt["si"] % nb
                cnt["si"] += 1
                Sp, pT = Sb[k], pTb[k]
                items[i][0](Sp)
                P.act(pT[:], Sp[:, :], AF.Exp, scale=0.125)
                slots.append(pT)
            if i >= depth:
                items[i - depth][1](slots[i - depth])

    def proj_fm(self, ps, Wt, c0, M, hT, t0, N):
        for kc in range(8):
            self.P.mm(ps[0:M, 0:N], lhsT=Wt[:, kc, c0:c0 + M], rhs=hT[:, kc, t0:t0 + N],
                      start=(kc == 0), stop=(kc == 7))

    def proj_tm(self, ps, Wt, c0, n, hT, tile):
        for kc in range(8):
            self.P.mm(ps[:, 0:n], lhsT=hT[:, kc, tile * 128:(tile + 1) * 128], rhs=Wt[:, kc, c0:c0 + n],
                      start=(kc == 0), stop=(kc == 7))

    def norm_fm(self, ps, ps2, M, N, gcol, out, sq, rs):
        P, C = self.P, self.C
        P.act(sq[0:M, 0:N], ps[0:M, 0:N], AF.Square)
        P.mm(ps2[0:M, 0:N], lhsT=C["blk64_f"][0:M, 0:M], rhs=sq[0:M, 0:N])
        P.act(rs[0:M, 0:N], ps2[0:M, 0:N], AF.Sqrt, bias=EPS, scale=1.0 / 64)
        P.recip(rs[0:M, 0:N], rs[0:M, 0:N])
        P.stt("dve", out, ps[0:M, 0:N], gcol, rs[0:M, 0:N], ALU.mult, ALU.mult)

    def gain_col(self, name, vec64, stack):
        t = self.P.sb(name, [128, 1], F32, stack)
        src = vec64.rearrange("(p o) -> p o", o=1)
        self.P.dma(t[0:64, :], src)
        self.P.dma(t[64:128, :], src)
        return t

    def phase_norm(self, xd, gvec, hT, store_T=None):
        P, C = self.P, self.C
        with P.scope() as st:
            gb = P.sb("gb", [128, D], F32, st)
            P.dma(gb[:], gvec.partition_broadcast(128))
            xb = [P.sb("xb", [128, D], F32, st) for _ in range(2)]
            sq = P.sb("sq", [128, D], F32, st)
            hb = [P.sb("hb", [128, D], BF16, st) for _ in range(2)]
            ss = P.sb("ss", [128, 2], F32, st)
            pt = [P.ps("pt", [128, 8, 128], BF16, st) for _ in range(2)]
            for tt in range(NT):
                x_t = xb[tt % 2]
                P.dma(x_t[:], xd[tt * 128:(tt + 1) * 128, :])
                s1 = ss[:, tt % 2:tt % 2 + 1]
                P.act(sq[:], x_t[:], AF.Square, accum=s1)
                P.act(s1, s1, AF.Sqrt, bias=EPS, scale=1.0 / D)
                P.recip(s1, s1)
                h_t = hb[tt % 2]
                P.stt("dve", h_t[:], x_t[:], s1, gb[:], ALU.mult, ALU.mult)
                p_t = pt[tt % 2]
                for kc in range(8):
                    P.tr(p_t[:, kc, :], h_t[:, kc * 128:(kc + 1) * 128], C["ident"][:], inc=(kc == 7))
                P.cp("act" if tt % 2 else "dve", hT[:, :, tt * 128:(tt + 1) * 128], p_t[:])

    def phase_dsa(self, l, hT, ya_d):
        P, C = self.P, self.C
        I = self.inp
        self.cast_engs = ["pool", "act"]
        w_in = I["w_in"]
        with P.scope() as st:
            Wq = P.sb("Wq", [128, 8, 1024], BF16, st)
            self.wload(Wq, w_in, l, OFF["dq"], 1024)
            Wk = P.sb("Wk", [128, 8, 128], BF16, st)
            self.wload(Wk[:, :, 0:64], w_in, l, OFF["dk"], 64)
            self.wload(Wk[:, :, 64:128], w_in, l, OFF["dk"], 64)
            Wv = P.sb("Wv", [128, 8, 64], BF16, st)
            self.wload(Wv, w_in, l, OFF["dv"], 64)
            Wiq = P.sb("Wiq", [128, 8, 256], BF16, st)
            self.wload(Wiq, w_in, l, OFF["iq"], 256)
            Wik = P.sb("Wik", [128, 8, 128], BF16, st)
            for r in range(4):
                self.wload(Wik[:, :, r * 32:(r + 1) * 32], w_in, l, OFF["ik"], 32)
            Wiw = P.sb("Wiw", [128, 8, 8], BF16, st)
            self.wload(Wiw, w_in, l, OFF["iw"], 8)
            gq = self.gain_col("gq", I["dsa_q_norm"][l], st)
            gk = self.gain_col("gk", I["dsa_k_norm"][l], st)

            kT = P.sb("kT", [128, S], BF16, st)
            v1 = P.sb("v1", [128, NT, 65], BF16, st)
            ikbd = P.sb("ikbd", [128, NT, 4, 128], BF16, st)
            qz = P.sb("qz", [128, 16, 512], BF16, st)
            iqT = P.sb("iqT", [128, 2, S], BF16, st)
            iw = P.sb("iw", [128, NT, 8], F32, st)
            acc = P.sb("acc", [128, S], F32, st)
            work = P.sb("work", [128, S], F32, st)
            mx = P.sb("mx", [128, 8], F32, st)
            mx2 = P.sb("mx2", [128, 8], F32, st)
            negm2 = P.sb("negm2", [128, S], BF16, st)
            negm = P.sb("negm", [128, S], BF16, st)
            ikT4 = negm
            negmT = [P.sb("negmT", [128, NT, 128], BF16, st) for _ in range(2)]
            rb = [P.sb("rb", [128, 512], F32, st) for _ in range(2)]
            pTb = [P.sb("pTb", [128, 512], BF16, st) for _ in range(2)]
            ytm = P.sb("ytm", [128, 1024], BF16, st)
            yT = P.sb("yT", [128, 8, 512], BF16, st)
            sq = P.sb("sq", [128, 512], F32, st)
            rs = P.sb("rs", [128, 512], F32, st)
            rc = P.sb("rc", [128, 4], F32, st)
            bank = [P.ps("bk", [128, 512], F32, st) for _ in range(6)]
            tb = [P.ps("tb", [128, 8, 128], BF16, st) for _ in range(2)]
            Sb, Ob, Ib = bank[0:2], bank[2:4], bank[4:6]

            P.memset("pool", v1[:, :, 64:65], 1.0)
            P.memset("pool", ikbd[:], 0.0)
            P.memset("pool", qz[:], 0.0)
            for Q in range(4):
                t0 = Q * 512
                self.proj_fm(Ib[0], Wk, 0, 128, hT, t0, 512)
                self.norm_fm(Ib[0], Ib[1], 128, 512, gk[:, 0:1], kT[:, t0:t0 + 512], sq, rs)
                self.proj_fm(Ib[0], Wik, 0, 128, hT, t0, 512)
                P.cp("act", ikT4[:, t0:t0 + 512], Ib[0][:, :])
                for c in range(2):
                    self.proj_fm(Ib[c], Wiq, c * 128, 128, hT, t0, 512)
                    P.cp("act", iqT[:, c, t0:t0 + 512], Ib[c][:, :])
            for hh in range(4):
                P.dma(ikbd[32 * hh:32 * hh + 32, :, hh, :],
                      ikT4[32 * hh:32 * hh + 32, :].rearrange("p (k s) -> p k s", s=128))
            for tt in range(NT):
                ps = Ib[tt % 2]
                self.proj_tm(ps, Wv, 0, 64, hT, tt)
                P.cp("act", v1[:, tt, 0:64], ps[:, 0:64])
                self.proj_tm(ps, Wiw, 64, 8, hT, tt) if False else None
            for tt in range(NT):
                ps = Ib[tt % 2]
                self.proj_tm(ps, Wiw, 0, 8, hT, tt)
                P.cp("act", iw[:, tt, :], ps[:, 0:8])

            cnt = {"si": 0, "ii": 0}

            def stage_q(Q):
                t0 = Q * 512
                for c in range(8):
                    self.proj_fm(Ib[0], Wq, c * 128, 128, hT, t0, 512)
                    P.act(sq[:, :], Ib[0][:, :], AF.Square)
                    P.mm(Ib[1][:, :], lhsT=C["blk64_f"][:], rhs=sq[:, :])
                    P.act(rs[:, :], Ib[1][:, :], AF.Sqrt, bias=EPS, scale=1.0 / 64)
                    P.recip(rs[:, :], rs[:, :])
                    for hf in range(2):
                        psl = slice(64 * hf, 64 * hf + 64)
                        P.stt("dve", qz[psl, 2 * c + hf, :], Ib[0][psl, :], gq[psl, 0:1], rs[psl, :], ALU.mult, ALU.mult)

            accs = [acc, work]
            mxs = [mx, mx2]
            negms = [negm, negm2]

            def stage_a_scores(qt, bf):
                nk = qt + 1
                tsl = slice(qt * 128, (qt + 1) * 128)
                ab = accs[bf]
                for kt in range(nk):
                    a_kt = ab[:, kt * 128:(kt + 1) * 128]
                    for c in range(2):
                        ps = Ib[cnt["ii"] % 2]
                        r = rb[cnt["ii"] % 2]
                        cnt["ii"] += 1
                        P.mm(ps[:, :], lhsT=iqT[:, c, tsl], rhs=ikbd[:, kt].rearrange("p a b -> p (a b)"))
                        P.act(r[:], ps[:, :], AF.Relu)
                        for hh in range(4):
                            h = 4 * c + hh
                            if h == 0:
                                P.ts("dve", a_kt, r[:, 0:128], iw[:, qt, 0:1], None, op0=ALU.mult)
                            else:
                                P.stt("dve", a_kt, r[:, hh * 128:(hh + 1) * 128], iw[:, qt, h:h + 1], a_kt,
                                      ALU.mult, ALU.add)
                d_sl = slice(qt * 128, (qt + 1) * 128)
                P.tt("dve", ab[:, d_sl], ab[:, d_sl], C["cnegtm"][:], ALU.add)

            def stage_a(tiles):
                todo = []
                for qt in tiles:
                    bf = qt % 2
                    if qt < 2:
                        P.memset("pool", negms[bf][:, 0:(qt + 1) * 128], 0.0)
                    else:
                        stage_a_scores(qt, bf)
                        todo.append((qt, bf))
                for r_ in range(32):
                    for (qt, bf) in todo:
                        n = (qt + 1) * 128
                        P.op("dve", lambda e: e.max(out=mxs[bf][:], in_=accs[bf][:, 0:n]), [accs[bf][:, 0:n]], [mxs[bf][:]])
                    for (qt, bf) in todo:
                        n = (qt + 1) * 128
                        P.op("dve", lambda e: e.match_replace(out=accs[bf][:, 0:n], in_to_replace=mxs[bf][:],
                                                              in_values=accs[bf][:, 0:n], imm_value=-1e30),
                             [mxs[bf][:], accs[bf][:, 0:n]], [accs[bf][:, 0:n]])
                for (qt, bf) in todo:
                    n = (qt + 1) * 128
                    P.ts("dve", negms[bf][:, 0:n], accs[bf][:, 0:n], -1e29, 1.0, op0=ALU.is_le, op1=ALU.subtract)

            def stage_b(qt):
                nk = qt + 1
                nT = negmT[qt % 2]
                negm = negms[qt % 2]
                for k0 in range(0, nk, 8):
                    k1 = min(nk, k0 + 8)
                    t_ = tb[0]
                    for kt in range(k0, k1):
                        P.tr(t_[:, kt - k0, :], negm[:, kt * 128:(kt + 1) * 128], C["ident"][:],
                             inc=(kt == k1 - 1))
                    P.amul(nT[:, k0:k1, :], t_[:, 0:k1 - k0, :], -NEGM)
                P.tt("pool", nT[:, qt, :], nT[:, qt, :], C["cneg"][:], ALU.add)

            def stage_c(qt):
                nk = qt + 1
                tl = qt % 4
                tsl = slice(tl * 128, (tl + 1) * 128)
                nT = negmT[qt % 2]
                items = []
                for hg in range(4):
                    O = Ob[hg % 2]
                    O3 = O[:, 0:260].rearrange("p (h e) -> p h e", e=65)
                    for kt in range(nk):
                        def em_s(Sp, hg=hg, kt=kt):
                            P.mm(Sp[:, :], lhsT=C["ident"][:], rhs=nT[:, kt, :].unsqueeze(1).to_broadcast([128, 4, 128]),
                                 start=True, stop=False, inc=False)
                            P.mm(Sp[:, :], lhsT=kT[:, kt * 128:(kt + 1) * 128], rhs=qz[:, 4 * hg:4 * hg + 4, tsl],
                                 start=False, stop=True)

                        def em_pv(pT, hg=hg, kt=kt, O3=O3):
                            for hh in range(4):
                                P.mm(O3[:, hh, :], lhsT=pT[:, hh * 128:(hh + 1) * 128], rhs=v1[:, kt, :],
                                     start=(kt == 0 and hh == 0), stop=(kt == nk - 1), inc=(hh == 3), sgc=True)
                            if kt == nk - 1:
                                P.recip(rc[:], O3[:, :, 64])
                                P.tt("dve", ytm[:, hg * 256:(hg + 1) * 256].rearrange("p (h e) -> p h e", e=64),
                                     O3[:, :, 0:64], rc[:].unsqueeze(2).to_broadcast([128, 4, 64]), ALU.mult)
                        items.append((em_s, em_pv))
                self.pipelined(items, Sb, pTb, cnt)
                t_ = tb[1]
                for c in range(8):
                    P.tr(t_[:, c, :], ytm[:, c * 128:(c + 1) * 128], C["ident"][:], inc=(c == 7))
                P.cp("act", yT[:, :, tsl], t_[:])
                if tl == 3:
                    Q = qt // 4
                    P.dma(ya_d[:, :, Q * 512:(Q + 1) * 512], yT[:], q="sp")

            nq = min(NT, self.lim)
            pairs = [list(range(i, min(i + 2, nq))) for i in range(0, nq, 2)]
            stage_a(pairs[0])
            for qt in pairs[0]:
                stage_b(qt)
            for pi, pr in enumerate(pairs):
                P.maybe_barrier()
                if pi + 1 < len(pairs):
                    stage_a(pairs[pi + 1])
                for qt in pr:
                    if qt % 4 == 0:
                        stage_q(qt // 4)
                    stage_c(qt)
                if pi + 1 < len(pairs):
                    for qt in pairs[pi + 1]:
                        stage_b(qt)

    def _nsa_kside(self, l, hT, st, env):
        P, C = self.P, self.C
        I = self.inp
        Wkv, Wks, Wkw = env["Wkv"], env["Wks"], env["Wkw"]
        gks, gkw, gkc = env["gks"], env["gkw"], env["gkc"]
        ksT, kwT, vs1, vw1, kcmpT, vext = env["ksT"], env["kwT"], env["vs1"], env["vw1"], env["kcmpT"], env["vext"]
        sq, rs, ss, Mb, tb = env["sq"], env["rs"], env["ss"], env["Mb"], env["tb"]
        cmpw = []
        posB = []
        for kv in range(2):
            cw = P.sb("cmpw", [128, 32, 64], BF16, st)
            stg = self.wstage[self.wsi % 2]
            self.wsi += 1
            sv = stg[:, 0:2048].rearrange("p (l e) -> p l e", e=64)
            src = I["nsa_cmp_w"][l, kv].rearrange("l d e -> d l e")
            P.dma(sv[0:64], src)
            P.dma(sv[64:128], src)
            P.cp("pool", cw[:], sv)
            cmpw.append(cw)
            pt_ = P.sb("posT", [128, 32], F32, st)
            psrc = I["nsa_cmp_pos"][l, kv].rearrange("l d -> d l")
            P.dma(pt_[0:64, :], psrc, allow_slow_non_contiguous=True)
            P.dma(pt_[64:128, :], psrc, allow_slow_non_contiguous=True)
            pb = P.sb("posB", [128, 32, 127], BF16, st)
            P.cp("pool", pb[:], pt_[:].unsqueeze(2).to_broadcast([128, 32, 127]))
            posB.append(pb)

        kcT = P.sb("kcT", [128, S], BF16, st)
        vcT = P.sb("vcT", [128, S], BF16, st)
        kd = P.sb("kd", [128, 128], BF16, st)
        P.memset("pool", vs1[:, :, :, 64:65], 1.0)
        P.memset("pool", vw1[:, :, :, 64:65], 1.0)
        P.memset("pool", kd[:], 0.0)
        for g in range(2):
            P.memset("pool", vext[g][:], 0.0)
        for Q in range(4):
            t0 = Q * 512
            self.proj_fm(Mb[0], Wkv, 0, 128, hT, t0, 512)
            P.cp("act", kcT[:, t0:t0 + 512], Mb[0][:, :])
            self.proj_fm(Mb[1], Wkv, 128, 128, hT, t0, 512)
            P.cp("act", vcT[:, t0:t0 + 512], Mb[1][:, :])
            for g in range(2):
                self.proj_fm(Mb[0], Wks, g * 128, 128, hT, t0, 512)
                self.norm_fm(Mb[0], Mb[1], 128, 512, gks[:, 0:1], ksT[g][:, t0:t0 + 512], sq, rs)
                self.proj_fm(Mb[0], Wkw, g * 128, 128, hT, t0, 512)
                self.norm_fm(Mb[0], Mb[1], 128, 512, gkw[:, 0:1], kwT[g][:, t0:t0 + 512], sq, rs)
        for tt in range(NT):
            ps = Mb[tt % 2]
            self.proj_tm(ps, Wkv, 384, 128, hT, tt)
            P.cp("act", vs1[:, tt, :, 0:64], ps[:, 0:128].rearrange("p (g e) -> p g e", e=64))
            self.proj_tm(ps, Wkv, 640, 128, hT, tt)
            P.cp("act", vw1[:, tt, :, 0:64], ps[:, 0:128].rearrange("p (g e) -> p g e", e=64))
        P.maybe_barrier()
        def strided(t, lo, off):
            return bass.AP(t[:].tensor, lo * S + off, [[S, 64], [16, 127]])
        for g in range(2):
            lo = g * 64
            for kv, src in ((0, kcT), (1, vcT)):
                ps = Mb[kv]
                for li in range(32):
                    P.mm(ps[0:127, 0:64], lhsT=strided(src, lo, li), rhs=cmpw[kv][lo:lo + 64, li, :],
                         start=(li == 0), stop=False, inc=False)
                for li in range(32):
                    P.mm(ps[0:127, 0:64], lhsT=posB[kv][lo:lo + 64, li, :], rhs=cmpw[kv][lo:lo + 64, li, :],
                         start=False, stop=(li == 31), inc=(li == 31))
            P.act(sq[0:127, 0:64], Mb[0][0:127, 0:64], AF.Square, accum=ss[0:127, :])
            P.act(ss[0:127, :], ss[0:127, :], AF.Sqrt, bias=EPS, scale=1.0 / 64)
            P.recip(ss[0:127, :], ss[0:127, :])
            P.stt("dve", kd[0:127, 0:64], Mb[0][0:127, 0:64], ss[0:127, 0:1], gkc[0:127, :], ALU.mult, ALU.mult)
            P.cp("pool", kd[0:127, 64:128], kd[0:127, 0:64])
            P.tr(tb[0][:, 0, :], kd[:], C["ident"][:])
            P.cp("act", kcmpT[g][:], tb[0][:, 0, :])
            P.cp("act", vext[g][0:127, 0:64], Mb[1][0:127, 0:64])
            P.cp("pool", vext[g][0:127, 64:97], C["cover"][0:127, :])
        P.maybe_barrier()


    def phase_nsa(self, l, hT, yb_d):
        P, C = self.P, self.C
        I = self.inp
        self.cast_engs = ["pool", "dve"]
        w_in = I["w_in"]
        KV = OFF["nkv"]
        with P.scope() as st:
            Wq = P.sb("Wnq", [128, 8, 1024], BF16, st)
            self.wload(Wq, w_in, l, OFF["nq"], 1024)
            Wkv = P.sb("Wkv", [128, 8, 768], BF16, st)
            self.wload(Wkv, w_in, l, KV, 768)
            Wks = P.sb("Wks", [128, 8, 256], BF16, st)
            Wkw = P.sb("Wkw", [128, 8, 256], BF16, st)
            for g in range(2):
                for r in range(2):
                    self.wload(Wks[:, :, g * 128 + r * 64:g * 128 + r * 64 + 64], w_in, l, KV + 256 + g * 64, 64)
                    self.wload(Wkw[:, :, g * 128 + r * 64:g * 128 + r * 64 + 64], w_in, l, KV + 512 + g * 64, 64)
            Wng = P.sb("Wng", [128, 8, 48], BF16, st)
            self.wload(Wng, w_in, l, OFF["ng"], 48)
            gq = self.gain_col("gnq", I["nsa_q_norm"][l], st)
            gks = self.gain_col("gks", I["nsa_k_norm"][l, 1], st)
            gkw = self.gain_col("gkw", I["nsa_k_norm"][l, 2], st)
            gkc = P.sb("gkc", [128, 64], F32, st)
            P.dma(gkc[:], I["nsa_k_norm"][l, 0].partition_broadcast(128))
            ksT = [P.sb("ksT", [128, S], BF16, st) for _ in range(2)]
            kwT = [P.sb("kwT", [128, S], BF16, st) for _ in range(2)]
            vs1 = P.sb("vs1", [128, NT, 2, 65], BF16, st)
            vw1 = P.sb("vw1", [128, NT, 2, 65], BF16, st)
            kcmpT = [P.sb("kcmpT", [128, 128], BF16, st) for _ in range(2)]
            vext = [P.sb("vext", [128, 97], BF16, st) for _ in range(2)]
            sq = P.sb("sq", [128, 512], F32, st)
            rs = P.sb("rs", [128, 512], F32, st)
            ss = P.sb("ss1", [128, 1], F32, st)
            bank = [P.ps("bk", [128, 512], F32, st) for _ in range(6)]
            tb = [P.ps("tb", [128, 8, 128], BF16, st) for _ in range(2)]
            Sb, Ob, Mb = bank[0:2], bank[2:4], bank[4:6]
            with P.scope() as st2:
                self._nsa_kside(l, hT, st2, locals())
            nqz = P.sb("nqz", [128, 16, 512], BF16, st)
            P.memset("pool", nqz[:], 0.0)
            gs = P.sb("gs", [128, 48], F32, st)
            yacc = P.sb("yacc", [128, 16, 64], F32, st)
            ytmp = P.sb("ytmp", [128, 4, 64], F32, st)
            impn = P.sb("impn", [128, 16, 32], F32, st)
            imp = P.sb("imp", [128, 32], F32, st)
            mx = P.sb("mx8", [128, 8], F32, st)
            negblk = P.sb("negblk", [128, 32], BF16, st)
            negblkT = [P.sb("negblkT", [32, 128], BF16, st) for _ in range(2)]
            pTb = [P.sb("pTb", [128, 512], BF16, st) for _ in range(3)]
            Sb = [bank[0], bank[1], bank[5]]
            ytm = P.sb("ytm", [128, 1024], BF16, st)
            yT = P.sb("yT", [128, 8, 512], BF16, st)
            rc = P.sb("rc", [128, 4], F32, st)
            cf = P.sb("cf", [128, 4], F32, st)
            Osb = [P.sb("Osb", [128, 4, 97], F32, st) for _ in range(2)]
            fin_cnt = [0]

            cnt = {"si": 0}
            oi = 0
            for Q in range(4):
                t0 = Q * 512
                for c in range(8):
                    self.proj_fm(Mb[0], Wq, c * 128, 128, hT, t0, 512)
                    P.act(sq[:, :], Mb[0][:, :], AF.Square)
                    P.mm(Mb[1][:, :], lhsT=C["blk64_f"][:], rhs=sq[:, :])
                    P.act(rs[:, :], Mb[1][:, :], AF.Sqrt, bias=EPS, scale=1.0 / 64)
                    P.recip(rs[:, :], rs[:, :])
                    for hf in range(2):
                        psl = slice(64 * hf, 64 * hf + 64)
                        P.stt("dve", nqz[psl, 2 * c + hf, :], Mb[0][psl, :], gq[psl, 0:1], rs[psl, :], ALU.mult, ALU.mult)
                for tl in range(4):
                    qt = Q * 4 + tl
                    if qt >= self.lim:
                        continue
                    P.maybe_barrier()
                    tsl = slice(tl * 128, (tl + 1) * 128)
                    self.proj_tm(Mb[0], Wng, 0, 48, hT, qt)
                    P.act(gs[:], Mb[0][:, 0:48], AF.Sigmoid)
                    gs3 = gs[:].rearrange("p (h b) -> p h b", b=3)

                    def scores(Sp, sub, kT_g, kt_cols, masks):
                        first = True
                        for (ml, mr) in masks:
                            kk = mr.shape[0]
                            P.mm(Sp[:, :], lhsT=ml, rhs=mr.unsqueeze(1).to_broadcast([kk, 4, 128]),
                                 start=first, stop=False, inc=False)
                            first = False
                        P.mm(Sp[:, :], lhsT=kT_g[:, kt_cols], rhs=nqz[:, 4 * sub:4 * sub + 4, tsl],
                             start=first, stop=True)

                    def finalize(O3p, sub, br, first):
                        w_ = 97 if br == 0 else 65
                        O3 = Osb[fin_cnt[0] % 2][:, :, 0:w_]
                        fin_cnt[0] += 1
                        P.cp("act", O3, O3p)
                        hs = slice(4 * sub, 4 * sub + 4)
                        P.ts("dve", rc[:], O3[:, :, 64], 1e-30, None, op0=ALU.max)
                        P.recip(rc[:], rc[:])
                        if br == 0:
                            P.tt("dve", impn[:, hs, :], O3[:, :, 65:97], rc[:].unsqueeze(2).to_broadcast([128, 4, 32]),
                                 ALU.mult)
                        P.tt("dve", cf[:], rc[:], gs3[:, hs, br], ALU.mult)
                        if first:
                            P.tt("dve", yacc[:, hs, :], O3[:, :, 0:64], cf[:].unsqueeze(2).to_broadcast([128, 4, 64]),
                                 ALU.mult)
                        else:
                            P.tt("dve", ytmp[:], O3[:, :, 0:64], cf[:].unsqueeze(2).to_broadcast([128, 4, 64]),
                                 ALU.mult)
                            P.tt("pool", yacc[:, hs, :], yacc[:, hs, :], ytmp[:], ALU.add)

                    items = []
                    for sub in range(4):
                        g = sub // 2
                        O = Ob[oi % 2]
                        oi += 1
                        O3 = O[:, 0:388].rearrange("p (h e) -> p h e", e=97)

                        def em_s(Sp, sub=sub, g=g):
                            scores(Sp, sub, kcmpT[g], slice(0, 128),
                                   [(C["ident"][:], C["visneg"][:, qt * 128:(qt + 1) * 128])])

                        def em_pv(pT, sub=sub, g=g, O3=O3):
                            for hh in range(4):
                                P.mm(O3[:, hh, :], lhsT=pT[:, hh * 128:(hh + 1) * 128], rhs=vext[g][:, :],
                                     start=(hh == 0), stop=True, inc=(hh == 3), sgc=True)
                            finalize(O3, sub, 0, True)
                        items.append((em_s, em_pv))
                    self.pipelined(items, Sb, pTb, cnt, depth=2)
                    for g in range(2):
                        P.op("dve", lambda e: e.tensor_reduce(out=imp[:], in_=impn[:, 8 * g:8 * g + 8, :].rearrange("p h j -> p j h"),
                                                              axis=AX.X, op=ALU.add),
                             [impn[:, 8 * g:8 * g + 8, :]], [imp[:]])
                        P.tt("dve", imp[:], imp[:], C["selkeep"][:, qt, :], ALU.mult)
                        P.tt("dve", imp[:], imp[:], C["selbias"][:, qt, :], ALU.add)
                        P.op("dve", lambda e: e.max(out=mx[:], in_=imp[:]), [imp[:]], [mx[:]])
                        P.ts("dve", negblk[:], imp[:], mx[:, 3:4], 1.0, op0=ALU.is_ge, op1=ALU.subtract)
                        P.tr(tb[0][0:32, 0, :], negblk[:], C["ident"][:])
                        P.amul(negblkT[g][:], tb[0][0:32, 0, :], -NEGM)
                    items = []
                    for br in (2, 1):
                        kts = list(range(0, qt + 1)) if br == 1 else list(range(max(0, qt - 4), qt + 1))
                        kT_l = ksT if br == 1 else kwT
                        v_l = vs1 if br == 1 else vw1
                        for sub in range(4):
                            g = sub // 2
                            O = Ob[oi % 2]
                            oi += 1
                            O3 = O[:, 0:260].rearrange("p (h e) -> p h e", e=65)
                            for ki, kt in enumerate(kts):
                                masks = []
                                if br == 1:
                                    masks.append((C["E"][:, kt * 128:(kt + 1) * 128], negblkT[g][:]))
                                if kt == qt:
                                    masks.append((C["ident"][:], C["cneg"][:]))
                                if br == 2 and kt == qt - 4:
                                    masks.append((C["ident"][:], C["wneg"][:]))

                                def em_s(Sp, sub=sub, g=g, kt=kt, masks=masks, kT_l=kT_l):
                                    scores(Sp, sub, kT_l[g], slice(kt * 128, (kt + 1) * 128), masks)

                                def em_pv(pT, sub=sub, g=g, kt=kt, ki=ki, nkt=len(kts), O3=O3, v_l=v_l, br=br):
                                    for hh in range(4):
                                        P.mm(O3[:, hh, :], lhsT=pT[:, hh * 128:(hh + 1) * 128], rhs=v_l[:, kt, g, :],
                                             start=(ki == 0 and hh == 0), stop=(ki == nkt - 1), inc=(hh == 3), sgc=True)
                                    if ki == nkt - 1:
                                        finalize(O3, sub, br, False)
                                items.append((em_s, em_pv))
                    self.pipelined(items, Sb, pTb, cnt, depth=2)
                    P.cp("act", ytm[:], yacc[:].rearrange("p h e -> p (h e)"))
                    t_ = tb[1]
                    for c in range(8):
                        P.tr(t_[:, c, :], ytm[:, c * 128:(c + 1) * 128], C["ident"][:], inc=(c == 7))
                    P.cp("act", yT[:, :, tsl], t_[:])
                P.dma(yb_d[:, :, t0:t0 + 512], yT[:], q="sp")

    def phase_ssd(self, l, hT, yc_d, zs_d, xact_d):
        P, C = self.P, self.C
        I = self.inp
        self.cast_engs = ["pool", "dve"]
        w_in = I["w_in"]
        with P.scope() as st:
            Wz = P.sb("Wz", [128, 8, 2048], BF16, st)
            self.wload(Wz, w_in, l, OFF["sz"], 2048)
            zt = [P.sb("zt", [128, 2048], F32, st) for _ in range(2)]
            bank = [P.ps("bk", [128, 512], F32, st) for _ in range(4)]
            bi = 0
            for tt in range(NT):
                z_t = zt[tt % 2]
                for nb in range(4):
                    ps = bank[bi % 4]
                    bi += 1
                    self.proj_tm(ps, Wz, nb * 512, 512, hT, tt)
                    P.act(z_t[:, nb * 512:(nb + 1) * 512], ps[:, :], AF.Silu)
                P.dma(zs_d[tt * 128:(tt + 1) * 128, :], z_t[:])
        with P.scope() as st:
            cw = P.sb("cw", [128, 24, 4], F32, st)
            cb = P.sb("cb", [128, 24], F32, st)
            for cc in range(24):
                P.dma(cw[:, cc, :], I["ssd_conv_w"][l][:, cc * 128:(cc + 1) * 128].rearrange("k c -> c k"),
                      allow_slow_non_contiguous=True)
                P.dma(cb[:, cc:cc + 1], I["ssd_conv_b"][l][cc * 128:(cc + 1) * 128].rearrange("(c o) -> c o", o=1))
            Wx = [P.sb("Wx", [128, 8, 512], BF16, st) for _ in range(2)]
            raw = [P.sb("raw", [128, 3 + S], F32, st) for _ in range(2)]
            accb = [P.sb("accb", [128, S], F32, st) for _ in range(2)]
            xa = [P.sb("xa", [128, S], BF16, st) for _ in range(2)]
            ctmp = P.sb("ctmp", [128, S], F32, st)
            bank = [P.ps("bk", [128, 512], F32, st) for _ in range(4)]
            for r_ in raw:
                P.memset("pool", r_[:, 0:3], 0.0)
            bi = 0
            for cc in range(24):
                W_ = Wx[(cc // 4) % 2]
                if cc % 4 == 0:
                    self.wload(W_, w_in, l, OFF["sxbc"] + cc * 128, 512)
                rw = raw[cc % 2]
                for Q in range(4):
                    ps = bank[bi % 4]
                    bi += 1
                    self.proj_fm(ps, W_, (cc % 4) * 128, 128, hT, Q * 512, 512)
                    P.cp("act", rw[:, 3 + Q * 512:3 + (Q + 1) * 512], ps[:, :])
                eng = "dve"
                ac = accb[cc % 2]
                P.ts(eng, ac[:], rw[:, 3:3 + S], cw[:, cc, 3:4], None, op0=ALU.mult)
                for k in (2, 1, 0):
                    if eng == "dve":
                        P.stt(eng, ac[:], rw[:, k:k + S], cw[:, cc, k:k + 1], ac[:], ALU.mult, ALU.add)
                    else:
                        P.ts(eng, ctmp[:], rw[:, k:k + S], cw[:, cc, k:k + 1], None, op0=ALU.mult)
                        P.tt(eng, ac[:], ac[:], ctmp[:], ALU.add)
                P.act(xa[cc % 2][:], ac[:], AF.Silu, bias=cb[:, cc:cc + 1])
                P.dma(xact_d[:, cc, :], xa[cc % 2][:])
                P.maybe_barrier()
        with P.scope() as st:
            Wdt = P.sb("Wdt", [128, 8, 32], BF16, st)
            self.wload(Wdt, w_in, l, OFF["sdt"], 32)
            dtb = P.sb("dtb", [128, 32], F32, st)
            P.dma(dtb[:], I["ssd_dt_bias"][l].partition_broadcast(128))
            a_b = P.sb("a_b", [128, 32], F32, st)
            P.dma(a_b[:], I["ssd_a_log"][l].partition_broadcast(128))
            P.act(a_b[:], a_b[:], AF.Exp)
            P.ts("dve", a_b[:], a_b[:], -1.0, None, op0=ALU.mult)
            D_b = P.sb("D_b", [128, 32], F32, st)
            P.dma(D_b[:], I["ssd_d"][l].partition_broadcast(128))
            ng_b = P.sb("ng_b", [128, 2048], F32, st)
            P.dma(ng_b[:], I["ssd_norm_g"][l].partition_broadcast(128))
            xab = [P.sb("xab", [128, 24, 128], BF16, st) for _ in range(2)]
            zsb = [P.sb("zsb", [128, 2048], F32, st) for _ in range(2)]
            xs_tm = P.sb("xs_tm", [128, 2048], BF16, st)
            bm_b = [P.sb("bm_tm", [128, 512], BF16, st) for _ in range(2)]
            xsw_b = [P.sb("xs_w", [128, 2048], BF16, st) for _ in range(2)]
            rhs1_b = [P.sb("rhs1", [128, 8, 128], F32, st) for _ in range(2)]
            rhs2_b = [P.sb("rhs2", [128, 8, 128], F32, st)] * 2
            Lg_b = [P.sb("Lg", [128, 8, 128], BF16, st) for _ in range(2)]
            MTg_b = [P.sb("MTg", [128, 8, 128], BF16, st) for _ in range(2)]
            tmp_b = [P.sb("tmpb", [128, 512], F32, st) for _ in range(2)]
            cbT = P.sb("cbT", [128, 4, 128], BF16, st)
            H = [P.sb("H", [128, 512], F32, st) for _ in range(4)]
            Hbf = [P.sb("Hbf", [128, 512], BF16, st) for _ in range(4)]
            y_b = [P.sb("y", [128, 2048], F32, st) for _ in range(2)]
            ynb = P.sb("ynb", [128, 2048], BF16, st)
            ycT = P.sb("ycT", [128, 16, 128], BF16, st)
            sma = {n: P.sb(n, [128, NT, 32], F32, st) for n in ("dtc", "da", "nb", "ea", "wj", "dec")}
            sma["lndt"] = sma["nb"]
            sma["acum"] = sma["ea"]
            sma["alast"] = sma["dec"]
            ss = P.sb("ss2", [128, 1], F32, st)
            R = [P.ps("R", [128, 512], F32, st) for _ in range(2)]
            Yp_b = [P.ps("Yp", [128, 512], F32, st) for _ in range(2)]
            Yo = P.ps("Yo", [128, 512], F32, st)
            STp = P.ps("STp", [128, 512], F32, st)
            M0 = P.ps("M0", [128, 512], F32, st)
            M1 = M0
            tb = P.ps("tb", [128, 8, 128], BF16, st)
            for g in range(4):
                P.memset("pool", H[g][:], 0.0)
                P.memset("pool", Hbf[g][:], 0.0)
            nch = min(NT, self.lim)
            fl = lambda t: t[:].rearrange("p c h -> p (c h)")
            for c in range(NT):
                for kc in range(8):
                    P.mm(M1[:, c * 32:(c + 1) * 32], lhsT=hT[:, kc, c * 128:(c + 1) * 128], rhs=Wdt[:, kc, :],
                         start=(c == 0 and kc == 0), stop=(kc == 7), inc=(kc == 7), sgc=True)
            P.tt("dve", sma["dtc"][:], M1[:, :].rearrange("p (c h) -> p c h", h=32),
                 dtb[:].unsqueeze(1).to_broadcast([128, NT, 32]), ALU.add)
            P.act(fl(sma["dtc"]), fl(sma["dtc"]), AF.Exp)
            P.act(fl(sma["dtc"]), fl(sma["dtc"]), AF.Ln, bias=1.0)
            P.act(fl(sma["lndt"]), fl(sma["dtc"]), AF.Ln)
            P.tt("dve", sma["da"][:], sma["dtc"][:], a_b[:].unsqueeze(1).to_broadcast([128, NT, 32]), ALU.mult)
            P.mm(M1[:, :], lhsT=C["tri_f"][:], rhs=fl(sma["da"]))
            P.cp("dve", fl(sma["acum"]), M1[:, :])
            P.mm(M1[:, :], lhsT=C["ones_f"][:], rhs=fl(sma["da"]))
            P.cp("dve", fl(sma["alast"]), M1[:, :])
            P.tt("dve", sma["nb"][:], sma["lndt"][:], sma["acum"][:], ALU.subtract)
            P.tt("dve", sma["wj"][:], sma["alast"][:], sma["acum"][:], ALU.subtract)
            P.act(fl(sma["wj"]), fl(sma["wj"]), AF.Exp)
            P.tt("dve", sma["wj"][:], sma["wj"][:], sma["dtc"][:], ALU.mult)
            P.act(fl(sma["ea"]), fl(sma["acum"]), AF.Exp)
            P.act(fl(sma["dec"]), fl(sma["alast"]), AF.Exp)
            sma = {k: sma[k] for k in ("da", "nb", "ea", "wj", "dec")}

            def load(c):
                csl = slice(c * 128, (c + 1) * 128)
                P.dma(xab[c % 2][:], xact_d[:, :, csl])
                P.dma(zsb[c % 2][:], zs_d[csl, :])

            def part1(c):
                p = c % 2
                xa_ = xab[p]
                sm = {k: v[:, c, :] for k, v in sma.items()}
                bm_tm, xs_w, y = bm_b[p], xsw_b[p], y_b[p]
                for k0 in (0, 8, 16):
                    n = 8 if k0 < 16 else 4
                    for j in range(n):
                        P.tr(tb[:, j, :], xa_[:, k0 + j, :], C["ident"][:], inc=(j == n - 1))
                    if k0 < 16:
                        P.cp("act", xs_tm[:, k0 * 128:(k0 + 8) * 128], tb[:].rearrange("p a b -> p (a b)"))
                    else:
                        P.cp("act", bm_tm[:], tb[:, 0:4, :].rearrange("p a b -> p (a b)"))
                for g in range(4):
                    P.mm(M0[:, g * 128:(g + 1) * 128], lhsT=xa_[:, 16 + g, :], rhs=xa_[:, 20 + g, :],
                         start=(g == 0), stop=True, inc=(g == 3), sgc=True)
                P.cp("act", cbT[:].rearrange("p a b -> p (a b)"), M0[:, :])
                P.tt("dve", xs_w[:].rearrange("p (h e) -> p h e", e=64), xs_tm[:].rearrange("p (h e) -> p h e", e=64),
                     sm["wj"][:].unsqueeze(2).to_broadcast([128, 32, 64]), ALU.mult)
                for g in range(4):
                    hs = slice(8 * g, 8 * g + 8)
                    gsl = slice(g * 512, (g + 1) * 512)
                    rhs1, rhs2, Lg, MTg, Yp, tmp = rhs1_b[g % 2], rhs2_b[g % 2], Lg_b[g % 2], MTg_b[g % 2], Yp_b[g % 2], tmp_b[g % 2]
                    P.tt("dve", rhs1[:], C["tri_f"][:].unsqueeze(1).to_broadcast([128, 8, 128]),
                         sm["da"][:, hs].unsqueeze(2).to_broadcast([128, 8, 128]), ALU.mult)
                    P.tt("dve", rhs2[:], C["cneg_f"][:].unsqueeze(1).to_broadcast([128, 8, 128]),
                         sm["nb"][:, hs].unsqueeze(2).to_broadcast([128, 8, 128]), ALU.add)
                    for hf in range(2):
                        P.mm(R[hf][:, :], lhsT=C["ones_f"][:], rhs=rhs1[:, 4 * hf:4 * hf + 4, :].rearrange("p a b -> p (a b)"),
                             start=True, stop=False, inc=False)
                        P.mm(R[hf][:, :], lhsT=C["ident_f"][:], rhs=rhs2[:, 4 * hf:4 * hf + 4, :].rearrange("p a b -> p (a b)"),
                             start=False, stop=True)
                        P.act(Lg[:, 4 * hf:4 * hf + 4, :].rearrange("p a b -> p (a b)"), R[hf][:, :], AF.Exp)
                    P.tt("dve" if g % 2 else "pool", MTg[:], Lg[:], cbT[:, g, :].unsqueeze(1).to_broadcast([128, 8, 128]), ALU.mult)
                    for hh in range(8):
                        P.mm(Yp[:, hh * 64:(hh + 1) * 64], lhsT=MTg[:, hh, :], rhs=xs_tm[:, (8 * g + hh) * 64:(8 * g + hh + 1) * 64],
                             start=(hh == 0), stop=True, inc=(hh == 7), sgc=True)
                    P.tt("pool", tmp[:].rearrange("p (h e) -> p h e", e=64), xs_tm[:, gsl].rearrange("p (h e) -> p h e", e=64),
                         D_b[:, hs].unsqueeze(2).to_broadcast([128, 8, 64]), ALU.mult)
                    P.tt("dve", y[:, gsl], Yp[:, :], tmp[:], ALU.add)

            def part2(c):
                p = c % 2
                csl = slice(c * 128, (c + 1) * 128)
                xa_ = xab[p]
                zs_ = zsb[p]
                sm = {k: v[:, c, :] for k, v in sma.items()}
                bm_tm, xs_w, y = bm_b[p], xsw_b[p], y_b[p]
                for g in range(4):
                    hs = slice(8 * g, 8 * g + 8)
                    gsl = slice(g * 512, (g + 1) * 512)
                    tmp = tmp_b[g % 2]
                    P.mm(Yo[:, :], lhsT=xa_[:, 20 + g, :], rhs=Hbf[g][:])
                    P.tt("dve", tmp[:].rearrange("p (h e) -> p h e", e=64), Yo[:, :].rearrange("p (h e) -> p h e", e=64),
                         sm["ea"][:, hs].unsqueeze(2).to_broadcast([128, 8, 64]), ALU.mult)
                    P.tt("pool", y[:, gsl], y[:, gsl], tmp[:], ALU.add)
                    P.mm(STp[:, :], lhsT=bm_tm[:, g * 128:(g + 1) * 128], rhs=xs_w[:, gsl])
                    Hv = H[g][:].rearrange("p (h e) -> p h e", e=64)
                    P.tt("pool", Hv, Hv, sm["dec"][:, hs].unsqueeze(2).to_broadcast([128, 8, 64]), ALU.mult)
                    P.tt("dve", H[g][:], STp[:, :], H[g][:], ALU.add)
                    P.cp("act", Hbf[g][:], H[g][:])
                P.tt("dve", y[:], y[:], zs_[:], ALU.mult)
                P.act(zs_[:], y[:], AF.Square, accum=ss[:])
                P.act(ss[:], ss[:], AF.Sqrt, bias=EPS, scale=1.0 / 2048)
                P.recip(ss[:], ss[:])
                P.stt("dve", ynb[:], y[:], ss[:, 0:1], ng_b[:], ALU.mult, ALU.mult)
                for k0 in (0, 8):
                    for j in range(8):
                        P.tr(tb[:, j, :], ynb[:, (k0 + j) * 128:(k0 + j + 1) * 128], C["ident"][:], inc=(j == 7))
                    P.cp("act", ycT[:, k0:k0 + 8, :], tb[:])
                P.dma(yc_d[:, :, csl], ycT[:])

            if nch > 0:
                load(0)
                if nch > 1:
                    load(1)
                part1(0)
            for c in range(nch):
                if P.sems_left() < 40:
                    P.barrier()
                if c + 1 < nch:
                    part1(c + 1)
                part2(c)
                if c + 2 < nch:
                    load(c + 2)

    def phase_merge(self, l, hT, ya_d, yb_d, yc_d, mix_d):
        P, C = self.P, self.C
        I = self.inp
        self.cast_engs = ["pool", "dve"]
        with P.scope() as st:
            yh = [P.sb("yha", [128, 8, 1024], BF16, st), P.sb("yhb", [128, 8, 1024], BF16, st),
                  P.sb("yhc", [128, 16, 1024], BF16, st)]
            Wy = [[P.sb("Wya", [128, 8, 128], BF16, st), P.sb("Wyb", [128, 8, 128], BF16, st),
                   P.sb("Wyc", [128, 16, 128], BF16, st)] for _ in range(2)]
            Wg = [[P.sb("Wg", [128, 8, 128], BF16, st) for _ in range(3)] for _ in range(2)]
            sg = [P.sb("sg", [128, 512], F32, st) for _ in range(2)]
            macc = P.sb("macc", [128, 512], F32, st)
            mt = P.sb("mt", [128, 512], F32, st)
            mixh = P.sb("mixh", [128, 8, 1024], BF16, st)
            bank = [P.ps("bk", [128, 512], F32, st) for _ in range(6)]
            wsrc = [I["w_br_dsa"], I["w_br_nsa"], I["w_br_ssd"]]
            ysrc = [ya_d, yb_d, yc_d]
            bi = 0
            gi = 0
            it = 0
            for half in range(2):
                hsl = slice(half * 1024, (half + 1) * 1024)
                for br in range(3):
                    P.dma(yh[br][:], ysrc[br][:, :, hsl])
                for oc in range(8):
                    Wy_ = Wy[it % 2]
                    Wg_ = Wg[it % 2]
                    it += 1
                    for br in range(3):
                        self.wload(Wy_[br], wsrc[br], l, oc * 128, 128)
                        self.wload(Wg_[br], I["w_in"], l, OFF["mg"] + br * 1024 + oc * 128, 128)
                    for pc in range(2):
                        psl = slice(pc * 512, (pc + 1) * 512)
                        tok = half * 1024 + pc * 512
                        for br in range(3):
                            psy = bank[bi % 6]
                            bi += 1
                            psg = bank[bi % 6]
                            bi += 1
                            nk = 16 if br == 2 else 8
                            for kc in range(nk):
                                P.mm(psy[:, :], lhsT=Wy_[br][:, kc, :], rhs=yh[br][:, kc, psl],
                                     start=(kc == 0), stop=(kc == nk - 1))
                            self.proj_fm(psg, Wg_[br], 0, 128, hT, tok, 512)
                            s_ = sg[gi % 2]
                            gi += 1
                            P.act(s_[:], psg[:, :], AF.Sigmoid)
                            if br == 0:
                                P.tt("dve", macc[:], psy[:, :], s_[:], ALU.mult)
                            else:
                                P.tt("dve", mt[:], psy[:, :], s_[:], ALU.mult)
                                if br == 1:
                                    P.tt("pool", macc[:], macc[:], mt[:], ALU.add)
                                else:
                                    P.tt("pool", mixh[:, oc, psl], macc[:], mt[:], ALU.add)
                    P.maybe_barrier()
                P.dma(mix_d[:, :, hsl], mixh[:])

    def phase_out_norm(self, l, x_d, mix_d, x1_d, hT):
        P, C = self.P, self.C
        I = self.inp
        self.cast_engs = ["pool", "dve"]
        with P.scope() as st:
            Wo = P.sb("Wo", [128, 8, 1024], BF16, st)
            self.wload(Wo, I["w_out"], l, 0, 1024)
            gb = P.sb("gb2", [128, D], F32, st)
            P.dma(gb[:], I["norm2_g"][l].partition_broadcast(128))
            mq = [P.sb("mq", [128, 8, 128], BF16, st) for _ in range(2)]
            xb = [P.sb("xb", [128, D], F32, st) for _ in range(2)]
            x1b = [P.sb("x1b", [128, D], F32, st) for _ in range(2)]
            sq = P.sb("sq", [128, D], F32, st)
            hb = [P.sb("hb", [128, D], BF16, st) for _ in range(2)]
            ss = P.sb("ss", [128, 2], F32, st)
            bank = [P.ps("bk", [128, 512], F32, st) for _ in range(4)]
            pt = [P.ps("pt", [128, 8, 128], BF16, st) for _ in range(2)]
            for tt in range(NT):
                tsl = slice(tt * 128, (tt + 1) * 128)
                m_ = mq[tt % 2]
                P.dma(m_[:], mix_d[:, :, tsl])
                x_t = xb[tt % 2]
                P.dma(x_t[:], x_d[tsl, :])
                x1 = x1b[tt % 2]
                for hf in range(2):
                    ps = bank[(2 * tt + hf) % 4]
                    for oc in range(8):
                        P.mm(ps[:, :], lhsT=m_[:, oc, :], rhs=Wo[:, oc, hf * 512:(hf + 1) * 512],
                             start=(oc == 0), stop=(oc == 7))
                    P.tt("dve", x1[:, hf * 512:(hf + 1) * 512], ps[:, :], x_t[:, hf * 512:(hf + 1) * 512], ALU.add)
                P.dma(x1_d[tsl, :], x1[:])
                s1 = ss[:, tt % 2:tt % 2 + 1]
                P.act(sq[:], x1[:], AF.Square, accum=s1)
                P.act(s1, s1, AF.Sqrt, bias=EPS, scale=1.0 / D)
                P.recip(s1, s1)
                h_t = hb[tt % 2]
                P.stt("dve", h_t[:], x1[:], s1, gb[:], ALU.mult, ALU.mult)
                p_t = pt[tt % 2]
                for kc in range(8):
                    P.tr(p_t[:, kc, :], h_t[:, kc * 128:(kc + 1) * 128], C["ident"][:], inc=(kc == 7))
                P.cp("act", hT[:, :, tsl], p_t[:])

    def phase_ffn(self, l, hT, x1_d, xo_d, xp_d):
        P, C = self.P, self.C
        I = self.inp
        self.cast_engs = ["pool", "dve", "act", "dve"]
        with P.scope() as st:
            W1h = P.sb("W1h", [128, 8, 2048], BF16, st)
            W2h = P.sb("W2h", [128, 16, 1024], BF16, st)
            aT = P.sb("aT", [128, 16, 512], BF16, st)
            rb = [P.sb("rbf", [128, 512], F32, st) for _ in range(2)]
            xt = [P.sb("xt", [128, 512], F32, st) for _ in range(4)]
            xo = [P.sb("xo", [128, 512], F32, st) for _ in range(4)]
            fb = [P.ps("fb", [128, 512], F32, st) for _ in range(2)]
            ob = [P.ps("ob", [128, 512], F32, st) for _ in range(4)]
            xi = 0
            for hp in range(2):
                self.wload(W1h, I["w_ff1"], l, hp * 2048, 2048)
                self.wload(W2h, I["w_ff2"], l, 0, 1024, rows=(hp * 2048, (hp + 1) * 2048))
                src_d = x1_d if hp == 0 else xp_d
                dst_d = xp_d if hp == 0 else xo_d
                for Q in range(4):
                    for fc in range(16):
                        ps = fb[fc % 2]
                        self.proj_fm(ps, W1h, fc * 128, 128, hT, Q * 512, 512)
                        r = rb[fc % 2]
                        P.act(r[:], ps[:, :], AF.Relu)
                        P.tt("dve", aT[:, fc, :], r[:], r[:], ALU.mult)
                    for hf in range(2):
                        cols = slice(hf * 512, (hf + 1) * 512)
                        xs_ = []
                        for tl in range(4):
                            rows = slice((Q * 4 + tl) * 128, (Q * 4 + tl + 1) * 128)
                            x_ = xt[xi % 4]
                            o_ = xo[xi % 4]
                            xi += 1
                            P.dma(x_[:], src_d[rows, cols])
                            xs_.append((x_, o_, rows))
                            for k in range(16):
                                P.mm(ob[tl][:, :], lhsT=aT[:, k, tl * 128:(tl + 1) * 128], rhs=W2h[:, k, cols],
                                     start=(k == 0), stop=(k == 15))
                        for tl in range(4):
                            x_, o_, rows = xs_[tl]
                            P.tt("dve", o_[:], ob[tl][:, :], x_[:], ALU.add)
                            P.dma(dst_d[rows, cols], o_[:])
                    P.maybe_barrier()

    def build(self):
        P = self.P
        I = self.inp
        self.din("x", [S, D])
        for name, shp in (("norm1_g", [DEPTH, D]), ("w_in", [DEPTH, D, D_IN]), ("dsa_q_norm", [DEPTH, 64]),
                          ("dsa_k_norm", [DEPTH, 64]), ("nsa_q_norm", [DEPTH, 64]), ("nsa_k_norm", [DEPTH, 3, 64]),
                          ("nsa_cmp_pos", [DEPTH, 2, 32, 64]), ("nsa_cmp_w", [DEPTH, 2, 32, 64, 64]),
                          ("ssd_conv_w", [DEPTH, 4, 3072]), ("ssd_conv_b", [DEPTH, 3072]),
                          ("ssd_dt_bias", [DEPTH, 32]), ("ssd_a_log", [DEPTH, 32]), ("ssd_d", [DEPTH, 32]),
                          ("ssd_norm_g", [DEPTH, 2048]), ("w_br_dsa", [DEPTH, D, D]), ("w_br_nsa", [DEPTH, D, D]),
                          ("w_br_ssd", [DEPTH, 2 * D, D]), ("w_out", [DEPTH, D, D]), ("norm2_g", [DEPTH, D]),
                          ("w_ff1", [DEPTH, D, 4 * D]), ("w_ff2", [DEPTH, 4 * D, D])):
            self.din(name, shp)
        self.load_consts()
        out = self.dout("out", [S, D])
        hT = P.sb("hT", [128, 8, S], BF16)
        ya_d = P.dram("ya", [128, 8, S], BF16)
        yb_d = P.dram("yb", [128, 8, S], BF16)
        yc_d = P.dram("yc", [128, 16, S], BF16)
        zs_d = P.dram("zs", [S, 2048], F32)
        xact_d = P.dram("xact", [128, 24, S], BF16)
        mix_d = P.dram("mix", [128, 8, S], BF16)
        x1_d = P.dram("x1", [S, D], F32)
        xm_d = P.dram("xm", [S, D], F32)
        xp_d = P.dram("xp", [S, D], F32)
        dbg = self.debug
        x_cur = I["x"]
        for l in range(self.nlayers):
            x_nxt = out if l == self.nlayers - 1 else xm_d
            self.phase_norm(x_cur, I["norm1_g"][l], hT)
            if dbg == "hT":
                return self._dump(hT[:], [128, 8, S], BF16)
            if dbg in ("nsa", "nsa_k"):
                self.phase_nsa(l, hT, yb_d)
                return self._dump(yb_d, [128, 8, S], BF16)
            if dbg == "ssd":
                self.phase_ssd(l, hT, yc_d, zs_d, xact_d)
                return self._dump(yc_d, [128, 16, S], BF16)
            self.phase_dsa(l, hT, ya_d)
            if dbg is not None and dbg.startswith("dsa"):
                return self._dump(ya_d, [128, 8, S], BF16)
            self.phase_nsa(l, hT, yb_d)
            self.phase_ssd(l, hT, yc_d, zs_d, xact_d)
            self.phase_merge(l, hT, ya_d, yb_d, yc_d, mix_d)
            if dbg == "mix":
                return self._dump(mix_d, [128, 8, S], BF16)
            self.phase_out_norm(l, x_cur, mix_d, x1_d, hT)
            if dbg == "x1":
                return self._dump(x1_d, [S, D], F32)
            self.phase_ffn(l, hT, x1_d, x_nxt, xp_d)
            x_cur = x_nxt
        P.finish()

    def _dump(self, src, shape, dt):
        if "dbg" not in self.out:
            o = self.dout("dbg", shape, dt)
            self.P.dma(o, src)
        self.P.finish()


_CACHE = {}


def kernel(**inputs):
    n = 8
    B = Builder()
    B.build()
    consts = _consts()
    shared = {}
    for name in B.inp:
        if name in consts:
            shared[name] = consts[name]
        elif name != "x":
            shared[name] = np.ascontiguousarray(np.asarray(inputs[name], dtype=np.float32))
    x = np.asarray(inputs["x"], dtype=np.float32)
    in_maps = []
    for b in range(n):
        m = dict(shared)
        m["x"] = np.ascontiguousarray(x[b])
        in_maps.append(m)
    res = run_bass_kernel_spmd(B.nc, in_maps, core_ids=list(range(n)))
    return np.stack([np.asarray(res.results[b]["out"], dtype=np.float32) for b in range(n)], axis=0)
```

```python
import contextlib
import os
import numpy as np
import concourse.bass as bass
import concourse.mybir as mybir
from concourse.bass_utils import run_bass_kernel_spmd

F32 = mybir.dt.float32
BF16 = mybir.dt.bfloat16
AF = mybir.ActivationFunctionType
ALU = mybir.AluOpType
AX = mybir.AxisListType

EPOCH = 240
NPOOL = 90

D = 1024
S = 2048
NT = 16
DEPTH = 2
EPS = 1e-6
OFF = dict(dq=0, dk=1024, dv=1088, iq=1152, ik=1408, iw=1440, nq=1448, nkv=2472, ng=3240,
           sz=3288, sxbc=5336, sdt=8408, mg=8440)
D_IN = 11512
NEGM = -30000.0


def _region(ap):
    t = ap.tensor
    name = t.name
    dims = [(int(s), int(c)) for s, c in ap.ap]
    off = int(ap.offset)
    if "DRam" in type(t).__name__:
        lo = hi = off
        for s, c in dims:
            if s >= 0:
                hi += s * (c - 1)
            else:
                lo += s * (c - 1)
        return (name, 0, 1, lo, hi + 1)
    if "PSum" in type(t).__name__:
        return (name, 0, 128, 0, 1 << 30)
    ps, pc = dims[0]
    if ps == 0:
        p0, f0 = 0, off
    else:
        p0 = off // ps
        f0 = off - p0 * ps
    lo = hi = f0
    for s, c in dims[1:]:
        if s >= 0:
            hi += s * (c - 1)
        else:
            lo += s * (c - 1)
    return (name, p0, p0 + pc, lo, hi + 1)


def _overlap(a, b):
    return a[1] < b[2] and b[1] < a[2] and a[3] < b[4] and b[3] < a[4]


def _contains(a, b):
    return a[1] <= b[1] and b[2] <= a[2] and a[3] <= b[3] and b[4] <= a[4]


class Prog:
    ENGS = ("pe", "act", "dve", "pool", "sp")

    def __init__(self, nc):
        self.nc = nc
        self.stack = contextlib.ExitStack()
        self.eng = {"pe": nc.tensor, "act": nc.scalar, "dve": nc.vector, "pool": nc.gpsimd, "sp": nc.sync}
        self.pool = [self.stack.enter_context(nc.semaphore(f"sp{i}")) for i in range(NPOOL)]
        self.bsem = [self.stack.enter_context(nc.semaphore(f"bar{i}")) for i in range(4)]
        self.nbar = 0
        self.uid = 0
        self.n_wait = 0
        self.n_ins = 0
        self.max_used = 0
        self._reset()

    def _reset(self):
        self.free = list(range(NPOOL))
        self.esem = {e: None for e in self.ENGS}
        self.ecnt = {e: 0 for e in self.ENGS}
        self.pend = {e: False for e in self.ENGS}
        self.seen = {e: {} for e in self.ENGS}
        self.dset = []
        self.dall = {}
        self.dnext = 0
        self.trk = {}

    def _alloc(self):
        assert self.free, "semaphore pool exhausted; add a barrier"
        i = self.free.pop()
        self.max_used = max(self.max_used, NPOOL - len(self.free))
        return i

    def sems_left(self):
        return len(self.free)

    def sb(self, name, shape, dtype, stack=None):
        self.uid += 1
        return (stack or self.stack).enter_context(
            self.nc.sbuf_tensor(f"{name}_{self.uid}", list(shape), dtype))

    def ps(self, name, shape, dtype, stack=None):
        self.uid += 1
        return (stack or self.stack).enter_context(
            self.nc.psum_tensor(f"{name}_{self.uid}", list(shape), dtype))

    def dram(self, name, shape, dtype):
        self.uid += 1
        return self.nc.dram_tensor(f"{name}_{self.uid}", list(shape), dtype).ap()

    @contextlib.contextmanager
    def scope(self):
        st = contextlib.ExitStack()
        try:
            yield st
        finally:
            self.barrier()
            st.close()

    def _deps(self, reads, writes):
        deps = []
        for ap in reads:
            r = _region(ap)
            t = self.trk.get(r[0])
            if t is None:
                continue
            for (wr, ev) in t["w"]:
                if _overlap(wr, r):
                    deps.append(ev)
        for ap in writes:
            r = _region(ap)
            t = self.trk.get(r[0])
            if t is None:
                continue
            for (wr, ev) in t["w"]:
                if _overlap(wr, r):
                    deps.append(ev)
            for (sk, rr), val in t["r"].items():
                if _overlap(rr, r):
                    deps.append((sk, val))
        return deps

    def _record(self, reads, writes, ev):
        for ap in reads:
            r = _region(ap)
            t = self.trk.setdefault(r[0], {"w": [], "r": {}})
            k = (ev[0], r)
            if t["r"].get(k, -1) < ev[1]:
                t["r"][k] = ev[1]
        for ap in writes:
            r = _region(ap)
            t = self.trk.setdefault(r[0], {"w": [], "r": {}})
            t["w"] = [(wr, e) for (wr, e) in t["w"] if not _contains(r, wr)]
            t["w"].append((r, ev))
            t["r"] = {k: v for k, v in t["r"].items() if not _contains(r, k[1])}

    def _wait(self, e, deps):
        eng = self.eng[e]
        need = {}
        for sk, val in deps:
            if self.seen[e].get(sk, 0) >= val:
                continue
            if need.get(sk, 0) < val:
                need[sk] = val
        for sk, val in need.items():
            assert 0 < val <= 255
            eng.wait_ge(self.pool[sk[1]], val)
            self.seen[e][sk] = val
            self.n_wait += 1

    def _next_ev(self, e, commit):
        if self.esem[e] is None or self.ecnt[e] >= EPOCH:
            self.esem[e] = self._alloc()
            self.ecnt[e] = 0
        ev = ((e, self.esem[e]), self.ecnt[e] + 1)
        if commit:
            self.ecnt[e] += 1
        return ev

    def op(self, e, fn, reads, writes, inc=True):
        deps = self._deps(reads, writes)
        if e == "pe":
            deps = [d for d in deps if d[0][0] != "pe"]
        self._wait(e, deps)
        ins = fn(self.eng[e])
        self.n_ins += 1
        ev = self._next_ev(e, inc)
        if inc:
            ins.then_inc(self.pool[ev[0][1]], 1)
            self.pend[e] = False
        else:
            self.pend[e] = True
        self._record(reads, writes, ev)
        return ins

    def dma(self, out, in_, q="sp", **kw):
        deps = self._deps([in_], [out])
        if q == "pool":
            ent = [self._alloc(), 0]
        else:
            if len(self.dset) < 8:
                self.dset.append([self._alloc(), 0])
            k = self.dnext % len(self.dset)
            self.dnext += 1
            if self.dset[k][1] >= 15:
                self.dset[k] = [self._alloc(), 0]
            ent = self.dset[k]
        if ent[1] > 0:
            deps.append((("d", ent[0]), 16 * ent[1]))
        self._wait(q, deps)
        ins = self.eng[q].dma_start(out=out, in_=in_, **kw)
        ins.then_inc(self.pool[ent[0]], 16)
        ent[1] += 1
        self.dall[ent[0]] = ent[1]
        self.n_ins += 1
        self._record([in_], [out], (("d", ent[0]), 16 * ent[1]))
        return ins

    def barrier(self):
        evs = []
        for e in self.ENGS:
            assert not self.pend[e], f"pending non-inc op on {e} at barrier"
            if self.esem[e] is not None and self.ecnt[e] > 0:
                evs.append(((e, self.esem[e]), self.ecnt[e]))
        for k, c in self.dall.items():
            evs.append((("d", k), 16 * c))
        for e in self.ENGS:
            self._wait(e, evs)
        b0 = self.bsem[2 * (self.nbar % 2)]
        b1 = self.bsem[2 * (self.nbar % 2) + 1]
        p0 = self.bsem[2 * ((self.nbar + 1) % 2)]
        p1 = self.bsem[2 * ((self.nbar + 1) % 2) + 1]
        for e in self.ENGS:
            if e != "sp":
                self.eng[e].sem_inc(b0, 1)
        sp = self.eng["sp"]
        sp.wait_ge(b0, 4)
        used = [i for i in range(NPOOL) if i not in set(self.free)]
        for i in used:
            sp.sem_clear(self.pool[i])
        sp.sem_clear(p0)
        sp.sem_clear(p1)
        sp.sem_inc(b1, 1)
        for e in self.ENGS:
            if e != "sp":
                self.eng[e].wait_ge(b1, 1)
        self.nbar += 1
        self._reset()

    def maybe_barrier(self, min_free=35):
        if len(self.free) < min_free:
            self.barrier()

    def finish(self):
        self.barrier()
        self.stack.close()

    def mm(self, out, lhsT, rhs, start=True, stop=True, inc=None, sgc=False):
        if inc is None:
            inc = stop
        return self.op("pe", lambda e: e.matmul(out, lhsT=lhsT, rhs=rhs, start=start, stop=stop,
                                                skip_group_check=sgc),
                       [lhsT, rhs], [out], inc=inc)

    def tr(self, out, in_, ident, inc=True):
        return self.op("pe", lambda e: e.transpose(out=out, in_=in_, identity=ident), [in_, ident], [out], inc=inc)

    def act(self, out, in_, func, bias=None, scale=None, accum=None):
        kw = {}
        reads = [in_]
        writes = [out]
        if bias is not None:
            kw["bias"] = bias
            if not isinstance(bias, (int, float)):
                reads.append(bias)
        if scale is not None:
            kw["scale"] = scale
            if not isinstance(scale, (int, float)):
                reads.append(scale)
        if accum is not None:
            kw["accum_out"] = accum
            writes.append(accum)
        return self.op("act", lambda e: e.activation(out=out, in_=in_, func=func, **kw), reads, writes)

    def cp(self, eng, out, in_):
        if eng == "act":
            return self.op("act", lambda e: e.copy(out=out, in_=in_), [in_], [out])
        return self.op(eng, lambda e: e.tensor_copy(out=out, in_=in_), [in_], [out])

    def tt(self, eng, out, in0, in1, op):
        return self.op(eng, lambda e: e.tensor_tensor(out=out, in0=in0, in1=in1, op=op), [in0, in1], [out])

    def ts(self, eng, out, in0, s1, s2=None, op0=ALU.mult, op1=None, accum=None):
        reads = [in0] + [s for s in (s1, s2) if s is not None and not isinstance(s, (int, float))]
        writes = [out] + ([accum] if accum is not None else [])
        kw = {}
        if op1 is not None:
            kw["op1"] = op1
        if accum is not None:
            kw["accum_out"] = accum
        return self.op(eng, lambda e: e.tensor_scalar(out=out, in0=in0, scalar1=s1, scalar2=s2, op0=op0, **kw),
                       reads, writes)

    def stt(self, eng, out, in0, scalar, in1, op0, op1):
        reads = [in0, in1] + ([scalar] if not isinstance(scalar, (int, float)) else [])
        return self.op(eng, lambda e: e.scalar_tensor_tensor(out=out, in0=in0, scalar=scalar, in1=in1,
                                                             op0=op0, op1=op1), reads, [out])

    def amul(self, out, in_, val):
        return self.op("act", lambda e: e.mul(out=out, in_=in_, mul=val), [in_], [out])

    def memset(self, eng, ap, val):
        return self.op(eng, lambda e: e.memset(ap, val), [], [ap])

    def recip(self, out, in_):
        return self.op("dve", lambda e: e.reciprocal(out=out, in_=in_), [in_], [out])


def _consts():
    c = {}
    p = np.arange(128)
    c["c_ident"] = np.eye(128, dtype=np.float32)
    c["c_ones"] = np.ones((128, 128), np.float32)
    c["c_blk64"] = (p[:, None] // 64 == p[None, :] // 64).astype(np.float32)
    c["c_tri"] = (p[:, None] <= p[None, :]).astype(np.float32)
    c["c_cneg"] = np.where(p[:, None] > p[None, :], NEGM, 0.0).astype(np.float32)
    c["c_wneg"] = np.where(p[:, None] <= p[None, :], NEGM, 0.0).astype(np.float32)
    c["c_cnegtm"] = np.where(p[None, :] > p[:, None], -3e30, 0.0).astype(np.float32)
    s = np.arange(S)
    c["c_E"] = (s[None, :] // 64 == np.arange(32)[:, None]).astype(np.float32)
    n = np.arange(128)
    vis = (16 * n[:, None] + 31 <= s[None, :])
    c["c_visneg"] = np.where(vis, 0.0, NEGM).astype(np.float32)
    j = np.arange(32)
    cover = ((16 * n[:, None] < 64 * j[None, :] + 64) & (16 * n[:, None] + 32 > 64 * j[None, :]))
    ce = np.zeros((128, 33), np.float32)
    ce[:, 0] = 1.0
    ce[:, 1:] = cover
    ce[127] = 0.0
    c["c_cover"] = ce
    t = (np.arange(NT)[None, :, None] * 128 + p[:, None, None])
    cur = t // 64
    jj = j[None, None, :]
    forced = (jj == cur) | (jj == 0)
    future = jj > cur
    c["c_selkeep"] = (~(forced | future)).astype(np.float32)
    c["c_selbias"] = np.where(forced, 1e9, np.where(future, -1e9, 0.0)).astype(np.float32)
    return c


class Builder:
    def __init__(self, debug=None, nlayers=DEPTH):
        self.debug = debug
        self.nlayers = nlayers
        import os
        self.lim = int(os.environ.get("KLIM", "16"))
        nc = bass.Bass("TRN2", target_bir_lowering=False)
        self.nc = nc
        self.P = Prog(nc)
        self.inp = {}
        self.out = {}

    def din(self, name, shape, dt=F32):
        self.inp[name] = self.nc.dram_tensor(name, list(shape), dt, kind="ExternalInput").ap()
        return self.inp[name]

    def dout(self, name, shape, dt=F32):
        self.out[name] = self.nc.dram_tensor(name, list(shape), dt, kind="ExternalOutput").ap()
        return self.out[name]

    def load_consts(self):
        P = self.P
        C = {}
        shapes = {k: v.shape for k, v in _consts().items()}
        for k, shp in shapes.items():
            self.din(k, shp)
        self.wstage = [P.sb("wstage", [128, 4096], F32) for _ in range(2)]
        self.wsi = 0
        self.cast_engs = ["pool"]

        def ld(name, key, shape, dt, rows=None):
            t = P.sb(name, shape, dt)
            src = self.inp[key]
            if dt == F32:
                P.dma(t[:], src)
            else:
                n = int(np.prod(shape[1:]))
                flat = "p a b -> p (a b)" if len(shape) == 3 else None
                for a0 in range(0, n, 4096):
                    b0 = min(n, a0 + 4096)
                    stg = self.wstage[self.wsi % 2]
                    self.wsi += 1
                    P.dma(stg[0:shape[0], 0:b0 - a0], src[:, a0:b0])
                    P.cp("pool", t[:, a0:b0], stg[0:shape[0], 0:b0 - a0])
            return t
        C["ident_f"] = ld("ident_f", "c_ident", [128, 128], F32)
        C["ident"] = ld("ident", "c_ident", [128, 128], BF16)
        C["ones_f"] = ld("ones_f", "c_ones", [128, 128], F32)
        C["blk64_f"] = ld("blk64_f", "c_blk64", [128, 128], F32)
        C["tri_f"] = ld("tri_f", "c_tri", [128, 128], F32)
        C["cneg"] = ld("cneg", "c_cneg", [128, 128], BF16)
        C["zero"] = P.sb("zero", [128, 128], BF16)
        P.memset("pool", C["zero"][:], 0.0)
        C["cneg_f"] = ld("cneg_f", "c_cneg", [128, 128], F32)
        C["wneg"] = ld("wneg", "c_wneg", [128, 128], BF16)
        C["cnegtm"] = ld("cnegtm", "c_cnegtm", [128, 128], F32)
        C["E"] = ld("E", "c_E", [32, S], BF16)
        C["visneg"] = ld("visneg", "c_visneg", [128, S], BF16)
        C["cover"] = ld("cover", "c_cover", [128, 33], BF16)
        C["selkeep"] = ld("selkeep", "c_selkeep", [128, NT, 32], F32)
        C["selbias"] = ld("selbias", "c_selbias", [128, NT, 32], F32)
        self.C = C

    def wload(self, dst, wdram, l, c0, n, rows=None):
        w2 = wdram[l] if rows is None else wdram[l][rows[0]:rows[1]]
        src = w2.rearrange("(kc p) n -> p kc n", p=128)[:, :, c0:c0 + n]
        nk = src.shape[1]
        step = max(1, 4096 // nk)
        for a in range(0, n, step):
            b = min(n, a + step)
            stg = self.wstage[self.wsi % 2]
            self.wsi += 1
            v = stg[:, 0:nk * (b - a)].rearrange("p (k n) -> p k n", k=nk)
            self.P.dma(v, src[:, :, a:b], q="sp")
            eng = self.cast_engs[self.wsi % len(self.cast_engs)]
            self.P.cp(eng, dst[:, :, a:b], v)

    def pipelined(self, items, Sb, pTb, cnt, depth=1):
        P = self.P
        n = len(items)
        nb = len(Sb)
        assert nb >= depth + 1 and len(pTb) >= depth + 1
        slots = []
        for i in range(n + depth):
            if i < n:
                k = cnt["si"] % nb
                cnt["si"] += 1
                Sp, pT = Sb[k], pTb[k]
                items[i][0](Sp)
                P.act(pT[:], Sp[:, :], AF.Exp, scale=0.125)
                slots.append(pT)
            if i >= depth:
                items[i - depth][1](slots[i - depth])

    def proj_fm(self, ps, Wt, c0, M, hT, t0, N):
        for kc in range(8):
            self.P.mm(ps[0:M, 0:N], lhsT=Wt[:, kc, c0:c0 + M], rhs=hT[:, kc, t0:t0 + N],
                      start=(kc == 0), stop=(kc == 7))

    def proj_tm(self, ps, Wt, c0, n, hT, tile):
        for kc in range(8):
            self.P.mm(ps[:, 0:n], lhsT=hT[:, kc, tile * 128:(tile + 1) * 128], rhs=Wt[:, kc, c0:c0 + n],
                      start=(kc == 0), stop=(kc == 7))

    def norm_fm(self, ps, ps2, M, N, gcol, out, sq, rs):
        P, C = self.P, self.C
        P.act(sq[0:M, 0:N], ps[0:M, 0:N], AF.Square)
        P.mm(ps2[0:M, 0:N], lhsT=C["blk64_f"][0:M, 0:M], rhs=sq[0:M, 0:N])
        P.act(rs[0:M, 0:N], ps2[0:M, 0:N], AF.Sqrt, bias=EPS, scale=1.0 / 64)
        P.recip(rs[0:M, 0:N], rs[0:M, 0:N])
        P.stt("dve", out, ps[0:M, 0:N], gcol, rs[0:M, 0:N], ALU.mult, ALU.mult)

    def gain_col(self, name, vec64, stack):
        t = self.P.sb(name, [128, 1], F32, stack)
        src = vec64.rearrange("(p o) -> p o", o=1)
        self.P.dma(t[0:64, :], src)
        self.P.dma(t[64:128, :], src)
        return t

    def phase_norm(self, xd, gvec, hT, store_T=None):
        P, C = self.P, self.C
        with P.scope() as st:
            gb = P.sb("gb", [128, D], F32, st)
            P.dma(gb[:], gvec.partition_broadcast(128))
            xb = [P.sb("xb", [128, D], F32, st) for _ in range(2)]
            sq = P.sb("sq", [128, D], F32, st)
            hb = [P.sb("hb", [128, D], BF16, st) for _ in range(2)]
            ss = P.sb("ss", [128, 2], F32, st)
            pt = [P.ps("pt", [128, 8, 128], BF16, st) for _ in range(2)]
            for tt in range(NT):
                x_t = xb[tt % 2]
                P.dma(x_t[:], xd[tt * 128:(tt + 1) * 128, :])
                s1 = ss[:, tt % 2:tt % 2 + 1]
                P.act(sq[:], x_t[:], AF.Square, accum=s1)
                P.act(s1, s1, AF.Sqrt, bias=EPS, scale=1.0 / D)
                P.recip(s1, s1)
                h_t = hb[tt % 2]
                P.stt("dve", h_t[:], x_t[:], s1, gb[:], ALU.mult, ALU.mult)
                p_t = pt[tt % 2]
                for kc in range(8):
                    P.tr(p_t[:, kc, :], h_t[:, kc * 128:(kc + 1) * 128], C["ident"][:], inc=(kc == 7))
                P.cp("act" if tt % 2 else "dve", hT[:, :, tt * 128:(tt + 1) * 128], p_t[:])

    def phase_dsa(self, l, hT, ya_d):
        P, C = self.P, self.C
        I = self.inp
        self.cast_engs = ["pool", "act"]
        w_in = I["w_in"]
        with P.scope() as st:
            Wq = P.sb("Wq", [128, 8, 1024], BF16, st)
            Wk = P.sb("Wk", [128, 8, 128], BF16, st)
            self.wload(Wk[:, :, 0:64], w_in, l, OFF["dk"], 64)
            self.wload(Wk[:, :, 64:128], w_in, l, OFF["dk"], 64)
            Wv = P.sb("Wv", [128, 8, 64], BF16, st)
            self.wload(Wv, w_in, l, OFF["dv"], 64)
            Wiq = P.sb("Wiq", [128, 8, 256], BF16, st)
            self.wload(Wiq, w_in, l, OFF["iq"], 256)
            Wik = P.sb("Wik", [128, 8, 128], BF16, st)
            for r in range(4):
                self.wload(Wik[:, :, r * 32:(r + 1) * 32], w_in, l, OFF["ik"], 32)
            Wiw = P.sb("Wiw", [128, 8, 8], BF16, st)
            self.wload(Wiw, w_in, l, OFF["iw"], 8)
            self.wload(Wq, w_in, l, OFF["dq"], 1024)
            gq = self.gain_col("gq", I["dsa_q_norm"][l], st)
            gk = self.gain_col("gk", I["dsa_k_norm"][l], st)

            kT = P.sb("kT", [128, S], BF16, st)
            v1 = P.sb("v1", [128, NT, 65], BF16, st)
            ikbd = P.sb("ikbd", [128, NT, 4, 128], BF16, st)
            qz = P.sb("qz", [128, 16, 512], BF16, st)
            iqT = P.sb("iqT", [128, 2, S], BF16, st)
            iw = P.sb("iw", [128, NT, 8], F32, st)
            acc = P.sb("acc", [128, S], F32, st)
            work = P.sb("work", [128, S], F32, st)
            mx = P.sb("mx", [128, 8], F32, st)
            mx2 = P.sb("mx2", [128, 8], F32, st)
            negm2 = P.sb("negm2", [128, S], BF16, st)
            negm = P.sb("negm", [128, S], BF16, st)
            ikT4 = negm
            negmT = [P.sb("negmT", [128, NT, 128], BF16, st) for _ in range(2)]
            rb = [P.sb("rb", [128, 512], F32, st) for _ in range(2)]
            pTb = [P.sb("pTb", [128, 512], BF16, st) for _ in range(2)]
            ytm = P.sb("ytm", [128, 1024], BF16, st)
            yT = P.sb("yT", [128, 8, 512], BF16, st)
            sq = P.sb("sq", [128, 512], F32, st)
            rs = P.sb("rs", [128, 512], F32, st)
            rc = P.sb("rc", [128, 4], F32, st)
            bank = [P.ps("bk", [128, 512], F32, st) for _ in range(6)]
            tb = [P.ps("tb", [128, 8, 128], BF16, st) for _ in range(2)]
            Sb, Ob, Ib = bank[0:2], bank[2:4], bank[4:6]

            P.memset("pool", v1[:, :, 64:65], 1.0)
            P.memset("pool", ikbd[:], 0.0)
            P.memset("pool", qz[:], 0.0)
            for Q in range(4):
                t0 = Q * 512
                self.proj_fm(Ib[0], Wk, 0, 128, hT, t0, 512)
                self.norm_fm(Ib[0], Ib[1], 128, 512, gk[:, 0:1], kT[:, t0:t0 + 512], sq, rs)
                self.proj_fm(Ib[0], Wik, 0, 128, hT, t0, 512)
                P.cp("act", ikT4[:, t0:t0 + 512], Ib[0][:, :])
                for c in range(2):
                    self.proj_fm(Ib[c], Wiq, c * 128, 128, hT, t0, 512)
                    P.cp("act", iqT[:, c, t0:t0 + 512], Ib[c][:, :])
            for hh in range(4):
                P.dma(ikbd[32 * hh:32 * hh + 32, :, hh, :],
                      ikT4[32 * hh:32 * hh + 32, :].rearrange("p (k s) -> p k s", s=128))
            for tt in range(NT):
                ps = Ib[tt % 2]
                self.proj_tm(ps, Wv, 0, 64, hT, tt)
                P.cp("act", v1[:, tt, 0:64], ps[:, 0:64])
                self.proj_tm(ps, Wiw, 64, 8, hT, tt) if False else None
            for tt in range(NT):
                ps = Ib[tt % 2]
                self.proj_tm(ps, Wiw, 0, 8, hT, tt)
                P.cp("act", iw[:, tt, :], ps[:, 0:8])

            cnt = {"si": 0, "ii": 0}

            def stage_q(Q):
                t0 = Q * 512
                for c in range(8):
                    self.proj_fm(Ib[0], Wq, c * 128, 128, hT, t0, 512)
                    P.act(sq[:, :], Ib[0][:, :], AF.Square)
                    P.mm(Ib[1][:, :], lhsT=C["blk64_f"][:], rhs=sq[:, :])
                    P.act(rs[:, :], Ib[1][:, :], AF.Sqrt, bias=EPS, scale=1.0 / 64)
                    P.recip(rs[:, :], rs[:, :])
                    for hf in range(2):
                        psl = slice(64 * hf, 64 * hf + 64)
                        P.stt("dve", qz[psl, 2 * c + hf, :], Ib[0][psl, :], gq[psl, 0:1], rs[psl, :], ALU.mult, ALU.mult)

            accs = [acc, work]
            mxs = [mx, mx2]
            negms = [negm, negm2]

            def stage_a_scores(qt, bf):
                nk = qt + 1
                tsl = slice(qt * 128, (qt + 1) * 128)
                ab = accs[bf]
                for kt in range(nk):
                    a_kt = ab[:, kt * 128:(kt + 1) * 128]
                    for c in range(2):
                        ps = Ib[cnt["ii"] % 2]
                        r = rb[cnt["ii"] % 2]
                        cnt["ii"] += 1
                        P.mm(ps[:, :], lhsT=iqT[:, c, tsl], rhs=ikbd[:, kt].rearrange("p a b -> p (a b)"))
                        P.act(r[:], ps[:, :], AF.Relu)
                        for hh in range(4):
                            h = 4 * c + hh
                            if h == 0:
                                P.ts("dve", a_kt, r[:, 0:128], iw[:, qt, 0:1], None, op0=ALU.mult)
                            else:
                                P.stt("dve", a_kt, r[:, hh * 128:(hh + 1) * 128], iw[:, qt, h:h + 1], a_kt,
                                      ALU.mult, ALU.add)
                d_sl = slice(qt * 128, (qt + 1) * 128)
                P.tt("dve", ab[:, d_sl], ab[:, d_sl], C["cnegtm"][:], ALU.add)

            def stage_a(tiles):
                todo = []
                for qt in tiles:
                    bf = qt % 2
                    if qt < 2:
                        P.memset("pool", negms[bf][:, 0:(qt + 1) * 128], 0.0)
                    else:
                        stage_a_scores(qt, bf)
                        todo.append((qt, bf))
                for r_ in range(32):
                    for (qt, bf) in todo:
                        n = (qt + 1) * 128
                        P.op("dve", lambda e: e.max(out=mxs[bf][:], in_=accs[bf][:, 0:n]), [accs[bf][:, 0:n]], [mxs[bf][:]])
                    for (qt, bf) in todo:
                        n = (qt + 1) * 128
                        P.op("dve", lambda e: e.match_replace(out=accs[bf][:, 0:n], in_to_replace=mxs[bf][:],
                                                              in_values=accs[bf][:, 0:n], imm_value=-1e30),
                             [mxs[bf][:], accs[bf][:, 0:n]], [accs[bf][:, 0:n]])
                for (qt, bf) in todo:
                    n = (qt + 1) * 128
                    P.ts("dve", negms[bf][:, 0:n], accs[bf][:, 0:n], -1e29, 1.0, op0=ALU.is_le, op1=ALU.subtract)

            def stage_b(qt):
                nk = qt + 1
                nT = negmT[qt % 2]
                negm = negms[qt % 2]
                for k0 in range(0, nk, 8):
                    k1 = min(nk, k0 + 8)
                    t_ = tb[0]
                    for kt in range(k0, k1):
                        P.tr(t_[:, kt - k0, :], negm[:, kt * 128:(kt + 1) * 128], C["ident"][:],
                             inc=(kt == k1 - 1))
                    P.amul(nT[:, k0:k1, :], t_[:, 0:k1 - k0, :], -NEGM)
                P.tt("pool", nT[:, qt, :], nT[:, qt, :], C["cneg"][:], ALU.add)

            def stage_c(qt):
                nk = qt + 1
                tl = qt % 4
                tsl = slice(tl * 128, (tl + 1) * 128)
                nT = negmT[qt % 2]
                items = []
                for hg in range(4):
                    O = Ob[hg % 2]
                    O3 = O[:, 0:260].rearrange("p (h e) -> p h e", e=65)
                    for kt in range(nk):
                        def em_s(Sp, hg=hg, kt=kt):
                            P.mm(Sp[:, :], lhsT=C["ident"][:], rhs=nT[:, kt, :].unsqueeze(1).to_broadcast([128, 4, 128]),
                                 start=True, stop=False, inc=False)
                            P.mm(Sp[:, :], lhsT=kT[:, kt * 128:(kt + 1) * 128], rhs=qz[:, 4 * hg:4 * hg + 4, tsl],
                                 start=False, stop=True)

                        def em_pv(pT, hg=hg, kt=kt, O3=O3):
                            for hh in range(4):
                                P.mm(O3[:, hh, :], lhsT=pT[:, hh * 128:(hh + 1) * 128], rhs=v1[:, kt, :],
                                     start=(kt == 0 and hh == 0), stop=(kt == nk - 1), inc=(hh == 3), sgc=True)
                            if kt == nk - 1:
                                P.recip(rc[:], O3[:, :, 64])
                                P.tt("dve", ytm[:, hg * 256:(hg + 1) * 256].rearrange("p (h e) -> p h e", e=64),
                                     O3[:, :, 0:64], rc[:].unsqueeze(2).to_broadcast([128, 4, 64]), ALU.mult)
                        items.append((em_s, em_pv))
                self.pipelined(items, Sb, pTb, cnt)
                t_ = tb[1]
                for c in range(8):
                    P.tr(t_[:, c, :], ytm[:, c * 128:(c + 1) * 128], C["ident"][:], inc=(c == 7))
                P.cp("act", yT[:, :, tsl], t_[:])
                if tl == 3:
                    Q = qt // 4
                    P.dma(ya_d[:, :, Q * 512:(Q + 1) * 512], yT[:], q="sp")

            nq = min(NT, self.lim)
            pairs = [list(range(i, min(i + 2, nq))) for i in range(0, nq, 2)]
            stage_a(pairs[0])
            for qt in pairs[0]:
                stage_b(qt)
            for pi, pr in enumerate(pairs):
                P.maybe_barrier()
                if pi + 1 < len(pairs):
                    stage_a(pairs[pi + 1])
                for qt in pr:
                    if qt % 4 == 0:
                        stage_q(qt // 4)
                    stage_c(qt)
                if pi + 1 < len(pairs):
                    for qt in pairs[pi + 1]:
                        stage_b(qt)

    def _nsa_kside(self, l, hT, st, env):
        P, C = self.P, self.C
        I = self.inp
        Wkv, Wks, Wkw = env["Wkv"], env["Wks"], env["Wkw"]
        gks, gkw, gkc = env["gks"], env["gkw"], env["gkc"]
        ksT, kwT, vs1, vw1, kcmpT, vext = env["ksT"], env["kwT"], env["vs1"], env["vw1"], env["kcmpT"], env["vext"]
        sq, rs, ss, Mb, tb = env["sq"], env["rs"], env["ss"], env["Mb"], env["tb"]
        cmpw = []
        posB = []
        for kv in range(2):
            cw = P.sb("cmpw", [128, 32, 64], BF16, st)
            stg = self.wstage[self.wsi % 2]
            self.wsi += 1
            sv = stg[:, 0:2048].rearrange("p (l e) -> p l e", e=64)
            src = I["nsa_cmp_w"][l, kv].rearrange("l d e -> d l e")
            P.dma(sv[0:64], src)
            P.dma(sv[64:128], src)
            P.cp("pool", cw[:], sv)
            cmpw.append(cw)
            pt_ = P.sb("posT", [128, 32], F32, st)
            psrc = I["nsa_cmp_pos"][l, kv].rearrange("l d -> d l")
            P.dma(pt_[0:64, :], psrc, allow_slow_non_contiguous=True)
            P.dma(pt_[64:128, :], psrc, allow_slow_non_contiguous=True)
            pb = P.sb("posB", [128, 32, 127], BF16, st)
            P.cp("pool", pb[:], pt_[:].unsqueeze(2).to_broadcast([128, 32, 127]))
            posB.append(pb)

        kcT = P.sb("kcT", [128, S], BF16, st)
        vcT = P.sb("vcT", [128, S], BF16, st)
        kd = P.sb("kd", [128, 128], BF16, st)
        P.memset("pool", vs1[:, :, :, 64:65], 1.0)
        P.memset("pool", vw1[:, :, :, 64:65], 1.0)
        P.memset("pool", kd[:], 0.0)
        for g in range(2):
            P.memset("pool", vext[g][:], 0.0)
        for Q in range(4):
            t0 = Q * 512
            self.proj_fm(Mb[0], Wkv, 0, 128, hT, t0, 512)
            P.cp("act", kcT[:, t0:t0 + 512], Mb[0][:, :])
            self.proj_fm(Mb[1], Wkv, 128, 128, hT, t0, 512)
            P.cp("act", vcT[:, t0:t0 + 512], Mb[1][:, :])
            for g in range(2):
                self.proj_fm(Mb[0], Wks, g * 128, 128, hT, t0, 512)
                self.norm_fm(Mb[0], Mb[1], 128, 512, gks[:, 0:1], ksT[g][:, t0:t0 + 512], sq, rs)
                self.proj_fm(Mb[0], Wkw, g * 128, 128, hT, t0, 512)
                self.norm_fm(Mb[0], Mb[1], 128, 512, gkw[:, 0:1], kwT[g][:, t0:t0 + 512], sq, rs)
        for tt in range(NT):
            ps = Mb[tt % 2]
            self.proj_tm(ps, Wkv, 384, 128, hT, tt)
            P.cp("act", vs1[:, tt, :, 0:64], ps[:, 0:128].rearrange("p (g e) -> p g e", e=64))
            self.proj_tm(ps, Wkv, 640, 128, hT, tt)
            P.cp("act", vw1[:, tt, :, 0:64], ps[:, 0:128].rearrange("p (g e) -> p g e", e=64))
        P.maybe_barrier()
        def strided(t, lo, off):
            return bass.AP(t[:].tensor, lo * S + off, [[S, 64], [16, 127]])
        for g in range(2):
            lo = g * 64
            for kv, src in ((0, kcT), (1, vcT)):
                ps = Mb[kv]
                for li in range(32):
                    P.mm(ps[0:127, 0:64], lhsT=strided(src, lo, li), rhs=cmpw[kv][lo:lo + 64, li, :],
                         start=(li == 0), stop=False, inc=False)
                for li in range(32):
                    P.mm(ps[0:127, 0:64], lhsT=posB[kv][lo:lo + 64, li, :], rhs=cmpw[kv][lo:lo + 64, li, :],
                         start=False, stop=(li == 31), inc=(li == 31))
            P.act(sq[0:127, 0:64], Mb[0][0:127, 0:64], AF.Square, accum=ss[0:127, :])
            P.act(ss[0:127, :], ss[0:127, :], AF.Sqrt, bias=EPS, scale=1.0 / 64)
            P.recip(ss[0:127, :], ss[0:127, :])
            P.stt("dve", kd[0:127, 0:64], Mb[0][0:127, 0:64], ss[0:127, 0:1], gkc[0:127, :], ALU.mult, ALU.mult)
            P.cp("pool", kd[0:127, 64:128], kd[0:127, 0:64])
            P.tr(tb[0][:, 0, :], kd[:], C["ident"][:])
            P.cp("act", kcmpT[g][:], tb[0][:, 0, :])
            P.cp("act", vext[g][0:127, 0:64], Mb[1][0:127, 0:64])
            P.cp("pool", vext[g][0:127, 64:97], C["cover"][0:127, :])
        P.maybe_barrier()


    def phase_nsa(self, l, hT, yb_d):
        P, C = self.P, self.C
        I = self.inp
        self.cast_engs = ["pool", "dve"]
        w_in = I["w_in"]
        KV = OFF["nkv"]
        with P.scope() as st:
            Wq = P.sb("Wnq", [128, 8, 1024], BF16, st)
            Wkv = P.sb("Wkv", [128, 8, 768], BF16, st)
            self.wload(Wkv, w_in, l, KV, 768)
            Wks = P.sb("Wks", [128, 8, 256], BF16, st)
            Wkw = P.sb("Wkw", [128, 8, 256], BF16, st)
            for g in range(2):
                for r in range(2):
                    self.wload(Wks[:, :, g * 128 + r * 64:g * 128 + r * 64 + 64], w_in, l, KV + 256 + g * 64, 64)
                    self.wload(Wkw[:, :, g * 128 + r * 64:g * 128 + r * 64 + 64], w_in, l, KV + 512 + g * 64, 64)
            Wng = P.sb("Wng", [128, 8, 48], BF16, st)
            self.wload(Wng, w_in, l, OFF["ng"], 48)
            self.wload(Wq, w_in, l, OFF["nq"], 1024)
            gq = self.gain_col("gnq", I["nsa_q_norm"][l], st)
            gks = self.gain_col("gks", I["nsa_k_norm"][l, 1], st)
            gkw = self.gain_col("gkw", I["nsa_k_norm"][l, 2], st)
            gkc = P.sb("gkc", [128, 64], F32, st)
            P.dma(gkc[:], I["nsa_k_norm"][l, 0].partition_broadcast(128))
            ksT = [P.sb("ksT", [128, S], BF16, st) for _ in range(2)]
            kwT = [P.sb("kwT", [128, S], BF16, st) for _ in range(2)]
            vs1 = P.sb("vs1", [128, NT, 2, 65], BF16, st)
            vw1 = P.sb("vw1", [128, NT, 2, 65], BF16, st)
            kcmpT = [P.sb("kcmpT", [128, 128], BF16, st) for _ in range(2)]
            vext = [P.sb("vext", [128, 97], BF16, st) for _ in range(2)]
            sq = P.sb("sq", [128, 512], F32, st)
            rs = P.sb("rs", [128, 512], F32, st)
            ss = P.sb("ss1", [128, 1], F32, st)
            bank = [P.ps("bk", [128, 512], F32, st) for _ in range(6)]
            tb = [P.ps("tb", [128, 8, 128], BF16, st) for _ in range(2)]
            Sb, Ob, Mb = bank[0:2], bank[2:4], bank[4:6]
            with P.scope() as st2:
                self._nsa_kside(l, hT, st2, locals())
            nqz = P.sb("nqz", [128, 16, 512], BF16, st)
            P.memset("pool", nqz[:], 0.0)
            gs = P.sb("gs", [128, 48], F32, st)
            yacc = P.sb("yacc", [128, 16, 64], F32, st)
            ytmp = P.sb("ytmp", [128, 4, 64], F32, st)
            impn = P.sb("impn", [128, 16, 32], F32, st)
            imp = P.sb("imp", [128, 32], F32, st)
            mx = P.sb("mx8", [128, 8], F32, st)
            negblk = P.sb("negblk", [128, 32], BF16, st)
            negblkT = [P.sb("negblkT", [32, 128], BF16, st) for _ in range(2)]
            pTb = [P.sb("pTb", [128, 512], BF16, st) for _ in range(3)]
            Sb = [bank[0], bank[1], bank[5]]
            ytm = P.sb("ytm", [128, 1024], BF16, st)
            yT = P.sb("yT", [128, 8, 512], BF16, st)
            rc = P.sb("rc", [128, 4], F32, st)
            cf = P.sb("cf", [128, 4], F32, st)
            Osb = [P.sb("Osb", [128, 4, 97], F32, st) for _ in range(2)]
            fin_cnt = [0]

            cnt = {"si": 0}
            oi = 0
            for Q in range(4):
                t0 = Q * 512
                for c in range(8):
                    self.proj_fm(Mb[0], Wq, c * 128, 128, hT, t0, 512)
                    P.act(sq[:, :], Mb[0][:, :], AF.Square)
                    P.mm(Mb[1][:, :], lhsT=C["blk64_f"][:], rhs=sq[:, :])
                    P.act(rs[:, :], Mb[1][:, :], AF.Sqrt, bias=EPS, scale=1.0 / 64)
                    P.recip(rs[:, :], rs[:, :])
                    for hf in range(2):
                        psl = slice(64 * hf, 64 * hf + 64)
                        P.stt("dve", nqz[psl, 2 * c + hf, :], Mb[0][psl, :], gq[psl, 0:1], rs[psl, :], ALU.mult, ALU.mult)
                for tl in range(4):
                    qt = Q * 4 + tl
                    if qt >= self.lim:
                        continue
                    P.maybe_barrier()
                    tsl = slice(tl * 128, (tl + 1) * 128)
                    self.proj_tm(Mb[0], Wng, 0, 48, hT, qt)
                    P.act(gs[:], Mb[0][:, 0:48], AF.Sigmoid)
                    gs3 = gs[:].rearrange("p (h b) -> p h b", b=3)

                    def scores(Sp, sub, kT_g, kt_cols, masks):
                        first = True
                        for (ml, mr) in masks:
                            kk = mr.shape[0]
                            P.mm(Sp[:, :], lhsT=ml, rhs=mr.unsqueeze(1).to_broadcast([kk, 4, 128]),
                                 start=first, stop=False, inc=False)
                            first = False
                        P.mm(Sp[:, :], lhsT=kT_g[:, kt_cols], rhs=nqz[:, 4 * sub:4 * sub + 4, tsl],
                             start=first, stop=True)

                    def finalize(O3p, sub, br, first):
                        w_ = 97 if br == 0 else 65
                        O3 = Osb[fin_cnt[0] % 2][:, :, 0:w_]
                        fin_cnt[0] += 1
                        P.cp("act", O3, O3p)
                        hs = slice(4 * sub, 4 * sub + 4)
                        P.ts("dve", rc[:], O3[:, :, 64], 1e-30, None, op0=ALU.max)
                        P.recip(rc[:], rc[:])
                        if br == 0:
                            P.tt("dve", impn[:, hs, :], O3[:, :, 65:97], rc[:].unsqueeze(2).to_broadcast([128, 4, 32]),
                                 ALU.mult)
                        P.tt("dve", cf[:], rc[:], gs3[:, hs, br], ALU.mult)
                        if first:
                            P.tt("dve", yacc[:, hs, :], O3[:, :, 0:64], cf[:].unsqueeze(2).to_broadcast([128, 4, 64]),
                                 ALU.mult)
                        else:
                            P.tt("dve", ytmp[:], O3[:, :, 0:64], cf[:].unsqueeze(2).to_broadcast([128, 4, 64]),
                                 ALU.mult)
                            P.tt("pool", yacc[:, hs, :], yacc[:, hs, :], ytmp[:], ALU.add)

                    items = []
                    for sub in range(4):
                        g = sub // 2
                        O = Ob[oi % 2]
                        oi += 1
                        O3 = O[:, 0:388].rearrange("p (h e) -> p h e", e=97)

                        def em_s(Sp, sub=sub, g=g):
                            scores(Sp, sub, kcmpT[g], slice(0, 128),
                                   [(C["ident"][:], C["visneg"][:, qt * 128:(qt + 1) * 128])])

                        def em_pv(pT, sub=sub, g=g, O3=O3):
                            for hh in range(4):
                                P.mm(O3[:, hh, :], lhsT=pT[:, hh * 128:(hh + 1) * 128], rhs=vext[g][:, :],
                                     start=(hh == 0), stop=True, inc=(hh == 3), sgc=True)
                            finalize(O3, sub, 0, True)
                        items.append((em_s, em_pv))
                    self.pipelined(items, Sb, pTb, cnt, depth=2)
                    for g in range(2):
                        P.op("dve", lambda e: e.tensor_reduce(out=imp[:], in_=impn[:, 8 * g:8 * g + 8, :].rearrange("p h j -> p j h"),
                                                              axis=AX.X, op=ALU.add),
                             [impn[:, 8 * g:8 * g + 8, :]], [imp[:]])
                        P.tt("dve", imp[:], imp[:], C["selkeep"][:, qt, :], ALU.mult)
                        P.tt("dve", imp[:], imp[:], C["selbias"][:, qt, :], ALU.add)
                        P.op("dve", lambda e: e.max(out=mx[:], in_=imp[:]), [imp[:]], [mx[:]])
                        P.ts("dve", negblk[:], imp[:], mx[:, 3:4], 1.0, op0=ALU.is_ge, op1=ALU.subtract)
                        P.tr(tb[0][0:32, 0, :], negblk[:], C["ident"][:])
                        P.amul(negblkT[g][:], tb[0][0:32, 0, :], -NEGM)
                    items = []
                    for br in (2, 1):
                        kts = list(range(0, qt + 1)) if br == 1 else list(range(max(0, qt - 4), qt + 1))
                        kT_l = ksT if br == 1 else kwT
                        v_l = vs1 if br == 1 else vw1
                        for sub in range(4):
                            g = sub // 2
                            O = Ob[oi % 2]
                            oi += 1
                            O3 = O[:, 0:260].rearrange("p (h e) -> p h e", e=65)
                            for ki, kt in enumerate(kts):
                                masks = []
                                if br == 1:
                                    masks.append((C["E"][:, kt * 128:(kt + 1) * 128], negblkT[g][:]))
                                if kt == qt:
                                    masks.append((C["ident"][:], C["cneg"][:]))
                                if br == 2 and kt == qt - 4:
                                    masks.append((C["ident"][:], C["wneg"][:]))

                                def em_s(Sp, sub=sub, g=g, kt=kt, masks=masks, kT_l=kT_l):
                                    scores(Sp, sub, kT_l[g], slice(kt * 128, (kt + 1) * 128), masks)

                                def em_pv(pT, sub=sub, g=g, kt=kt, ki=ki, nkt=len(kts), O3=O3, v_l=v_l, br=br):
                                    for hh in range(4):
                                        P.mm(O3[:, hh, :], lhsT=pT[:, hh * 128:(hh + 1) * 128], rhs=v_l[:, kt, g, :],
                                             start=(ki == 0 and hh == 0), stop=(ki == nkt - 1), inc=(hh == 3), sgc=True)
                                    if ki == nkt - 1:
                                        finalize(O3, sub, br, False)
                                items.append((em_s, em_pv))
                    self.pipelined(items, Sb, pTb, cnt, depth=2)
                    P.cp("act", ytm[:], yacc[:].rearrange("p h e -> p (h e)"))
                    t_ = tb[1]
                    for c in range(8):
                        P.tr(t_[:, c, :], ytm[:, c * 128:(c + 1) * 128], C["ident"][:], inc=(c == 7))
                    P.cp("act", yT[:, :, tsl], t_[:])
                P.dma(yb_d[:, :, t0:t0 + 512], yT[:], q="sp")

    def phase_ssd(self, l, hT, yc_d, zs_d, xact_d):
        P, C = self.P, self.C
        I = self.inp
        self.cast_engs = ["pool", "dve"]
        w_in = I["w_in"]
        with P.scope() as st:
            Wz = P.sb("Wz", [128, 8, 2048], BF16, st)
            self.wload(Wz, w_in, l, OFF["sz"], 2048)
            zt = [P.sb("zt", [128, 2048], F32, st) for _ in range(2)]
            bank = [P.ps("bk", [128, 512], F32, st) for _ in range(4)]
            bi = 0
            for tt in range(NT):
                z_t = zt[tt % 2]
                for nb in range(4):
                    ps = bank[bi % 4]
                    bi += 1
                    self.proj_tm(ps, Wz, nb * 512, 512, hT, tt)
                    P.act(z_t[:, nb * 512:(nb + 1) * 512], ps[:, :], AF.Silu)
                P.dma(zs_d[tt * 128:(tt + 1) * 128, :], z_t[:])
        with P.scope() as st:
            cw = P.sb("cw", [128, 24, 4], F32, st)
            cb = P.sb("cb", [128, 24], F32, st)
            for cc in range(24):
                P.dma(cw[:, cc, :], I["ssd_conv_w"][l][:, cc * 128:(cc + 1) * 128].rearrange("k c -> c k"),
                      allow_slow_non_contiguous=True)
                P.dma(cb[:, cc:cc + 1], I["ssd_conv_b"][l][cc * 128:(cc + 1) * 128].rearrange("(c o) -> c o", o=1))
            Wx = [P.sb("Wx", [128, 8, 512], BF16, st) for _ in range(2)]
            raw = [P.sb("raw", [128, 3 + S], F32, st) for _ in range(4)]
            accb = [P.sb("accb", [128, S], F32, st) for _ in range(4)]
            xa = [P.sb("xa", [128, S], BF16, st) for _ in range(4)]
            bank = [P.ps("bk", [128, 512], F32, st) for _ in range(4)]
            for r_ in raw:
                P.memset("pool", r_[:, 0:3], 0.0)
            bi = 0
            for c0 in range(0, 24, 2):
                pair = (c0, c0 + 1)
                for cc in pair:
                    W_ = Wx[(cc // 4) % 2]
                    if cc % 4 == 0:
                        self.wload(W_, w_in, l, OFF["sxbc"] + cc * 128, 512)
                    rw = raw[cc % 4]
                    for Q in range(4):
                        ps = bank[bi % 4]
                        bi += 1
                        self.proj_fm(ps, W_, (cc % 4) * 128, 128, hT, Q * 512, 512)
                        P.cp("act", rw[:, 3 + Q * 512:3 + (Q + 1) * 512], ps[:, :])
                for cc in pair:
                    P.ts("dve", accb[cc % 4][:], raw[cc % 4][:, 3:3 + S], cw[:, cc, 3:4], None, op0=ALU.mult)
                for k in (2, 1, 0):
                    for cc in pair:
                        P.stt("dve", accb[cc % 4][:], raw[cc % 4][:, k:k + S], cw[:, cc, k:k + 1], accb[cc % 4][:],
                              ALU.mult, ALU.add)
                for cc in pair:
                    P.act(xa[cc % 4][:], accb[cc % 4][:], AF.Silu, bias=cb[:, cc:cc + 1])
                    P.dma(xact_d[:, cc, :], xa[cc % 4][:])
                P.maybe_barrier()
        with P.scope() as st:
            Wdt = P.sb("Wdt", [128, 8, 32], BF16, st)
            self.wload(Wdt, w_in, l, OFF["sdt"], 32)
            dtb = P.sb("dtb", [128, 32], F32, st)
            P.dma(dtb[:], I["ssd_dt_bias"][l].partition_broadcast(128))
            a_b = P.sb("a_b", [128, 32], F32, st)
            P.dma(a_b[:], I["ssd_a_log"][l].partition_broadcast(128))
            P.act(a_b[:], a_b[:], AF.Exp)
            P.ts("dve", a_b[:], a_b[:], -1.0, None, op0=ALU.mult)
            D_b = P.sb("D_b", [128, 32], F32, st)
            P.dma(D_b[:], I["ssd_d"][l].partition_broadcast(128))
            ng_b = P.sb("ng_b", [128, 2048], F32, st)
            P.dma(ng_b[:], I["ssd_norm_g"][l].partition_broadcast(128))
            xab = [P.sb("xab", [128, 24, 128], BF16, st) for _ in range(2)]
            zsb = [P.sb("zsb", [128, 2048], F32, st) for _ in range(2)]
            xs_tm = P.sb("xs_tm", [128, 2048], BF16, st)
            bm_b = [P.sb("bm_tm", [128, 512], BF16, st) for _ in range(2)]
            xsw_b = [P.sb("xs_w", [128, 2048], BF16, st) for _ in range(2)]
            rhs1_b = [P.sb("rhs1", [128, 8, 128], F32, st) for _ in range(2)]
            rhs2_b = [P.sb("rhs2", [128, 8, 128], F32, st)] * 2
            Lg_b = [P.sb("Lg", [128, 8, 128], BF16, st) for _ in range(2)]
            MTg_b = [P.sb("MTg", [128, 8, 128], BF16, st) for _ in range(2)]
            tmp_b = [P.sb("tmpb", [128, 512], F32, st) for _ in range(2)]
            cbT = P.sb("cbT", [128, 4, 128], BF16, st)
            H = [P.sb("H", [128, 512], F32, st) for _ in range(4)]
            Hbf = [P.sb("Hbf", [128, 512], BF16, st) for _ in range(4)]
            y_b = [P.sb("y", [128, 2048], F32, st) for _ in range(2)]
            ynb = P.sb("ynb", [128, 2048], BF16, st)
            ycT = P.sb("ycT", [128, 16, 128], BF16, st)
            sma = {n: P.sb(n, [128, NT, 32], F32, st) for n in ("dtc", "da", "nb", "ea", "wj", "dec")}
            sma["lndt"] = sma["nb"]
            sma["acum"] = sma["ea"]
            sma["alast"] = sma["dec"]
            ss = P.sb("ss2", [128, 1], F32, st)
            R = [P.ps("R", [128, 512], F32, st) for _ in range(2)]
            Yp_b = [P.ps("Yp", [128, 512], F32, st) for _ in range(2)]
            Yo = P.ps("Yo", [128, 512], F32, st)
            STp = P.ps("STp", [128, 512], F32, st)
            M0 = P.ps("M0", [128, 512], F32, st)
            M1 = M0
            tb = P.ps("tb", [128, 8, 128], BF16, st)
            for g in range(4):
                P.memset("pool", H[g][:], 0.0)
                P.memset("pool", Hbf[g][:], 0.0)
            nch = min(NT, self.lim)
            fl = lambda t: t[:].rearrange("p c h -> p (c h)")
            for c in range(NT):
                for kc in range(8):
                    P.mm(M1[:, c * 32:(c + 1) * 32], lhsT=hT[:, kc, c * 128:(c + 1) * 128], rhs=Wdt[:, kc, :],
                         start=(c == 0 and kc == 0), stop=(kc == 7), inc=(kc == 7), sgc=True)
            P.tt("dve", sma["dtc"][:], M1[:, :].rearrange("p (c h) -> p c h", h=32),
                 dtb[:].unsqueeze(1).to_broadcast([128, NT, 32]), ALU.add)
            P.act(fl(sma["dtc"]), fl(sma["dtc"]), AF.Exp)
            P.act(fl(sma["dtc"]), fl(sma["dtc"]), AF.Ln, bias=1.0)
            P.act(fl(sma["lndt"]), fl(sma["dtc"]), AF.Ln)
            P.tt("dve", sma["da"][:], sma["dtc"][:], a_b[:].unsqueeze(1).to_broadcast([128, NT, 32]), ALU.mult)
            P.mm(M1[:, :], lhsT=C["tri_f"][:], rhs=fl(sma["da"]))
            P.cp("dve", fl(sma["acum"]), M1[:, :])
            P.mm(M1[:, :], lhsT=C["ones_f"][:], rhs=fl(sma["da"]))
            P.cp("dve", fl(sma["alast"]), M1[:, :])
            P.tt("dve", sma["nb"][:], sma["lndt"][:], sma["acum"][:], ALU.subtract)
            P.tt("dve", sma["wj"][:], sma["alast"][:], sma["acum"][:], ALU.subtract)
            P.act(fl(sma["wj"]), fl(sma["wj"]), AF.Exp)
            P.tt("dve", sma["wj"][:], sma["wj"][:], sma["dtc"][:], ALU.mult)
            P.act(fl(sma["ea"]), fl(sma["acum"]), AF.Exp)
            P.act(fl(sma["dec"]), fl(sma["alast"]), AF.Exp)
            sma = {k: sma[k] for k in ("da", "nb", "ea", "wj", "dec")}

            def load(c):
                csl = slice(c * 128, (c + 1) * 128)
                P.dma(xab[c % 2][:], xact_d[:, :, csl])
                P.dma(zsb[c % 2][:], zs_d[csl, :])

            def part1(c):
                p = c % 2
                xa_ = xab[p]
                sm = {k: v[:, c, :] for k, v in sma.items()}
                bm_tm, xs_w, y = bm_b[p], xsw_b[p], y_b[p]
                for k0 in (0, 8, 16):
                    n = 8 if k0 < 16 else 4
                    for j in range(n):
                        P.tr(tb[:, j, :], xa_[:, k0 + j, :], C["ident"][:], inc=(j == n - 1))
                    if k0 < 16:
                        P.cp("act", xs_tm[:, k0 * 128:(k0 + 8) * 128], tb[:].rearrange("p a b -> p (a b)"))
                    else:
                        P.cp("act", bm_tm[:], tb[:, 0:4, :].rearrange("p a b -> p (a b)"))
                for g in range(4):
                    P.mm(M0[:, g * 128:(g + 1) * 128], lhsT=xa_[:, 16 + g, :], rhs=xa_[:, 20 + g, :],
                         start=(g == 0), stop=True, inc=(g == 3), sgc=True)
                P.cp("act", cbT[:].rearrange("p a b -> p (a b)"), M0[:, :])
                P.tt("dve", xs_w[:].rearrange("p (h e) -> p h e", e=64), xs_tm[:].rearrange("p (h e) -> p h e", e=64),
                     sm["wj"][:].unsqueeze(2).to_broadcast([128, 32, 64]), ALU.mult)
                for g in range(4):
                    hs = slice(8 * g, 8 * g + 8)
                    gsl = slice(g * 512, (g + 1) * 512)
                    rhs1, rhs2, Lg, MTg, Yp, tmp = rhs1_b[g % 2], rhs2_b[g % 2], Lg_b[g % 2], MTg_b[g % 2], Yp_b[g % 2], tmp_b[g % 2]
                    P.tt("dve", rhs1[:], C["tri_f"][:].unsqueeze(1).to_broadcast([128, 8, 128]),
                         sm["da"][:, hs].unsqueeze(2).to_broadcast([128, 8, 128]), ALU.mult)
                    P.tt("dve", rhs2[:], C["cneg_f"][:].unsqueeze(1).to_broadcast([128, 8, 128]),
                         sm["nb"][:, hs].unsqueeze(2).to_broadcast([128, 8, 128]), ALU.add)
                    for hf in range(2):
                        P.mm(R[hf][:, :], lhsT=C["ones_f"][:], rhs=rhs1[:, 4 * hf:4 * hf + 4, :].rearrange("p a b -> p (a b)"),
                             start=True, stop=False, inc=False)
                        P.mm(R[hf][:, :], lhsT=C["ident_f"][:], rhs=rhs2[:, 4 * hf:4 * hf + 4, :].rearrange("p a b -> p (a b)"),
                             start=False, stop=True)
                        P.act(Lg[:, 4 * hf:4 * hf + 4, :].rearrange("p a b -> p (a b)"), R[hf][:, :], AF.Exp)
                    P.tt("dve" if g % 2 else "pool", MTg[:], Lg[:], cbT[:, g, :].unsqueeze(1).to_broadcast([128, 8, 128]), ALU.mult)
                    for hh in range(8):
                        P.mm(Yp[:, hh * 64:(hh + 1) * 64], lhsT=MTg[:, hh, :], rhs=xs_tm[:, (8 * g + hh) * 64:(8 * g + hh + 1) * 64],
                             start=(hh == 0), stop=True, inc=(hh == 7), sgc=True)
                    P.tt("pool", tmp[:].rearrange("p (h e) -> p h e", e=64), xs_tm[:, gsl].rearrange("p (h e) -> p h e", e=64),
                         D_b[:, hs].unsqueeze(2).to_broadcast([128, 8, 64]), ALU.mult)
                    P.tt("dve", y[:, gsl], Yp[:, :], tmp[:], ALU.add)

            def part2(c):
                p = c % 2
                csl = slice(c * 128, (c + 1) * 128)
                xa_ = xab[p]
                zs_ = zsb[p]
                sm = {k: v[:, c, :] for k, v in sma.items()}
                bm_tm, xs_w, y = bm_b[p], xsw_b[p], y_b[p]
                for g in range(4):
                    hs = slice(8 * g, 8 * g + 8)
                    gsl = slice(g * 512, (g + 1) * 512)
                    tmp = tmp_b[g % 2]
                    P.mm(Yo[:, :], lhsT=xa_[:, 20 + g, :], rhs=Hbf[g][:])
                    P.tt("dve", tmp[:].rearrange("p (h e) -> p h e", e=64), Yo[:, :].rearrange("p (h e) -> p h e", e=64),
                         sm["ea"][:, hs].unsqueeze(2).to_broadcast([128, 8, 64]), ALU.mult)
                    P.tt("pool", y[:, gsl], y[:, gsl], tmp[:], ALU.add)
                    P.mm(STp[:, :], lhsT=bm_tm[:, g * 128:(g + 1) * 128], rhs=xs_w[:, gsl])
                    Hv = H[g][:].rearrange("p (h e) -> p h e", e=64)
                    P.tt("pool", Hv, Hv, sm["dec"][:, hs].unsqueeze(2).to_broadcast([128, 8, 64]), ALU.mult)
                    P.tt("dve", H[g][:], STp[:, :], H[g][:], ALU.add)
                    P.cp("act", Hbf[g][:], H[g][:])
                P.tt("dve", y[:], y[:], zs_[:], ALU.mult)
                P.act(zs_[:], y[:], AF.Square, accum=ss[:])
                P.act(ss[:], ss[:], AF.Sqrt, bias=EPS, scale=1.0 / 2048)
                P.recip(ss[:], ss[:])
                P.stt("dve", ynb[:], y[:], ss[:, 0:1], ng_b[:], ALU.mult, ALU.mult)
                for k0 in (0, 8):
                    for j in range(8):
                        P.tr(tb[:, j, :], ynb[:, (k0 + j) * 128:(k0 + j + 1) * 128], C["ident"][:], inc=(j == 7))
                    P.cp("act", ycT[:, k0:k0 + 8, :], tb[:])
                P.dma(yc_d[:, :, csl], ycT[:])

            if nch > 0:
                load(0)
                if nch > 1:
                    load(1)
                part1(0)
            for c in range(nch):
                if P.sems_left() < 40:
                    P.barrier()
                if c + 1 < nch:
                    part1(c + 1)
                part2(c)
                if c + 2 < nch:
                    load(c + 2)

    def phase_merge(self, l, hT, ya_d, yb_d, yc_d, mix_d):
        P, C = self.P, self.C
        I = self.inp
        self.cast_engs = ["pool", "dve"]
        with P.scope() as st:
            yh = [P.sb("yha", [128, 8, 1024], BF16, st), P.sb("yhb", [128, 8, 1024], BF16, st),
                  P.sb("yhc", [128, 16, 1024], BF16, st)]
            Wy = [[P.sb("Wya", [128, 8, 128], BF16, st), P.sb("Wyb", [128, 8, 128], BF16, st),
                   P.sb("Wyc", [128, 16, 128], BF16, st)] for _ in range(2)]
            Wg = [[P.sb("Wg", [128, 8, 128], BF16, st) for _ in range(3)] for _ in range(2)]
            sg = [P.sb("sg", [128, 512], F32, st) for _ in range(2)]
            macc = P.sb("macc", [128, 512], F32, st)
            mt = P.sb("mt", [128, 512], F32, st)
            mixh = P.sb("mixh", [128, 8, 1024], BF16, st)
            bank = [P.ps("bk", [128, 512], F32, st) for _ in range(6)]
            wsrc = [I["w_br_dsa"], I["w_br_nsa"], I["w_br_ssd"]]
            ysrc = [ya_d, yb_d, yc_d]
            bi = 0
            gi = 0
            it = 0
            for half in range(2):
                hsl = slice(half * 1024, (half + 1) * 1024)
                for br in range(3):
                    P.dma(yh[br][:], ysrc[br][:, :, hsl])
                for oc in range(8):
                    Wy_ = Wy[it % 2]
                    Wg_ = Wg[it % 2]
                    it += 1
                    for br in range(3):
                        self.wload(Wy_[br], wsrc[br], l, oc * 128, 128)
                        self.wload(Wg_[br], I["w_in"], l, OFF["mg"] + br * 1024 + oc * 128, 128)
                    for pc in range(2):
                        psl = slice(pc * 512, (pc + 1) * 512)
                        tok = half * 1024 + pc * 512
                        for br in range(3):
                            psy = bank[bi % 6]
                            bi += 1
                            psg = bank[bi % 6]
                            bi += 1
                            nk = 16 if br == 2 else 8
                            for kc in range(nk):
                                P.mm(psy[:, :], lhsT=Wy_[br][:, kc, :], rhs=yh[br][:, kc, psl],
                                     start=(kc == 0), stop=(kc == nk - 1))
                            self.proj_fm(psg, Wg_[br], 0, 128, hT, tok, 512)
                            s_ = sg[gi % 2]
                            gi += 1
                            P.act(s_[:], psg[:, :], AF.Sigmoid)
                            if br == 0:
                                P.tt("dve", macc[:], psy[:, :], s_[:], ALU.mult)
                            else:
                                P.tt("dve", mt[:], psy[:, :], s_[:], ALU.mult)
                                if br == 1:
                                    P.tt("pool", macc[:], macc[:], mt[:], ALU.add)
                                else:
                                    P.tt("pool", mixh[:, oc, psl], macc[:], mt[:], ALU.add)
                    P.maybe_barrier()
                P.dma(mix_d[:, :, hsl], mixh[:])

    def phase_out_norm(self, l, x_d, mix_d, x1_d, hT):
        P, C = self.P, self.C
        I = self.inp
        self.cast_engs = ["pool", "dve"]
        with P.scope() as st:
            Wo = P.sb("Wo", [128, 8, 1024], BF16, st)
            self.wload(Wo, I["w_out"], l, 0, 1024)
            gb = P.sb("gb2", [128, D], F32, st)
            P.dma(gb[:], I["norm2_g"][l].partition_broadcast(128))
            mq = [P.sb("mq", [128, 8, 128], BF16, st) for _ in range(2)]
            xb = [P.sb("xb", [128, D], F32, st) for _ in range(2)]
            x1b = [P.sb("x1b", [128, D], F32, st) for _ in range(2)]
            sq = P.sb("sq", [128, D], F32, st)
            hb = [P.sb("hb", [128, D], BF16, st) for _ in range(2)]
            ss = P.sb("ss", [128, 2], F32, st)
            bank = [P.ps("bk", [128, 512], F32, st) for _ in range(4)]
            pt = [P.ps("pt", [128, 8, 128], BF16, st) for _ in range(2)]
            for tt in range(NT):
                tsl = slice(tt * 128, (tt + 1) * 128)
                m_ = mq[tt % 2]
                P.dma(m_[:], mix_d[:, :, tsl])
                x_t = xb[tt % 2]
                P.dma(x_t[:], x_d[tsl, :])
                x1 = x1b[tt % 2]
                for hf in range(2):
                    ps = bank[(2 * tt + hf) % 4]
                    for oc in range(8):
                        P.mm(ps[:, :], lhsT=m_[:, oc, :], rhs=Wo[:, oc, hf * 512:(hf + 1) * 512],
                             start=(oc == 0), stop=(oc == 7))
                    P.tt("dve", x1[:, hf * 512:(hf + 1) * 512], ps[:, :], x_t[:, hf * 512:(hf + 1) * 512], ALU.add)
                P.dma(x1_d[tsl, :], x1[:])
                s1 = ss[:, tt % 2:tt % 2 + 1]
                P.act(sq[:], x1[:], AF.Square, accum=s1)
                P.act(s1, s1, AF.Sqrt, bias=EPS, scale=1.0 / D)
                P.recip(s1, s1)
                h_t = hb[tt % 2]
                P.stt("dve", h_t[:], x1[:], s1, gb[:], ALU.mult, ALU.mult)
                p_t = pt[tt % 2]
                for kc in range(8):
                    P.tr(p_t[:, kc, :], h_t[:, kc * 128:(kc + 1) * 128], C["ident"][:], inc=(kc == 7))
                P.cp("act", hT[:, :, tsl], p_t[:])

    def phase_ffn(self, l, hT, x1_d, xo_d, xp_d):
        P, C = self.P, self.C
        I = self.inp
        self.cast_engs = ["pool", "dve", "act", "dve"]
        with P.scope() as st:
            W1h = P.sb("W1h", [128, 8, 2048], BF16, st)
            W2h = P.sb("W2h", [128, 16, 1024], BF16, st)
            aT = P.sb("aT", [128, 16, 512], BF16, st)
            rb = [P.sb("rbf", [128, 512], F32, st) for _ in range(2)]
            xt = [P.sb("xt", [128, 512], F32, st) for _ in range(4)]
            xo = [P.sb("xo", [128, 512], F32, st) for _ in range(4)]
            fb = [P.ps("fb", [128, 512], F32, st) for _ in range(2)]
            ob = [P.ps("ob", [128, 512], F32, st) for _ in range(4)]
            xi = 0
            for hp in range(2):
                self.wload(W1h, I["w_ff1"], l, hp * 2048, 2048)
                self.wload(W2h, I["w_ff2"], l, 0, 1024, rows=(hp * 2048, (hp + 1) * 2048))
                src_d = x1_d if hp == 0 else xp_d
                dst_d = xp_d if hp == 0 else xo_d
                for Q in range(4):
                    for fc in range(16):
                        ps = fb[fc % 2]
                        self.proj_fm(ps, W1h, fc * 128, 128, hT, Q * 512, 512)
                        r = rb[fc % 2]
                        P.act(r[:], ps[:, :], AF.Relu)
                        P.tt("dve", aT[:, fc, :], r[:], r[:], ALU.mult)
                    for hf in range(2):
                        cols = slice(hf * 512, (hf + 1) * 512)
                        xs_ = []
                        for tl in range(4):
                            rows = slice((Q * 4 + tl) * 128, (Q * 4 + tl + 1) * 128)
                            x_ = xt[xi % 4]
                            o_ = xo[xi % 4]
                            xi += 1
                            P.dma(x_[:], src_d[rows, cols])
                            xs_.append((x_, o_, rows))
                            for k in range(16):
                                P.mm(ob[tl][:, :], lhsT=aT[:, k, tl * 128:(tl + 1) * 128], rhs=W2h[:, k, cols],
                                     start=(k == 0), stop=(k == 15))
                        for tl in range(4):
                            x_, o_, rows = xs_[tl]
                            P.tt("dve", o_[:], ob[tl][:, :], x_[:], ALU.add)
                            P.dma(dst_d[rows, cols], o_[:])
                    P.maybe_barrier()

    def build(self):
        P = self.P
        I = self.inp
        self.din("x", [S, D])
        for name, shp in (("norm1_g", [DEPTH, D]), ("w_in", [DEPTH, D, D_IN]), ("dsa_q_norm", [DEPTH, 64]),
                          ("dsa_k_norm", [DEPTH, 64]), ("nsa_q_norm", [DEPTH, 64]), ("nsa_k_norm", [DEPTH, 3, 64]),
                          ("nsa_cmp_pos", [DEPTH, 2, 32, 64]), ("nsa_cmp_w", [DEPTH, 2, 32, 64, 64]),
                          ("ssd_conv_w", [DEPTH, 4, 3072]), ("ssd_conv_b", [DEPTH, 3072]),
                          ("ssd_dt_bias", [DEPTH, 32]), ("ssd_a_log", [DEPTH, 32]), ("ssd_d", [DEPTH, 32]),
                          ("ssd_norm_g", [DEPTH, 2048]), ("w_br_dsa", [DEPTH, D, D]), ("w_br_nsa", [DEPTH, D, D]),
                          ("w_br_ssd", [DEPTH, 2 * D, D]), ("w_out", [DEPTH, D, D]), ("norm2_g", [DEPTH, D]),
                          ("w_ff1", [DEPTH, D, 4 * D]), ("w_ff2", [DEPTH, 4 * D, D])):
            self.din(name, shp)
        self.load_consts()
        out = self.dout("out", [S, D])
        hT = P.sb("hT", [128, 8, S], BF16)
        ya_d = P.dram("ya", [128, 8, S], BF16)
        yb_d = P.dram("yb", [128, 8, S], BF16)
        yc_d = P.dram("yc", [128, 16, S], BF16)
        zs_d = P.dram("zs", [S, 2048], F32)
        xact_d = P.dram("xact", [128, 24, S], BF16)
        mix_d = P.dram("mix", [128, 8, S], BF16)
        x1_d = P.dram("x1", [S, D], F32)
        xm_d = P.dram("xm", [S, D], F32)
        xp_d = P.dram("xp", [S, D], F32)
        dbg = self.debug
        x_cur = I["x"]
        for l in range(self.nlayers):
            x_nxt = out if l == self.nlayers - 1 else xm_d
            self.phase_norm(x_cur, I["norm1_g"][l], hT)
            if dbg == "hT":
                return self._dump(hT[:], [128, 8, S], BF16)
            if dbg in ("nsa", "nsa_k"):
                self.phase_nsa(l, hT, yb_d)
                return self._dump(yb_d, [128, 8, S], BF16)
            if dbg == "ssd":
                self.phase_ssd(l, hT, yc_d, zs_d, xact_d)
                return self._dump(yc_d, [128, 16, S], BF16)
            self.phase_dsa(l, hT, ya_d)
            if dbg is not None and dbg.startswith("dsa"):
                return self._dump(ya_d, [128, 8, S], BF16)
            self.phase_nsa(l, hT, yb_d)
            self.phase_ssd(l, hT, yc_d, zs_d, xact_d)
            self.phase_merge(l, hT, ya_d, yb_d, yc_d, mix_d)
            if dbg == "mix":
                return self._dump(mix_d, [128, 8, S], BF16)
            self.phase_out_norm(l, x_cur, mix_d, x1_d, hT)
            if dbg == "x1":
                return self._dump(x1_d, [S, D], F32)
            self.phase_ffn(l, hT, x1_d, x_nxt, xp_d)
            x_cur = x_nxt
        P.finish()

    def _dump(self, src, shape, dt):
        if "dbg" not in self.out:
            o = self.dout("dbg", shape, dt)
            self.P.dma(o, src)
        self.P.finish()


_CACHE = {}


def kernel(**inputs):
    n = 8
    B = Builder()
    B.build()
    consts = _consts()
    shared = {}
    for name in B.inp:
        if name in consts:
            shared[name] = consts[name]
        elif name != "x":
            shared[name] = np.ascontiguousarray(np.asarray(inputs[name], dtype=np.float32))
    x = np.asarray(inputs["x"], dtype=np.float32)
    in_maps = []
    for b in range(n):
        m = dict(shared)
        m["x"] = np.ascontiguousarray(x[b])
        in_maps.append(m)
    res = run_bass_kernel_spmd(B.nc, in_maps, core_ids=list(range(n)))
    return np.stack([np.asarray(res.results[b]["out"], dtype=np.float32) for b in range(n)], axis=0)
```

```python
import contextlib
import os
import numpy as np
import concourse.bass as bass
import concourse.mybir as mybir
from concourse.bass_utils import run_bass_kernel_spmd

F32 = mybir.dt.float32
BF16 = mybir.dt.bfloat16
AF = mybir.ActivationFunctionType
ALU = mybir.AluOpType
AX = mybir.AxisListType

EPOCH = 240
NPOOL = 90

D = 1024
S = 2048
NT = 16
DEPTH = 2
EPS = 1e-6
OFF = dict(dq=0, dk=1024, dv=1088, iq=1152, ik=1408, iw=1440, nq=1448, nkv=2472, ng=3240,
           sz=3288, sxbc=5336, sdt=8408, mg=8440)
D_IN = 11512
NEGM = -30000.0


def _region(ap):
    t = ap.tensor
    name = t.name
    dims = [(int(s), int(c)) for s, c in ap.ap]
    off = int(ap.offset)
    if "DRam" in type(t).__name__:
        lo = hi = off
        for s, c in dims:
            if s >= 0:
                hi += s * (c - 1)
            else:
                lo += s * (c - 1)
        return (name, 0, 1, lo, hi + 1)
    if "PSum" in type(t).__name__:
        return (name, 0, 128, 0, 1 << 30)
    ps, pc = dims[0]
    if ps == 0:
        p0, f0 = 0, off
    else:
        p0 = off // ps
        f0 = off - p0 * ps
    lo = hi = f0
    for s, c in dims[1:]:
        if s >= 0:
            hi += s * (c - 1)
        else:
            lo += s * (c - 1)
    return (name, p0, p0 + pc, lo, hi + 1)


def _overlap(a, b):
    return a[1] < b[2] and b[1] < a[2] and a[3] < b[4] and b[3] < a[4]


def _contains(a, b):
    return a[1] <= b[1] and b[2] <= a[2] and a[3] <= b[3] and b[4] <= a[4]


class Prog:
    ENGS = ("pe", "act", "dve", "pool", "sp")

    def __init__(self, nc):
        self.nc = nc
        self.stack = contextlib.ExitStack()
        self.eng = {"pe": nc.tensor, "act": nc.scalar, "dve": nc.vector, "pool": nc.gpsimd, "sp": nc.sync}
        self.pool = [self.stack.enter_context(nc.semaphore(f"sp{i}")) for i in range(NPOOL)]
        self.bsem = [self.stack.enter_context(nc.semaphore(f"bar{i}")) for i in range(4)]
        self.nbar = 0
        self.uid = 0
        self.n_wait = 0
        self.n_ins = 0
        self.max_used = 0
        self._reset()

    def _reset(self):
        self.free = list(range(NPOOL))
        self.esem = {e: None for e in self.ENGS}
        self.ecnt = {e: 0 for e in self.ENGS}
        self.pend = {e: False for e in self.ENGS}
        self.seen = {e: {} for e in self.ENGS}
        self.dset = []
        self.dall = {}
        self.dnext = 0
        self.trk = {}

    def _alloc(self):
        assert self.free, "semaphore pool exhausted; add a barrier"
        i = self.free.pop()
        self.max_used = max(self.max_used, NPOOL - len(self.free))
        return i

    def sems_left(self):
        return len(self.free)

    def sb(self, name, shape, dtype, stack=None):
        self.uid += 1
        return (stack or self.stack).enter_context(
            self.nc.sbuf_tensor(f"{name}_{self.uid}", list(shape), dtype))

    def ps(self, name, shape, dtype, stack=None):
        self.uid += 1
        return (stack or self.stack).enter_context(
            self.nc.psum_tensor(f"{name}_{self.uid}", list(shape), dtype))

    def dram(self, name, shape, dtype):
        self.uid += 1
        return self.nc.dram_tensor(f"{name}_{self.uid}", list(shape), dtype).ap()

    @contextlib.contextmanager
    def scope(self):
        st = contextlib.ExitStack()
        try:
            yield st
        finally:
            self.barrier()
            st.close()

    def _deps(self, reads, writes):
        deps = []
        for ap in reads:
            r = _region(ap)
            t = self.trk.get(r[0])
            if t is None:
                continue
            for (wr, ev) in t["w"]:
                if _overlap(wr, r):
                    deps.append(ev)
        for ap in writes:
            r = _region(ap)
            t = self.trk.get(r[0])
            if t is None:
                continue
            for (wr, ev) in t["w"]:
                if _overlap(wr, r):
                    deps.append(ev)
            for (sk, rr), val in t["r"].items():
                if _overlap(rr, r):
                    deps.append((sk, val))
        return deps

    def _record(self, reads, writes, ev):
        for ap in reads:
            r = _region(ap)
            t = self.trk.setdefault(r[0], {"w": [], "r": {}})
            k = (ev[0], r)
            if t["r"].get(k, -1) < ev[1]:
                t["r"][k] = ev[1]
        for ap in writes:
            r = _region(ap)
            t = self.trk.setdefault(r[0], {"w": [], "r": {}})
            t["w"] = [(wr, e) for (wr, e) in t["w"] if not _contains(r, wr)]
            t["w"].append((r, ev))
            t["r"] = {k: v for k, v in t["r"].items() if not _contains(r, k[1])}

    def _wait(self, e, deps):
        eng = self.eng[e]
        need = {}
        for sk, val in deps:
            if self.seen[e].get(sk, 0) >= val:
                continue
            if need.get(sk, 0) < val:
                need[sk] = val
        for sk, val in need.items():
            assert 0 < val <= 255
            eng.wait_ge(self.pool[sk[1]], val)
            self.seen[e][sk] = val
            self.n_wait += 1

    def _next_ev(self, e, commit):
        if self.esem[e] is None or self.ecnt[e] >= EPOCH:
            self.esem[e] = self._alloc()
            self.ecnt[e] = 0
        ev = ((e, self.esem[e]), self.ecnt[e] + 1)
        if commit:
            self.ecnt[e] += 1
        return ev

    def op(self, e, fn, reads, writes, inc=True):
        deps = self._deps(reads, writes)
        if e == "pe":
            deps = [d for d in deps if d[0][0] != "pe"]
        self._wait(e, deps)
        ins = fn(self.eng[e])
        self.n_ins += 1
        ev = self._next_ev(e, inc)
        if inc:
            ins.then_inc(self.pool[ev[0][1]], 1)
            self.pend[e] = False
        else:
            self.pend[e] = True
        self._record(reads, writes, ev)
        return ins

    def dma(self, out, in_, q="sp", **kw):
        deps = self._deps([in_], [out])
        if q == "pool":
            ent = [self._alloc(), 0]
        else:
            if len(self.dset) < 8:
                self.dset.append([self._alloc(), 0])
            k = self.dnext % len(self.dset)
            self.dnext += 1
            if self.dset[k][1] >= 15:
                self.dset[k] = [self._alloc(), 0]
            ent = self.dset[k]
        if ent[1] > 0:
            deps.append((("d", ent[0]), 16 * ent[1]))
        self._wait(q, deps)
        ins = self.eng[q].dma_start(out=out, in_=in_, **kw)
        ins.then_inc(self.pool[ent[0]], 16)
        ent[1] += 1
        self.dall[ent[0]] = ent[1]
        self.n_ins += 1
        self._record([in_], [out], (("d", ent[0]), 16 * ent[1]))
        return ins

    def barrier(self):
        evs = []
        for e in self.ENGS:
            assert not self.pend[e], f"pending non-inc op on {e} at barrier"
            if self.esem[e] is not None and self.ecnt[e] > 0:
                evs.append(((e, self.esem[e]), self.ecnt[e]))
        for k, c in self.dall.items():
            evs.append((("d", k), 16 * c))
        for e in self.ENGS:
            self._wait(e, evs)
        b0 = self.bsem[2 * (self.nbar % 2)]
        b1 = self.bsem[2 * (self.nbar % 2) + 1]
        p0 = self.bsem[2 * ((self.nbar + 1) % 2)]
        p1 = self.bsem[2 * ((self.nbar + 1) % 2) + 1]
        for e in self.ENGS:
            if e != "sp":
                self.eng[e].sem_inc(b0, 1)
        sp = self.eng["sp"]
        sp.wait_ge(b0, 4)
        used = [i for i in range(NPOOL) if i not in set(self.free)]
        for i in used:
            sp.sem_clear(self.pool[i])
        sp.sem_clear(p0)
        sp.sem_clear(p1)
        sp.sem_inc(b1, 1)
        for e in self.ENGS:
            if e != "sp":
                self.eng[e].wait_ge(b1, 1)
        self.nbar += 1
        self._reset()

    def maybe_barrier(self, min_free=35):
        if len(self.free) < min_free:
            self.barrier()

    def finish(self):
        self.barrier()
        self.stack.close()

    def mm(self, out, lhsT, rhs, start=True, stop=True, inc=None, sgc=False):
        if inc is None:
            inc = stop
        return self.op("pe", lambda e: e.matmul(out, lhsT=lhsT, rhs=rhs, start=start, stop=stop,
                                                skip_group_check=sgc),
                       [lhsT, rhs], [out], inc=inc)

    def tr(self, out, in_, ident, inc=True):
        return self.op("pe", lambda e: e.transpose(out=out, in_=in_, identity=ident), [in_, ident], [out], inc=inc)

    def act(self, out, in_, func, bias=None, scale=None, accum=None):
        kw = {}
        reads = [in_]
        writes = [out]
        if bias is not None:
            kw["bias"] = bias
            if not isinstance(bias, (int, float)):
                reads.append(bias)
        if scale is not None:
            kw["scale"] = scale
            if not isinstance(scale, (int, float)):
                reads.append(scale)
        if accum is not None:
            kw["accum_out"] = accum
            writes.append(accum)
        return self.op("act", lambda e: e.activation(out=out, in_=in_, func=func, **kw), reads, writes)

    def cp(self, eng, out, in_):
        if eng == "act":
            return self.op("act", lambda e: e.copy(out=out, in_=in_), [in_], [out])
        return self.op(eng, lambda e: e.tensor_copy(out=out, in_=in_), [in_], [out])

    def tt(self, eng, out, in0, in1, op):
        return self.op(eng, lambda e: e.tensor_tensor(out=out, in0=in0, in1=in1, op=op), [in0, in1], [out])

    def ts(self, eng, out, in0, s1, s2=None, op0=ALU.mult, op1=None, accum=None):
        reads = [in0] + [s for s in (s1, s2) if s is not None and not isinstance(s, (int, float))]
        writes = [out] + ([accum] if accum is not None else [])
        kw = {}
        if op1 is not None:
            kw["op1"] = op1
        if accum is not None:
            kw["accum_out"] = accum
        return self.op(eng, lambda e: e.tensor_scalar(out=out, in0=in0, scalar1=s1, scalar2=s2, op0=op0, **kw),
                       reads, writes)

    def stt(self, eng, out, in0, scalar, in1, op0, op1):
        reads = [in0, in1] + ([scalar] if not isinstance(scalar, (int, float)) else [])
        return self.op(eng, lambda e: e.scalar_tensor_tensor(out=out, in0=in0, scalar=scalar, in1=in1,
                                                             op0=op0, op1=op1), reads, [out])

    def amul(self, out, in_, val):
        return self.op("act", lambda e: e.mul(out=out, in_=in_, mul=val), [in_], [out])

    def memset(self, eng, ap, val):
        return self.op(eng, lambda e: e.memset(ap, val), [], [ap])

    def recip(self, out, in_):
        return self.op("dve", lambda e: e.reciprocal(out=out, in_=in_), [in_], [out])


def _consts():
    c = {}
    p = np.arange(128)
    c["c_ident"] = np.eye(128, dtype=np.float32)
    c["c_ones"] = np.ones((128, 128), np.float32)
    c["c_blk64"] = (p[:, None] // 64 == p[None, :] // 64).astype(np.float32)
    c["c_tri"] = (p[:, None] <= p[None, :]).astype(np.float32)
    c["c_cneg"] = np.where(p[:, None] > p[None, :], NEGM, 0.0).astype(np.float32)
    c["c_wneg"] = np.where(p[:, None] <= p[None, :], NEGM, 0.0).astype(np.float32)
    c["c_cnegtm"] = np.where(p[None, :] > p[:, None], -3e30, 0.0).astype(np.float32)
    s = np.arange(S)
    c["c_E"] = (s[None, :] // 64 == np.arange(32)[:, None]).astype(np.float32)
    n = np.arange(128)
    vis = (16 * n[:, None] + 31 <= s[None, :])
    c["c_visneg"] = np.where(vis, 0.0, NEGM).astype(np.float32)
    j = np.arange(32)
    cover = ((16 * n[:, None] < 64 * j[None, :] + 64) & (16 * n[:, None] + 32 > 64 * j[None, :]))
    ce = np.zeros((128, 33), np.float32)
    ce[:, 0] = 1.0
    ce[:, 1:] = cover
    ce[127] = 0.0
    c["c_cover"] = ce
    t = (np.arange(NT)[None, :, None] * 128 + p[:, None, None])
    cur = t // 64
    jj = j[None, None, :]
    forced = (jj == cur) | (jj == 0)
    future = jj > cur
    c["c_selkeep"] = (~(forced | future)).astype(np.float32)
    c["c_selbias"] = np.where(forced, 1e9, np.where(future, -1e9, 0.0)).astype(np.float32)
    return c


class Builder:
    def __init__(self, debug=None, nlayers=DEPTH):
        self.debug = debug
        self.nlayers = nlayers
        import os
        self.lim = int(os.environ.get("KLIM", "16"))
        nc = bass.Bass("TRN2", target_bir_lowering=False)
        self.nc = nc
        self.P = Prog(nc)
        self.inp = {}
        self.out = {}

    def din(self, name, shape, dt=F32):
        self.inp[name] = self.nc.dram_tensor(name, list(shape), dt, kind="ExternalInput").ap()
        return self.inp[name]

    def dout(self, name, shape, dt=F32):
        self.out[name] = self.nc.dram_tensor(name, list(shape), dt, kind="ExternalOutput").ap()
        return self.out[name]

    def load_consts(self):
        P = self.P
        C = {}
        shapes = {k: v.shape for k, v in _consts().items()}
        for k, shp in shapes.items():
            self.din(k, shp)
        self.wstage = [P.sb("wstage", [128, 4096], F32) for _ in range(2)]
        self.wsi = 0
        self.cast_engs = ["pool"]

        def ld(name, key, shape, dt, rows=None):
            t = P.sb(name, shape, dt)
            src = self.inp[key]
            if dt == F32:
                P.dma(t[:], src)
            else:
                n = int(np.prod(shape[1:]))
                flat = "p a b -> p (a b)" if len(shape) == 3 else None
                for a0 in range(0, n, 4096):
                    b0 = min(n, a0 + 4096)
                    stg = self.wstage[self.wsi % 2]
                    self.wsi += 1
                    P.dma(stg[0:shape[0], 0:b0 - a0], src[:, a0:b0])
                    P.cp("pool", t[:, a0:b0], stg[0:shape[0], 0:b0 - a0])
            return t
        C["ident_f"] = ld("ident_f", "c_ident", [128, 128], F32)
        C["ident"] = ld("ident", "c_ident", [128, 128], BF16)
        C["ones_f"] = ld("ones_f", "c_ones", [128, 128], F32)
        C["blk64_f"] = ld("blk64_f", "c_blk64", [128, 128], F32)
        C["tri_f"] = ld("tri_f", "c_tri", [128, 128], F32)
        C["cneg"] = ld("cneg", "c_cneg", [128, 128], BF16)
        C["zero"] = P.sb("zero", [128, 128], BF16)
        P.memset("pool", C["zero"][:], 0.0)
        C["cneg_f"] = ld("cneg_f", "c_cneg", [128, 128], F32)
        C["wneg"] = ld("wneg", "c_wneg", [128, 128], BF16)
        C["cnegtm"] = ld("cnegtm", "c_cnegtm", [128, 128], F32)
        C["E"] = ld("E", "c_E", [32, S], BF16)
        C["visneg"] = ld("visneg", "c_visneg", [128, S], BF16)
        C["cover"] = ld("cover", "c_cover", [128, 33], BF16)
        C["selkeep"] = ld("selkeep", "c_selkeep", [128, NT, 32], F32)
        C["selbias"] = ld("selbias", "c_selbias", [128, NT, 32], F32)
        self.C = C

    def wload(self, dst, wdram, l, c0, n, rows=None):
        w2 = wdram[l] if rows is None else wdram[l][rows[0]:rows[1]]
        src = w2.rearrange("(kc p) n -> p kc n", p=128)[:, :, c0:c0 + n]
        nk = src.shape[1]
        step = max(1, 4096 // nk)
        for a in range(0, n, step):
            b = min(n, a + step)
            stg = self.wstage[self.wsi % 2]
            self.wsi += 1
            v = stg[:, 0:nk * (b - a)].rearrange("p (k n) -> p k n", k=nk)
            self.P.dma(v, src[:, :, a:b], q="sp")
            eng = self.cast_engs[self.wsi % len(self.cast_engs)]
            self.P.cp(eng, dst[:, :, a:b], v)

    def pipelined(self, items, Sb, pTb, cnt, depth=1):
        P = self.P
        n = len(items)
        nb = len(Sb)
        assert nb >= depth + 1 and len(pTb) >= depth + 1
        slots = []
        for i in range(n + depth):
            if i < n:
                k = cnt["si"] % nb
                cnt["si"] += 1
                Sp, pT = Sb[k], pTb[k]
                items[i][0](Sp)
                P.act(pT[:], Sp[:, :], AF.Exp, scale=0.125)
                slots.append(pT)
            if i >= depth:
                items[i - depth][1](slots[i - depth])

    def proj_fm(self, ps, Wt, c0, M, hT, t0, N):
        for kc in range(8):
            self.P.mm(ps[0:M, 0:N], lhsT=Wt[:, kc, c0:c0 + M], rhs=hT[:, kc, t0:t0 + N],
                      start=(kc == 0), stop=(kc == 7))

    def proj_tm(self, ps, Wt, c0, n, hT, tile):
        for kc in range(8):
            self.P.mm(ps[:, 0:n], lhsT=hT[:, kc, tile * 128:(tile + 1) * 128], rhs=Wt[:, kc, c0:c0 + n],
                      start=(kc == 0), stop=(kc == 7))

    def norm_fm(self, ps, ps2, M, N, gcol, out, sq, rs):
        P, C = self.P, self.C
        P.act(sq[0:M, 0:N], ps[0:M, 0:N], AF.Square)
        P.mm(ps2[0:M, 0:N], lhsT=C["blk64_f"][0:M, 0:M], rhs=sq[0:M, 0:N])
        P.act(rs[0:M, 0:N], ps2[0:M, 0:N], AF.Sqrt, bias=EPS, scale=1.0 / 64)
        P.recip(rs[0:M, 0:N], rs[0:M, 0:N])
        P.stt("dve", out, ps[0:M, 0:N], gcol, rs[0:M, 0:N], ALU.mult, ALU.mult)

    def gain_col(self, name, vec64, stack):
        t = self.P.sb(name, [128, 1], F32, stack)
        src = vec64.rearrange("(p o) -> p o", o=1)
        self.P.dma(t[0:64, :], src)
        self.P.dma(t[64:128, :], src)
        return t

    def phase_norm(self, xd, gvec, hT, store_T=None):
        P, C = self.P, self.C
        with P.scope() as st:
            gb = P.sb("gb", [128, D], F32, st)
            P.dma(gb[:], gvec.partition_broadcast(128))
            xb = [P.sb("xb", [128, D], F32, st) for _ in range(2)]
            sq = P.sb("sq", [128, D], F32, st)
            hb = [P.sb("hb", [128, D], BF16, st) for _ in range(2)]
            ss = P.sb("ss", [128, 2], F32, st)
            pt = [P.ps("pt", [128, 8, 128], BF16, st) for _ in range(2)]
            for tt in range(NT):
                x_t = xb[tt % 2]
                P.dma(x_t[:], xd[tt * 128:(tt + 1) * 128, :])
                s1 = ss[:, tt % 2:tt % 2 + 1]
                P.act(sq[:], x_t[:], AF.Square, accum=s1)
                P.act(s1, s1, AF.Sqrt, bias=EPS, scale=1.0 / D)
                P.recip(s1, s1)
                h_t = hb[tt % 2]
                P.stt("dve", h_t[:], x_t[:], s1, gb[:], ALU.mult, ALU.mult)
                p_t = pt[tt % 2]
                for kc in range(8):
                    P.tr(p_t[:, kc, :], h_t[:, kc * 128:(kc + 1) * 128], C["ident"][:], inc=(kc == 7))
                P.cp("act" if tt % 2 else "dve", hT[:, :, tt * 128:(tt + 1) * 128], p_t[:])

    def phase_dsa(self, l, hT, ya_d):
        P, C = self.P, self.C
        I = self.inp
        self.cast_engs = ["pool", "act"]
        w_in = I["w_in"]
        with P.scope() as st:
            Wq = P.sb("Wq", [128, 8, 1024], BF16, st)
            Wk = P.sb("Wk", [128, 8, 128], BF16, st)
            self.wload(Wk[:, :, 0:64], w_in, l, OFF["dk"], 64)
            self.wload(Wk[:, :, 64:128], w_in, l, OFF["dk"], 64)
            Wv = P.sb("Wv", [128, 8, 64], BF16, st)
            self.wload(Wv, w_in, l, OFF["dv"], 64)
            Wiq = P.sb("Wiq", [128, 8, 256], BF16, st)
            self.wload(Wiq, w_in, l, OFF["iq"], 256)
            Wik = P.sb("Wik", [128, 8, 128], BF16, st)
            for r in range(4):
                self.wload(Wik[:, :, r * 32:(r + 1) * 32], w_in, l, OFF["ik"], 32)
            Wiw = P.sb("Wiw", [128, 8, 8], BF16, st)
            self.wload(Wiw, w_in, l, OFF["iw"], 8)
            self.wload(Wq, w_in, l, OFF["dq"], 1024)
            gq = self.gain_col("gq", I["dsa_q_norm"][l], st)
            gk = self.gain_col("gk", I["dsa_k_norm"][l], st)

            kT = P.sb("kT", [128, S], BF16, st)
            v1 = P.sb("v1", [128, NT, 65], BF16, st)
            ikbd = P.sb("ikbd", [128, NT, 4, 128], BF16, st)
            qz = P.sb("qz", [128, 16, 512], BF16, st)
            iqT = P.sb("iqT", [128, 2, S], BF16, st)
            iw = P.sb("iw", [128, NT, 8], F32, st)
            acc = P.sb("acc", [128, S], F32, st)
            work = P.sb("work", [128, S], F32, st)
            mx = P.sb("mx", [128, 8], F32, st)
            mx2 = P.sb("mx2", [128, 8], F32, st)
            negm2 = P.sb("negm2", [128, S], BF16, st)
            negm = P.sb("negm", [128, S], BF16, st)
            ikT4 = negm
            negmT = [P.sb("negmT", [128, NT, 128], BF16, st) for _ in range(2)]
            rb = [P.sb("rb", [128, 512], F32, st) for _ in range(2)]
            pTb = [P.sb("pTb", [128, 512], BF16, st) for _ in range(2)]
            ytm = P.sb("ytm", [128, 1024], BF16, st)
            yT = P.sb("yT", [128, 8, 512], BF16, st)
            sq = P.sb("sq", [128, 512], F32, st)
            rs = P.sb("rs", [128, 512], F32, st)
            rc = P.sb("rc", [128, 4], F32, st)
            bank = [P.ps("bk", [128, 512], F32, st) for _ in range(6)]
            tb = [P.ps("tb", [128, 8, 128], BF16, st) for _ in range(2)]
            Sb, Ob, Ib = bank[0:2], bank[2:4], bank[4:6]

            P.memset("pool", v1[:, :, 64:65], 1.0)
            P.memset("pool", ikbd[:], 0.0)
            P.memset("pool", qz[:], 0.0)
            for Q in range(4):
                t0 = Q * 512
                self.proj_fm(Ib[0], Wk, 0, 128, hT, t0, 512)
                self.norm_fm(Ib[0], Ib[1], 128, 512, gk[:, 0:1], kT[:, t0:t0 + 512], sq, rs)
                self.proj_fm(Ib[0], Wik, 0, 128, hT, t0, 512)
                P.cp("act", ikT4[:, t0:t0 + 512], Ib[0][:, :])
                for c in range(2):
                    self.proj_fm(Ib[c], Wiq, c * 128, 128, hT, t0, 512)
                    P.cp("act", iqT[:, c, t0:t0 + 512], Ib[c][:, :])
            for hh in range(4):
                P.dma(ikbd[32 * hh:32 * hh + 32, :, hh, :],
                      ikT4[32 * hh:32 * hh + 32, :].rearrange("p (k s) -> p k s", s=128))
            for tt in range(NT):
                ps = Ib[tt % 2]
                self.proj_tm(ps, Wv, 0, 64, hT, tt)
                P.cp("act", v1[:, tt, 0:64], ps[:, 0:64])
                self.proj_tm(ps, Wiw, 64, 8, hT, tt) if False else None
            for tt in range(NT):
                ps = Ib[tt % 2]
                self.proj_tm(ps, Wiw, 0, 8, hT, tt)
                P.cp("act", iw[:, tt, :], ps[:, 0:8])

            cnt = {"si": 0, "ii": 0}

            def stage_q(Q):
                t0 = Q * 512
                for c in range(8):
                    self.proj_fm(Ib[0], Wq, c * 128, 128, hT, t0, 512)
                    P.act(sq[:, :], Ib[0][:, :], AF.Square)
                    P.mm(Ib[1][:, :], lhsT=C["blk64_f"][:], rhs=sq[:, :])
                    P.act(rs[:, :], Ib[1][:, :], AF.Sqrt, bias=EPS, scale=1.0 / 64)
                    P.recip(rs[:, :], rs[:, :])
                    for hf in range(2):
                        psl = slice(64 * hf, 64 * hf + 64)
                        P.stt("dve", qz[psl, 2 * c + hf, :], Ib[0][psl, :], gq[psl, 0:1], rs[psl, :], ALU.mult, ALU.mult)

            accs = [acc, work]
            mxs = [mx, mx2]
            negms = [negm, negm2]

            def stage_a_scores(qt, bf):
                nk = qt + 1
                tsl = slice(qt * 128, (qt + 1) * 128)
                ab = accs[bf]
                for kt in range(nk):
                    a_kt = ab[:, kt * 128:(kt + 1) * 128]
                    for c in range(2):
                        ps = Ib[cnt["ii"] % 2]
                        r = rb[cnt["ii"] % 2]
                        cnt["ii"] += 1
                        P.mm(ps[:, :], lhsT=iqT[:, c, tsl], rhs=ikbd[:, kt].rearrange("p a b -> p (a b)"))
                        P.act(r[:], ps[:, :], AF.Relu)
                        for hh in range(4):
                            h = 4 * c + hh
                            if h == 0:
                                P.ts("dve", a_kt, r[:, 0:128], iw[:, qt, 0:1], None, op0=ALU.mult)
                            else:
                                P.stt("dve", a_kt, r[:, hh * 128:(hh + 1) * 128], iw[:, qt, h:h + 1], a_kt,
                                      ALU.mult, ALU.add)
                d_sl = slice(qt * 128, (qt + 1) * 128)
                P.tt("dve", ab[:, d_sl], ab[:, d_sl], C["cnegtm"][:], ALU.add)

            def stage_a(tiles):
                todo = []
                for qt in tiles:
                    bf = qt % 2
                    if qt < 2:
                        P.memset("pool", negms[bf][:, 0:(qt + 1) * 128], 0.0)
                    else:
                        stage_a_scores(qt, bf)
                        todo.append((qt, bf))
                for r_ in range(32):
                    for (qt, bf) in todo:
                        n = (qt + 1) * 128
                        P.op("dve", lambda e: e.max(out=mxs[bf][:], in_=accs[bf][:, 0:n]), [accs[bf][:, 0:n]], [mxs[bf][:]])
                    for (qt, bf) in todo:
                        n = (qt + 1) * 128
                        P.op("dve", lambda e: e.match_replace(out=accs[bf][:, 0:n], in_to_replace=mxs[bf][:],
                                                              in_values=accs[bf][:, 0:n], imm_value=-1e30),
                             [mxs[bf][:], accs[bf][:, 0:n]], [accs[bf][:, 0:n]])
                for (qt, bf) in todo:
                    n = (qt + 1) * 128
                    P.ts("dve", negms[bf][:, 0:n], accs[bf][:, 0:n], -1e29, 1.0, op0=ALU.is_le, op1=ALU.subtract)

            def stage_b(qt):
                nk = qt + 1
                nT = negmT[qt % 2]
                negm = negms[qt % 2]
                for k0 in range(0, nk, 8):
                    k1 = min(nk, k0 + 8)
                    t_ = tb[0]
                    for kt in range(k0, k1):
                        P.tr(t_[:, kt - k0, :], negm[:, kt * 128:(kt + 1) * 128], C["ident"][:],
                             inc=(kt == k1 - 1))
                    P.amul(nT[:, k0:k1, :], t_[:, 0:k1 - k0, :], -NEGM)
                P.tt("pool", nT[:, qt, :], nT[:, qt, :], C["cneg"][:], ALU.add)

            def stage_c(qt):
                nk = qt + 1
                tl = qt % 4
                tsl = slice(tl * 128, (tl + 1) * 128)
                nT = negmT[qt % 2]
                items = []
                for hg in range(4):
                    O = Ob[hg % 2]
                    O3 = O[:, 0:260].rearrange("p (h e) -> p h e", e=65)
                    for kt in range(nk):
                        def em_s(Sp, hg=hg, kt=kt):
                            P.mm(Sp[:, :], lhsT=C["ident"][:], rhs=nT[:, kt, :].unsqueeze(1).to_broadcast([128, 4, 128]),
                                 start=True, stop=False, inc=False)
                            P.mm(Sp[:, :], lhsT=kT[:, kt * 128:(kt + 1) * 128], rhs=qz[:, 4 * hg:4 * hg + 4, tsl],
                                 start=False, stop=True)

                        def em_pv(pT, hg=hg, kt=kt, O3=O3):
                            for hh in range(4):
                                P.mm(O3[:, hh, :], lhsT=pT[:, hh * 128:(hh + 1) * 128], rhs=v1[:, kt, :],
                                     start=(kt == 0 and hh == 0), stop=(kt == nk - 1), inc=(hh == 3), sgc=True)
                            if kt == nk - 1:
                                P.recip(rc[:], O3[:, :, 64])
                                P.tt("dve", ytm[:, hg * 256:(hg + 1) * 256].rearrange("p (h e) -> p h e", e=64),
                                     O3[:, :, 0:64], rc[:].unsqueeze(2).to_broadcast([128, 4, 64]), ALU.mult)
                        items.append((em_s, em_pv))
                self.pipelined(items, Sb, pTb, cnt)
                t_ = tb[1]
                for c in range(8):
                    P.tr(t_[:, c, :], ytm[:, c * 128:(c + 1) * 128], C["ident"][:], inc=(c == 7))
                P.cp("act", yT[:, :, tsl], t_[:])
                if tl == 3:
                    Q = qt // 4
                    P.dma(ya_d[:, :, Q * 512:(Q + 1) * 512], yT[:], q="sp")

            nq = min(NT, self.lim)
            pairs = [list(range(i, min(i + 2, nq))) for i in range(0, nq, 2)]
            stage_a(pairs[0])
            for qt in pairs[0]:
                stage_b(qt)
            for pi, pr in enumerate(pairs):
                P.maybe_barrier()
                if pi + 1 < len(pairs):
                    stage_a(pairs[pi + 1])
                for qt in pr:
                    if qt % 4 == 0:
                        stage_q(qt // 4)
                    stage_c(qt)
                if pi + 1 < len(pairs):
                    for qt in pairs[pi + 1]:
                        stage_b(qt)

    def _nsa_kside(self, l, hT, st, env):
        P, C = self.P, self.C
        I = self.inp
        Wkv, Wks, Wkw = env["Wkv"], env["Wks"], env["Wkw"]
        gks, gkw, gkc = env["gks"], env["gkw"], env["gkc"]
        ksT, kwT, vs1, vw1, kcmpT, vext = env["ksT"], env["kwT"], env["vs1"], env["vw1"], env["kcmpT"], env["vext"]
        sq, rs, ss, Mb, tb = env["sq"], env["rs"], env["ss"], env["Mb"], env["tb"]
        cmpw = []
        posB = []
        for kv in range(2):
            cw = P.sb("cmpw", [128, 32, 64], BF16, st)
            stg = self.wstage[self.wsi % 2]
            self.wsi += 1
            sv = stg[:, 0:2048].rearrange("p (l e) -> p l e", e=64)
            src = I["nsa_cmp_w"][l, kv].rearrange("l d e -> d l e")
            P.dma(sv[0:64], src)
            P.dma(sv[64:128], src)
            P.cp("pool", cw[:], sv)
            cmpw.append(cw)
            pt_ = P.sb("posT", [128, 32], F32, st)
            psrc = I["nsa_cmp_pos"][l, kv].rearrange("l d -> d l")
            P.dma(pt_[0:64, :], psrc, allow_slow_non_contiguous=True)
            P.dma(pt_[64:128, :], psrc, allow_slow_non_contiguous=True)
            pb = P.sb("posB", [128, 32, 127], BF16, st)
            P.cp("pool", pb[:], pt_[:].unsqueeze(2).to_broadcast([128, 32, 127]))
            posB.append(pb)

        kcT = P.sb("kcT", [128, S], BF16, st)
        vcT = P.sb("vcT", [128, S], BF16, st)
        kd = P.sb("kd", [128, 128], BF16, st)
        P.memset("pool", vs1[:, :, :, 64:65], 1.0)
        P.memset("pool", vw1[:, :, :, 64:65], 1.0)
        P.memset("pool", kd[:], 0.0)
        for g in range(2):
            P.memset("pool", vext[g][:], 0.0)
        for Q in range(4):
            t0 = Q * 512
            self.proj_fm(Mb[0], Wkv, 0, 128, hT, t0, 512)
            P.cp("act", kcT[:, t0:t0 + 512], Mb[0][:, :])
            self.proj_fm(Mb[1], Wkv, 128, 128, hT, t0, 512)
            P.cp("act", vcT[:, t0:t0 + 512], Mb[1][:, :])
            for g in range(2):
                self.proj_fm(Mb[0], Wks, g * 128, 128, hT, t0, 512)
                self.norm_fm(Mb[0], Mb[1], 128, 512, gks[:, 0:1], ksT[g][:, t0:t0 + 512], sq, rs)
                self.proj_fm(Mb[0], Wkw, g * 128, 128, hT, t0, 512)
                self.norm_fm(Mb[0], Mb[1], 128, 512, gkw[:, 0:1], kwT[g][:, t0:t0 + 512], sq, rs)
        for tt in range(NT):
            ps = Mb[tt % 2]
            self.proj_tm(ps, Wkv, 384, 128, hT, tt)
            P.cp("act", vs1[:, tt, :, 0:64], ps[:, 0:128].rearrange("p (g e) -> p g e", e=64))
            self.proj_tm(ps, Wkv, 640, 128, hT, tt)
            P.cp("act", vw1[:, tt, :, 0:64], ps[:, 0:128].rearrange("p (g e) -> p g e", e=64))
        P.maybe_barrier()
        def strided(t, lo, off):
            return bass.AP(t[:].tensor, lo * S + off, [[S, 64], [16, 127]])
        for g in range(2):
            lo = g * 64
            for kv, src in ((0, kcT), (1, vcT)):
                ps = Mb[kv]
                for li in range(32):
                    P.mm(ps[0:127, 0:64], lhsT=strided(src, lo, li), rhs=cmpw[kv][lo:lo + 64, li, :],
                         start=(li == 0), stop=False, inc=False)
                for li in range(32):
                    P.mm(ps[0:127, 0:64], lhsT=posB[kv][lo:lo + 64, li, :], rhs=cmpw[kv][lo:lo + 64, li, :],
                         start=False, stop=(li == 31), inc=(li == 31))
            P.act(sq[0:127, 0:64], Mb[0][0:127, 0:64], AF.Square, accum=ss[0:127, :])
            P.act(ss[0:127, :], ss[0:127, :], AF.Sqrt, bias=EPS, scale=1.0 / 64)
            P.recip(ss[0:127, :], ss[0:127, :])
            P.stt("dve", kd[0:127, 0:64], Mb[0][0:127, 0:64], ss[0:127, 0:1], gkc[0:127, :], ALU.mult, ALU.mult)
            P.cp("pool", kd[0:127, 64:128], kd[0:127, 0:64])
            P.tr(tb[0][:, 0, :], kd[:], C["ident"][:])
            P.cp("act", kcmpT[g][:], tb[0][:, 0, :])
            P.cp("act", vext[g][0:127, 0:64], Mb[1][0:127, 0:64])
            P.cp("pool", vext[g][0:127, 64:97], C["cover"][0:127, :])
        P.maybe_barrier()


    def phase_nsa(self, l, hT, yb_d):
        P, C = self.P, self.C
        I = self.inp
        self.cast_engs = ["pool", "dve"]
        w_in = I["w_in"]
        KV = OFF["nkv"]
        with P.scope() as st:
            Wq = P.sb("Wnq", [128, 8, 1024], BF16, st)
            Wkv = P.sb("Wkv", [128, 8, 768], BF16, st)
            self.wload(Wkv, w_in, l, KV, 768)
            Wks = P.sb("Wks", [128, 8, 256], BF16, st)
            Wkw = P.sb("Wkw", [128, 8, 256], BF16, st)
            for g in range(2):
                for r in range(2):
                    self.wload(Wks[:, :, g * 128 + r * 64:g * 128 + r * 64 + 64], w_in, l, KV + 256 + g * 64, 64)
                    self.wload(Wkw[:, :, g * 128 + r * 64:g * 128 + r * 64 + 64], w_in, l, KV + 512 + g * 64, 64)
            Wng = P.sb("Wng", [128, 8, 48], BF16, st)
            self.wload(Wng, w_in, l, OFF["ng"], 48)
            self.wload(Wq, w_in, l, OFF["nq"], 1024)
            gq = self.gain_col("gnq", I["nsa_q_norm"][l], st)
            gks = self.gain_col("gks", I["nsa_k_norm"][l, 1], st)
            gkw = self.gain_col("gkw", I["nsa_k_norm"][l, 2], st)
            gkc = P.sb("gkc", [128, 64], F32, st)
            P.dma(gkc[:], I["nsa_k_norm"][l, 0].partition_broadcast(128))
            ksT = [P.sb("ksT", [128, S], BF16, st) for _ in range(2)]
            kwT = [P.sb("kwT", [128, S], BF16, st) for _ in range(2)]
            vs1 = P.sb("vs1", [128, NT, 2, 65], BF16, st)
            vw1 = P.sb("vw1", [128, NT, 2, 65], BF16, st)
            kcmpT = [P.sb("kcmpT", [128, 128], BF16, st) for _ in range(2)]
            vext = [P.sb("vext", [128, 97], BF16, st) for _ in range(2)]
            sq = P.sb("sq", [128, 512], F32, st)
            rs = P.sb("rs", [128, 512], F32, st)
            ss = P.sb("ss1", [128, 1], F32, st)
            bank = [P.ps("bk", [128, 512], F32, st) for _ in range(6)]
            tb = [P.ps("tb", [128, 8, 128], BF16, st) for _ in range(2)]
            Sb, Ob, Mb = bank[0:2], bank[2:4], bank[4:6]
            with P.scope() as st2:
                self._nsa_kside(l, hT, st2, locals())
            nqz = P.sb("nqz", [128, 16, 512], BF16, st)
            P.memset("pool", nqz[:], 0.0)
            gs = P.sb("gs", [128, 48], F32, st)
            yacc = P.sb("yacc", [128, 16, 64], F32, st)
            ytmp = P.sb("ytmp", [128, 4, 64], F32, st)
            impn = P.sb("impn", [128, 16, 32], F32, st)
            imp = P.sb("imp", [128, 32], F32, st)
            mx = P.sb("mx8", [128, 8], F32, st)
            negblk = P.sb("negblk", [128, 32], BF16, st)
            negblkT = [P.sb("negblkT", [32, 128], BF16, st) for _ in range(2)]
            pTb = [P.sb("pTb", [128, 512], BF16, st) for _ in range(3)]
            Sb = [bank[0], bank[1], bank[5]]
            ytm = P.sb("ytm", [128, 1024], BF16, st)
            yT = P.sb("yT", [128, 8, 512], BF16, st)
            rc = P.sb("rc", [128, 4], F32, st)
            cf = P.sb("cf", [128, 4], F32, st)
            Osb = [P.sb("Osb", [128, 4, 97], F32, st) for _ in range(2)]
            fin_cnt = [0]

            cnt = {"si": 0}
            oi = 0
            for Q in range(4):
                t0 = Q * 512
                for c in range(8):
                    self.proj_fm(Mb[0], Wq, c * 128, 128, hT, t0, 512)
                    P.act(sq[:, :], Mb[0][:, :], AF.Square)
                    P.mm(Mb[1][:, :], lhsT=C["blk64_f"][:], rhs=sq[:, :])
                    P.act(rs[:, :], Mb[1][:, :], AF.Sqrt, bias=EPS, scale=1.0 / 64)
                    P.recip(rs[:, :], rs[:, :])
                    for hf in range(2):
                        psl = slice(64 * hf, 64 * hf + 64)
                        P.stt("dve", nqz[psl, 2 * c + hf, :], Mb[0][psl, :], gq[psl, 0:1], rs[psl, :], ALU.mult, ALU.mult)
                for tl in range(4):
                    qt = Q * 4 + tl
                    if qt >= self.lim:
                        continue
                    P.maybe_barrier()
                    tsl = slice(tl * 128, (tl + 1) * 128)
                    self.proj_tm(Mb[0], Wng, 0, 48, hT, qt)
                    P.act(gs[:], Mb[0][:, 0:48], AF.Sigmoid)
                    gs3 = gs[:].rearrange("p (h b) -> p h b", b=3)

                    def scores(Sp, sub, kT_g, kt_cols, masks):
                        first = True
                        for (ml, mr) in masks:
                            kk = mr.shape[0]
                            P.mm(Sp[:, :], lhsT=ml, rhs=mr.unsqueeze(1).to_broadcast([kk, 4, 128]),
                                 start=first, stop=False, inc=False)
                            first = False
                        P.mm(Sp[:, :], lhsT=kT_g[:, kt_cols], rhs=nqz[:, 4 * sub:4 * sub + 4, tsl],
                             start=first, stop=True)

                    def finalize(O3p, sub, br, first):
                        w_ = 97 if br == 0 else 65
                        O3 = Osb[fin_cnt[0] % 2][:, :, 0:w_]
                        fin_cnt[0] += 1
                        P.cp("act", O3, O3p)
                        hs = slice(4 * sub, 4 * sub + 4)
                        P.ts("dve", rc[:], O3[:, :, 64], 1e-30, None, op0=ALU.max)
                        P.recip(rc[:], rc[:])
                        if br == 0:
                            P.tt("dve", impn[:, hs, :], O3[:, :, 65:97], rc[:].unsqueeze(2).to_broadcast([128, 4, 32]),
                                 ALU.mult)
                        P.tt("dve", cf[:], rc[:], gs3[:, hs, br], ALU.mult)
                        if first:
                            P.tt("dve", yacc[:, hs, :], O3[:, :, 0:64], cf[:].unsqueeze(2).to_broadcast([128, 4, 64]),
                                 ALU.mult)
                        else:
                            P.tt("dve", ytmp[:], O3[:, :, 0:64], cf[:].unsqueeze(2).to_broadcast([128, 4, 64]),
                                 ALU.mult)
                            P.tt("pool", yacc[:, hs, :], yacc[:, hs, :], ytmp[:], ALU.add)

                    items = []
                    for sub in range(4):
                        g = sub // 2
                        O = Ob[oi % 2]
                        oi += 1
                        O3 = O[:, 0:388].rearrange("p (h e) -> p h e", e=97)

                        def em_s(Sp, sub=sub, g=g):
                            scores(Sp, sub, kcmpT[g], slice(0, 128),
                                   [(C["ident"][:], C["visneg"][:, qt * 128:(qt + 1) * 128])])

                        def em_pv(pT, sub=sub, g=g, O3=O3):
                            for hh in range(4):
                                P.mm(O3[:, hh, :], lhsT=pT[:, hh * 128:(hh + 1) * 128], rhs=vext[g][:, :],
                                     start=(hh == 0), stop=True, inc=(hh == 3), sgc=True)
                            finalize(O3, sub, 0, True)
                        items.append((em_s, em_pv))
                    self.pipelined(items, Sb, pTb, cnt, depth=2)
                    for g in range(2):
                        P.op("dve", lambda e: e.tensor_reduce(out=imp[:], in_=impn[:, 8 * g:8 * g + 8, :].rearrange("p h j -> p j h"),
                                                              axis=AX.X, op=ALU.add),
                             [impn[:, 8 * g:8 * g + 8, :]], [imp[:]])
                        P.tt("dve", imp[:], imp[:], C["selkeep"][:, qt, :], ALU.mult)
                        P.tt("dve", imp[:], imp[:], C["selbias"][:, qt, :], ALU.add)
                        P.op("dve", lambda e: e.max(out=mx[:], in_=imp[:]), [imp[:]], [mx[:]])
                        P.ts("dve", negblk[:], imp[:], mx[:, 3:4], 1.0, op0=ALU.is_ge, op1=ALU.subtract)
                        P.tr(tb[0][0:32, 0, :], negblk[:], C["ident"][:])
                        P.amul(negblkT[g][:], tb[0][0:32, 0, :], -NEGM)
                    items = []
                    for br in (2, 1):
                        kts = list(range(0, qt + 1)) if br == 1 else list(range(max(0, qt - 4), qt + 1))
                        kT_l = ksT if br == 1 else kwT
                        v_l = vs1 if br == 1 else vw1
                        for sub in range(4):
                            g = sub // 2
                            O = Ob[oi % 2]
                            oi += 1
                            O3 = O[:, 0:260].rearrange("p (h e) -> p h e", e=65)
                            for ki, kt in enumerate(kts):
                                masks = []
                                if br == 1:
                                    masks.append((C["E"][:, kt * 128:(kt + 1) * 128], negblkT[g][:]))
                                if kt == qt:
                                    masks.append((C["ident"][:], C["cneg"][:]))
                                if br == 2 and kt == qt - 4:
                                    masks.append((C["ident"][:], C["wneg"][:]))

                                def em_s(Sp, sub=sub, g=g, kt=kt, masks=masks, kT_l=kT_l):
                                    scores(Sp, sub, kT_l[g], slice(kt * 128, (kt + 1) * 128), masks)

                                def em_pv(pT, sub=sub, g=g, kt=kt, ki=ki, nkt=len(kts), O3=O3, v_l=v_l, br=br):
                                    for hh in range(4):
                                        P.mm(O3[:, hh, :], lhsT=pT[:, hh * 128:(hh + 1) * 128], rhs=v_l[:, kt, g, :],
                                             start=(ki == 0 and hh == 0), stop=(ki == nkt - 1), inc=(hh == 3), sgc=True)
                                    if ki == nkt - 1:
                                        finalize(O3, sub, br, False)
                                items.append((em_s, em_pv))
                    self.pipelined(items, Sb, pTb, cnt, depth=2)
                    P.cp("act", ytm[:], yacc[:].rearrange("p h e -> p (h e)"))
                    t_ = tb[1]
                    for c in range(8):
                        P.tr(t_[:, c, :], ytm[:, c * 128:(c + 1) * 128], C["ident"][:], inc=(c == 7))
                    P.cp("act", yT[:, :, tsl], t_[:])
                P.dma(yb_d[:, :, t0:t0 + 512], yT[:], q="sp")

    def phase_ssd(self, l, hT, yc_d, zs_d, xact_d):
        P, C = self.P, self.C
        I = self.inp
        self.cast_engs = ["pool", "dve"]
        w_in = I["w_in"]
        with P.scope() as st:
            Wz = P.sb("Wz", [128, 8, 2048], BF16, st)
            self.wload(Wz, w_in, l, OFF["sz"], 2048)
            zt = [P.sb("zt", [128, 2048], F32, st) for _ in range(2)]
            bank = [P.ps("bk", [128, 512], F32, st) for _ in range(4)]
            bi = 0
            for tt in range(NT):
                z_t = zt[tt % 2]
                for nb in range(4):
                    ps = bank[bi % 4]
                    bi += 1
                    self.proj_tm(ps, Wz, nb * 512, 512, hT, tt)
                    P.act(z_t[:, nb * 512:(nb + 1) * 512], ps[:, :], AF.Silu)
                P.dma(zs_d[tt * 128:(tt + 1) * 128, :], z_t[:])
        with P.scope() as st:
            cw = P.sb("cw", [128, 24, 4], F32, st)
            cb = P.sb("cb", [128, 24], F32, st)
            for cc in range(24):
                P.dma(cw[:, cc, :], I["ssd_conv_w"][l][:, cc * 128:(cc + 1) * 128].rearrange("k c -> c k"),
                      allow_slow_non_contiguous=True)
                P.dma(cb[:, cc:cc + 1], I["ssd_conv_b"][l][cc * 128:(cc + 1) * 128].rearrange("(c o) -> c o", o=1))
            Wx = [P.sb("Wx", [128, 8, 512], BF16, st) for _ in range(2)]
            raw = [P.sb("raw", [128, 3 + S], F32, st) for _ in range(4)]
            accb = [P.sb("accb", [128, S], F32, st) for _ in range(4)]
            xa = [P.sb("xa", [128, S], BF16, st) for _ in range(4)]
            bank = [P.ps("bk", [128, 512], F32, st) for _ in range(4)]
            for r_ in raw:
                P.memset("pool", r_[:, 0:3], 0.0)
            bi = 0
            for c0 in range(0, 24, 2):
                pair = (c0, c0 + 1)
                for cc in pair:
                    W_ = Wx[(cc // 4) % 2]
                    if cc % 4 == 0:
                        self.wload(W_, w_in, l, OFF["sxbc"] + cc * 128, 512)
                    rw = raw[cc % 4]
                    for Q in range(4):
                        ps = bank[bi % 4]
                        bi += 1
                        self.proj_fm(ps, W_, (cc % 4) * 128, 128, hT, Q * 512, 512)
                        P.cp("act", rw[:, 3 + Q * 512:3 + (Q + 1) * 512], ps[:, :])
                for cc in pair:
                    P.ts("dve", accb[cc % 4][:], raw[cc % 4][:, 3:3 + S], cw[:, cc, 3:4], None, op0=ALU.mult)
                for k in (2, 1, 0):
                    for cc in pair:
                        P.stt("dve", accb[cc % 4][:], raw[cc % 4][:, k:k + S], cw[:, cc, k:k + 1], accb[cc % 4][:],
                              ALU.mult, ALU.add)
                for cc in pair:
                    P.act(xa[cc % 4][:], accb[cc % 4][:], AF.Silu, bias=cb[:, cc:cc + 1])
                    P.dma(xact_d[:, cc, :], xa[cc % 4][:])
                P.maybe_barrier()
        with P.scope() as st:
            Wdt = P.sb("Wdt", [128, 8, 32], BF16, st)
            self.wload(Wdt, w_in, l, OFF["sdt"], 32)
            dtb = P.sb("dtb", [128, 32], F32, st)
            P.dma(dtb[:], I["ssd_dt_bias"][l].partition_broadcast(128))
            a_b = P.sb("a_b", [128, 32], F32, st)
            P.dma(a_b[:], I["ssd_a_log"][l].partition_broadcast(128))
            P.act(a_b[:], a_b[:], AF.Exp)
            P.ts("dve", a_b[:], a_b[:], -1.0, None, op0=ALU.mult)
            D_b = P.sb("D_b", [128, 32], F32, st)
            P.dma(D_b[:], I["ssd_d"][l].partition_broadcast(128))
            ng_b = P.sb("ng_b", [128, 2048], F32, st)
            P.dma(ng_b[:], I["ssd_norm_g"][l].partition_broadcast(128))
            xab = [P.sb("xab", [128, 24, 128], BF16, st) for _ in range(2)]
            zsb = [P.sb("zsb", [128, 2048], F32, st) for _ in range(2)]
            xs_tm = P.sb("xs_tm", [128, 2048], BF16, st)
            bm_b = [P.sb("bm_tm", [128, 512], BF16, st) for _ in range(2)]
            xsw_b = [P.sb("xs_w", [128, 2048], BF16, st) for _ in range(2)]
            rhs1_b = [P.sb("rhs1", [128, 8, 128], F32, st) for _ in range(2)]
            rhs2_b = [P.sb("rhs2", [128, 8, 128], F32, st)] * 2
            Lg_b = [P.sb("Lg", [128, 8, 128], BF16, st) for _ in range(2)]
            MTg_b = [P.sb("MTg", [128, 8, 128], BF16, st) for _ in range(2)]
            tmp_b = [P.sb("tmpb", [128, 512], F32, st) for _ in range(2)]
            cbT = P.sb("cbT", [128, 4, 128], BF16, st)
            H = [P.sb("H", [128, 512], F32, st) for _ in range(4)]
            Hbf = [P.sb("Hbf", [128, 512], BF16, st) for _ in range(4)]
            y_b = [P.sb("y", [128, 2048], F32, st) for _ in range(2)]
            ynb = P.sb("ynb", [128, 2048], BF16, st)
            ycT = P.sb("ycT", [128, 16, 128], BF16, st)
            sma = {n: P.sb(n, [128, NT, 32], F32, st) for n in ("dtc", "da", "nb", "ea", "wj", "dec")}
            sma["lndt"] = sma["nb"]
            sma["acum"] = sma["ea"]
            sma["alast"] = sma["dec"]
            ss = P.sb("ss2", [128, 1], F32, st)
            R = [P.ps("R", [128, 512], F32, st) for _ in range(2)]
            Yp_b = [P.ps("Yp", [128, 512], F32, st) for _ in range(2)]
            Yo = P.ps("Yo", [128, 512], F32, st)
            STp = P.ps("STp", [128, 512], F32, st)
            M0 = P.ps("M0", [128, 512], F32, st)
            M1 = M0
            tb = P.ps("tb", [128, 8, 128], BF16, st)
            for g in range(4):
                P.memset("pool", H[g][:], 0.0)
                P.memset("pool", Hbf[g][:], 0.0)
            nch = min(NT, self.lim)
            fl = lambda t: t[:].rearrange("p c h -> p (c h)")
            for c in range(NT):
                for kc in range(8):
                    P.mm(M1[:, c * 32:(c + 1) * 32], lhsT=hT[:, kc, c * 128:(c + 1) * 128], rhs=Wdt[:, kc, :],
                         start=(c == 0 and kc == 0), stop=(kc == 7), inc=(kc == 7), sgc=True)
            P.tt("dve", sma["dtc"][:], M1[:, :].rearrange("p (c h) -> p c h", h=32),
                 dtb[:].unsqueeze(1).to_broadcast([128, NT, 32]), ALU.add)
            P.act(fl(sma["dtc"]), fl(sma["dtc"]), AF.Exp)
            P.act(fl(sma["dtc"]), fl(sma["dtc"]), AF.Ln, bias=1.0)
            P.act(fl(sma["lndt"]), fl(sma["dtc"]), AF.Ln)
            P.tt("dve", sma["da"][:], sma["dtc"][:], a_b[:].unsqueeze(1).to_broadcast([128, NT, 32]), ALU.mult)
            P.mm(M1[:, :], lhsT=C["tri_f"][:], rhs=fl(sma["da"]))
            P.cp("dve", fl(sma["acum"]), M1[:, :])
            P.mm(M1[:, :], lhsT=C["ones_f"][:], rhs=fl(sma["da"]))
            P.cp("dve", fl(sma["alast"]), M1[:, :])
            P.tt("dve", sma["nb"][:], sma["lndt"][:], sma["acum"][:], ALU.subtract)
            P.tt("dve", sma["wj"][:], sma["alast"][:], sma["acum"][:], ALU.subtract)
            P.act(fl(sma["wj"]), fl(sma["wj"]), AF.Exp)
            P.tt("dve", sma["wj"][:], sma["wj"][:], sma["dtc"][:], ALU.mult)
            P.act(fl(sma["ea"]), fl(sma["acum"]), AF.Exp)
            P.act(fl(sma["dec"]), fl(sma["alast"]), AF.Exp)
            sma = {k: sma[k] for k in ("da", "nb", "ea", "wj", "dec")}

            def load(c):
                csl = slice(c * 128, (c + 1) * 128)
                P.dma(xab[c % 2][:], xact_d[:, :, csl])
                P.dma(zsb[c % 2][:], zs_d[csl, :])

            def part1(c):
                p = c % 2
                xa_ = xab[p]
                sm = {k: v[:, c, :] for k, v in sma.items()}
                bm_tm, xs_w, y = bm_b[p], xsw_b[p], y_b[p]
                for k0 in (0, 8, 16):
                    n = 8 if k0 < 16 else 4
                    for j in range(n):
                        P.tr(tb[:, j, :], xa_[:, k0 + j, :], C["ident"][:], inc=(j == n - 1))
                    if k0 < 16:
                        P.cp("act", xs_tm[:, k0 * 128:(k0 + 8) * 128], tb[:].rearrange("p a b -> p (a b)"))
                    else:
                        P.cp("act", bm_tm[:], tb[:, 0:4, :].rearrange("p a b -> p (a b)"))
                for g in range(4):
                    P.mm(M0[:, g * 128:(g + 1) * 128], lhsT=xa_[:, 16 + g, :], rhs=xa_[:, 20 + g, :],
                         start=(g == 0), stop=True, inc=(g == 3), sgc=True)
                P.cp("act", cbT[:].rearrange("p a b -> p (a b)"), M0[:, :])
                P.tt("dve", xs_w[:].rearrange("p (h e) -> p h e", e=64), xs_tm[:].rearrange("p (h e) -> p h e", e=64),
                     sm["wj"][:].unsqueeze(2).to_broadcast([128, 32, 64]), ALU.mult)
                for g in range(4):
                    hs = slice(8 * g, 8 * g + 8)
                    gsl = slice(g * 512, (g + 1) * 512)
                    rhs1, rhs2, Lg, MTg, Yp, tmp = rhs1_b[g % 2], rhs2_b[g % 2], Lg_b[g % 2], MTg_b[g % 2], Yp_b[g % 2], tmp_b[g % 2]
                    P.tt("dve", rhs1[:], C["tri_f"][:].unsqueeze(1).to_broadcast([128, 8, 128]),
                         sm["da"][:, hs].unsqueeze(2).to_broadcast([128, 8, 128]), ALU.mult)
                    P.tt("dve", rhs2[:], C["cneg_f"][:].unsqueeze(1).to_broadcast([128, 8, 128]),
                         sm["nb"][:, hs].unsqueeze(2).to_broadcast([128, 8, 128]), ALU.add)
                    for hf in range(2):
                        P.mm(R[hf][:, :], lhsT=C["ones_f"][:], rhs=rhs1[:, 4 * hf:4 * hf + 4, :].rearrange("p a b -> p (a b)"),
                             start=True, stop=False, inc=False)
                        P.mm(R[hf][:, :], lhsT=C["ident_f"][:], rhs=rhs2[:, 4 * hf:4 * hf + 4, :].rearrange("p a b -> p (a b)"),
                             start=False, stop=True)
                        P.act(Lg[:, 4 * hf:4 * hf + 4, :].rearrange("p a b -> p (a b)"), R[hf][:, :], AF.Exp)
                    P.tt("dve", MTg[:], Lg[:], cbT[:, g, :].unsqueeze(1).to_broadcast([128, 8, 128]), ALU.mult)
                    for hh in range(8):
                        P.mm(Yp[:, hh * 64:(hh + 1) * 64], lhsT=MTg[:, hh, :], rhs=xs_tm[:, (8 * g + hh) * 64:(8 * g + hh + 1) * 64],
                             start=(hh == 0), stop=True, inc=(hh == 7), sgc=True)
                    P.tt("dve", tmp[:].rearrange("p (h e) -> p h e", e=64), xs_tm[:, gsl].rearrange("p (h e) -> p h e", e=64),
                         D_b[:, hs].unsqueeze(2).to_broadcast([128, 8, 64]), ALU.mult)
                    P.tt("dve", y[:, gsl], Yp[:, :], tmp[:], ALU.add)

            def part2(c):
                p = c % 2
                csl = slice(c * 128, (c + 1) * 128)
                xa_ = xab[p]
                zs_ = zsb[p]
                sm = {k: v[:, c, :] for k, v in sma.items()}
                bm_tm, xs_w, y = bm_b[p], xsw_b[p], y_b[p]
                for g in range(4):
                    hs = slice(8 * g, 8 * g + 8)
                    gsl = slice(g * 512, (g + 1) * 512)
                    tmp = tmp_b[g % 2]
                    P.mm(Yo[:, :], lhsT=xa_[:, 20 + g, :], rhs=Hbf[g][:])
                    P.tt("dve", tmp[:].rearrange("p (h e) -> p h e", e=64), Yo[:, :].rearrange("p (h e) -> p h e", e=64),
                         sm["ea"][:, hs].unsqueeze(2).to_broadcast([128, 8, 64]), ALU.mult)
                    P.tt("dve", y[:, gsl], y[:, gsl], tmp[:], ALU.add)
                    P.mm(STp[:, :], lhsT=bm_tm[:, g * 128:(g + 1) * 128], rhs=xs_w[:, gsl])
                    Hv = H[g][:].rearrange("p (h e) -> p h e", e=64)
                    P.tt("dve", Hv, Hv, sm["dec"][:, hs].unsqueeze(2).to_broadcast([128, 8, 64]), ALU.mult)
                    P.tt("dve", H[g][:], STp[:, :], H[g][:], ALU.add)
                    P.cp("act", Hbf[g][:], H[g][:])
                P.tt("dve", y[:], y[:], zs_[:], ALU.mult)
                P.act(zs_[:], y[:], AF.Square, accum=ss[:])
                P.act(ss[:], ss[:], AF.Sqrt, bias=EPS, scale=1.0 / 2048)
                P.recip(ss[:], ss[:])
                P.stt("dve", ynb[:], y[:], ss[:, 0:1], ng_b[:], ALU.mult, ALU.mult)
                for k0 in (0, 8):
                    for j in range(8):
                        P.tr(tb[:, j, :], ynb[:, (k0 + j) * 128:(k0 + j + 1) * 128], C["ident"][:], inc=(j == 7))
                    P.cp("act", ycT[:, k0:k0 + 8, :], tb[:])
                P.dma(yc_d[:, :, csl], ycT[:])

            if nch > 0:
                load(0)
                if nch > 1:
                    load(1)
                part1(0)
            for c in range(nch):
                if P.sems_left() < 40:
                    P.barrier()
                if c + 1 < nch:
                    part1(c + 1)
                part2(c)
                if c + 2 < nch:
                    load(c + 2)

    def phase_merge(self, l, hT, ya_d, yb_d, yc_d, mix_d):
        P, C = self.P, self.C
        I = self.inp
        self.cast_engs = ["pool", "dve"]
        with P.scope() as st:
            yh = [P.sb("yha", [128, 8, 1024], BF16, st), P.sb("yhb", [128, 8, 1024], BF16, st),
                  P.sb("yhc", [128, 16, 1024], BF16, st)]
            Wy = [[P.sb("Wya", [128, 8, 128], BF16, st), P.sb("Wyb", [128, 8, 128], BF16, st),
                   P.sb("Wyc", [128, 16, 128], BF16, st)] for _ in range(2)]
            Wg = [[P.sb("Wg", [128, 8, 128], BF16, st) for _ in range(3)] for _ in range(2)]
            sg = [P.sb("sg", [128, 512], F32, st) for _ in range(2)]
            macc = P.sb("macc", [128, 512], F32, st)
            mt = P.sb("mt", [128, 512], F32, st)
            mixh = P.sb("mixh", [128, 8, 1024], BF16, st)
            bank = [P.ps("bk", [128, 512], F32, st) for _ in range(6)]
            wsrc = [I["w_br_dsa"], I["w_br_nsa"], I["w_br_ssd"]]
            ysrc = [ya_d, yb_d, yc_d]
            bi = 0
            gi = 0
            it = 0
            for half in range(2):
                hsl = slice(half * 1024, (half + 1) * 1024)
                for br in range(3):
                    P.dma(yh[br][:], ysrc[br][:, :, hsl])
                for oc in range(8):
                    Wy_ = Wy[it % 2]
                    Wg_ = Wg[it % 2]
                    it += 1
                    for br in range(3):
                        self.wload(Wy_[br], wsrc[br], l, oc * 128, 128)
                        self.wload(Wg_[br], I["w_in"], l, OFF["mg"] + br * 1024 + oc * 128, 128)
                    for pc in range(2):
                        psl = slice(pc * 512, (pc + 1) * 512)
                        tok = half * 1024 + pc * 512
                        for br in range(3):
                            psy = bank[bi % 6]
                            bi += 1
                            psg = bank[bi % 6]
                            bi += 1
                            nk = 16 if br == 2 else 8
                            for kc in range(nk):
                                P.mm(psy[:, :], lhsT=Wy_[br][:, kc, :], rhs=yh[br][:, kc, psl],
                                     start=(kc == 0), stop=(kc == nk - 1))
                            self.proj_fm(psg, Wg_[br], 0, 128, hT, tok, 512)
                            s_ = sg[gi % 2]
                            gi += 1
                            P.act(s_[:], psg[:, :], AF.Sigmoid)
                            if br == 0:
                                P.tt("dve", macc[:], psy[:, :], s_[:], ALU.mult)
                            else:
                                P.tt("dve", mt[:], psy[:, :], s_[:], ALU.mult)
                                if br == 1:
                                    P.tt("pool", macc[:], macc[:], mt[:], ALU.add)
                                else:
                                    P.tt("pool", mixh[:, oc, psl], macc[:], mt[:], ALU.add)
                    P.maybe_barrier()
                P.dma(mix_d[:, :, hsl], mixh[:])

    def phase_out_norm(self, l, x_d, mix_d, x1_d, hT):
        P, C = self.P, self.C
        I = self.inp
        self.cast_engs = ["pool", "dve"]
        with P.scope() as st:
            Wo = P.sb("Wo", [128, 8, 1024], BF16, st)
            self.wload(Wo, I["w_out"], l, 0, 1024)
            gb = P.sb("gb2", [128, D], F32, st)
            P.dma(gb[:], I["norm2_g"][l].partition_broadcast(128))
            mq = [P.sb("mq", [128, 8, 128], BF16, st) for _ in range(2)]
            xb = [P.sb("xb", [128, D], F32, st) for _ in range(2)]
            x1b = [P.sb("x1b", [128, D], F32, st) for _ in range(2)]
            sq = P.sb("sq", [128, D], F32, st)
            hb = [P.sb("hb", [128, D], BF16, st) for _ in range(2)]
            ss = P.sb("ss", [128, 2], F32, st)
            bank = [P.ps("bk", [128, 512], F32, st) for _ in range(4)]
            pt = [P.ps("pt", [128, 8, 128], BF16, st) for _ in range(2)]
            for tt in range(NT):
                tsl = slice(tt * 128, (tt + 1) * 128)
                m_ = mq[tt % 2]
                P.dma(m_[:], mix_d[:, :, tsl])
                x_t = xb[tt % 2]
                P.dma(x_t[:], x_d[tsl, :])
                x1 = x1b[tt % 2]
                for hf in range(2):
                    ps = bank[(2 * tt + hf) % 4]
                    for oc in range(8):
                        P.mm(ps[:, :], lhsT=m_[:, oc, :], rhs=Wo[:, oc, hf * 512:(hf + 1) * 512],
                             start=(oc == 0), stop=(oc == 7))
                    P.tt("dve", x1[:, hf * 512:(hf + 1) * 512], ps[:, :], x_t[:, hf * 512:(hf + 1) * 512], ALU.add)
                P.dma(x1_d[tsl, :], x1[:])
                s1 = ss[:, tt % 2:tt % 2 + 1]
                P.act(sq[:], x1[:], AF.Square, accum=s1)
                P.act(s1, s1, AF.Sqrt, bias=EPS, scale=1.0 / D)
                P.recip(s1, s1)
                h_t = hb[tt % 2]
                P.stt("dve", h_t[:], x1[:], s1, gb[:], ALU.mult, ALU.mult)
                p_t = pt[tt % 2]
                for kc in range(8):
                    P.tr(p_t[:, kc, :], h_t[:, kc * 128:(kc + 1) * 128], C["ident"][:], inc=(kc == 7))
                P.cp("act", hT[:, :, tsl], p_t[:])

    def phase_ffn(self, l, hT, x1_d, xo_d, xp_d):
        P, C = self.P, self.C
        I = self.inp
        self.cast_engs = ["pool", "dve", "act", "dve"]
        with P.scope() as st:
            W1h = P.sb("W1h", [128, 8, 2048], BF16, st)
            W2h = P.sb("W2h", [128, 16, 1024], BF16, st)
            aT = P.sb("aT", [128, 16, 512], BF16, st)
            rb = [P.sb("rbf", [128, 512], F32, st) for _ in range(2)]
            xt = [P.sb("xt", [128, 512], F32, st) for _ in range(4)]
            xo = [P.sb("xo", [128, 512], F32, st) for _ in range(4)]
            fb = [P.ps("fb", [128, 512], F32, st) for _ in range(2)]
            ob = [P.ps("ob", [128, 512], F32, st) for _ in range(4)]
            xi = 0
            for hp in range(2):
                self.wload(W1h, I["w_ff1"], l, hp * 2048, 2048)
                self.wload(W2h, I["w_ff2"], l, 0, 1024, rows=(hp * 2048, (hp + 1) * 2048))
                src_d = x1_d if hp == 0 else xp_d
                dst_d = xp_d if hp == 0 else xo_d
                for Q in range(4):
                    for fc in range(16):
                        ps = fb[fc % 2]
                        self.proj_fm(ps, W1h, fc * 128, 128, hT, Q * 512, 512)
                        r = rb[fc % 2]
                        P.act(r[:], ps[:, :], AF.Relu)
                        P.tt("dve", aT[:, fc, :], r[:], r[:], ALU.mult)
                    for hf in range(2):
                        cols = slice(hf * 512, (hf + 1) * 512)
                        xs_ = []
                        for tl in range(4):
                            rows = slice((Q * 4 + tl) * 128, (Q * 4 + tl + 1) * 128)
                            x_ = xt[xi % 4]
                            o_ = xo[xi % 4]
                            xi += 1
                            P.dma(x_[:], src_d[rows, cols])
                            xs_.append((x_, o_, rows))
                            for k in range(16):
                                P.mm(ob[tl][:, :], lhsT=aT[:, k, tl * 128:(tl + 1) * 128], rhs=W2h[:, k, cols],
                                     start=(k == 0), stop=(k == 15))
                        for tl in range(4):
                            x_, o_, rows = xs_[tl]
                            P.tt("dve", o_[:], ob[tl][:, :], x_[:], ALU.add)
                            P.dma(dst_d[rows, cols], o_[:])
                    P.maybe_barrier()

    def build(self):
        P = self.P
        I = self.inp
        self.din("x", [S, D])
        for name, shp in (("norm1_g", [DEPTH, D]), ("w_in", [DEPTH, D, D_IN]), ("dsa_q_norm", [DEPTH, 64]),
                          ("dsa_k_norm", [DEPTH, 64]), ("nsa_q_norm", [DEPTH, 64]), ("nsa_k_norm", [DEPTH, 3, 64]),
                          ("nsa_cmp_pos", [DEPTH, 2, 32, 64]), ("nsa_cmp_w", [DEPTH, 2, 32, 64, 64]),
                          ("ssd_conv_w", [DEPTH, 4, 3072]), ("ssd_conv_b", [DEPTH, 3072]),
                          ("ssd_dt_bias", [DEPTH, 32]), ("ssd_a_log", [DEPTH, 32]), ("ssd_d", [DEPTH, 32]),
                          ("ssd_norm_g", [DEPTH, 2048]), ("w_br_dsa", [DEPTH, D, D]), ("w_br_nsa", [DEPTH, D, D]),
                          ("w_br_ssd", [DEPTH, 2 * D, D]), ("w_out", [DEPTH, D, D]), ("norm2_g", [DEPTH, D]),
                          ("w_ff1", [DEPTH, D, 4 * D]), ("w_ff2", [DEPTH, 4 * D, D])):
            self.din(name, shp)
        self.load_consts()
        out = self.dout("out", [S, D])
        hT = P.sb("hT", [128, 8, S], BF16)
        ya_d = P.dram("ya", [128, 8, S], BF16)
        yb_d = P.dram("yb", [128, 8, S], BF16)
        yc_d = P.dram("yc", [128, 16, S], BF16)
        zs_d = P.dram("zs", [S, 2048], F32)
        xact_d = P.dram("xact", [128, 24, S], BF16)
        mix_d = P.dram("mix", [128, 8, S], BF16)
        x1_d = P.dram("x1", [S, D], F32)
        xm_d = P.dram("xm", [S, D], F32)
        xp_d = P.dram("xp", [S, D], F32)
        dbg = self.debug
        x_cur = I["x"]
        for l in range(self.nlayers):
            x_nxt = out if l == self.nlayers - 1 else xm_d
            self.phase_norm(x_cur, I["norm1_g"][l], hT)
            if dbg == "hT":
                return self._dump(hT[:], [128, 8, S], BF16)
            if dbg in ("nsa", "nsa_k"):
                self.phase_nsa(l, hT, yb_d)
                return self._dump(yb_d, [128, 8, S], BF16)
            if dbg == "ssd":
                self.phase_ssd(l, hT, yc_d, zs_d, xact_d)
                return self._dump(yc_d, [128, 16, S], BF16)
            self.phase_dsa(l, hT, ya_d)
            if dbg is not None and dbg.startswith("dsa"):
                return self._dump(ya_d, [128, 8, S], BF16)
            self.phase_nsa(l, hT, yb_d)
            self.phase_ssd(l, hT, yc_d, zs_d, xact_d)
            self.phase_merge(l, hT, ya_d, yb_d, yc_d, mix_d)
            if dbg == "mix":
                return self._dump(mix_d, [128, 8, S], BF16)
            self.phase_out_norm(l, x_cur, mix_d, x1_d, hT)
            if dbg == "x1":
                return self._dump(x1_d, [S, D], F32)
            self.phase_ffn(l, hT, x1_d, x_nxt, xp_d)
            x_cur = x_nxt
        P.finish()

    def _dump(self, src, shape, dt):
        if "dbg" not in self.out:
            o = self.dout("dbg", shape, dt)
            self.P.dma(o, src)
        self.P.finish()


_CACHE = {}


def kernel(**inputs):
    n = 8
    B = Builder()
    B.build()
    consts = _consts()
    shared = {}
    for name in B.inp:
        if name in consts:
            shared[name] = consts[name]
        elif name != "x":
            shared[name] = np.ascontiguousarray(np.asarray(inputs[name], dtype=np.float32))
    x = np.asarray(inputs["x"], dtype=np.float32)
    in_maps = []
    for b in range(n):
        m = dict(shared)
        m["x"] = np.ascontiguousarray(x[b])
        in_maps.append(m)
    res = run_bass_kernel_spmd(B.nc, in_maps, core_ids=list(range(n)))
    return np.stack([np.asarray(res.results[b]["out"], dtype=np.float32) for b in range(n)], axis=0)
```
